# Optimizing a Trainium2 kernel written in Bass

```python
import jax, jax.numpy as jnp
from jax import lax
import numpy as np


D_MODEL = 1024
BATCH = 4
SEQ = 4096
DEPTH = 2
DEC_BATCH = 128
DEC_SEQ = 8
PAST_LEN = 2048
PAGE_SIZE = 128

N_BRANCH = 4
MIX_W = D_MODEL // 4
SB_HEADS = 4
SB_HD = MIX_W // SB_HEADS
SB_BLOCK = 128
SB_BIAS_INIT = -7.0
CONV_W = 3
RW_HD = 64
RW_HEADS = MIX_W // RW_HD
RW_DECAY_LORA = 32
RW_A_LORA = 32
RW_G_LORA = 64
RW_COLS = 3 * MIX_W + RW_DECAY_LORA + RW_A_LORA + RW_G_LORA
RW_GN_EPS = 64e-5
CHUNK = 128
SGU_GROUPS = 4
SGU_GD = MIX_W // SGU_GROUPS
N_MEM = 256
X_HEADS = 4
X_HD = D_MODEL // X_HEADS
D_FF = 4 * D_MODEL
ALPHA = (2 * DEPTH) ** 0.25
BETA = (8 * DEPTH) ** -0.25
LN_EPS = 1e-5
OFF_B = 3 * MIX_W
OFF_C = 6 * MIX_W
OFF_D = OFF_C + RW_COLS
IN_COLS = OFF_D + 2 * MIX_W

kernel_name = 'stickbreak_conv_rwkv7_sgu_hybrid_step'


def _ln(x, g, b, eps=LN_EPS):
    xf = x.astype(jnp.float32)
    mu = jnp.mean(xf, -1, keepdims=True)
    var = jnp.mean(jnp.square(xf - mu), -1, keepdims=True)
    return ((xf - mu) * lax.rsqrt(var + eps) * g + b).astype(x.dtype)


def _sb_scores(q, q_pos, k, v, k_pos, bias):
    z = jnp.einsum('nqhe,nkhe->nhqk', q, k, preferred_element_type=jnp.float32) * (SB_HD ** -0.5)
    z = z + bias.astype(jnp.float32)[None, :, None, None]
    causal = k_pos[None, :] < q_pos[:, None]
    log_stop = jnp.where(causal, jax.nn.log_sigmoid(-z), 0.0)
    later = lax.cumsum(log_stop, axis=3, reverse=True) - log_stop
    wts = jnp.where(causal, jnp.exp(jax.nn.log_sigmoid(z) + later), 0.0)
    return jnp.einsum('nhqk,nkhe->nqhe', wts.astype(v.dtype), v)


def _sb_prompt(q, k, v, bias):
    n, L = q.shape[0], q.shape[1]
    nb = L // SB_BLOCK
    pos = jnp.arange(L, dtype=jnp.int32)
    qb = q.reshape(n, nb, SB_BLOCK, SB_HEADS, SB_HD).swapaxes(0, 1)
    pb = pos.reshape(nb, SB_BLOCK)
    out = lax.map(lambda a: _sb_scores(a[0], a[1], k, v, pos, bias), (qb, pb))
    return out.swapaxes(0, 1).reshape(n, L, SB_HEADS, SB_HD)


def _short_conv(z, prev, w):
    L = z.shape[1]
    zp = jnp.concatenate([prev, z], 1)
    y = zp[:, 0:L] * w[0]
    for j in range(1, CONV_W):
        y = y + zp[:, j:j + L] * w[j]
    return y, zp[:, -(CONV_W - 1):]


def _rwkv7(p, shift_prev, S0, mu, w0, w2, a0, a2, g2, kk_s, ka_s, rk, gn_g, gn_b):
    n, L = p.shape[0], p.shape[1]
    f32 = jnp.float32
    p_prev = jnp.concatenate([shift_prev[:, None], p[:, :-1]], 1)
    xs = p + (p_prev - p) * mu
    r = xs[..., 0:MIX_W]
    k = xs[..., MIX_W:2 * MIX_W]
    v = xs[..., 2 * MIX_W:3 * MIX_W]
    o = 3 * MIX_W
    wl = xs[..., o:o + RW_DECAY_LORA]
    o = o + RW_DECAY_LORA
    al = xs[..., o:o + RW_A_LORA]
    gl = xs[..., o + RW_A_LORA:]
    w_log = -jax.nn.softplus(-(w0 + jnp.tanh(wl) @ w2).astype(f32)) - 0.5
    decay = jnp.exp(-jnp.exp(w_log))
    a = jax.nn.sigmoid((a0 + al @ a2).astype(f32))
    g = (jax.nn.sigmoid(gl) @ g2).astype(f32)
    hs = lambda t: t.reshape(n, L, RW_HEADS, RW_HD)
    kk = hs((k * kk_s).astype(f32))
    kk = kk / jnp.maximum(jnp.sqrt(jnp.sum(kk * kk, -1, keepdims=True)), 1e-12)
    k_eff = hs(k.astype(f32) * (1.0 + (a - 1.0) * ka_s))
    r_h = hs(r.astype(f32))
    v_h = hs(v.astype(f32))
    seqs = tuple(t.swapaxes(0, 1) for t in (r_h, hs(decay), k_eff, v_h, -kk, kk * hs(a)))

    def step(S, inp):
        r_t, w_t, k_t, v_t, a_t, b_t = inp
        sa = jnp.einsum('nhij,nhj->nhi', S, a_t)
        S = S * w_t[:, :, None, :] + sa[..., None] * b_t[:, :, None, :] + v_t[..., None] * k_t[:, :, None, :]
        return S, jnp.einsum('nhij,nhj->nhi', S, r_t)

    S_fin, y = lax.scan(step, S0.astype(f32), seqs)
    y = y.swapaxes(0, 1)
    m = jnp.mean(y, -1, keepdims=True)
    var = jnp.mean(jnp.square(y - m), -1, keepdims=True)
    y = ((y - m) * lax.rsqrt(var + RW_GN_EPS)).reshape(n, L, MIX_W) * gn_g + gn_b
    bonus = (jnp.sum(r_h * k_eff * rk, -1, keepdims=True) * v_h).reshape(n, L, MIX_W)
    y = (y + bonus) * g
    return y.astype(p.dtype), p[:, -1], S_fin.astype(S0.dtype)


def _sgu(z, ln_g, ln_b, ws, bias, prompt):
    n, L = z.shape[0], z.shape[1]
    z = jax.nn.gelu(z)
    u, v = z[..., 0:MIX_W], z[..., MIX_W:]
    v = _ln(v, ln_g, ln_b)
    ws_c = ws * jnp.tril(jnp.ones((CHUNK, CHUNK), ws.dtype))
    if prompt:
        vc = v.reshape(n, L // CHUNK, CHUNK, SGU_GROUPS, SGU_GD)
        mixed = jnp.einsum('gts,ncsge->nctge', ws_c, vc) + bias.T[None, None, :, :, None]
    else:
        vc = v.reshape(n, L, SGU_GROUPS, SGU_GD)
        mixed = jnp.einsum('gts,nsge->ntge', ws_c[:, :L, :L], vc) + bias[:, :L].T[None, :, :, None]
    return u * mixed.reshape(n, L, MIX_W), v


def _layer(x, lp, mem_k, mem_v, past_k, past_v, conv_prev, shift_prev, wkv_prev, prompt):
    n, L = x.shape[0], x.shape[1]
    h = x @ lp['w_in']
    q = h[..., 0:MIX_W].reshape(n, L, SB_HEADS, SB_HD)
    k = h[..., MIX_W:2 * MIX_W].reshape(n, L, SB_HEADS, SB_HD)
    v = h[..., 2 * MIX_W:3 * MIX_W].reshape(n, L, SB_HEADS, SB_HD)
    if prompt:
        ya = _sb_prompt(q, k, v, lp['sb_bias'])
    else:
        past_len = past_k.shape[1]
        k_all = jnp.concatenate([past_k, k], 1)
        v_all = jnp.concatenate([past_v, v], 1)
        q_pos = past_len + jnp.arange(L, dtype=jnp.int32)
        k_pos = jnp.arange(past_len + L, dtype=jnp.int32)
        ya = _sb_scores(q, q_pos, k_all, v_all, k_pos, lp['sb_bias'])
    gb = h[..., OFF_B:OFF_B + MIX_W]
    gc = h[..., OFF_B + MIX_W:OFF_B + 2 * MIX_W]
    hb = h[..., OFF_B + 2 * MIX_W:OFF_C]
    conv_out, conv_new = _short_conv(gc * hb, conv_prev, lp['conv_w'])
    yb = gb * conv_out
    yc, shift_new, wkv_new = _rwkv7(h[..., OFF_C:OFF_D], shift_prev, wkv_prev, lp['rw_mu'], lp['rw_w0'],
                                    lp['rw_w2'], lp['rw_a0'], lp['rw_a2'], lp['rw_g2'], lp['rw_kk'],
                                    lp['rw_ka'], lp['rw_rk'], lp['rw_gn_g'], lp['rw_gn_b'])
    yd, sgu_v = _sgu(h[..., OFF_D:], lp['sgu_ln_g'], lp['sgu_ln_b'], lp['sgu_ws'], lp['sgu_b'], prompt)
    br = jnp.stack([ya.reshape(n, L, MIX_W), yb, yc, yd], axis=2)
    proj = jnp.einsum('nlic,icd->nlid', br, lp['w_branch'])
    gates = jax.nn.sigmoid(x @ lp['w_gate'] + lp['b_gate']).reshape(n, L, N_BRANCH, D_MODEL)
    mix = jnp.sum(gates * proj, axis=2) @ lp['w_o']
    x = _ln(ALPHA * x + mix, lp['ln1_g'], lp['ln1_b'])
    qm = (x @ lp['w_mq']).reshape(n, L, X_HEADS, X_HD)
    s = jnp.einsum('nqhe,nkhe->nhqk', qm, mem_k, preferred_element_type=jnp.float32) * (X_HD ** -0.5)
    pr = jax.nn.softmax(s, axis=-1)
    att = jnp.einsum('nhqk,nkhe->nqhe', pr.astype(mem_v.dtype), mem_v).reshape(n, L, D_MODEL) @ lp['w_mo']
    x = _ln(ALPHA * x + att, lp['ln2_g'], lp['ln2_b'])
    ff = jnp.square(jax.nn.relu(x @ lp['w_up'])) @ lp['w_down']
    x = _ln(ALPHA * x + ff, lp['ln3_g'], lp['ln3_b'])
    return x, k, v, conv_new, shift_new, wkv_new, sgu_v


def setup_inputs(seed: int = 0) -> dict:
    key = jax.random.key(seed)
    keys = iter(jax.random.split(key, 64))
    nrm = lambda shape, s=1.0: jax.random.normal(next(keys), shape, jnp.float32) * s
    n_pages = PAST_LEN // PAGE_SIZE
    n_used = DEC_BATCH * n_pages
    n_phys = (5 * n_used) // 4
    perm = jax.random.permutation(next(keys), n_phys)
    page_table = perm[:n_used].reshape(DEC_BATCH, n_pages).astype(jnp.int32)
    Dd = DEPTH
    return {
        'x_prompt': nrm((BATCH, SEQ, D_MODEL)),
        'x_sample': nrm((DEC_BATCH, DEC_SEQ, D_MODEL)),
        'mem_prompt': nrm((BATCH, N_MEM, D_MODEL)),
        'cache_k': nrm((Dd, n_phys, PAGE_SIZE, SB_HEADS, SB_HD)),
        'cache_v': nrm((Dd, n_phys, PAGE_SIZE, SB_HEADS, SB_HD)),
        'page_table': page_table,
        'cache_mem_k': nrm((Dd, DEC_BATCH, N_MEM, X_HEADS, X_HD)),
        'cache_mem_v': nrm((Dd, DEC_BATCH, N_MEM, X_HEADS, X_HD)),
        'state_conv': nrm((Dd, DEC_BATCH, CONV_W - 1, MIX_W)),
        'state_wkv': nrm((Dd, DEC_BATCH, RW_HEADS, RW_HD, RW_HD), 0.3),
        'state_shift': nrm((Dd, DEC_BATCH, RW_COLS)),
        'w_in': nrm((Dd, D_MODEL, IN_COLS), D_MODEL ** -0.5),
        'sb_bias': SB_BIAS_INIT + nrm((Dd, SB_HEADS), 0.1),
        'w_gate': nrm((Dd, D_MODEL, N_BRANCH * D_MODEL), D_MODEL ** -0.5),
        'b_gate': nrm((Dd, N_BRANCH * D_MODEL), 0.01),
        'w_branch': nrm((Dd, N_BRANCH, MIX_W, D_MODEL), MIX_W ** -0.5),
        'w_o': nrm((Dd, D_MODEL, D_MODEL), BETA * D_MODEL ** -0.5),
        'conv_w': nrm((Dd, CONV_W, MIX_W), CONV_W ** -0.5),
        'rw_mu': jax.random.uniform(next(keys), (Dd, RW_COLS), jnp.float32, 0.0, 1.0),
        'rw_w0': jax.random.uniform(next(keys), (Dd, MIX_W), jnp.float32, -6.0, 1.0),
        'rw_w2': nrm((Dd, RW_DECAY_LORA, MIX_W), 0.1 * RW_DECAY_LORA ** -0.5),
        'rw_a0': nrm((Dd, MIX_W), 0.1),
        'rw_a2': nrm((Dd, RW_A_LORA, MIX_W), RW_A_LORA ** -0.5),
        'rw_g2': nrm((Dd, RW_G_LORA, MIX_W), RW_G_LORA ** -0.5),
        'rw_kk': 0.85 + nrm((Dd, MIX_W), 0.05),
        'rw_ka': 1.0 + nrm((Dd, MIX_W), 0.05),
        'rw_rk': nrm((Dd, RW_HEADS, RW_HD), 0.1),
        'rw_gn_g': 1.0 + nrm((Dd, MIX_W), 0.05),
        'rw_gn_b': nrm((Dd, MIX_W), 0.05),
        'sgu_ln_g': 1.0 + nrm((Dd, MIX_W), 0.05),
        'sgu_ln_b': nrm((Dd, MIX_W), 0.05),
        'sgu_ws': nrm((Dd, SGU_GROUPS, CHUNK, CHUNK), CHUNK ** -0.5),
        'sgu_b': 1.0 + nrm((Dd, SGU_GROUPS, CHUNK), 0.05),
        'w_mq': nrm((Dd, D_MODEL, D_MODEL), D_MODEL ** -0.5),
        'w_mk': nrm((Dd, D_MODEL, D_MODEL), D_MODEL ** -0.5),
        'w_mv': nrm((Dd, D_MODEL, D_MODEL), D_MODEL ** -0.5),
        'w_mo': nrm((Dd, D_MODEL, D_MODEL), BETA * D_MODEL ** -0.5),
        'w_up': nrm((Dd, D_MODEL, D_FF), D_MODEL ** -0.5),
        'w_down': nrm((Dd, D_FF, D_MODEL), BETA * D_FF ** -0.5),
        'ln1_g': 1.0 + nrm((Dd, D_MODEL), 0.05),
        'ln1_b': nrm((Dd, D_MODEL), 0.05),
        'ln2_g': 1.0 + nrm((Dd, D_MODEL), 0.05),
        'ln2_b': nrm((Dd, D_MODEL), 0.05),
        'ln3_g': 1.0 + nrm((Dd, D_MODEL), 0.05),
        'ln3_b': nrm((Dd, D_MODEL), 0.05),
    }


def reference(x_prompt, x_sample, mem_prompt, cache_k, cache_v, page_table, cache_mem_k, cache_mem_v,
              state_conv, state_wkv, state_shift, w_in, sb_bias, w_gate, b_gate, w_branch, w_o, conv_w,
              rw_mu, rw_w0, rw_w2, rw_a0, rw_a2, rw_g2, rw_kk, rw_ka, rw_rk, rw_gn_g, rw_gn_b,
              sgu_ln_g, sgu_ln_b, sgu_ws, sgu_b, w_mq, w_mk, w_mv, w_mo, w_up, w_down,
              ln1_g, ln1_b, ln2_g, ln2_b, ln3_g, ln3_b):
    n_p = x_prompt.shape[0]
    n_s = x_sample.shape[0]
    dt = x_prompt.dtype
    conv0 = jnp.zeros((n_p, CONV_W - 1, MIX_W), dt)
    shift0 = jnp.zeros((n_p, RW_COLS), dt)
    wkv0 = jnp.zeros((n_p, RW_HEADS, RW_HD, RW_HD), dt)
    xp = x_prompt
    xs = x_sample
    p_k, p_v, p_mk, p_mv, p_conv, p_wkv, p_shift = [], [], [], [], [], [], []
    s_k, s_v, s_conv, s_wkv, s_shift, s_chunk = [], [], [], [], [], []
    for l in range(DEPTH):
        lp = {'w_in': w_in[l], 'sb_bias': sb_bias[l], 'w_gate': w_gate[l], 'b_gate': b_gate[l],
              'w_branch': w_branch[l],
              'w_o': w_o[l], 'conv_w': conv_w[l], 'rw_mu': rw_mu[l], 'rw_w0': rw_w0[l], 'rw_w2': rw_w2[l],
              'rw_a0': rw_a0[l], 'rw_a2': rw_a2[l], 'rw_g2': rw_g2[l], 'rw_kk': rw_kk[l], 'rw_ka': rw_ka[l],
              'rw_rk': rw_rk[l], 'rw_gn_g': rw_gn_g[l], 'rw_gn_b': rw_gn_b[l], 'sgu_ln_g': sgu_ln_g[l],
              'sgu_ln_b': sgu_ln_b[l], 'sgu_ws': sgu_ws[l], 'sgu_b': sgu_b[l], 'w_mq': w_mq[l],
              'w_mo': w_mo[l], 'w_up': w_up[l], 'w_down': w_down[l], 'ln1_g': ln1_g[l], 'ln1_b': ln1_b[l],
              'ln2_g': ln2_g[l], 'ln2_b': ln2_b[l], 'ln3_g': ln3_g[l], 'ln3_b': ln3_b[l]}
        mk = (mem_prompt @ w_mk[l]).reshape(n_p, -1, X_HEADS, X_HD)
        mv = (mem_prompt @ w_mv[l]).reshape(n_p, -1, X_HEADS, X_HD)
        xp, k_new, v_new, c_new, sh_new, st_new, _ = _layer(xp, lp, mk, mv, None, None, conv0, shift0, wkv0, True)
        p_k.append(k_new)
        p_v.append(v_new)
        p_mk.append(mk)
        p_mv.append(mv)
        p_conv.append(c_new)
        p_shift.append(sh_new)
        p_wkv.append(st_new)
        past_k = cache_k[l][page_table].reshape(n_s, -1, SB_HEADS, SB_HD)
        past_v = cache_v[l][page_table].reshape(n_s, -1, SB_HEADS, SB_HD)
        xs, k_new, v_new, c_new, sh_new, st_new, cv_new = _layer(
            xs, lp, cache_mem_k[l], cache_mem_v[l], past_k, past_v,
            state_conv[l], state_shift[l], state_wkv[l], False)
        s_k.append(k_new)
        s_v.append(v_new)
        s_conv.append(c_new)
        s_shift.append(sh_new)
        s_wkv.append(st_new)
        s_chunk.append(cv_new)
    return (xp, xs,
            jnp.stack(p_k), jnp.stack(p_v), jnp.stack(p_mk), jnp.stack(p_mv),
            jnp.stack(p_conv), jnp.stack(p_wkv), jnp.stack(p_shift),
            jnp.stack(s_k), jnp.stack(s_v), jnp.stack(s_conv), jnp.stack(s_wkv), jnp.stack(s_shift),
            jnp.stack(s_chunk))
```

```python
import contextlib
import os
import numpy as np
SKIP = os.environ.get('KSKIP', '')
import concourse.bass as bass
import concourse.mybir as mybir
from concourse.bass_utils import run_bass_kernel_spmd

F32 = mybir.dt.float32
BF16 = mybir.dt.bfloat16
I32 = mybir.dt.int32
AF = mybir.ActivationFunctionType
ALU = mybir.AluOpType
AX = mybir.AxisListType

D = 1024
DEPTH = 2
MIXW = 256
RWC = 896
INC = 2944
DFF = 4096
NMEM = 256
ALPHA = (2 * DEPTH) ** 0.25
LN_EPS = 1e-5
GN_EPS = 64e-5
NSS = 16
SL = 8


class Prog:
    def __init__(self, nc, es):
        self.nc = nc
        self.es = es
        self.eh = {'pe': nc.tensor, 'act': nc.scalar, 'dve': nc.vector, 'pool': nc.gpsimd, 'sp': nc.sync}
        self.ops = []
        self.last_w = {}
        self.readers = {}
        self.EPOCH = 12000
        self.barriers = []
        self.NSLOT = 12

    def op(self, eng, fn, r=(), w=(), dma=False):
        pr = [k for k in r if isinstance(k, str) and (k.startswith('pf') or k.startswith('pb'))]
        if pr:
            r = [k for k in r if k not in pr]
            w = list(w) + pr
        deps = set()
        for k in r:
            if k in self.last_w:
                deps.add(self.last_w[k])
        for k in w:
            if k in self.last_w:
                deps.add(self.last_w[k])
            deps.update(self.readers.get(k, ()))
        idx = len(self.ops)
        self.ops.append(dict(eng=eng, fn=fn, deps=deps, dma=dma, sig=False, bar=len(self.barriers)))
        for k in r:
            self.readers.setdefault(k, []).append(idx)
        for k in w:
            self.last_w[k] = idx
            self.readers[k] = []
        return idx

    def dma(self, q, out, in_, r=(), w=(), **kw):
        return self.op(q, lambda e: e.dma_start(out=out, in_=in_, **kw), r=r, w=w, dma=True)

    def barrier(self):
        lastc = {}
        ndma = {e: 0 for e in self.eh}
        for i, o in enumerate(self.ops):
            if o['dma']:
                ndma[o['eng']] += 1
            else:
                lastc[o['eng']] = i
        self.barriers.append((lastc, ndma))
        self.bar_at = getattr(self, 'bar_at', []) + [len(self.ops)]
        self.last_w = {}
        self.readers = {}

    def emit(self):
        nc = self.nc
        kstop = int(os.environ.get('KSTOP', '0'))
        print('PROG n_ops', len(self.ops), 'kstop', kstop, flush=True)
        if kstop:
            self.ops = self.ops[:kstop]
            nbar = sum(1 for a in getattr(self, 'bar_at', []) if a <= kstop)
            self.barriers = self.barriers[:nbar]
            o_ = self.ops[-1]
            print('LAST OP', o_['eng'], o_['dma'], o_['fn'].__code__.co_firstlineno if o_['fn'] else None, flush=True)
        self.barrier()
        ops = self.ops
        n = len(ops)
        for (lastc, ndma) in self.barriers:
            for e, i in lastc.items():
                ops[i]['sig'] = True
        for i, o in enumerate(ops):
            best = {}
            dd = []
            for d in o['deps']:
                od = ops[d]
                if od['dma']:
                    dd.append(d)
                else:
                    e = od['eng']
                    if e == 'pe' and o['eng'] == 'pe' and not o['dma']:
                        continue
                    if e not in best or best[e] < d:
                        best[e] = d
            o['pd'] = dd + list(best.values())
            for d in o['pd']:
                ops[d]['sig'] = True
        cnt = {e: 0 for e in self.eh}
        dcnt = {e: 0 for e in self.eh}
        nsig = {e: 0 for e in self.eh}
        for o in ops:
            if (not o['dma']) and o['sig']:
                nsig[o['eng']] += 1
        sems = {}
        for e in self.eh:
            ne = nsig[e] // self.EPOCH + 1
            sems[e] = [self.es.enter_context(nc.semaphore(f"s_{e}_{k}")) for k in range(ne)]
        dsems = {e: [self.es.enter_context(nc.semaphore(f"d_{e}_{k}")) for k in range(self.NSLOT)]
                 for e in ['sp', 'pool', 'act']}
        for o in ops:
            e = o['eng']
            if o['dma']:
                j = dcnt[e]
                dcnt[e] += 1
                o['tok'] = (dsems[e][j % self.NSLOT], 16 * (j // self.NSLOT + 1))
                o['prev'] = (dsems[e][j % self.NSLOT], 16 * (j // self.NSLOT)) if j >= self.NSLOT else None
            elif o['sig']:
                c = cnt[e]
                cnt[e] += 1
                o['tok'] = (sems[e][c // self.EPOCH], c % self.EPOCH + 1)
        waited = {e: {} for e in self.eh}
        nwait = [0]

        def wait(e, tok):
            s, v = tok
            k = id(s)
            if waited[e].get(k, 0) < v:
                self.eh[e].wait_ge(s, v)
                waited[e][k] = v
                nwait[0] += 1

        def bar_wait(e, b):
            lastc, ndma = self.barriers[b]
            for e2, i in lastc.items():
                if e2 != e:
                    wait(e, ops[i]['tok'])
            for q in ['sp', 'pool', 'act']:
                m = ndma[q]
                for sl in range(self.NSLOT):
                    if m <= sl:
                        continue
                    j = ((m - 1 - sl) // self.NSLOT) * self.NSLOT + sl
                    wait(e, (dsems[q][sl], 16 * (j // self.NSLOT + 1)))

        curbar = {e: 0 for e in self.eh}
        for o in ops:
            e = o['eng']
            while curbar[e] < o['bar']:
                bar_wait(e, curbar[e])
                curbar[e] += 1
            toks = {}
            for d in o['pd']:
                s_, v_ = ops[d]['tok']
                if id(s_) not in toks or toks[id(s_)][1] < v_:
                    toks[id(s_)] = (s_, v_)
            for tk in toks.values():
                wait(e, tk)
            if o['dma'] and o['prev'] is not None:
                wait(e, o['prev'])
            ins = o['fn'](self.eh[e])
            if o['dma']:
                ins.then_inc(o['tok'][0], 16)
            elif o['sig']:
                ins.then_inc(o['tok'][0], 1)
        bar_wait('sp', len(self.barriers) - 1)
        self.n_ops = n
        self.n_wait = nwait[0]


class Cfg:
    def __init__(self, SEQ=4096, NPAGES=16, NPHYS=2560, debug=False, nlayers=DEPTH, stages="0ARBCD"):
        self.SEQ = SEQ
        self.NPAGES = NPAGES
        self.NPHYS = NPHYS
        self.NU = SEQ // 512
        self.NT = SEQ // 128
        self.debug = debug
        self.nlayers = nlayers
        self.stages = stages


def host_constants(cfg):
    c = {}
    c['ident'] = np.eye(128, dtype=np.float32)
    j = np.arange(128)
    c['lx'] = (j[:, None] < j[None, :]).astype(np.float32)
    c['ones'] = np.ones((128, 128), np.float32)
    c['sbmask'] = (j[:, None] < j[None, :]).astype(np.float32)
    sj, tj = j // SL, j % SL
    c['sbmask_s'] = ((sj[:, None] == sj[None, :]) & (tj[:, None] < tj[None, :])).astype(np.float32)
    def rwm(ch):
        cj = j // ch
        same = cj[:, None] == cj[None, :]
        su = same & (j[:, None] < j[None, :])
        iu = same & (j[:, None] <= j[None, :])
        sl = same & (j[:, None] > j[None, :])
        m1 = np.concatenate([su, iu], 1).astype(np.float32)
        return (np.tile(m1[:, None, :], (1, 4, 1)).reshape(128, 1024),
                np.tile(sl.astype(np.float32)[:, None, :], (1, 4, 1)).reshape(128, 512),
                iu.astype(np.float32))
    c['rwm1_p'], c['rwm3_p'], c['tri_p'] = rwm(64)
    c['rwm1_s'], c['rwm3_s'], c['tri_s'] = rwm(SL)
    sel = np.zeros((128, 2), np.float32)
    sel[63, 0] = 1
    sel[127, 1] = 1
    c['sel_p'] = sel
    sels = np.zeros((128, NSS), np.float32)
    for s in range(NSS):
        sels[s * SL + SL - 1, s] = 1
    c['sel_s'] = sels
    sf = np.zeros((NSS, 128), np.float32)
    for s in range(NSS):
        sf[s, s * SL] = 1
    c['self_s'] = sf
    c['seqmask'] = (sj[:, None] == np.arange(NSS)[None, :]).astype(np.float32)
    c['piota'] = np.arange(128, dtype=np.float32).reshape(128, 1)
    c['sgumask_s'] = ((sj[:, None] == sj[None, :]) & (tj[:, None] <= tj[None, :])).astype(np.float32)
    c['tril'] = (j[:, None] <= j[None, :]).astype(np.float32)
    c['bdm'] = ((j[:, None] // 64) == (j[None, :] // 64)).astype(np.float32)
    ohm = np.zeros((NSS, NSS * 128), np.float32)
    for s_ in range(NSS):
        ohm[s_, s_ * 128] = 1
    c['oh'] = ohm
    c['selrep'] = (j[:, None] == (j[None, :] % SL)).astype(np.float32)
    return c


CONST_SHAPES = None


def build_program(cfg):
    nc = bass.Bass("TRN2", target_bir_lowering=False)
    es = contextlib.ExitStack()
    with es:
        _build(nc, es, cfg)
    return nc


def _build(nc, es, cfg):
    P = Prog(nc, es)
    SEQ, NU, NT, NPAGES, NPHYS = cfg.SEQ, cfg.NU, cfg.NT, cfg.NPAGES, cfg.NPHYS
    NUA = NU + 1
    NTA = NT + 1
    dbg = cfg.debug

    def din(name, shape, dt=F32):
        return nc.dram_tensor(name, list(shape), dt, kind="ExternalInput").ap()

    def dout(name, shape, dt=F32):
        return nc.dram_tensor(name, list(shape), dt, kind="ExternalOutput").ap()

    def dscr(name, shape, dt=F32):
        if dbg:
            return nc.dram_tensor(name, list(shape), dt, kind="ExternalOutput").ap()
        return nc.dram_tensor(name, list(shape), dt, kind="Internal").ap()

    xp = din("xp", [SEQ, D])
    xs = din("xs", [128, D])
    memp = din("memp", [NMEM, D])
    cache_k = [din(f"cache_k{i}", [NPHYS * 128, 256]) for i in range(DEPTH)]
    cache_v = [din(f"cache_v{i}", [NPHYS * 128, 256]) for i in range(DEPTH)]
    ptab = din("ptab", [NSS * NPAGES], I32)
    cmk = din("cmk", [DEPTH, NSS, NMEM, D])
    cmv = din("cmv", [DEPTH, NSS, NMEM, D])
    sconv = din("sconv", [DEPTH, NSS, 2, MIXW])
    swkv = din("swkv", [DEPTH, NSS, 4, 64, 64])
    sshift = din("sshift", [DEPTH, NSS, RWC])
    w_in = din("w_in", [DEPTH, D, INC])
    sb_bias = din("sb_bias", [DEPTH, 4])
    w_gate = din("w_gate", [DEPTH, D, 4 * D])
    b_gate = din("b_gate", [DEPTH, 4 * D])
    w_branch = din("w_branch", [DEPTH, 4 * MIXW, D])
    w_o = din("w_o", [DEPTH, D, D])
    conv_w = din("conv_w", [DEPTH, 3, MIXW])
    rw_mu = din("rw_mu", [DEPTH, RWC])
    rw_w0 = din("rw_w0", [DEPTH, MIXW])
    rw_w2 = din("rw_w2", [DEPTH, 32, MIXW])
    rw_a0 = din("rw_a0", [DEPTH, MIXW])
    rw_a2 = din("rw_a2", [DEPTH, 32, MIXW])
    rw_g2 = din("rw_g2", [DEPTH, 64, MIXW])
    rw_kk = din("rw_kk", [DEPTH, MIXW])
    rw_ka = din("rw_ka", [DEPTH, MIXW])
    rw_rk = din("rw_rk", [DEPTH, MIXW])
    rw_gn_g = din("rw_gn_g", [DEPTH, MIXW])
    rw_gn_b = din("rw_gn_b", [DEPTH, MIXW])
    sgu_ln_g = din("sgu_ln_g", [DEPTH, MIXW])
    sgu_ln_b = din("sgu_ln_b", [DEPTH, MIXW])
    sgu_ws = din("sgu_ws", [DEPTH, 4, 128, 128])
    sgu_b = din("sgu_b", [DEPTH, 4, 128])
    w_mq = din("w_mq", [DEPTH, D, D])
    w_mk = din("w_mk", [DEPTH, D, D])
    w_mv = din("w_mv", [DEPTH, D, D])
    w_mo = din("w_mo", [DEPTH, D, D])
    w_up = din("w_up", [DEPTH, D, DFF])
    w_down = din("w_down", [DEPTH, DFF, D])
    lng = [din(f"ln{i}_g", [DEPTH, D]) for i in (1, 2, 3)]
    lnb = [din(f"ln{i}_b", [DEPTH, D]) for i in (1, 2, 3)]
    consts = host_constants(cfg)
    cin = {k: din("c_" + k, v.shape) for k, v in consts.items()}

    y_p = dout("y_p", [SEQ, D])
    y_s = dout("y_s", [128, D])
    p_k = dout("p_k", [DEPTH, SEQ, 256])
    p_v = dout("p_v", [DEPTH, SEQ, 256])
    p_mk = dout("p_mk", [DEPTH, NMEM, D])
    p_mv = dout("p_mv", [DEPTH, NMEM, D])
    p_conv = dout("p_conv", [DEPTH, 2, MIXW])
    p_wkv = dout("p_wkv", [DEPTH, 4, 64, 64])
    p_shift = dout("p_shift", [DEPTH, RWC])
    s_k = dout("s_k", [DEPTH, 128, 256])
    s_v = dout("s_v", [DEPTH, 128, 256])
    s_conv = dout("s_conv", [DEPTH, NSS, 2, MIXW])
    s_wkv = dout("s_wkv", [DEPTH, NSS, 4, 64, 64])
    s_shift = dout("s_shift", [DEPTH, NSS, RWC])
    s_chunk = dout("s_chunk", [DEPTH, 128, 256])

    xT_s = dscr("xT_s", [NUA, 128, 8, 512], BF16)
    xres_s = dscr("xres_s", [NTA, 128, D])
    brT_s = dscr("brT_s", [NUA, 128, 8, 512], BF16)
    mixT_s = dscr("mixT_s", [NUA, 128, 8, 512], BF16)

    FA = 20480
    BA = 61440
    fa = es.enter_context(nc.sbuf_tensor("fa", [128, FA], F32))
    ba = es.enter_context(nc.sbuf_tensor("ba", [128, BA], BF16))
    ia = es.enter_context(nc.sbuf_tensor("ia", [128, 512], I32))
    pf = [es.enter_context(nc.psum_tensor(f"pf{i}", [128, 512], F32)) for i in range(6)]
    pb = [es.enter_context(nc.psum_tensor(f"pb{i}", [128, 1024], BF16)) for i in range(2)]

    class Arena:
        def __init__(self, t, size, nm):
            self.t, self.size, self.nm, self.off, self.marks = t, size, nm, 0, []

        def alloc(self, n):
            n2 = (n + 15) // 16 * 16
            assert self.off + n2 <= self.size, (self.nm, self.off, n2, self.size)
            a = self.t[:, self.off:self.off + n]
            self.off += n2
            return a

        def mark(self):
            self.marks.append(self.off)

        def release(self):
            self.off = self.marks.pop()

    AF_, AB_ = Arena(fa, FA, 'fa'), Arena(ba, BA, 'ba')

    def falloc(n):
        return AF_.alloc(n)

    def balloc(n):
        return AB_.alloc(n)

    units = [('p', u, 512, 4) for u in range(NU)] + [('s', NU, 128, 1)]

    def unit_tiles(u):
        kind, ui, W, nt = units[u]
        return list(range(4 * ui, 4 * ui + nt)) if kind == 'p' else [NT]

    ident_b = balloc(128)
    lx_b = balloc(128)
    ones_b = balloc(128)
    ident_f = falloc(128)
    P.dma('pool', ident_b, cin['ident'][:, :], w=['ident_b'])
    P.dma('pool', lx_b, cin['lx'][:, :], w=['lx_b'])
    P.dma('pool', ones_b, cin['ones'][:, :], w=['ones_b'])
    P.dma('sp', ident_f, cin['ident'][:, :], w=['ident_f'])
    AF_.mark()
    AB_.mark()

    def mm(out, lhsT, rhs, start, stop, r, w):
        return P.op('pe', lambda e: e.matmul(out, lhsT=lhsT, rhs=rhs, start=start, stop=stop), r=r, w=w)

    def tr(out, in_, idt, r, w):
        return P.op('pe', lambda e: e.transpose(out=out, in_=in_, identity=idt), r=r, w=w)

    def act(out, in_, func, r, w, **kw):
        return P.op('act', lambda e: e.activation(out=out, in_=in_, func=func, **kw), r=r, w=w)

    def tt(eng, out, in0, in1, op, r, w):
        return P.op(eng, lambda e: e.tensor_tensor(out=out, in0=in0, in1=in1, op=op), r=r, w=w)

    def ts(eng, out, in0, s1, s2, op0, op1, r, w):
        if op1 is None:
            return P.op(eng, lambda e: e.tensor_scalar(out=out, in0=in0, scalar1=s1, scalar2=None, op0=op0), r=r, w=w)
        return P.op(eng, lambda e: e.tensor_scalar(out=out, in0=in0, scalar1=s1, scalar2=s2, op0=op0, op1=op1), r=r, w=w)

    def stt(out, in0, sc, in1, op0, op1, r, w):
        return P.op('dve', lambda e: e.scalar_tensor_tensor(out=out, in0=in0, scalar=sc, in1=in1, op0=op0, op1=op1), r=r, w=w)

    def cp(eng, out, in_, r, w):
        if eng == 'act':
            return P.op('act', lambda e: e.activation(out=out, in_=in_, func=AF.Copy), r=r, w=w)
        return P.op(eng, lambda e: e.tensor_copy(out=out, in_=in_), r=r, w=w)

    def rsqrt_(out, in_, eps, r, w):
        act(out, in_, AF.Sqrt, r=r, w=w, bias=eps, scale=1.0)
        P.op('dve', lambda e: e.reciprocal(out=out, in_=out), r=w, w=w)

    def memset(eng, ap, val, w):
        return P.op(eng, lambda e: e.memset(ap, val), w=w)

    bankc = [0]

    def nb():
        b = bankc[0] % 4
        bankc[0] += 1
        return b

    def v3(ap, a, b):
        return ap.rearrange("p (a b) -> p a b", a=a, b=b)

    def layernorm(t, tk, g_t, b_t, out, outk, outb, outbk, scr, scrk):
        st = scr[:, 0:12]
        mv = scr[:, 12:14]
        rs = scr[:, 14:15]
        P.op('dve', lambda e: e.bn_stats(out=st[:, 0:6], in_=t[:, 0:512]), r=[tk], w=[scrk])
        P.op('dve', lambda e: e.bn_stats(out=st[:, 6:12], in_=t[:, 512:1024]), r=[tk], w=[scrk])
        P.op('dve', lambda e: e.bn_aggr(out=mv, in_=st), r=[scrk], w=[scrk])
        rsqrt_(rs, mv[:, 1:2], LN_EPS, [scrk], [scrk])
        ts('dve', t, t, mv[:, 0:1], rs, ALU.subtract, ALU.mult, r=[tk, scrk], w=[tk])
        tt('pool', t, t, g_t, ALU.mult, r=[tk, 'lnp'], w=[tk])
        tt('dve', out, t, b_t, ALU.add, r=[tk, 'lnp'], w=[outk])
        if outb is not None:
            cp('act', outb, out, r=[outk], w=[outbk])

    def transpose_tile(src_b, srck, dst, dstk, pbi):
        pt = v3(pb[pbi][:, :], 8, 128)
        for kc in range(8):
            tr(pt[:, kc, :], src_b[:, kc * 128:(kc + 1) * 128], ident_b, r=[srck, 'ident_b'], w=[f'pb{pbi}'])
        cp('act', dst, pt, r=[f'pb{pbi}'], w=[dstk])

    def stage0():
        AF_.mark(); AB_.mark()
        xin = [balloc(1024) for _ in range(2)]
        xu = [v3(balloc(8 * 512), 8, 512) for _ in range(2)]
        for u in range(NUA):
            kind, ui, W, nt = units[u]
            ub = u % 2
            for ti, t in enumerate(unit_tiles(u)):
                sl = (t) % 2
                src = xp[t * 128:(t + 1) * 128, :] if kind == 'p' else xs[:, :]
                P.dma('pool', xin[sl], src, w=[f'xin{sl}'])
                transpose_tile(xin[sl], f'xin{sl}', xu[ub][:, :, ti * 128:(ti + 1) * 128], f'xu{ub}', sl)
            P.dma('sp', xT_s[u, :, :, 0:W], xu[ub][:, :, 0:W], r=[f'xu{ub}'], w=[f'xT_s{u}'])
        AF_.release(); AB_.release()
        P.barrier()


    def load_w(dst3, src2d, c0, c1, key, d0=0):
        kc_n = src2d.shape[0] // 128
        for kc in range(kc_n):
            P.dma('pool', dst3[:, kc, d0:d0 + (c1 - c0)], src2d[kc * 128:(kc + 1) * 128, c0:c1], w=[key])

    def bcast_row(dst, src1d, key, q='sp'):
        P.dma(q, dst, src1d.partition_broadcast(128), w=[key])

    def stageA1(l):
        AF_.mark(); AB_.mark()
        Wc = v3(balloc(8 * 2048), 8, 2048)
        load_w(Wc, w_in[l], 0, 1536, 'Wc', 0)
        load_w(Wc, w_in[l], 2432, 2944, 'Wc', 1536)
        KTm = balloc(max(2 * SEQ, 8192))
        VCm = balloc(max(NT * 256, 4096))
        KT = v3(KTm[:, 0:2 * SEQ], 2, SEQ)
        VC = v3(VCm[:, 0:NT * 256], NT, 256)
        vsn = balloc(256)
        gst = v3(falloc(4096), NSS, 256)
        piota = falloc(1)
        P.dma('sp', piota, cin['piota'][:, :], w=['piota'])
        xTu = [v3(balloc(8 * 512), 8, 512) for _ in range(2)]
        qT = v3(balloc(2 * 512), 2, 512)
        kTs = v3(balloc(2 * 128), 2, 128)
        brTu = v3(balloc(8 * 512), 8, 512)
        spb = v3(balloc(4 * 512), 4, 512)
        wtb = v3(balloc(4 * 512), 4, 512)
        rsb = v3(balloc(4 * 512), 4, 512)
        sbm = balloc(128)
        sbm_s = balloc(128)
        tril_b = balloc(128)
        sgm_s = balloc(128)
        WsT = v3(balloc(4 * 128), 4, 128)
        WsT_s = v3(balloc(4 * 128), 4, 128)
        wsraw = v3(balloc(4 * 128), 4, 128)
        wsx = v3(balloc(4 * 128), 4, 128)
        selrep = balloc(128)
        vlnb = balloc(256)
        ones64 = ones_b[:, 0:64]
        e1 = v3(falloc(4 * 512), 4, 512)
        rsf = v3(falloc(4 * 512), 4, 512)
        fac = falloc(512)
        kvst = [falloc(512) for _ in range(2)]
        sbb = falloc(4)
        cw = v3(falloc(6), 2, 3)
        lg_t = falloc(256)
        lb_t = falloc(256)
        sgb_p = v3(falloc(256), 2, 128)
        sgb_s = v3(falloc(256), 2, 128)
        hbs = v3(falloc(2 * 512), 2, 512)
        zc = falloc(2 * 640)
        cva = v3(falloc(2 * 512), 2, 512)
        uT = v3(falloc(2 * 512), 2, 512)
        gtmp = falloc(512)
        svt = falloc(256)
        svn = falloc(256)
        lnscr = falloc(16)
        zhist = falloc(4)
        wsxf = v3(falloc(4 * 128), 4, 128)

        P.dma('pool', sbm, cin['sbmask'][:, :], w=['sbm'])
        P.dma('pool', sbm_s, cin['sbmask_s'][:, :], w=['sbm_s'])
        P.dma('pool', tril_b, cin['tril'][:, :], w=['tril_b'])
        P.dma('pool', sgm_s, cin['sgumask_s'][:, :], w=['sgm_s'])
        P.dma('pool', selrep, cin['selrep'][:, :], w=['selrep'])
        P.dma('sp', sbb, sb_bias[l].partition_broadcast(128), w=['sbb'])
        for c in range(2):
            P.dma('sp', cw[:, c, :], conv_w[l][:, c * 128:(c + 1) * 128].rearrange("j p -> p j"), w=['cw'])
        bcast_row(lg_t, sgu_ln_g[l], 'sgp')
        bcast_row(lb_t, sgu_ln_b[l], 'sgp')
        for g in range(4):
            po = (g % 2) * 64
            P.dma('sp', sgb_p[po:po + 64, g // 2, :], sgu_b[l, g].partition_broadcast(64), w=['sgb'])
            src = bass.AP(tensor=sgu_b.tensor, offset=(l * 4 + g) * 128, ap=[[0, 64], [0, NSS], [1, SL]])
            P.dma('sp', sgb_s[po:po + 64, g // 2, :].rearrange("p (a b) -> p a b", a=NSS, b=SL), src, w=['sgb'])
            P.dma('pool', wsraw[:, g, :], sgu_ws[l, g], w=['wsraw'])
        b0 = nb()
        pw = v3(pb[0][:, 0:512], 4, 128)
        for g in range(4):
            tr(pw[:, g, :], wsraw[:, g, :], ident_b, r=['wsraw', 'ident_b'], w=['pb0'])
        for g in range(4):
            tt('dve', WsT[:, g, :], pw[:, g, :], tril_b, ALU.mult, r=['pb0', 'tril_b'], w=['WsT'])
        for g in range(4):
            cp('dve', wsx[0:SL, g, :].rearrange("p (a b) -> p a b", a=NSS, b=SL),
               WsT[0:SL, g, 0:SL].unsqueeze(1).to_broadcast([SL, NSS, SL]), r=['WsT'], w=['wsx'])
        pw2 = v3(pf[b0][:, :], 4, 128)
        for g in range(4):
            mm(pw2[:, g, :], selrep[0:SL, :], wsx[0:SL, g, :], True, True, r=['selrep', 'wsx'], w=[f'pf{b0}'])
        for g in range(4):
            tt('dve', WsT_s[:, g, :], pw2[:, g, :], sgm_s, ALU.mult, r=[f'pf{b0}', 'sgm_s'], w=['WsT'])
        memset('pool', zhist, 0.0, w=['zhist'])

        def proj_fm(c0, W, xt, xk, bank):
            for kc in range(8):
                mm(pf[bank][:, 0:W], Wc[:, kc, c0:c0 + 128], xt[:, kc, 0:W], kc == 0, kc == 7,
                   r=['Wc', xk], w=[f'pf{bank}'])

        for u in range(NUA):
            kind, ui, W, ntl = units[u]
            ub = u % 2
            xt = xTu[ub]
            xk = f'xTu{ub}'
            P.dma('sp', xt[:, :, 0:W], xT_s[u, :, :, 0:W], r=[f'xT_s{u}'], w=[xk])
            tiles = unit_tiles(u)
            for c in range(2):
                b = nb()
                proj_fm(c * 128, W, xt, xk, b)
                act(qT[:, c, 0:W], pf[b][:, 0:W], AF.Copy, r=[f'pf{b}'], w=['qT'], scale=0.125)
                b = nb()
                proj_fm(256 + c * 128, W, xt, xk, b)
                if kind == 'p':
                    cp('dve', KT[:, c, ui * 512:ui * 512 + W], pf[b][:, 0:W], r=[f'pf{b}'], w=['KT'])
                else:
                    cp('dve', kTs[:, c, :], pf[b][:, 0:W], r=[f'pf{b}'], w=['kTs'])
            for ti, t in enumerate(tiles):
                b = nb()
                for kc in range(8):
                    mm(pf[b][:, :], xt[:, kc, ti * 128:(ti + 1) * 128], Wc[:, kc, 256:768], kc == 0, kc == 7,
                       r=['Wc', xk], w=[f'pf{b}'])
                sl = t % 2
                cp('act', kvst[sl], pf[b][:, :], r=[f'pf{b}'], w=[f'kvst{sl}'])
                if kind == 'p':
                    cp('dve', VC[:, t, :], pf[b][:, 256:512], r=[f'pf{b}'], w=['VC'])
                    P.dma('sp', p_k[l, t * 128:(t + 1) * 128, :], kvst[sl][:, 0:256], r=[f'kvst{sl}'], w=['o_pk'])
                    P.dma('sp', p_v[l, t * 128:(t + 1) * 128, :], kvst[sl][:, 256:512], r=[f'kvst{sl}'], w=['o_pv'])
                else:
                    cp('dve', vsn, pf[b][:, 256:512], r=[f'pf{b}'], w=['vsn'])
                    P.dma('sp', s_k[l, :, :], kvst[sl][:, 0:256], r=[f'kvst{sl}'], w=['o_sk'])
                    P.dma('sp', s_v[l, :, :], kvst[sl][:, 256:512], r=[f'kvst{sl}'], w=['o_sv'])
            if 'sb' in SKIP:
                pass
            elif kind == 'p':
                sb_prompt(l, ui, qT, KT, VC, brTu, spb, wtb, rsb, rsf, e1, fac, sbb, sbm, ones64)
            else:
                sb_sample(l, qT, kTs, vsn, brTu, spb, wtb, rsb, rsf, e1, fac, sbb, sbm_s, ones64, KTm, VCm, gst, piota)
            nseq, Lq = (1, 512) if kind == 'p' else (NSS, SL)
            zv = zc[:, 0:2 * nseq * (Lq + 2)].rearrange("p (c s t) -> p c s t", c=2, s=nseq, t=Lq + 2)
            for c in range(2):
                b1 = nb()
                proj_fm(768 + 256 + 256 + c * 128, W, xt, xk, b1)
                cp('act', hbs[:, c, 0:W], pf[b1][:, 0:W], r=[f'pf{b1}'], w=['hbs'])
                b2 = nb()
                proj_fm(768 + 256 + c * 128, W, xt, xk, b2)
                tt('dve', zv[:, c, :, 2:Lq + 2],
                   pf[b2][:, 0:W].rearrange("p (s t) -> p s t", s=nseq, t=Lq),
                   hbs[:, c, 0:W].rearrange("p (s t) -> p s t", s=nseq, t=Lq), ALU.mult,
                   r=[f'pf{b2}', 'hbs'], w=['zc'])
            if kind == 'p':
                cp('pool', zv[:, :, 0, 0:2], v3(zhist[:, 0:4], 2, 2), r=['zhist'], w=['zc'])
            else:
                for c in range(2):
                    for jj in range(2):
                        P.dma('sp', zv[:, c, :, jj], sconv[l][:, jj, c * 128:(c + 1) * 128].rearrange("s p -> p s"), w=['zc'])
            for c in range(2):
                cv = cva[:, c, 0:W].rearrange("p (s t) -> p s t", s=nseq, t=Lq)
                ts('dve', cv, zv[:, c, :, 0:Lq], cw[:, c, 0:1], None, ALU.mult, None, r=['zc', 'cw'], w=['cva'])
                stt(cv, zv[:, c, :, 1:Lq + 1], cw[:, c, 1:2], cv, ALU.mult, ALU.add, r=['zc', 'cw', 'cva'], w=['cva'])
                stt(cv, zv[:, c, :, 2:Lq + 2], cw[:, c, 2:3], cv, ALU.mult, ALU.add, r=['zc', 'cw', 'cva'], w=['cva'])
                b3 = nb()
                proj_fm(768 + c * 128, W, xt, xk, b3)
                tt('dve', brTu[:, 2 + c, 0:W], pf[b3][:, 0:W], cva[:, c, 0:W], ALU.mult, r=[f'pf{b3}', 'cva'], w=['brTu'])
            if kind == 'p':
                cp('pool', v3(zhist[:, 0:4], 2, 2), zv[:, :, 0, Lq:Lq + 2], r=['zc'], w=['zhist'])
                if ui == NU - 1:
                    for c in range(2):
                        P.dma('sp', p_conv[l][:, c * 128:(c + 1) * 128].rearrange("j p -> p j"), zhist[:, 2 * c:2 * c + 2], r=['zhist'], w=['o_pconv'])
            else:
                for c in range(2):
                    for jj in range(2):
                        P.dma('sp', s_conv[l][:, jj, c * 128:(c + 1) * 128].rearrange("s p -> p s"), zv[:, c, :, Lq + jj], r=['zc'], w=['o_sconv'])
            for c in range(2):
                b = nb()
                proj_fm(1536 + c * 128, W, xt, xk, b)
                gelu(uT[:, c, 0:W], pf[b][:, 0:W], f'pf{b}', 'uT', gtmp[:, 0:W], W)
            for ti, t in enumerate(tiles):
                b = nb()
                for kc in range(8):
                    mm(pf[b][:, 0:256], xt[:, kc, ti * 128:(ti + 1) * 128], Wc[:, kc, 1792:2048], kc == 0, kc == 7,
                       r=['Wc', xk], w=[f'pf{b}'])
                gelu(svt, pf[b][:, 0:256], f'pf{b}', 'svt', gtmp[:, 0:256], 256)
                P.op('dve', lambda e: e.bn_stats(out=lnscr[:, 0:6], in_=svt), r=['svt'], w=['lnscr'])
                P.op('dve', lambda e: e.bn_aggr(out=lnscr[:, 6:8], in_=lnscr[:, 0:6]), r=['lnscr'], w=['lnscr'])
                rsqrt_(lnscr[:, 8:9], lnscr[:, 7:8], LN_EPS, ['lnscr'], ['lnscr'])
                ts('dve', svt, svt, lnscr[:, 6:7], lnscr[:, 8:9], ALU.subtract, ALU.mult, r=['svt', 'lnscr'], w=['svt'])
                tt('pool', svt, svt, lg_t, ALU.mult, r=['svt', 'sgp'], w=['svt'])
                tt('dve', svn, svt, lb_t, ALU.add, r=['svt', 'sgp'], w=['svn'])
                cp('act', vlnb, svn, r=['svn'], w=['vlnb'])
                if kind == 's':
                    P.dma('sp', s_chunk[l, :, :], svn, r=['svn'], w=['o_schunk'])
                b = nb()
                wst = WsT if kind == 'p' else WsT_s
                sgb = sgb_p if kind == 'p' else sgb_s
                for g in range(4):
                    po = (g % 2) * 64
                    mm(pf[b][po:po + 64, (g // 2) * 128:(g // 2 + 1) * 128], vlnb[:, g * 64:(g + 1) * 64], wst[:, g, :],
                       True, True, r=['vlnb', 'WsT'], w=[f'pf{b}'])
                mx = v3(pf[b][:, 0:256], 2, 128)
                tt('dve', cva[:, :, 0:128], mx, sgb, ALU.add, r=[f'pf{b}', 'sgb'], w=['cva'])
                tt('pool', brTu[:, 6:8, ti * 128:(ti + 1) * 128], cva[:, :, 0:128], uT[:, :, ti * 128:(ti + 1) * 128], ALU.mult,
                   r=['cva', 'uT'], w=['brTu'])
            P.dma('sp', brT_s[u, :, 0:4, 0:W], brTu[:, 0:4, 0:W], r=['brTu'], w=[f'brT_s{u}a'])
            P.dma('sp', brT_s[u, :, 6:8, 0:W], brTu[:, 6:8, 0:W], r=['brTu'], w=[f'brT_s{u}b'])
            if 'R' not in cfg.stages:
                memset('pool', brTu[:, 4:6, 0:W], 0.0, w=['brTu'])
                P.dma('sp', brT_s[u, :, 4:6, 0:W], brTu[:, 4:6, 0:W], r=['brTu'], w=[f'brT_s{u}c'])
        AF_.release(); AB_.release()
        P.barrier()

    def gelu(out, in_ps, ink, outk, tmp, W):
        tk = 'gtmp'
        act(tmp, in_ps, AF.Square, r=[ink], w=[tk])
        ts('dve', tmp, tmp, 0.044715, 1.0, ALU.mult, ALU.add, r=[tk], w=[tk])
        tt('dve', tmp, tmp, in_ps, ALU.mult, r=[tk, ink], w=[tk])
        act(tmp, tmp, AF.Sigmoid, r=[tk], w=[tk], scale=1.5957691216057308)
        tt('dve', out, tmp, in_ps, ALU.mult, r=[tk, ink], w=[outk])

    def sb_block(Zb, qk_list, nq, c0, hsl, bias_ap, mask, maskcols, first, e1h, sph, wth, rsbh, rsfh, keys, acc_list):
        zk = f'pf{Zb}'
        for (lt, rh, a, n_) in qk_list:
            mm(pf[Zb][:, a:a + n_], lt, rh, True, True, r=keys, w=[zk])
        act(e1h[:, c0:nq], pf[Zb][:, c0:nq], AF.Exp, r=[zk, 'sbb'], w=['e1' + hsl], bias=bias_ap)
        act(sph[:, c0:nq], e1h[:, c0:nq], AF.Ln, r=['e1' + hsl], w=['sp' + hsl], bias=1.0)
        if mask is not None:
            a, n_ = maskcols
            tt('pool', sph[:, a:a + n_], sph[:, a:a + n_], mask, ALU.mult, r=['sp' + hsl, 'sbm', 'sbm_s'], w=['sp' + hsl])
        mm(pf[Zb][:, c0:nq], lx_b, sph[:, c0:nq], True, False, r=['lx_b', 'sp' + hsl], w=[zk])
        if not first:
            mm(pf[Zb][:, c0:nq], ones_b, rsbh[:, c0:nq], False, False, r=['ones_b', 'rsb' + hsl], w=[zk])
        for i, (lt, rh, a, n_) in enumerate(qk_list):
            mm(pf[Zb][:, a:a + n_], lt, rh, False, i == len(qk_list) - 1, r=keys, w=[zk])
        act(wth[:, c0:nq], pf[Zb][:, c0:nq], AF.Exp, r=[zk, 'sbb'], w=['wt' + hsl], bias=bias_ap)
        if mask is not None:
            a, n_ = maskcols
            tt('pool', wth[:, a:a + n_], wth[:, a:a + n_], mask, ALU.mult, r=['wt' + hsl, 'sbm', 'sbm_s'], w=['wt' + hsl])
        for (o_ap, lv, a, n_, st_, sp_, ks, ok) in acc_list:
            mm(o_ap, lv, wth[:, a:a + n_], st_, sp_, r=['wt' + hsl] + ks, w=[ok])
        if first:
            cp('dve', rsfh[:, c0:nq], sph[:, c0:nq], r=['sp' + hsl], w=['rsf' + hsl])
        else:
            tt('dve', rsfh[:, c0:nq], rsfh[:, c0:nq], sph[:, c0:nq], ALU.add, r=['sp' + hsl, 'rsf' + hsl], w=['rsf' + hsl])
        cp('pool', rsbh[:, c0:nq], rsfh[:, c0:nq], r=['rsf' + hsl], w=['rsb' + hsl])

    def sb_finish(h, nq, ob, rsbh, hsl, fac, brTu, ones64, tb):
        po = (h % 2) * 64
        mm(pf[tb][po:po + 64, 0:nq], ones64, rsbh[:, 0:nq], True, True, r=['ones_b', 'rsb' + hsl], w=[f'pf{tb}'])
        act(fac[po:po + 64, 0:nq], pf[tb][po:po + 64, 0:nq], AF.Exp, r=[f'pf{tb}'], w=['fac'], scale=-1.0)
        tt('dve', brTu[po:po + 64, h // 2, 0:nq], pf[ob][po:po + 64, 0:nq], fac[po:po + 64, 0:nq], ALU.mult,
           r=[f'pf{ob}', 'fac'], w=['brTu'])

    def sb_prompt(l, ui, qT, KT, VC, brTu, spb, wtb, rsb, rsf, e1, fac, sbb, sbm, ones64):
        nkb = 4 * ui + 4
        for hp in range(2):
            for kb in range(nkb):
                d = kb - 4 * ui
                c0 = 128 * d if d > 0 else 0
                for h2 in range(2):
                    h = hp * 2 + h2
                    po = h2 * 64
                    hsl = str(h2)
                    Zb = h2 * 2 + (kb % 2)
                    ob = 4 + h2
                    lt = KT[po:po + 64, hp, kb * 128:(kb + 1) * 128]
                    rh = qT[po:po + 64, hp, c0:512]
                    mask = sbm if d >= 0 else None
                    acc = [(pf[ob][po:po + 64, c0:512], VC[:, kb, h * 64:(h + 1) * 64], c0, 512 - c0, kb == 0, kb == nkb - 1,
                            ['VC'], f'pf{ob}')]
                    sb_block(Zb, [(lt, rh, c0, 512 - c0)], 512, c0, hsl, sbb[:, h:h + 1], mask, (c0, 128), kb == 0,
                             e1[:, h2, :], spb[:, h2, :], wtb[:, h2, :], rsb[:, h2, :], rsf[:, h2, :], ['KT', 'qT'], acc)
            for h2 in range(2):
                sb_finish(hp * 2 + h2, 512, 4 + h2, rsb[:, h2, :], str(h2), fac, brTu, ones64, h2)

    def sb_sample(l, qT, kTs, vsn, brTu, spb, wtb, rsb, rsf, e1, fac, sbb, sbm_s, ones64, KTm, VCm, gst, piota):
        NP = NPAGES
        ptb = ia[:, 0:NSS * NP]
        idx = ia[:, 256:256 + NSS * NP]
        P.dma('sp', ptb, ptab.partition_broadcast(128), w=['ptb'])
        ts('dve', idx, ptb, 128.0, piota[:, 0:1], ALU.mult, ALU.add, r=['ptb', 'piota'], w=['idx'])
        kbf = v3(KTm[:, 0:4096], NSS, 256)
        ktb = KTm[:, 4096:8192].rearrange("p (c s k) -> p c s k", c=2, s=NSS, k=128)
        vbf = v3(VCm[:, 0:4096], NSS, 256)
        maskb = sbm_s.unsqueeze(1).to_broadcast([128, 2, 128])

        for ob in (4, 5):
            memset('dve', pf[ob][:, 0:256], 0.0, w=[f'pf{ob}'])

        def hv(t, par):
            return t[:, par:4:2, 0:128]
        for kb in range(NP + 1):
            last = kb == NP
            first = kb == 0
            if not last:
                for si in range(NSS):
                    col = si * NP + kb
                    P.op('pool', (lambda e, si=si, col=col: e.indirect_dma_start(
                        out=gst[:, si, :], out_offset=None, in_=cache_k[l][:, :],
                        in_offset=bass.IndirectOffsetOnAxis(ap=idx[:, col:col + 1], axis=0))),
                        r=['idx'], w=['gst'], dma=True)
                cp('dve', kbf, gst, r=['gst'], w=['kbf'])
                for si in range(NSS):
                    col = si * NP + kb
                    P.op('pool', (lambda e, si=si, col=col: e.indirect_dma_start(
                        out=gst[:, si, :], out_offset=None, in_=cache_v[l][:, :],
                        in_offset=bass.IndirectOffsetOnAxis(ap=idx[:, col:col + 1], axis=0))),
                        r=['idx'], w=['gst'], dma=True)
                cp('dve', vbf, gst, r=['gst'], w=['vbf'])
                for g in range(4):
                    c, sh = g // 2, g % 2
                    pt = v3(pb[g % 2][:, :], 8, 128)
                    for s8 in range(8):
                        si = sh * 8 + s8
                        tr(pt[:, s8, :], kbf[:, si, c * 128:(c + 1) * 128], ident_b, r=['kbf', 'ident_b'], w=[f'pb{g % 2}'])
                    cp('act', ktb[:, c, sh * 8:sh * 8 + 8, :], pt, r=[f'pb{g % 2}'], w=['ktb'])

            def zb(h):
                return (kb % 2) * 2 + (h % 2)

            def zcol(h):
                return (h // 2) * 128

            def qk(h, startf, stop_last):
                hp, po = h // 2, (h % 2) * 64
                Zb, zc0, zk = zb(h), zcol(h), f'pf{zb(h)}'
                if last:
                    mm(pf[Zb][:, zc0:zc0 + 128], kTs[po:po + 64, hp, :], qT[po:po + 64, hp, 0:128], startf, stop_last,
                       r=['kTs', 'qT'], w=[zk])
                else:
                    for si in range(NSS):
                        mm(pf[Zb][:, zc0 + si * SL:zc0 + (si + 1) * SL], ktb[po:po + 64, hp, si, :],
                           qT[po:po + 64, hp, si * SL:(si + 1) * SL], startf, (stop_last and si == NSS - 1) or startf,
                           r=['ktb', 'qT'], w=[zk])
            for h in range(4):
                qk(h, True, True)
            for h in range(4):
                Zb, zc0 = zb(h), zcol(h)
                act(e1[:, h, 0:128], pf[Zb][:, zc0:zc0 + 128], AF.Exp, r=[f'pf{Zb}', 'sbb'], w=['e1s'], bias=sbb[:, h:h + 1])
            act(spb[:, :, 0:128], e1[:, :, 0:128], AF.Ln, r=['e1s'], w=['sps'], bias=1.0)
            if last:
                for par in range(2):
                    tt('pool', hv(spb, par), hv(spb, par), maskb, ALU.mult, r=['sps', 'sbm_s'], w=['sps'])
            for h in range(4):
                Zb, zc0 = zb(h), zcol(h)
                mm(pf[Zb][:, zc0:zc0 + 128], lx_b, spb[:, h, 0:128], True, False, r=['lx_b', 'sps'], w=[f'pf{Zb}'])
                if not first:
                    mm(pf[Zb][:, zc0:zc0 + 128], ones_b, rsb[:, h, 0:128], False, False, r=['ones_b', 'rsbs'], w=[f'pf{Zb}'])
                qk(h, False, True)
            for h in range(4):
                Zb, zc0 = zb(h), zcol(h)
                act(wtb[:, h, 0:128], pf[Zb][:, zc0:zc0 + 128], AF.Exp, r=[f'pf{Zb}', 'sbb'], w=['wts'], bias=sbb[:, h:h + 1])
            if last:
                for par in range(2):
                    tt('pool', hv(wtb, par), hv(wtb, par), maskb, ALU.mult, r=['wts', 'sbm_s'], w=['wts'])
            for h in range(4):
                po = (h % 2) * 64
                ob = 4 + (h % 2)
                oc0 = (h // 2) * 128
                if last:
                    P.op('pe', (lambda e, ob=ob, po=po, oc0=oc0, h=h: e.matmul(
                        pf[ob][po:po + 64, oc0:oc0 + 128], lhsT=vsn[:, h * 64:(h + 1) * 64], rhs=wtb[:, h, 0:128],
                        start=False, stop=(h >= 2), skip_group_check=True)), r=['vsn', 'wts'], w=[f'pf{ob}'])
                else:
                    for si in range(NSS):
                        P.op('pe', (lambda e, ob=ob, po=po, oc0=oc0, h=h, si=si: e.matmul(
                            pf[ob][po:po + 64, oc0 + si * SL:oc0 + (si + 1) * SL], lhsT=vbf[:, si, h * 64:(h + 1) * 64],
                            rhs=wtb[:, h, si * SL:(si + 1) * SL],
                            start=False, stop=False, skip_group_check=True)),
                            r=['vbf', 'wts'], w=[f'pf{ob}'])
            if first:
                cp('dve', rsf[:, :, 0:128], spb[:, :, 0:128], r=['sps'], w=['rsfs'])
            else:
                tt('dve', rsf[:, :, 0:128], rsf[:, :, 0:128], spb[:, :, 0:128], ALU.add, r=['sps', 'rsfs'], w=['rsfs'])
            cp('pool', rsb[:, :, 0:128], rsf[:, :, 0:128], r=['rsfs'], w=['rsbs'])
        for h in range(4):
            po = (h % 2) * 64
            tb = h % 2
            ob = 4 + (h % 2)
            oc0 = (h // 2) * 128
            mm(pf[tb][po:po + 64, 0:128], ones64, rsb[:, h, 0:128], True, True, r=['ones_b', 'rsbs'], w=[f'pf{tb}'])
            act(fac[po:po + 64, 0:128], pf[tb][po:po + 64, 0:128], AF.Exp, r=[f'pf{tb}'], w=['fac'], scale=-1.0)
            tt('dve', brTu[po:po + 64, h // 2, 0:128], pf[ob][po:po + 64, oc0:oc0 + 128], fac[po:po + 64, 0:128], ALU.mult,
               r=[f'pf{ob}', 'fac'], w=['brTu'])


    ffp_s = dscr("ffp_s", [NTA, 128, D])

    def xres_src(l, t):
        if l == 0:
            return xp[t * 128:(t + 1) * 128, :] if t < NT else xs[:, :]
        return xres_s[t]

    def stageB(l):
        AF_.mark(); AB_.mark()
        Wg = v3(balloc(8 * 4096), 8, 4096)
        Wb = v3(balloc(8 * 1024), 8, 1024)
        load_w(Wg, w_gate[l], 0, 4096, 'Wg')
        load_w(Wb, w_branch[l], 0, 1024, 'Wb')
        bg = falloc(32)
        P.dma('sp', bg, b_gate[l].rearrange("(j p) -> p j", p=128), w=['bg'])
        xTu = [v3(balloc(8 * 512), 8, 512) for _ in range(2)]
        brTu = [v3(balloc(8 * 512), 8, 512)] * 2
        mixTu = [v3(balloc(8 * 512), 8, 512)] * 2
        sg = [falloc(512) for _ in range(2)]
        acc = [falloc(512) for _ in range(2)]
        tmp = [falloc(512) for _ in range(2)]
        n = 0
        for u in range(NUA):
            kind, ui, W, ntl = units[u]
            ub = u % 2
            P.dma('sp', xTu[ub][:, :, 0:W], xT_s[u, :, :, 0:W], r=[f'xT_s{u}'], w=[f'xTu{ub}'])
            P.dma('sp', brTu[ub][:, :, 0:W], brT_s[u, :, :, 0:W], r=[f'brT_s{u}a', f'brT_s{u}b', f'brT_s{u}c'], w=['brTuB'])
            for c in range(8):
                ab = c % 2
                for i in range(4):
                    sb_ = n % 2
                    n += 1
                    bgk = nb()
                    for kc in range(8):
                        mm(pf[bgk][:, 0:W], Wg[:, kc, i * 1024 + c * 128:i * 1024 + (c + 1) * 128], xTu[ub][:, kc, 0:W],
                           kc == 0, kc == 7, r=['Wg', f'xTu{ub}'], w=[f'pf{bgk}'])
                    act(sg[sb_][:, 0:W], pf[bgk][:, 0:W], AF.Sigmoid, r=[f'pf{bgk}', 'bg'], w=[f'sg{sb_}'],
                        bias=bg[:, i * 8 + c:i * 8 + c + 1])
                    bpk = nb()
                    for k2 in range(2):
                        mm(pf[bpk][:, 0:W], Wb[:, 2 * i + k2, c * 128:(c + 1) * 128], brTu[ub][:, 2 * i + k2, 0:W],
                           k2 == 0, k2 == 1, r=['Wb', 'brTuB'], w=[f'pf{bpk}'])
                    if i == 0:
                        tt('dve', acc[ab][:, 0:W], pf[bpk][:, 0:W], sg[sb_][:, 0:W], ALU.mult, r=[f'pf{bpk}', f'sg{sb_}'], w=[f'acc{ab}'])
                    else:
                        tt('dve', tmp[sb_][:, 0:W], pf[bpk][:, 0:W], sg[sb_][:, 0:W], ALU.mult, r=[f'pf{bpk}', f'sg{sb_}'], w=[f'tmp{sb_}'])
                        dst = acc[ab][:, 0:W] if i < 3 else mixTu[ub][:, c, 0:W]
                        dk = f'acc{ab}' if i < 3 else 'mixTuB'
                        tt('pool', dst, acc[ab][:, 0:W], tmp[sb_][:, 0:W], ALU.add, r=[f'acc{ab}', f'tmp{sb_}'], w=[dk])
            P.dma('sp', mixT_s[u, :, :, 0:W], mixTu[ub][:, :, 0:W], r=['mixTuB'], w=[f'mixT_s{u}'])
        AF_.release(); AB_.release()
        P.barrier()

    def load_ln(l, i, gt, bt):
        P.dma('sp', gt, lng[i][l].partition_broadcast(128), w=['lnp'])
        P.dma('sp', bt, lnb[i][l].partition_broadcast(128), w=['lnp'])

    def proj_tm_ln(lhs3, lhsk, Wt, Wk, ncol_kc, ti, xr, xrk, tbuf, tk):
        for hh in range(2):
            b = nb()
            for kc in range(ncol_kc):
                mm(pf[b][:, :], lhs3[:, kc, ti * 128:(ti + 1) * 128], Wt[:, kc, hh * 512:(hh + 1) * 512], kc == 0, kc == ncol_kc - 1,
                   r=[lhsk, Wk], w=[f'pf{b}'])
            stt(tbuf[:, hh * 512:(hh + 1) * 512], xr[:, hh * 512:(hh + 1) * 512], ALPHA, pf[b][:, :], ALU.mult, ALU.add,
                r=[xrk, f'pf{b}'], w=[tk])

    def stageC(l):
        AF_.mark(); AB_.mark()
        Wo = v3(balloc(8 * 1024), 8, 1024)
        Wq = v3(balloc(8 * 1024), 8, 1024)
        Wmo = v3(balloc(8 * 1024), 8, 1024)
        load_w(Wo, w_o[l], 0, 1024, 'Wo')
        load_w(Wq, w_mq[l], 0, 1024, 'Wq')
        load_w(Wmo, w_mo[l], 0, 1024, 'Wmo')
        g1 = falloc(1024); b1 = falloc(1024); g2 = falloc(1024); b2 = falloc(1024)
        load_ln(l, 0, g1, b1)
        load_ln(l, 1, g2, b2)
        mkT = v3(balloc(8 * 256), 8, 256)
        mvb = v3(balloc(2 * 1024), 2, 1024)
        stg = [falloc(512) for _ in range(2)]
        AB_.mark()
        memT = v3(balloc(8 * 256), 8, 256)
        Wmk = v3(balloc(8 * 1024), 8, 1024)
        Wmv = v3(balloc(8 * 1024), 8, 1024)
        load_w(Wmk, w_mk[l], 0, 1024, 'Wmk')
        load_w(Wmv, w_mv[l], 0, 1024, 'Wmv')
        mtl = [balloc(1024) for _ in range(2)]
        for mt in range(2):
            P.dma('pool', mtl[mt], memp[mt * 128:(mt + 1) * 128, :], w=[f'mtl{mt}'])
            transpose_tile(mtl[mt], f'mtl{mt}', memT[:, :, mt * 128:(mt + 1) * 128], 'memT', mt)
        n = 0
        for (Wt, Wk, outd, isv) in ((Wmk, 'Wmk', p_mk, False), (Wmv, 'Wmv', p_mv, True)):
            for mt in range(2):
                for hh in range(2):
                    b = nb()
                    for kc in range(8):
                        mm(pf[b][:, :], memT[:, kc, mt * 128:(mt + 1) * 128], Wt[:, kc, hh * 512:(hh + 1) * 512], kc == 0, kc == 7,
                           r=['memT', Wk], w=[f'pf{b}'])
                    sl = n % 2
                    n += 1
                    cp('act', stg[sl], pf[b][:, :], r=[f'pf{b}'], w=[f'stg{sl}'])
                    if isv:
                        cp('dve', mvb[:, mt, hh * 512:(hh + 1) * 512], pf[b][:, :], r=[f'pf{b}'], w=['mvb'])
                    P.dma('sp', outd[l, mt * 128:(mt + 1) * 128, hh * 512:(hh + 1) * 512], stg[sl], r=[f'stg{sl}'], w=['o_pm'])
        for c in range(8):
            b = nb()
            for kc in range(8):
                mm(pf[b][:, 0:256], Wmk[:, kc, c * 128:(c + 1) * 128], memT[:, kc, 0:256], kc == 0, kc == 7, r=['memT', 'Wmk'], w=[f'pf{b}'])
            cp('dve', mkT[:, c, :], pf[b][:, 0:256], r=[f'pf{b}'], w=['mkT'])
        P.barrier()
        AB_.release()
        mixTu = [v3(balloc(8 * 512), 8, 512)] * 2
        x1Tu = v3(balloc(8 * 512), 8, 512)
        qmT = v3(balloc(8 * 512), 8, 512)
        x2Tu = qmT
        attT = v3(balloc(8 * 512), 8, 512)
        prb = v3(balloc(2 * 512), 2, 512)
        xb = [balloc(1024) for _ in range(2)]
        smk = [v3(balloc(2 * 1024), 2, 1024)] * 2
        smv = [v3(balloc(2 * 1024), 2, 1024)] * 2
        smkT = v3(balloc(8 * 256), 8, 256)
        prs = balloc(1024)
        xr = [falloc(1024) for _ in range(2)]
        x1 = v3(falloc(4 * 1024), 4, 1024)
        tb_ = [falloc(1024) for _ in range(2)]
        rden = falloc(512)
        lnscr = falloc(16)
        for u in range(NUA):
            kind, ui, W, ntl = units[u]
            ub = u % 2
            tiles = unit_tiles(u)
            P.dma('sp', mixTu[ub][:, :, 0:W], mixT_s[u, :, :, 0:W], r=[f'mixT_s{u}'], w=['mixTuC'])
            for ti, t in enumerate(tiles):
                sl = t % 2
                P.dma('sp', xr[sl], xres_src(l, t), r=[f'xres_s{t}'], w=[f'xr{sl}'])
                proj_tm_ln(mixTu[ub], 'mixTuC', Wo, 'Wo', 8, ti, xr[sl], f'xr{sl}', tb_[sl], f'tb{sl}')
                layernorm(tb_[sl], f'tb{sl}', g1, b1, x1[:, ti, :], 'x1', xb[sl], f'xb{sl}', lnscr, 'lnscrC')
                transpose_tile(xb[sl], f'xb{sl}', x1Tu[:, :, ti * 128:(ti + 1) * 128], 'x1Tu', sl)
            for c in range(8):
                b = nb()
                for kc in range(8):
                    mm(pf[b][:, 0:W], Wq[:, kc, c * 128:(c + 1) * 128], x1Tu[:, kc, 0:W], kc == 0, kc == 7, r=['Wq', 'x1Tu'], w=[f'pf{b}'])
                act(qmT[:, c, 0:W], pf[b][:, 0:W], AF.Copy, r=[f'pf{b}'], w=['qmT'], scale=1.0 / 16.0)
            if dbg and os.environ.get('DBGC'):
                srcd = {'x1T': x1Tu, 'qmT': qmT}[os.environ['DBGC']]
                P.dma('sp', mixT_s[u, :, :, 0:W], srcd[:, :, 0:W], r=['x1Tu', 'qmT'], w=[f'mixT_s{u}'])
            if kind == 'p':
                for h in range(4):
                    for km in range(2):
                        b = nb()
                        for ec in range(2):
                            mm(pf[b][:, 0:W], mkT[:, 2 * h + ec, km * 128:(km + 1) * 128], qmT[:, 2 * h + ec, 0:W], ec == 0, ec == 1,
                               r=['mkT', 'qmT'], w=[f'pf{b}'])
                        act(prb[:, km, 0:W], pf[b][:, 0:W], AF.Exp, r=[f'pf{b}'], w=['prb'])
                    b = nb()
                    for km in range(2):
                        mm(pf[b][:, 0:W], ones_b, prb[:, km, 0:W], km == 0, km == 1, r=['ones_b', 'prb'], w=[f'pf{b}'])
                    P.op('dve', lambda e, b=b, W=W: e.reciprocal(out=rden[:, 0:W], in_=pf[b][:, 0:W]), r=[f'pf{b}'], w=['rden'])
                    for ec in range(2):
                        b = nb()
                        for km in range(2):
                            mm(pf[b][:, 0:W], mvb[:, km, (2 * h + ec) * 128:(2 * h + ec + 1) * 128], prb[:, km, 0:W], km == 0, km == 1,
                               r=['mvb', 'prb'], w=[f'pf{b}'])
                        tt('dve', attT[:, 2 * h + ec, 0:W], pf[b][:, 0:W], rden[:, 0:W], ALU.mult, r=[f'pf{b}', 'rden'], w=['attT'])
            else:
                xattn_sample(l, qmT, attT, smk, smv, smkT, prs, rden)
            for ti, t in enumerate(tiles):
                sl = t % 2
                proj_tm_ln(attT, 'attT', Wmo, 'Wmo', 8, ti, x1[:, ti, :], 'x1', tb_[sl], f'tb{sl}')
                if dbg:
                    P.dma('sp', ffp_s[t], tb_[sl], r=[f'tb{sl}'], w=[f'ffp_s{t}'])
                layernorm(tb_[sl], f'tb{sl}', g2, b2, xr[sl], f'xr{sl}', xb[sl], f'xb{sl}', lnscr, 'lnscrC')
                P.dma('sp', xres_s[t], xr[sl], r=[f'xr{sl}'], w=[f'xres_s{t}'])
                transpose_tile(xb[sl], f'xb{sl}', x2Tu[:, :, ti * 128:(ti + 1) * 128], 'qmT', sl)
            P.dma('sp', xT_s[u, :, :, 0:W], x2Tu[:, :, 0:W], r=['qmT'], w=[f'xT_s{u}'])
        AF_.release(); AB_.release()
        P.barrier()

    def xattn_sample(l, qmT, attT, smk, smv, smkT, prs, rden):
        prv = prs.rearrange("p (k s h q) -> p k s h q", k=2, s=NSS, h=4, q=SL)
        for si in range(NSS):
            sb_ = si % 2
            P.dma('pool', smk[sb_], cmk[l, si].rearrange("(m p) d -> p m d", p=128), w=['smkC'])
            P.dma('pool', smv[sb_], cmv[l, si].rearrange("(m p) d -> p m d", p=128), w=['smvC'])
            for half in range(2):
                pt = v3(pb[half][:, :], 8, 128)
                for j in range(8):
                    c = half * 4 + j // 2
                    km = j % 2
                    tr(pt[:, j, :], smk[sb_][:, km, c * 128:(c + 1) * 128], ident_b, r=['smkC', 'ident_b'], w=[f'pb{half}'])
                cp('act', smkT[:, half * 4:half * 4 + 4, :].rearrange("p c (m k) -> p (c m) k", m=2, k=128), pt, r=[f'pb{half}'], w=['smkT'])
            for km in range(2):
                for h in range(4):
                    for ec in range(2):
                        mm(pf[km][:, si * 32 + h * SL:si * 32 + (h + 1) * SL], smkT[:, 2 * h + ec, km * 128:(km + 1) * 128],
                           qmT[:, 2 * h + ec, si * SL:(si + 1) * SL], ec == 0, ec == 1, r=['smkT', 'qmT'], w=[f'pf{km}'])
            for km in range(2):
                act(prv[:, km, si], pf[km][:, si * 32:(si + 1) * 32].rearrange("p (h q) -> p h q", h=4, q=SL), AF.Exp,
                    r=[f'pf{km}'], w=['prs'])
            for h in range(4):
                for ec in range(2):
                    c = 2 * h + ec
                    ob = 2 + c // 4
                    oc = (c % 4) * 128 + si * SL
                    for km in range(2):
                        mm(pf[ob][:, oc:oc + SL], smv[sb_][:, km, c * 128:(c + 1) * 128], prv[:, km, si, h, :], km == 0, km == 1,
                           r=['smvC', 'prs'], w=[f'pf{ob}'])
        db = 4
        for km in range(2):
            mm(pf[db][:, :], ones_b, prs[:, km * 512:(km + 1) * 512], km == 0, km == 1, r=['ones_b', 'prs'], w=[f'pf{db}'])
        P.op('dve', lambda e: e.reciprocal(out=rden[:, 0:512], in_=pf[db][:, :]), r=[f'pf{db}'], w=['rden'])
        rv = rden[:, 0:512].rearrange("p (s h q) -> p s h q", s=NSS, h=4, q=SL)
        for c in range(8):
            h = c // 2
            ob = 2 + c // 4
            oc = (c % 4) * 128
            tt('dve', attT[:, c, 0:128].rearrange("p (s q) -> p s q", s=NSS, q=SL),
               pf[ob][:, oc:oc + 128].rearrange("p (s q) -> p s q", s=NSS, q=SL), rv[:, :, h, :], ALU.mult,
               r=[f'pf{ob}', 'rden'], w=['attT'])

    def stageD(l, f):
        AF_.mark(); AB_.mark()
        Wu = v3(balloc(8 * 2048), 8, 2048)
        Wd = v3(balloc(16 * 1024), 16, 1024)
        load_w(Wu, w_up[l], f * 2048, (f + 1) * 2048, 'Wu')
        load_w(Wd, w_down[l][f * 2048:(f + 1) * 2048, :], 0, 1024, 'Wd')
        g3 = falloc(1024); b3 = falloc(1024)
        if f == 1:
            load_ln(l, 2, g3, b3)
        xTu = [v3(balloc(8 * 512), 8, 512) for _ in range(2)]
        hid = v3(balloc(16 * 512), 16, 512)
        xoT = v3(balloc(8 * 512), 8, 512)
        xb = [balloc(1024) for _ in range(2)]
        rl = [falloc(512) for _ in range(2)]
        fft = [falloc(1024) for _ in range(2)]
        xr = [falloc(1024) for _ in range(2)]
        fp_ = [falloc(1024) for _ in range(2)]
        lnscr = falloc(16)
        lastl = (l == cfg.nlayers - 1)
        for u in range(NUA):
            kind, ui, W, ntl = units[u]
            ub = u % 2
            tiles = unit_tiles(u)
            P.dma('sp', xTu[ub][:, :, 0:W], xT_s[u, :, :, 0:W], r=[f'xT_s{u}'], w=[f'xTu{ub}'])
            for c in range(16):
                b = nb()
                for kc in range(8):
                    mm(pf[b][:, 0:W], Wu[:, kc, c * 128:(c + 1) * 128], xTu[ub][:, kc, 0:W], kc == 0, kc == 7, r=['Wu', f'xTu{ub}'], w=[f'pf{b}'])
                sl = c % 2
                act(rl[sl][:, 0:W], pf[b][:, 0:W], AF.Relu, r=[f'pf{b}'], w=[f'rl{sl}'])
                tt('dve' if c % 4 else 'pool', hid[:, c, 0:W], rl[sl][:, 0:W], rl[sl][:, 0:W], ALU.mult, r=[f'rl{sl}'], w=['hid'])
            for ti, t in enumerate(tiles):
                sl = t % 2
                if f == 1:
                    P.dma('sp', xr[sl], xres_s[t], r=[f'xres_s{t}'], w=[f'xr{sl}'])
                    P.dma('sp', fp_[sl], ffp_s[t], r=[f'ffp_s{t}'], w=[f'fp{sl}'])
                for hh in range(2):
                    b = nb()
                    for kc in range(16):
                        mm(pf[b][:, :], hid[:, kc, ti * 128:(ti + 1) * 128], Wd[:, kc, hh * 512:(hh + 1) * 512], kc == 0, kc == 15,
                           r=['hid', 'Wd'], w=[f'pf{b}'])
                    cs = slice(hh * 512, (hh + 1) * 512)
                    if f == 0:
                        cp('act', fft[sl][:, cs], pf[b][:, :], r=[f'pf{b}'], w=[f'fft{sl}'])
                    else:
                        tt('dve', fft[sl][:, cs], pf[b][:, :], fp_[sl][:, cs], ALU.add, r=[f'pf{b}', f'fp{sl}'], w=[f'fft{sl}'])
                        stt(fft[sl][:, cs], xr[sl][:, cs], ALPHA, fft[sl][:, cs], ALU.mult, ALU.add, r=[f'xr{sl}', f'fft{sl}'], w=[f'fft{sl}'])
                if f == 0:
                    P.dma('sp', ffp_s[t], fft[sl], r=[f'fft{sl}'], w=[f'ffp_s{t}'])
                else:
                    layernorm(fft[sl], f'fft{sl}', g3, b3, xr[sl], f'xr{sl}', None if lastl else xb[sl], f'xb{sl}', lnscr, 'lnscrD')
                    if lastl:
                        dst = y_p[t * 128:(t + 1) * 128, :] if kind == 'p' else y_s[:, :]
                        P.dma('sp', dst, xr[sl], r=[f'xr{sl}'], w=[f'o_y{t}'])
                    else:
                        P.dma('sp', xres_s[t], xr[sl], r=[f'xr{sl}'], w=[f'xres_s{t}'])
                        transpose_tile(xb[sl], f'xb{sl}', xoT[:, :, ti * 128:(ti + 1) * 128], 'xoT', sl)
            if f == 1 and not lastl:
                P.dma('sp', xT_s[u, :, :, 0:W], xoT[:, :, 0:W], r=['xoT'], w=[f'xT_s{u}'])
        AF_.release(); AB_.release()
        P.barrier()


    def stageA2(l):
        AF_.mark(); AB_.mark()
        W1 = v3(balloc(8 * RWC), 8, RWC)
        W2 = v3(balloc(8 * RWC), 8, RWC)
        load_w(W1, w_in[l], 1536, 1536 + RWC, 'W1')
        mu_t = falloc(RWC)
        bcast_row(mu_t, rw_mu[l], 'mu_t')
        for kc in range(8):
            tt('dve', W2[:, kc, :], W1[:, kc, :], mu_t, ALU.mult, r=['W1', 'mu_t'], w=['W2'])
            tt('pool', W1[:, kc, :], W1[:, kc, :], W2[:, kc, :], ALU.subtract, r=['W1', 'W2'], w=['W1'])
        LW = balloc(768)
        memset('pool', LW, 0.0, w=['LW'])
        P.dma('pool', LW[0:32, 0:256], rw_w2[l], r=['LW'], w=['LW'])
        P.dma('pool', LW[32:64, 256:512], rw_a2[l], r=['LW'], w=['LW'])
        P.dma('pool', LW[64:128, 512:768], rw_g2[l], r=['LW'], w=['LW'])
        prm = {}
        for nm, src in (('w0', rw_w0), ('a0', rw_a0), ('kks', rw_kk), ('ka', rw_ka), ('rk', rw_rk), ('gng', rw_gn_g), ('gnb', rw_gn_b)):
            prm[nm] = falloc(256)
            bcast_row(prm[nm], src[l], 'rwp')
        omka = falloc(256)
        ts('dve', omka, prm['ka'], -1.0, 1.0, ALU.mult, ALU.add, r=['rwp'], w=['rwp2'])
        m1 = {}
        m3 = {}
        tri = {}
        for kd in ('p', 's'):
            m1[kd] = balloc(1024); m3[kd] = balloc(512); tri[kd] = falloc(128)
            P.dma('pool', m1[kd], cin['rwm1_' + kd][:, :], w=['rwmask'])
            P.dma('pool', m3[kd], cin['rwm3_' + kd][:, :], w=['rwmask'])
            P.dma('sp', tri[kd], cin['tri_' + kd][:, :], w=['rwmask'])
        bdm = falloc(128)
        P.dma('sp', bdm, cin['bdm'][:, :], w=['rwmask'])
        selp = falloc(128)
        P.dma('sp', selp, cin['ident'][:, :], w=['rwmask'])
        oh = falloc(NSS * 128)
        P.dma('sp', oh[0:NSS, :], cin['oh'][:, :], w=['rwmask'])
        ssm = falloc(RWC)
        P.dma('sp', ssm[0:NSS, :], sshift[l], w=['ssm'])
        tt('dve', ssm[0:NSS, :], ssm[0:NSS, :], mu_t[0:NSS, :], ALU.mult, r=['ssm', 'mu_t'], w=['ssm'])
        xTu = [v3(balloc(8 * 513), 8, 513) for _ in range(2)]
        xTq = v3(balloc(8 * 136), 8, 136)
        TM = v3(balloc(4 * 256), 4, 256)
        TT = v3(balloc(8 * 128), 8, 128)
        M1 = v3(balloc(4 * 256), 4, 256)
        M2 = v3(balloc(4 * 256), 4, 256)
        Qb = [v3(balloc(4 * 128), 4, 128) for _ in range(2)]
        QTb = [v3(balloc(4 * 128), 4, 128) for _ in range(2)]
        PTb = [v3(balloc(4 * 128), 4, 128) for _ in range(2)]
        RHSb = balloc(256)
        Ub = balloc(256)
        vbf = balloc(256)
        STb = v3(balloc(256), 2, 128)
        loraT = balloc(128)
        ycb = balloc(256)
        brC = v3(balloc(2 * 512), 2, 512)
        rks = falloc(512); vsb = falloc(256); xw = falloc(256); lw = falloc(256); aa = falloc(256); gg = falloc(256)
        kk = falloc(256); kkn = falloc(256); kef = falloc(256); t1 = falloc(256); t2 = falloc(256)
        Dinc = falloc(256); Dexc = falloc(256); Dinv = falloc(256); ysb = falloc(256)
        sm = falloc(64)
        STf = v3(falloc(256), 2, 128)
        Xn = v3(falloc(256), 2, 128)
        DCt = falloc(4)
        psh = falloc(RWC)
        memset('dve', RHSb, 0.0, w=['RHSb'])
        memset('dve', Ub, 0.0, w=['Ub'])
        memset('pool', xTq, 0.0, w=['xTq'])
        memset('pool', Xn, 0.0, w=['Xn'])

        def rw_tile(xt, xk, c0, kd, chunks, extra_s=None):
            cur = lambda kc: xt[:, kc, c0:c0 + 128]
            prv = lambda kc: xt[:, kc, c0 - 1:c0 + 127]
            bl = 0
            n_mm = 16 + (1 if extra_s is not None else 0)
            i = 0
            for kc in range(8):
                for (Wt, Wk, src) in ((W1, 'W1', cur(kc)), (W2, 'W2', prv(kc))):
                    mm(pf[bl][:, 0:128], Wt[:, kc, 768:896], src, i == 0, i == n_mm - 1, r=[Wk, xk], w=[f'pf{bl}'])
                    i += 1
            if extra_s is not None:
                mm(pf[bl][:, 0:128], ssm[0:NSS, 768:896], oh[0:NSS, extra_s * 128:(extra_s + 1) * 128], False, True, r=['ssm', 'rwmask'], w=[f'pf{bl}'])
            act(loraT[0:32, :], pf[bl][0:32, 0:128], AF.Tanh, r=[f'pf{bl}'], w=['loraT'])
            act(loraT[32:64, :], pf[bl][32:64, 0:128], AF.Copy, r=[f'pf{bl}'], w=['loraT'])
            act(loraT[64:128, :], pf[bl][64:128, 0:128], AF.Sigmoid, r=[f'pf{bl}'], w=['loraT'])
            for (bk, ca, cb) in ((1, 0, 512), (2, 512, 768)):
                i = 0
                for kc in range(8):
                    for (Wt, Wk, src) in ((W1, 'W1', cur(kc)), (W2, 'W2', prv(kc))):
                        mm(pf[bk][:, 0:cb - ca], src, Wt[:, kc, ca:cb], i == 0, i == n_mm - 1, r=[Wk, xk], w=[f'pf{bk}'])
                        i += 1
                if extra_s is not None:
                    mm(pf[bk][:, 0:cb - ca], oh[0:NSS, extra_s * 128:(extra_s + 1) * 128], ssm[0:NSS, ca:cb], False, True,
                       r=['ssm', 'rwmask'], w=[f'pf{bk}'])
            cp('act', rks, pf[1][:, :], r=['pf1'], w=['rks'])
            cp('act', vsb, pf[2][:, 0:256], r=['pf2'], w=['vsb'])
            cp('pool', vbf, vsb, r=['vsb'], w=['vbf'])
            mm(pf[3][:, :], loraT, LW[:, 0:512], True, True, r=['loraT', 'LW'], w=['pf3'])
            mm(pf[4][:, 0:256], loraT, LW[:, 512:768], True, True, r=['loraT', 'LW'], w=['pf4'])
            tt('dve', xw, pf[3][:, 0:256], prm['w0'], ALU.add, r=['pf3', 'rwp'], w=['xw'])
            act(xw, xw, AF.Sigmoid, r=['xw'], w=['xw'])
            ts('dve', lw, xw, -0.6065306597126334, None, ALU.mult, None, r=['xw'], w=['lw'])
            tt('dve', aa, pf[3][:, 256:512], prm['a0'], ALU.add, r=['pf3', 'rwp'], w=['aa'])
            act(aa, aa, AF.Sigmoid, r=['aa'], w=['aa'])
            cp('act', gg, pf[4][:, 0:256], r=['pf4'], w=['gg'])
            rr = rks[:, 0:256]
            kx = rks[:, 256:512]
            tt('pool', kk, kx, prm['kks'], ALU.mult, r=['rks', 'rwp'], w=['kk'])
            tt('dve', t1, kk, kk, ALU.mult, r=['kk'], w=['t1'])
            P.op('dve', lambda e: e.tensor_reduce(out=sm[:, 0:4], in_=v3(t1, 4, 64), axis=AX.X, op=ALU.add), r=['t1'], w=['sm'])
            act(sm[:, 0:4], sm[:, 0:4], AF.Sqrt, r=['sm'], w=['sm'])
            ts('dve', sm[:, 0:4], sm[:, 0:4], 1e-12, None, ALU.max, None, r=['sm'], w=['sm'])
            P.op('dve', lambda e: e.reciprocal(out=sm[:, 4:8], in_=sm[:, 0:4]), r=['sm'], w=['sm'])
            tt('dve', v3(kkn, 4, 64), v3(kk, 4, 64), sm[:, 4:8].unsqueeze(2).to_broadcast([128, 4, 64]), ALU.mult, r=['kk', 'sm'], w=['kkn'])
            tt('pool', t2, aa, prm['ka'], ALU.mult, r=['aa', 'rwp'], w=['t2'])
            tt('pool', t2, t2, omka, ALU.add, r=['t2', 'rwp2'], w=['t2'])
            tt('pool', kef, kx, t2, ALU.mult, r=['rks', 't2'], w=['kef'])
            tt('dve', t1, rr, kef, ALU.mult, r=['rks', 'kef'], w=['t1'])
            tt('dve', t1, t1, prm['rk'], ALU.mult, r=['t1', 'rwp'], w=['t1'])
            P.op('dve', lambda e: e.tensor_reduce(out=sm[:, 8:12], in_=v3(t1, 4, 64), axis=AX.X, op=ALU.add), r=['t1'], w=['sm'])
            mm(pf[0][:, 0:256], tri[kd], lw, True, True, r=['rwmask', 'lw'], w=['pf0'])
            act(Dinc, pf[0][:, 0:256], AF.Exp, r=['pf0'], w=['Dinc'])
            act(Dinv, pf[0][:, 0:256], AF.Exp, r=['pf0'], w=['Dinv'], scale=-1.0)
            tt('dve', Dexc, pf[0][:, 0:256], lw, ALU.subtract, r=['pf0', 'lw'], w=['Dexc'])
            act(Dexc, Dexc, AF.Exp, r=['Dexc'], w=['Dexc'])
            stt(TM[:, 0, :], kkn, -1.0, Dexc, ALU.mult, ALU.mult, r=['kkn', 'Dexc'], w=['TM'])
            tt('pool', TM[:, 1, :], rr, Dinc, ALU.mult, r=['rks', 'Dinc'], w=['TM'])
            tt('dve', t1, kkn, aa, ALU.mult, r=['kkn', 'aa', 'sm'], w=['t1'])
            tt('dve', TM[:, 2, :], t1, Dinv, ALU.mult, r=['t1', 'Dinv'], w=['TM'])
            tt('pool', TM[:, 3, :], kef, Dinv, ALU.mult, r=['kef', 'Dinv'], w=['TM'])
            ptt = v3(pb[0][:, :], 8, 128)
            for hp in range(2):
                for arr in range(4):
                    tr(ptt[:, hp * 4 + arr, :], TM[:, arr, hp * 128:(hp + 1) * 128], ident_b, r=['TM', 'ident_b'], w=['pb0'])
            cp('act', TT, ptt, r=['pb0'], w=['TT'])
            for h in range(4):
                hp, po, par = h // 2, (h % 2) * 64, h % 2
                co = (h // 2) * 256
                ar = TT[po:po + 64, hp * 4 + 0:hp * 4 + 2, :]
                mm(pf[0 + par][:, co:co + 256].rearrange("p (a t) -> p a t", a=2, t=128), TT[po:po + 64, hp * 4 + 2, :], ar, True, True,
                   r=['TT'], w=[f'pf{0 + par}'])
                mm(pf[2 + par][:, co:co + 256].rearrange("p (a t) -> p a t", a=2, t=128), TT[po:po + 64, hp * 4 + 3, :], ar, True, True,
                   r=['TT'], w=[f'pf{2 + par}'])
                mm(pf[4 + par][:, hp * 128:(hp + 1) * 128], TT[po:po + 64, hp * 4 + 0, :], TT[po:po + 64, hp * 4 + 2, :], True, True,
                   r=['TT'], w=[f'pf{4 + par}'])
            mk1 = v3(m1[kd], 4, 256)
            mk3 = v3(m3[kd], 4, 128)
            for par in range(2):
                tt('dve', M1[:, par:4:2, :], v3(pf[0 + par][:, :], 2, 256), mk1[:, 0:2, :], ALU.mult, r=[f'pf{0 + par}', 'rwmask'], w=['M1'])
                tt('dve', M2[:, par:4:2, :], v3(pf[2 + par][:, :], 2, 256), mk1[:, 0:2, :], ALU.mult, r=[f'pf{2 + par}', 'rwmask'], w=['M2'])
                tt('dve', Qb[0][:, par:4:2, :], v3(pf[4 + par][:, 0:256], 2, 128), mk3[:, 0:2, :], ALU.mult, r=[f'pf{4 + par}', 'rwmask'], w=['Q0'])
            cp('pool', QTb[0], M1[:, :, 0:128], r=['M1'], w=['QT0'])
            tt('pool', PTb[0], M1[:, :, 0:128], ident_b.unsqueeze(1).to_broadcast([128, 4, 128]), ALU.add, r=['M1', 'ident_b'], w=['PT0'])
            nstep = 5 if kd == 'p' else 2
            for st_ in range(nstep):
                a, b = st_ % 2, (st_ + 1) % 2
                lastst = st_ == nstep - 1
                for h in range(4):
                    mm(pf[0][:, h * 128:(h + 1) * 128], QTb[a][:, h, :], Qb[a][:, h, :], True, True, r=[f'QT{a}', f'Q{a}'], w=['pf0'])
                cp('act', Qb[b], v3(pf[0][:, :], 4, 128), r=['pf0'], w=[f'Q{b}'])
                if not lastst:
                    for h in range(4):
                        mm(pf[1][:, h * 128:(h + 1) * 128], Qb[a][:, h, :], QTb[a][:, h, :], True, True, r=[f'QT{a}', f'Q{a}'], w=['pf1'])
                    cp('dve', QTb[b], v3(pf[1][:, :], 4, 128), r=['pf1'], w=[f'QT{b}'])
                for h in range(4):
                    mm(pf[2][:, h * 128:(h + 1) * 128], Qb[b][:, h, :], PTb[a][:, h, :], True, True, r=[f'Q{b}', f'PT{a}'], w=['pf2'])
                tt('dve', PTb[b], v3(pf[2][:, :], 4, 128), PTb[a], ALU.add, r=['pf2', f'PT{a}'], w=[f'PT{b}'])
            PT = PTb[nstep % 2]
            ptk = f'PT{nstep % 2}'
            for ci, (r0, R) in enumerate(chunks):
                for hp in range(2):
                    mm(pf[3][:, hp * 2 + ci:hp * 2 + ci + 1], Dinc[:, hp * 128:(hp + 1) * 128], selp[:, r0 + R - 1:r0 + R], True, True,
                       r=['Dinc', 'rwmask'], w=['pf3'])
            cp('act', DCt, pf[3][:, 0:4], r=['pf3'], w=['DCt'])
            for ci, (r0, R) in enumerate(chunks):
                rs = slice(r0, r0 + 64)
                cs_ = slice(r0, r0 + 64)
                for hp in range(2):
                    mm(pf[4][rs, hp * 128:(hp + 1) * 128], TT[:, hp * 4 + 0, cs_], STb[:, hp, :], True, False, r=['TT', 'STb'], w=['pf4'])
                    for h2 in range(2):
                        h = hp * 2 + h2
                        mm(pf[4][rs, h * 64:(h + 1) * 64], M2[:, h, cs_], vbf[:, h * 64:(h + 1) * 64], False, h2 == 1, r=['M2', 'vbf'], w=['pf4'])
                cp('act', RHSb[rs, :], pf[4][rs, 0:256], r=['pf4'], w=['RHSb'])
                for h in range(4):
                    mm(pf[5][rs, h * 64:(h + 1) * 64], PT[:, h, cs_], RHSb[:, h * 64:(h + 1) * 64], True, True, r=[ptk, 'RHSb'], w=['pf5'])
                cp('dve', Ub[rs, :], pf[5][rs, 0:256], r=['pf5'], w=['Ub'])
                for hp in range(2):
                    mm(pf[4][rs, hp * 128:(hp + 1) * 128], TT[:, hp * 4 + 1, cs_], STb[:, hp, :], True, False, r=['TT', 'STb'], w=['pf4'])
                    for h2 in range(2):
                        h = hp * 2 + h2
                        mm(pf[4][rs, h * 64:(h + 1) * 64], M1[:, h, 128 + r0:128 + r0 + 64], Ub[:, h * 64:(h + 1) * 64], False, False, r=['M1', 'Ub'], w=['pf4'])
                        mm(pf[4][rs, h * 64:(h + 1) * 64], M2[:, h, 128 + r0:128 + r0 + 64], vbf[:, h * 64:(h + 1) * 64], False, h2 == 1, r=['M2', 'vbf'], w=['pf4'])
                cp('act', ysb[rs, :], pf[4][rs, 0:256], r=['pf4'], w=['ysb'])
                rr_ = slice(r0, r0 + R)
                for hp in range(2):
                    mm(pf[5][:, hp * 128:(hp + 1) * 128], TM[rr_, 2, hp * 128:(hp + 1) * 128], Ub[rr_, hp * 128:(hp + 1) * 128], True, False, r=['TM', 'Ub'], w=['pf5'])
                    mm(pf[5][:, hp * 128:(hp + 1) * 128], TM[rr_, 3, hp * 128:(hp + 1) * 128], vbf[rr_, hp * 128:(hp + 1) * 128], False, True, r=['TM', 'vbf'], w=['pf5'])
                for hp in range(2):
                    dc = DCt[:, hp * 2 + ci:hp * 2 + ci + 1]
                    tt('dve', t1[:, 0:128], pf[5][:, hp * 128:(hp + 1) * 128], bdm, ALU.mult, r=['pf5', 'rwmask'], w=['t1'])
                    ts('dve', STf[:, hp, :], STf[:, hp, :], dc, None, ALU.mult, None, r=['STf', 'DCt'], w=['STf'])
                    stt(STf[:, hp, :], t1[:, 0:128], dc, STf[:, hp, :], ALU.mult, ALU.add, r=['t1', 'DCt', 'STf'], w=['STf'])
                cp('pool', STb, STf, r=['STf'], w=['STb'])
            y3 = v3(ysb, 4, 64)
            for h in range(4):
                P.op('dve', lambda e, h=h: e.bn_stats(out=sm[:, 16 + 6 * h:22 + 6 * h], in_=ysb[:, h * 64:(h + 1) * 64]), r=['ysb'], w=['sm2'])
                P.op('dve', lambda e, h=h: e.bn_aggr(out=sm[:, 40 + 2 * h:42 + 2 * h], in_=sm[:, 16 + 6 * h:22 + 6 * h]), r=['sm2'], w=['sm2'])
            mvv = v3(sm[:, 40:48], 4, 2)
            act(sm[:, 48:52], mvv[:, :, 1], AF.Sqrt, r=['sm2'], w=['sm3'], bias=GN_EPS, scale=1.0)
            P.op('dve', lambda e: e.reciprocal(out=sm[:, 48:52], in_=sm[:, 48:52]), r=['sm3'], w=['sm3'])
            tt('dve', y3, y3, mvv[:, :, 0].unsqueeze(2).to_broadcast([128, 4, 64]), ALU.subtract, r=['ysb', 'sm2'], w=['ysb'])
            tt('dve', y3, y3, sm[:, 48:52].unsqueeze(2).to_broadcast([128, 4, 64]), ALU.mult, r=['ysb', 'sm3'], w=['ysb'])
            tt('pool', ysb, ysb, prm['gng'], ALU.mult, r=['ysb', 'rwp'], w=['ysb'])
            tt('pool', ysb, ysb, prm['gnb'], ALU.add, r=['ysb', 'rwp'], w=['ysb'])
            tt('dve', v3(t2, 4, 64), v3(vsb, 4, 64), sm[:, 8:12].unsqueeze(2).to_broadcast([128, 4, 64]), ALU.mult, r=['vsb', 'sm', 't2'], w=['t2'])
            tt('dve', ysb, ysb, t2, ALU.add, r=['ysb', 't2'], w=['ysb'])
            tt('dve', ycb, ysb, gg, ALU.mult, r=['ysb', 'gg'], w=['ycb'])

        def yc_to_brT(dst, dstk, ncols):
            pty = v3(pb[1][:, 0:256], 2, 128)
            for c in range(2):
                tr(pty[:, c, :], ycb[:, c * 128:(c + 1) * 128], ident_b, r=['ycb', 'ident_b'], w=['pb1'])
            cp('act', dst, pty[:, :, 0:ncols], r=['pb1'], w=[dstk])

        def shift_out(xt, xk, col, dst):
            for (bk, ca, cb) in ((1, 0, 512), (2, 512, RWC)):
                i = 0
                for kc in range(8):
                    for (Wt, Wk) in ((W1, 'W1'), (W2, 'W2')):
                        mm(pf[bk][0:1, 0:cb - ca], xt[:, kc, col:col + 1], Wt[:, kc, ca:cb], i == 0, i == 15, r=[Wk, xk], w=[f'pf{bk}'])
                        i += 1
                cp('act', psh[0:1, ca:cb], pf[bk][0:1, 0:cb - ca], r=[f'pf{bk}'], w=['psh'])
            P.dma('sp', dst, psh[0:1, :], r=['psh'], w=['o_shift'])

        def state_out(dst4):
            for hp in range(2):
                tr(pf[3][:, hp * 128:(hp + 1) * 128], STf[:, hp, :], ident_f, r=['STf', 'ident_f'], w=['pf3'])
            cp('act', v3(t1, 2, 128), v3(pf[3][:, 0:256], 2, 128), r=['pf3'], w=['t1'])
            t1v = v3(t1, 2, 128)
            for hp in range(2):
                for h2 in range(2):
                    P.dma('sp', dst4[hp * 2 + h2], t1v[h2 * 64:(h2 + 1) * 64, hp, h2 * 64:(h2 + 1) * 64], r=['t1'], w=['o_wkv'])

        memset('dve', STf, 0.0, w=['STf'])
        memset('pool', STb, 0.0, w=['STb'])
        for u in range(NU):
            ub = u % 2
            xt = xTu[ub]
            xk = f'xTu{ub}'
            P.dma('sp', xt[:, :, 1:513], xT_s[u, :, :, :], r=[f'xT_s{u}'], w=[xk])
            if u == 0:
                memset('pool', xt[:, :, 0:1], 0.0, w=[xk])
            else:
                cp('pool', xt[:, :, 0:1], xTu[1 - ub][:, :, 512:513], r=[f'xTu{1 - ub}'], w=[xk])
            for ti in range(4):
                rw_tile(xt, xk, 1 + ti * 128, 'p', [(0, 64), (64, 64)])
                yc_to_brT(brC[:, :, ti * 128:(ti + 1) * 128], 'brC', 128)
            P.dma('sp', brT_s[u, :, 4:6, :], brC, r=['brC'], w=[f'brT_s{u}c'])
            if u == NU - 1:
                shift_out(xt, xk, 512, p_shift[l:l + 1, :])
        state_out(p_wkv[l])
        xts = xTu[0]
        P.dma('sp', xts[:, :, 0:128], xT_s[NU, :, :, 0:128], r=[f'xT_s{NU}'], w=['xTu0'])
        for si in range(NSS):
            cp('pool', xTq[:, :, 1:1 + SL], xts[:, :, si * SL:(si + 1) * SL], r=['xTu0'], w=['xTq'])
            for hp in range(2):
                for h2 in range(2):
                    P.dma('sp', Xn[h2 * 64:(h2 + 1) * 64, hp, h2 * 64:(h2 + 1) * 64], swkv[l, si, hp * 2 + h2], r=['Xn'], w=['Xn'])
            for hp in range(2):
                tr(pf[3][:, hp * 128:(hp + 1) * 128], Xn[:, hp, :], ident_f, r=['Xn', 'ident_f'], w=['pf3'])
            cp('act', STf, v3(pf[3][:, 0:256], 2, 128), r=['pf3'], w=['STf'])
            cp('pool', STb, STf, r=['STf'], w=['STb'])
            rw_tile(xTq, 'xTq', 1, 's', [(0, SL)], extra_s=si)
            yc_to_brT(brC[:, :, si * SL:(si + 1) * SL], 'brC', SL)
            shift_out(xTq, 'xTq', SL, s_shift[l, si:si + 1, :])
            state_out(s_wkv[l, si])
        P.dma('sp', brT_s[NU, :, 4:6, 0:128], brC[:, :, 0:128], r=['brC'], w=[f'brT_s{NU}c'])
        AF_.release(); AB_.release()
        P.barrier()

    if '0' in cfg.stages:
        stage0()
    for l in range(cfg.nlayers):
        if 'A' in cfg.stages:
            stageA1(l)
        if 'R' in cfg.stages:
            stageA2(l)
        if 'B' in cfg.stages:
            stageB(l)
        if 'C' in cfg.stages:
            stageC(l)
        if 'D' in cfg.stages:
            stageD(l, 0)
            stageD(l, 1)

    with nc.allow_non_contiguous_dma(reason="small parameter / state transfers"):
        P.emit()
    return P


def make_in_maps(cfg, inputs, ncores=8):
    consts = host_constants(cfg)
    f = lambda a: np.ascontiguousarray(np.asarray(a))
    maps = []
    nb = inputs['x_prompt'].shape[0]
    for c in range(ncores):
        b = c % nb
        ss = slice(NSS * c, NSS * (c + 1))
        m = {
            'xp': f(inputs['x_prompt'][b]),
            'xs': f(inputs['x_sample'][ss]).reshape(128, D),
            'memp': f(inputs['mem_prompt'][b]),
            'cache_k0': f(inputs['cache_k'][0]).reshape(-1, 256),
            'cache_k1': f(inputs['cache_k'][1]).reshape(-1, 256),
            'cache_v0': f(inputs['cache_v'][0]).reshape(-1, 256),
            'cache_v1': f(inputs['cache_v'][1]).reshape(-1, 256),
            'ptab': f(inputs['page_table'][ss]).reshape(-1).astype(np.int32),
            'cmk': f(inputs['cache_mem_k'][:, ss]).reshape(DEPTH, NSS, NMEM, D),
            'cmv': f(inputs['cache_mem_v'][:, ss]).reshape(DEPTH, NSS, NMEM, D),
            'sconv': f(inputs['state_conv'][:, ss]),
            'swkv': f(inputs['state_wkv'][:, ss]),
            'sshift': f(inputs['state_shift'][:, ss]),
            'w_branch': f(inputs['w_branch']).reshape(DEPTH, 4 * MIXW, D),
            'rw_rk': f(inputs['rw_rk']).reshape(DEPTH, MIXW),
        }
        for k in ['w_in', 'sb_bias', 'w_gate', 'b_gate', 'w_o', 'conv_w', 'rw_mu', 'rw_w0', 'rw_w2', 'rw_a0',
                  'rw_a2', 'rw_g2', 'rw_kk', 'rw_ka', 'rw_gn_g', 'rw_gn_b', 'sgu_ln_g', 'sgu_ln_b', 'sgu_ws',
                  'sgu_b', 'w_mq', 'w_mk', 'w_mv', 'w_mo', 'w_up', 'w_down', 'ln1_g', 'ln1_b', 'ln2_g', 'ln2_b',
                  'ln3_g', 'ln3_b']:
            m[k] = f(inputs[k])
        for k, v in consts.items():
            m['c_' + k] = v
        maps.append(m)
    return maps


def run(cfg, inputs, ncores=8):
    nc = build_program(cfg)
    maps = make_in_maps(cfg, inputs, ncores)
    res = run_bass_kernel_spmd(nc, maps, core_ids=list(range(ncores)))
    return res.results


def assemble(cfg, R, nb=4, ncores=8):
    SEQ = cfg.SEQ
    g = lambda name, cores: np.stack([R[c][name] for c in cores])
    pc = list(range(nb))
    ac = list(range(ncores))
    y_p = g('y_p', pc)
    y_s = g('y_s', ac).reshape(ncores * NSS, SL, D)
    def pl(name, shp):
        a = g(name, pc)
        return np.ascontiguousarray(np.moveaxis(a, 0, 1)).reshape(shp)
    def sl_(name, shp):
        a = g(name, ac)
        return np.ascontiguousarray(np.moveaxis(a, 0, 1)).reshape(shp)
    NS = ncores * NSS
    return (y_p, y_s,
            pl('p_k', (DEPTH, nb, SEQ, 4, 64)), pl('p_v', (DEPTH, nb, SEQ, 4, 64)),
            pl('p_mk', (DEPTH, nb, NMEM, 4, 256)), pl('p_mv', (DEPTH, nb, NMEM, 4, 256)),
            pl('p_conv', (DEPTH, nb, 2, MIXW)), pl('p_wkv', (DEPTH, nb, 4, 64, 64)), pl('p_shift', (DEPTH, nb, RWC)),
            sl_('s_k', (DEPTH, NS, SL, 4, 64)), sl_('s_v', (DEPTH, NS, SL, 4, 64)),
            sl_('s_conv', (DEPTH, NS, 2, MIXW)), sl_('s_wkv', (DEPTH, NS, 4, 64, 64)),
            sl_('s_shift', (DEPTH, NS, RWC)), sl_('s_chunk', (DEPTH, NS, SL, MIXW)))


def kernel(**inputs):
    SEQ = inputs['x_prompt'].shape[1]
    NPAGES = inputs['page_table'].shape[1]
    NPHYS = inputs['cache_k'].shape[1]
    cfg = Cfg(SEQ=SEQ, NPAGES=NPAGES, NPHYS=NPHYS)
    R = run(cfg, inputs)
    outs = assemble(cfg, R, nb=inputs['x_prompt'].shape[0])
    return tuple(np.ascontiguousarray(o.astype(np.float32)) for o in outs)
```

```python
import contextlib
import os
import numpy as np
SKIP = os.environ.get('KSKIP', '')
import concourse.bass as bass
import concourse.mybir as mybir
from concourse.bass_utils import run_bass_kernel_spmd

F32 = mybir.dt.float32
BF16 = mybir.dt.bfloat16
I32 = mybir.dt.int32
AF = mybir.ActivationFunctionType
ALU = mybir.AluOpType
AX = mybir.AxisListType

D = 1024
DEPTH = 2
MIXW = 256
RWC = 896
INC = 2944
DFF = 4096
NMEM = 256
ALPHA = (2 * DEPTH) ** 0.25
LN_EPS = 1e-5
GN_EPS = 64e-5
NSS = 16
SL = 8


class Prog:
    def __init__(self, nc, es):
        self.nc = nc
        self.es = es
        self.eh = {'pe': nc.tensor, 'act': nc.scalar, 'dve': nc.vector, 'pool': nc.gpsimd, 'sp': nc.sync}
        self.ops = []
        self.last_w = {}
        self.readers = {}
        self.EPOCH = 12000
        self._rec = None
        self.barriers = []
        self.NSLOT = 12

    def rec_start(self):
        self._rec = []

    def rec_stop(self):
        r_ = self._rec
        self._rec = None
        return r_

    def merge(self, lists):
        n = [len(x) for x in lists]
        pos = [0] * len(lists)
        tot = max(n)
        for step in range(tot):
            for i, lst in enumerate(lists):
                tgt = (step + 1) * n[i] // tot
                while pos[i] < tgt:
                    e_, f_, r_, w_, d_ = lst[pos[i]]
                    self.op(e_, f_, r=r_, w=w_, dma=d_)
                    pos[i] += 1

    def op(self, eng, fn, r=(), w=(), dma=False):
        if self._rec is not None:
            self._rec.append((eng, fn, tuple(r), tuple(w), dma))
            return -1
        pr = [k for k in r if isinstance(k, str) and (k.startswith('pf') or k.startswith('pb'))]
        if pr:
            r = [k for k in r if k not in pr]
            w = list(w) + pr
        deps = set()
        for k in r:
            if k in self.last_w:
                deps.add(self.last_w[k])
        for k in w:
            if k in self.last_w:
                deps.add(self.last_w[k])
            deps.update(self.readers.get(k, ()))
        idx = len(self.ops)
        self.ops.append(dict(eng=eng, fn=fn, deps=deps, dma=dma, sig=False, bar=len(self.barriers)))
        for k in r:
            self.readers.setdefault(k, []).append(idx)
        for k in w:
            self.last_w[k] = idx
            self.readers[k] = []
        return idx

    def dma(self, q, out, in_, r=(), w=(), **kw):
        return self.op(q, lambda e: e.dma_start(out=out, in_=in_, **kw), r=r, w=w, dma=True)

    def barrier(self):
        lastc = {}
        ndma = {e: 0 for e in self.eh}
        for i, o in enumerate(self.ops):
            if o['dma']:
                ndma[o['eng']] += 1
            else:
                lastc[o['eng']] = i
        self.barriers.append((lastc, ndma))
        self.bar_at = getattr(self, 'bar_at', []) + [len(self.ops)]
        self.last_w = {}
        self.readers = {}

    def emit(self):
        nc = self.nc
        kstop = int(os.environ.get('KSTOP', '0'))
        print('PROG n_ops', len(self.ops), 'kstop', kstop, flush=True)
        if kstop:
            self.ops = self.ops[:kstop]
            nbar = sum(1 for a in getattr(self, 'bar_at', []) if a <= kstop)
            self.barriers = self.barriers[:nbar]
            o_ = self.ops[-1]
            print('LAST OP', o_['eng'], o_['dma'], o_['fn'].__code__.co_firstlineno if o_['fn'] else None, flush=True)
        self.barrier()
        ops = self.ops
        n = len(ops)
        for (lastc, ndma) in self.barriers:
            for e, i in lastc.items():
                ops[i]['sig'] = True
        for i, o in enumerate(ops):
            best = {}
            dd = []
            for d in o['deps']:
                od = ops[d]
                if od['dma']:
                    dd.append(d)
                else:
                    e = od['eng']
                    if e == 'pe' and o['eng'] == 'pe' and not o['dma']:
                        continue
                    if e not in best or best[e] < d:
                        best[e] = d
            o['pd'] = dd + list(best.values())
            for d in o['pd']:
                ops[d]['sig'] = True
        cnt = {e: 0 for e in self.eh}
        dcnt = {e: 0 for e in self.eh}
        nsig = {e: 0 for e in self.eh}
        for o in ops:
            if (not o['dma']) and o['sig']:
                nsig[o['eng']] += 1
        sems = {}
        for e in self.eh:
            ne = nsig[e] // self.EPOCH + 1
            sems[e] = [self.es.enter_context(nc.semaphore(f"s_{e}_{k}")) for k in range(ne)]
        dsems = {e: [self.es.enter_context(nc.semaphore(f"d_{e}_{k}")) for k in range(self.NSLOT)]
                 for e in ['sp', 'pool', 'act']}
        for o in ops:
            e = o['eng']
            if o['dma']:
                j = dcnt[e]
                dcnt[e] += 1
                o['tok'] = (dsems[e][j % self.NSLOT], 16 * (j // self.NSLOT + 1))
                o['prev'] = (dsems[e][j % self.NSLOT], 16 * (j // self.NSLOT)) if j >= self.NSLOT else None
            elif o['sig']:
                c = cnt[e]
                cnt[e] += 1
                o['tok'] = (sems[e][c // self.EPOCH], c % self.EPOCH + 1)
        waited = {e: {} for e in self.eh}
        nwait = [0]

        def wait(e, tok):
            s, v = tok
            k = id(s)
            if waited[e].get(k, 0) < v:
                self.eh[e].wait_ge(s, v)
                waited[e][k] = v
                nwait[0] += 1

        def bar_wait(e, b):
            lastc, ndma = self.barriers[b]
            for e2, i in lastc.items():
                if e2 != e:
                    wait(e, ops[i]['tok'])
            for q in ['sp', 'pool', 'act']:
                m = ndma[q]
                for sl in range(self.NSLOT):
                    if m <= sl:
                        continue
                    j = ((m - 1 - sl) // self.NSLOT) * self.NSLOT + sl
                    wait(e, (dsems[q][sl], 16 * (j // self.NSLOT + 1)))

        curbar = {e: 0 for e in self.eh}
        for o in ops:
            e = o['eng']
            while curbar[e] < o['bar']:
                bar_wait(e, curbar[e])
                curbar[e] += 1
            toks = {}
            for d in o['pd']:
                s_, v_ = ops[d]['tok']
                if id(s_) not in toks or toks[id(s_)][1] < v_:
                    toks[id(s_)] = (s_, v_)
            for tk in toks.values():
                wait(e, tk)
            if o['dma'] and o['prev'] is not None:
                wait(e, o['prev'])
            ins = o['fn'](self.eh[e])
            if o['dma']:
                ins.then_inc(o['tok'][0], 16)
            elif o['sig']:
                ins.then_inc(o['tok'][0], 1)
        bar_wait('sp', len(self.barriers) - 1)
        self.n_ops = n
        self.n_wait = nwait[0]


class Cfg:
    def __init__(self, SEQ=4096, NPAGES=16, NPHYS=2560, debug=False, nlayers=DEPTH, stages="0ARBCD"):
        self.SEQ = SEQ
        self.NPAGES = NPAGES
        self.NPHYS = NPHYS
        self.NU = SEQ // 512
        self.NT = SEQ // 128
        self.debug = debug
        self.nlayers = nlayers
        self.stages = stages


def host_constants(cfg):
    c = {}
    c['ident'] = np.eye(128, dtype=np.float32)
    j = np.arange(128)
    c['lx'] = (j[:, None] < j[None, :]).astype(np.float32)
    c['ones'] = np.ones((128, 128), np.float32)
    c['sbmask'] = (j[:, None] < j[None, :]).astype(np.float32)
    sj, tj = j // SL, j % SL
    c['sbmask_s'] = ((sj[:, None] == sj[None, :]) & (tj[:, None] < tj[None, :])).astype(np.float32)
    def rwm(ch):
        cj = j // ch
        same = cj[:, None] == cj[None, :]
        su = same & (j[:, None] < j[None, :])
        iu = same & (j[:, None] <= j[None, :])
        sl = same & (j[:, None] > j[None, :])
        m1 = np.concatenate([su, iu], 1).astype(np.float32)
        return (np.tile(m1[:, None, :], (1, 4, 1)).reshape(128, 1024),
                np.tile(sl.astype(np.float32)[:, None, :], (1, 4, 1)).reshape(128, 512),
                iu.astype(np.float32))
    c['rwm1_p'], c['rwm3_p'], c['tri_p'] = rwm(64)
    c['rwm1_s'], c['rwm3_s'], c['tri_s'] = rwm(SL)
    sel = np.zeros((128, 2), np.float32)
    sel[63, 0] = 1
    sel[127, 1] = 1
    c['sel_p'] = sel
    sels = np.zeros((128, NSS), np.float32)
    for s in range(NSS):
        sels[s * SL + SL - 1, s] = 1
    c['sel_s'] = sels
    sf = np.zeros((NSS, 128), np.float32)
    for s in range(NSS):
        sf[s, s * SL] = 1
    c['self_s'] = sf
    c['seqmask'] = (sj[:, None] == np.arange(NSS)[None, :]).astype(np.float32)
    c['piota'] = np.arange(128, dtype=np.float32).reshape(128, 1)
    c['sgumask_s'] = ((sj[:, None] == sj[None, :]) & (tj[:, None] <= tj[None, :])).astype(np.float32)
    c['tril'] = (j[:, None] <= j[None, :]).astype(np.float32)
    c['bdm'] = ((j[:, None] // 64) == (j[None, :] // 64)).astype(np.float32)
    ohm = np.zeros((NSS, NSS * 128), np.float32)
    for s_ in range(NSS):
        ohm[s_, s_ * 128] = 1
    c['oh'] = ohm
    c['selrep'] = (j[:, None] == (j[None, :] % SL)).astype(np.float32)
    return c


CONST_SHAPES = None


def build_program(cfg):
    nc = bass.Bass("TRN2", target_bir_lowering=False)
    es = contextlib.ExitStack()
    with es:
        _build(nc, es, cfg)
    return nc


def _build(nc, es, cfg):
    P = Prog(nc, es)
    SEQ, NU, NT, NPAGES, NPHYS = cfg.SEQ, cfg.NU, cfg.NT, cfg.NPAGES, cfg.NPHYS
    NUA = NU + 1
    NTA = NT + 1
    dbg = cfg.debug

    def din(name, shape, dt=F32):
        return nc.dram_tensor(name, list(shape), dt, kind="ExternalInput").ap()

    def dout(name, shape, dt=F32):
        return nc.dram_tensor(name, list(shape), dt, kind="ExternalOutput").ap()

    def dscr(name, shape, dt=F32):
        if dbg:
            return nc.dram_tensor(name, list(shape), dt, kind="ExternalOutput").ap()
        return nc.dram_tensor(name, list(shape), dt, kind="Internal").ap()

    xp = din("xp", [SEQ, D])
    xs = din("xs", [128, D])
    memp = din("memp", [NMEM, D])
    cache_k = [din(f"cache_k{i}", [NPHYS * 128, 256]) for i in range(DEPTH)]
    cache_v = [din(f"cache_v{i}", [NPHYS * 128, 256]) for i in range(DEPTH)]
    ptab = din("ptab", [NSS * NPAGES], I32)
    cmk = din("cmk", [DEPTH, NSS, NMEM, D])
    cmv = din("cmv", [DEPTH, NSS, NMEM, D])
    sconv = din("sconv", [DEPTH, NSS, 2, MIXW])
    swkv = din("swkv", [DEPTH, NSS, 4, 64, 64])
    sshift = din("sshift", [DEPTH, NSS, RWC])
    w_in = din("w_in", [DEPTH, D, INC])
    sb_bias = din("sb_bias", [DEPTH, 4])
    w_gate = din("w_gate", [DEPTH, D, 4 * D])
    b_gate = din("b_gate", [DEPTH, 4 * D])
    w_branch = din("w_branch", [DEPTH, 4 * MIXW, D])
    w_o = din("w_o", [DEPTH, D, D])
    conv_w = din("conv_w", [DEPTH, 3, MIXW])
    rw_mu = din("rw_mu", [DEPTH, RWC])
    rw_w0 = din("rw_w0", [DEPTH, MIXW])
    rw_w2 = din("rw_w2", [DEPTH, 32, MIXW])
    rw_a0 = din("rw_a0", [DEPTH, MIXW])
    rw_a2 = din("rw_a2", [DEPTH, 32, MIXW])
    rw_g2 = din("rw_g2", [DEPTH, 64, MIXW])
    rw_kk = din("rw_kk", [DEPTH, MIXW])
    rw_ka = din("rw_ka", [DEPTH, MIXW])
    rw_rk = din("rw_rk", [DEPTH, MIXW])
    rw_gn_g = din("rw_gn_g", [DEPTH, MIXW])
    rw_gn_b = din("rw_gn_b", [DEPTH, MIXW])
    sgu_ln_g = din("sgu_ln_g", [DEPTH, MIXW])
    sgu_ln_b = din("sgu_ln_b", [DEPTH, MIXW])
    sgu_ws = din("sgu_ws", [DEPTH, 4, 128, 128])
    sgu_b = din("sgu_b", [DEPTH, 4, 128])
    w_mq = din("w_mq", [DEPTH, D, D])
    w_mk = din("w_mk", [DEPTH, D, D])
    w_mv = din("w_mv", [DEPTH, D, D])
    w_mo = din("w_mo", [DEPTH, D, D])
    w_up = din("w_up", [DEPTH, D, DFF])
    w_down = din("w_down", [DEPTH, DFF, D])
    lng = [din(f"ln{i}_g", [DEPTH, D]) for i in (1, 2, 3)]
    lnb = [din(f"ln{i}_b", [DEPTH, D]) for i in (1, 2, 3)]
    consts = host_constants(cfg)
    cin = {k: din("c_" + k, v.shape) for k, v in consts.items()}

    y_p = dout("y_p", [SEQ, D])
    y_s = dout("y_s", [128, D])
    p_k = dout("p_k", [DEPTH, SEQ, 256])
    p_v = dout("p_v", [DEPTH, SEQ, 256])
    p_mk = dout("p_mk", [DEPTH, NMEM, D])
    p_mv = dout("p_mv", [DEPTH, NMEM, D])
    p_conv = dout("p_conv", [DEPTH, 2, MIXW])
    p_wkv = dout("p_wkv", [DEPTH, 4, 64, 64])
    p_shift = dout("p_shift", [DEPTH, RWC])
    s_k = dout("s_k", [DEPTH, 128, 256])
    s_v = dout("s_v", [DEPTH, 128, 256])
    s_conv = dout("s_conv", [DEPTH, NSS, 2, MIXW])
    s_wkv = dout("s_wkv", [DEPTH, NSS, 4, 64, 64])
    s_shift = dout("s_shift", [DEPTH, NSS, RWC])
    s_chunk = dout("s_chunk", [DEPTH, 128, 256])

    xT_s = dscr("xT_s", [NUA, 128, 8, 512], BF16)
    xres_s = dscr("xres_s", [NTA, 128, D])
    brT_s = dscr("brT_s", [NUA, 128, 8, 512], BF16)
    mixT_s = dscr("mixT_s", [NUA, 128, 8, 512], BF16)

    FA = 20480
    BA = 61440
    fa = es.enter_context(nc.sbuf_tensor("fa", [128, FA], F32))
    ba = es.enter_context(nc.sbuf_tensor("ba", [128, BA], BF16))
    ia = es.enter_context(nc.sbuf_tensor("ia", [128, 512], I32))
    pf = [es.enter_context(nc.psum_tensor(f"pf{i}", [128, 512], F32)) for i in range(6)]
    pb = [es.enter_context(nc.psum_tensor(f"pb{i}", [128, 1024], BF16)) for i in range(2)]

    class Arena:
        def __init__(self, t, size, nm):
            self.t, self.size, self.nm, self.off, self.marks = t, size, nm, 0, []

        def alloc(self, n):
            n2 = (n + 15) // 16 * 16
            assert self.off + n2 <= self.size, (self.nm, self.off, n2, self.size)
            a = self.t[:, self.off:self.off + n]
            self.off += n2
            return a

        def mark(self):
            self.marks.append(self.off)

        def release(self):
            self.off = self.marks.pop()

    AF_, AB_ = Arena(fa, FA, 'fa'), Arena(ba, BA, 'ba')

    def falloc(n):
        return AF_.alloc(n)

    def balloc(n):
        return AB_.alloc(n)

    units = [('p', u, 512, 4) for u in range(NU)] + [('s', NU, 128, 1)]

    def unit_tiles(u):
        kind, ui, W, nt = units[u]
        return list(range(4 * ui, 4 * ui + nt)) if kind == 'p' else [NT]

    ident_b = balloc(128)
    lx_b = balloc(128)
    ones_b = balloc(128)
    ident_f = falloc(128)
    P.dma('pool', ident_b, cin['ident'][:, :], w=['ident_b'])
    P.dma('pool', lx_b, cin['lx'][:, :], w=['lx_b'])
    P.dma('pool', ones_b, cin['ones'][:, :], w=['ones_b'])
    P.dma('sp', ident_f, cin['ident'][:, :], w=['ident_f'])
    AF_.mark()
    AB_.mark()

    def mm(out, lhsT, rhs, start, stop, r, w):
        return P.op('pe', lambda e: e.matmul(out, lhsT=lhsT, rhs=rhs, start=start, stop=stop), r=r, w=w)

    def tr(out, in_, idt, r, w):
        return P.op('pe', lambda e: e.transpose(out=out, in_=in_, identity=idt), r=r, w=w)

    def act(out, in_, func, r, w, **kw):
        return P.op('act', lambda e: e.activation(out=out, in_=in_, func=func, **kw), r=r, w=w)

    def tt(eng, out, in0, in1, op, r, w):
        return P.op(eng, lambda e: e.tensor_tensor(out=out, in0=in0, in1=in1, op=op), r=r, w=w)

    def ts(eng, out, in0, s1, s2, op0, op1, r, w):
        if op1 is None:
            return P.op(eng, lambda e: e.tensor_scalar(out=out, in0=in0, scalar1=s1, scalar2=None, op0=op0), r=r, w=w)
        return P.op(eng, lambda e: e.tensor_scalar(out=out, in0=in0, scalar1=s1, scalar2=s2, op0=op0, op1=op1), r=r, w=w)

    def stt(out, in0, sc, in1, op0, op1, r, w):
        return P.op('dve', lambda e: e.scalar_tensor_tensor(out=out, in0=in0, scalar=sc, in1=in1, op0=op0, op1=op1), r=r, w=w)

    def cp(eng, out, in_, r, w):
        if eng == 'act':
            return P.op('act', lambda e: e.activation(out=out, in_=in_, func=AF.Copy), r=r, w=w)
        return P.op(eng, lambda e: e.tensor_copy(out=out, in_=in_), r=r, w=w)

    def rsqrt_(out, in_, eps, r, w):
        act(out, in_, AF.Sqrt, r=r, w=w, bias=eps, scale=1.0)
        P.op('dve', lambda e: e.reciprocal(out=out, in_=out), r=w, w=w)

    def memset(eng, ap, val, w):
        return P.op(eng, lambda e: e.memset(ap, val), w=w)

    bankc = [0]

    def nb():
        b = bankc[0] % 4
        bankc[0] += 1
        return b

    def v3(ap, a, b):
        return ap.rearrange("p (a b) -> p a b", a=a, b=b)

    def layernorm(t, tk, g_t, b_t, out, outk, outb, outbk, scr, scrk):
        st = scr[:, 0:12]
        mv = scr[:, 12:14]
        rs = scr[:, 14:15]
        P.op('dve', lambda e: e.bn_stats(out=st[:, 0:6], in_=t[:, 0:512]), r=[tk], w=[scrk])
        P.op('dve', lambda e: e.bn_stats(out=st[:, 6:12], in_=t[:, 512:1024]), r=[tk], w=[scrk])
        P.op('dve', lambda e: e.bn_aggr(out=mv, in_=st), r=[scrk], w=[scrk])
        rsqrt_(rs, mv[:, 1:2], LN_EPS, [scrk], [scrk])
        ts('dve', t, t, mv[:, 0:1], rs, ALU.subtract, ALU.mult, r=[tk, scrk], w=[tk])
        tt('pool', t, t, g_t, ALU.mult, r=[tk, 'lnp'], w=[tk])
        tt('dve', out, t, b_t, ALU.add, r=[tk, 'lnp'], w=[outk])
        if outb is not None:
            cp('act', outb, out, r=[outk], w=[outbk])

    def transpose_tile(src_b, srck, dst, dstk, pbi):
        pt = v3(pb[pbi][:, :], 8, 128)
        for kc in range(8):
            tr(pt[:, kc, :], src_b[:, kc * 128:(kc + 1) * 128], ident_b, r=[srck, 'ident_b'], w=[f'pb{pbi}'])
        cp('act', dst, pt, r=[f'pb{pbi}'], w=[dstk])

    def stage0():
        AF_.mark(); AB_.mark()
        xin = [balloc(1024) for _ in range(2)]
        xu = [v3(balloc(8 * 512), 8, 512) for _ in range(2)]
        for u in range(NUA):
            kind, ui, W, nt = units[u]
            ub = u % 2
            for ti, t in enumerate(unit_tiles(u)):
                sl = (t) % 2
                src = xp[t * 128:(t + 1) * 128, :] if kind == 'p' else xs[:, :]
                P.dma('pool', xin[sl], src, w=[f'xin{sl}'])
                transpose_tile(xin[sl], f'xin{sl}', xu[ub][:, :, ti * 128:(ti + 1) * 128], f'xu{ub}', sl)
            P.dma('sp', xT_s[u, :, :, 0:W], xu[ub][:, :, 0:W], r=[f'xu{ub}'], w=[f'xT_s{u}'])
        AF_.release(); AB_.release()
        P.barrier()


    def load_w(dst3, src2d, c0, c1, key, d0=0):
        kc_n = src2d.shape[0] // 128
        for kc in range(kc_n):
            P.dma('pool', dst3[:, kc, d0:d0 + (c1 - c0)], src2d[kc * 128:(kc + 1) * 128, c0:c1], w=[key])

    def bcast_row(dst, src1d, key, q='sp'):
        P.dma(q, dst, src1d.partition_broadcast(128), w=[key])

    def stageA1(l):
        AF_.mark(); AB_.mark()
        Wc = v3(balloc(8 * 2048), 8, 2048)
        load_w(Wc, w_in[l], 0, 1536, 'Wc', 0)
        load_w(Wc, w_in[l], 2432, 2944, 'Wc', 1536)
        KTm = balloc(max(2 * SEQ, 8192))
        VCm = balloc(max(NT * 256, 4096))
        KT = v3(KTm[:, 0:2 * SEQ], 2, SEQ)
        VC = v3(VCm[:, 0:NT * 256], NT, 256)
        vsn = balloc(256)
        gst = v3(falloc(4096), NSS, 256)
        piota = falloc(1)
        P.dma('sp', piota, cin['piota'][:, :], w=['piota'])
        xTu = [v3(balloc(8 * 512), 8, 512) for _ in range(2)]
        qT = v3(balloc(2 * 512), 2, 512)
        kTs = v3(balloc(2 * 128), 2, 128)
        brTu = v3(balloc(8 * 512), 8, 512)
        spb = v3(balloc(4 * 512), 4, 512)
        wtb = v3(balloc(4 * 512), 4, 512)
        rsb = v3(balloc(4 * 512), 4, 512)
        sbm = balloc(128)
        sbm_s = balloc(128)
        tril_b = balloc(128)
        sgm_s = balloc(128)
        WsT = v3(balloc(4 * 128), 4, 128)
        WsT_s = v3(balloc(4 * 128), 4, 128)
        wsraw = v3(balloc(4 * 128), 4, 128)
        wsx = v3(balloc(4 * 128), 4, 128)
        selrep = balloc(128)
        vlnb = balloc(256)
        ones64 = ones_b[:, 0:64]
        e1 = v3(falloc(4 * 512), 4, 512)
        rsf = v3(falloc(4 * 512), 4, 512)
        fac = falloc(512)
        kvst = [falloc(512) for _ in range(2)]
        sbb = falloc(4)
        cw = v3(falloc(6), 2, 3)
        lg_t = falloc(256)
        lb_t = falloc(256)
        sgb_p = v3(falloc(256), 2, 128)
        sgb_s = v3(falloc(256), 2, 128)
        hbs = v3(falloc(2 * 512), 2, 512)
        zc = falloc(2 * 640)
        cva = v3(falloc(2 * 512), 2, 512)
        uT = v3(falloc(2 * 512), 2, 512)
        gtmp = falloc(512)
        svt = falloc(256)
        svn = falloc(256)
        lnscr = falloc(16)
        zhist = falloc(4)
        wsxf = v3(falloc(4 * 128), 4, 128)

        P.dma('pool', sbm, cin['sbmask'][:, :], w=['sbm'])
        P.dma('pool', sbm_s, cin['sbmask_s'][:, :], w=['sbm_s'])
        P.dma('pool', tril_b, cin['tril'][:, :], w=['tril_b'])
        P.dma('pool', sgm_s, cin['sgumask_s'][:, :], w=['sgm_s'])
        P.dma('pool', selrep, cin['selrep'][:, :], w=['selrep'])
        P.dma('sp', sbb, sb_bias[l].partition_broadcast(128), w=['sbb'])
        for c in range(2):
            P.dma('sp', cw[:, c, :], conv_w[l][:, c * 128:(c + 1) * 128].rearrange("j p -> p j"), w=['cw'])
        bcast_row(lg_t, sgu_ln_g[l], 'sgp')
        bcast_row(lb_t, sgu_ln_b[l], 'sgp')
        for g in range(4):
            po = (g % 2) * 64
            P.dma('sp', sgb_p[po:po + 64, g // 2, :], sgu_b[l, g].partition_broadcast(64), w=['sgb'])
            src = bass.AP(tensor=sgu_b.tensor, offset=(l * 4 + g) * 128, ap=[[0, 64], [0, NSS], [1, SL]])
            P.dma('sp', sgb_s[po:po + 64, g // 2, :].rearrange("p (a b) -> p a b", a=NSS, b=SL), src, w=['sgb'])
            P.dma('pool', wsraw[:, g, :], sgu_ws[l, g], w=['wsraw'])
        b0 = nb()
        pw = v3(pb[0][:, 0:512], 4, 128)
        for g in range(4):
            tr(pw[:, g, :], wsraw[:, g, :], ident_b, r=['wsraw', 'ident_b'], w=['pb0'])
        for g in range(4):
            tt('dve', WsT[:, g, :], pw[:, g, :], tril_b, ALU.mult, r=['pb0', 'tril_b'], w=['WsT'])
        for g in range(4):
            cp('dve', wsx[0:SL, g, :].rearrange("p (a b) -> p a b", a=NSS, b=SL),
               WsT[0:SL, g, 0:SL].unsqueeze(1).to_broadcast([SL, NSS, SL]), r=['WsT'], w=['wsx'])
        pw2 = v3(pf[b0][:, :], 4, 128)
        for g in range(4):
            mm(pw2[:, g, :], selrep[0:SL, :], wsx[0:SL, g, :], True, True, r=['selrep', 'wsx'], w=[f'pf{b0}'])
        for g in range(4):
            tt('dve', WsT_s[:, g, :], pw2[:, g, :], sgm_s, ALU.mult, r=[f'pf{b0}', 'sgm_s'], w=['WsT'])
        memset('pool', zhist, 0.0, w=['zhist'])

        def proj_fm(c0, W, xt, xk, bank):
            for kc in range(8):
                mm(pf[bank][:, 0:W], Wc[:, kc, c0:c0 + 128], xt[:, kc, 0:W], kc == 0, kc == 7,
                   r=['Wc', xk], w=[f'pf{bank}'])

        for u in range(NUA):
            kind, ui, W, ntl = units[u]
            ub = u % 2
            xt = xTu[ub]
            xk = f'xTu{ub}'
            P.dma('sp', xt[:, :, 0:W], xT_s[u, :, :, 0:W], r=[f'xT_s{u}'], w=[xk])
            tiles = unit_tiles(u)
            for c in range(2):
                b = nb()
                proj_fm(c * 128, W, xt, xk, b)
                act(qT[:, c, 0:W], pf[b][:, 0:W], AF.Copy, r=[f'pf{b}'], w=['qT'], scale=0.125)
                b = nb()
                proj_fm(256 + c * 128, W, xt, xk, b)
                if kind == 'p':
                    cp('dve', KT[:, c, ui * 512:ui * 512 + W], pf[b][:, 0:W], r=[f'pf{b}'], w=['KT'])
                else:
                    cp('dve', kTs[:, c, :], pf[b][:, 0:W], r=[f'pf{b}'], w=['kTs'])
            for ti, t in enumerate(tiles):
                b = nb()
                for kc in range(8):
                    mm(pf[b][:, :], xt[:, kc, ti * 128:(ti + 1) * 128], Wc[:, kc, 256:768], kc == 0, kc == 7,
                       r=['Wc', xk], w=[f'pf{b}'])
                sl = t % 2
                cp('act', kvst[sl], pf[b][:, :], r=[f'pf{b}'], w=[f'kvst{sl}'])
                if kind == 'p':
                    cp('dve', VC[:, t, :], pf[b][:, 256:512], r=[f'pf{b}'], w=['VC'])
                    P.dma('sp', p_k[l, t * 128:(t + 1) * 128, :], kvst[sl][:, 0:256], r=[f'kvst{sl}'], w=['o_pk'])
                    P.dma('sp', p_v[l, t * 128:(t + 1) * 128, :], kvst[sl][:, 256:512], r=[f'kvst{sl}'], w=['o_pv'])
                else:
                    cp('dve', vsn, pf[b][:, 256:512], r=[f'pf{b}'], w=['vsn'])
                    P.dma('sp', s_k[l, :, :], kvst[sl][:, 0:256], r=[f'kvst{sl}'], w=['o_sk'])
                    P.dma('sp', s_v[l, :, :], kvst[sl][:, 256:512], r=[f'kvst{sl}'], w=['o_sv'])
            if 'sb' in SKIP:
                pass
            elif kind == 'p':
                sb_prompt(l, ui, qT, KT, VC, brTu, spb, wtb, rsb, rsf, e1, fac, sbb, sbm, ones64)
            else:
                sb_sample(l, qT, kTs, vsn, brTu, spb, wtb, rsb, rsf, e1, fac, sbb, sbm_s, ones64, KTm, VCm, gst, piota)
            nseq, Lq = (1, 512) if kind == 'p' else (NSS, SL)
            zv = zc[:, 0:2 * nseq * (Lq + 2)].rearrange("p (c s t) -> p c s t", c=2, s=nseq, t=Lq + 2)
            for c in range(2):
                b1 = nb()
                proj_fm(768 + 256 + 256 + c * 128, W, xt, xk, b1)
                cp('act', hbs[:, c, 0:W], pf[b1][:, 0:W], r=[f'pf{b1}'], w=['hbs'])
                b2 = nb()
                proj_fm(768 + 256 + c * 128, W, xt, xk, b2)
                tt('dve', zv[:, c, :, 2:Lq + 2],
                   pf[b2][:, 0:W].rearrange("p (s t) -> p s t", s=nseq, t=Lq),
                   hbs[:, c, 0:W].rearrange("p (s t) -> p s t", s=nseq, t=Lq), ALU.mult,
                   r=[f'pf{b2}', 'hbs'], w=['zc'])
            if kind == 'p':
                cp('pool', zv[:, :, 0, 0:2], v3(zhist[:, 0:4], 2, 2), r=['zhist'], w=['zc'])
            else:
                for c in range(2):
                    for jj in range(2):
                        P.dma('sp', zv[:, c, :, jj], sconv[l][:, jj, c * 128:(c + 1) * 128].rearrange("s p -> p s"), w=['zc'])
            for c in range(2):
                cv = cva[:, c, 0:W].rearrange("p (s t) -> p s t", s=nseq, t=Lq)
                ts('dve', cv, zv[:, c, :, 0:Lq], cw[:, c, 0:1], None, ALU.mult, None, r=['zc', 'cw'], w=['cva'])
                stt(cv, zv[:, c, :, 1:Lq + 1], cw[:, c, 1:2], cv, ALU.mult, ALU.add, r=['zc', 'cw', 'cva'], w=['cva'])
                stt(cv, zv[:, c, :, 2:Lq + 2], cw[:, c, 2:3], cv, ALU.mult, ALU.add, r=['zc', 'cw', 'cva'], w=['cva'])
                b3 = nb()
                proj_fm(768 + c * 128, W, xt, xk, b3)
                tt('dve', brTu[:, 2 + c, 0:W], pf[b3][:, 0:W], cva[:, c, 0:W], ALU.mult, r=[f'pf{b3}', 'cva'], w=['brTu'])
            if kind == 'p':
                cp('pool', v3(zhist[:, 0:4], 2, 2), zv[:, :, 0, Lq:Lq + 2], r=['zc'], w=['zhist'])
                if ui == NU - 1:
                    for c in range(2):
                        P.dma('sp', p_conv[l][:, c * 128:(c + 1) * 128].rearrange("j p -> p j"), zhist[:, 2 * c:2 * c + 2], r=['zhist'], w=['o_pconv'])
            else:
                for c in range(2):
                    for jj in range(2):
                        P.dma('sp', s_conv[l][:, jj, c * 128:(c + 1) * 128].rearrange("s p -> p s"), zv[:, c, :, Lq + jj], r=['zc'], w=['o_sconv'])
            for c in range(2):
                b = nb()
                proj_fm(1536 + c * 128, W, xt, xk, b)
                gelu(uT[:, c, 0:W], pf[b][:, 0:W], f'pf{b}', 'uT', gtmp[:, 0:W], W)
            for ti, t in enumerate(tiles):
                b = nb()
                for kc in range(8):
                    mm(pf[b][:, 0:256], xt[:, kc, ti * 128:(ti + 1) * 128], Wc[:, kc, 1792:2048], kc == 0, kc == 7,
                       r=['Wc', xk], w=[f'pf{b}'])
                gelu(svt, pf[b][:, 0:256], f'pf{b}', 'svt', gtmp[:, 0:256], 256)
                P.op('dve', lambda e: e.bn_stats(out=lnscr[:, 0:6], in_=svt), r=['svt'], w=['lnscr'])
                P.op('dve', lambda e: e.bn_aggr(out=lnscr[:, 6:8], in_=lnscr[:, 0:6]), r=['lnscr'], w=['lnscr'])
                rsqrt_(lnscr[:, 8:9], lnscr[:, 7:8], LN_EPS, ['lnscr'], ['lnscr'])
                ts('dve', svt, svt, lnscr[:, 6:7], lnscr[:, 8:9], ALU.subtract, ALU.mult, r=['svt', 'lnscr'], w=['svt'])
                tt('pool', svt, svt, lg_t, ALU.mult, r=['svt', 'sgp'], w=['svt'])
                tt('dve', svn, svt, lb_t, ALU.add, r=['svt', 'sgp'], w=['svn'])
                cp('act', vlnb, svn, r=['svn'], w=['vlnb'])
                if kind == 's':
                    P.dma('sp', s_chunk[l, :, :], svn, r=['svn'], w=['o_schunk'])
                b = nb()
                wst = WsT if kind == 'p' else WsT_s
                sgb = sgb_p if kind == 'p' else sgb_s
                for g in range(4):
                    po = (g % 2) * 64
                    mm(pf[b][po:po + 64, (g // 2) * 128:(g // 2 + 1) * 128], vlnb[:, g * 64:(g + 1) * 64], wst[:, g, :],
                       True, True, r=['vlnb', 'WsT'], w=[f'pf{b}'])
                mx = v3(pf[b][:, 0:256], 2, 128)
                tt('dve', cva[:, :, 0:128], mx, sgb, ALU.add, r=[f'pf{b}', 'sgb'], w=['cva'])
                tt('pool', brTu[:, 6:8, ti * 128:(ti + 1) * 128], cva[:, :, 0:128], uT[:, :, ti * 128:(ti + 1) * 128], ALU.mult,
                   r=['cva', 'uT'], w=['brTu'])
            P.dma('sp', brT_s[u, :, 0:4, 0:W], brTu[:, 0:4, 0:W], r=['brTu'], w=[f'brT_s{u}a'])
            P.dma('sp', brT_s[u, :, 6:8, 0:W], brTu[:, 6:8, 0:W], r=['brTu'], w=[f'brT_s{u}b'])
            if 'R' not in cfg.stages:
                memset('pool', brTu[:, 4:6, 0:W], 0.0, w=['brTu'])
                P.dma('sp', brT_s[u, :, 4:6, 0:W], brTu[:, 4:6, 0:W], r=['brTu'], w=[f'brT_s{u}c'])
        AF_.release(); AB_.release()
        P.barrier()

    def gelu(out, in_ps, ink, outk, tmp, W):
        tk = 'gtmp'
        act(tmp, in_ps, AF.Square, r=[ink], w=[tk])
        ts('dve', tmp, tmp, 0.044715, 1.0, ALU.mult, ALU.add, r=[tk], w=[tk])
        tt('dve', tmp, tmp, in_ps, ALU.mult, r=[tk, ink], w=[tk])
        act(tmp, tmp, AF.Sigmoid, r=[tk], w=[tk], scale=1.5957691216057308)
        tt('dve', out, tmp, in_ps, ALU.mult, r=[tk, ink], w=[outk])

    def sb_block(Zb, qk_list, nq, c0, hsl, bias_ap, mask, maskcols, first, e1h, sph, wth, rsbh, rsfh, keys, acc_list):
        zk = f'pf{Zb}'
        for (lt, rh, a, n_) in qk_list:
            mm(pf[Zb][:, a:a + n_], lt, rh, True, True, r=keys, w=[zk])
        act(e1h[:, c0:nq], pf[Zb][:, c0:nq], AF.Exp, r=[zk, 'sbb'], w=['e1' + hsl], bias=bias_ap)
        act(sph[:, c0:nq], e1h[:, c0:nq], AF.Ln, r=['e1' + hsl], w=['sp' + hsl], bias=1.0)
        if mask is not None:
            a, n_ = maskcols
            tt('pool', sph[:, a:a + n_], sph[:, a:a + n_], mask, ALU.mult, r=['sp' + hsl, 'sbm', 'sbm_s'], w=['sp' + hsl])
        mm(pf[Zb][:, c0:nq], lx_b, sph[:, c0:nq], True, False, r=['lx_b', 'sp' + hsl], w=[zk])
        if not first:
            mm(pf[Zb][:, c0:nq], ones_b, rsbh[:, c0:nq], False, False, r=['ones_b', 'rsb' + hsl], w=[zk])
        for i, (lt, rh, a, n_) in enumerate(qk_list):
            mm(pf[Zb][:, a:a + n_], lt, rh, False, i == len(qk_list) - 1, r=keys, w=[zk])
        act(wth[:, c0:nq], pf[Zb][:, c0:nq], AF.Exp, r=[zk, 'sbb'], w=['wt' + hsl], bias=bias_ap)
        if mask is not None:
            a, n_ = maskcols
            tt('pool', wth[:, a:a + n_], wth[:, a:a + n_], mask, ALU.mult, r=['wt' + hsl, 'sbm', 'sbm_s'], w=['wt' + hsl])
        for (o_ap, lv, a, n_, st_, sp_, ks, ok) in acc_list:
            mm(o_ap, lv, wth[:, a:a + n_], st_, sp_, r=['wt' + hsl] + ks, w=[ok])
        if first:
            cp('dve', rsfh[:, c0:nq], sph[:, c0:nq], r=['sp' + hsl], w=['rsf' + hsl])
        else:
            tt('dve', rsfh[:, c0:nq], rsfh[:, c0:nq], sph[:, c0:nq], ALU.add, r=['sp' + hsl, 'rsf' + hsl], w=['rsf' + hsl])
        cp('pool', rsbh[:, c0:nq], rsfh[:, c0:nq], r=['rsf' + hsl], w=['rsb' + hsl])

    def sb_finish(h, nq, ob, rsbh, hsl, fac, brTu, ones64, tb):
        po = (h % 2) * 64
        mm(pf[tb][po:po + 64, 0:nq], ones64, rsbh[:, 0:nq], True, True, r=['ones_b', 'rsb' + hsl], w=[f'pf{tb}'])
        act(fac[po:po + 64, 0:nq], pf[tb][po:po + 64, 0:nq], AF.Exp, r=[f'pf{tb}'], w=['fac'], scale=-1.0)
        tt('dve', brTu[po:po + 64, h // 2, 0:nq], pf[ob][po:po + 64, 0:nq], fac[po:po + 64, 0:nq], ALU.mult,
           r=[f'pf{ob}', 'fac'], w=['brTu'])

    def sb_prompt(l, ui, qT, KT, VC, brTu, spb, wtb, rsb, rsf, e1, fac, sbb, sbm, ones64):
        nkb = 4 * ui + 4
        for hp in range(2):
            for kb in range(nkb):
                d = kb - 4 * ui
                c0 = 128 * d if d > 0 else 0
                for h2 in range(2):
                    h = hp * 2 + h2
                    po = h2 * 64
                    hsl = str(h2)
                    Zb = h2 * 2 + (kb % 2)
                    ob = 4 + h2
                    lt = KT[po:po + 64, hp, kb * 128:(kb + 1) * 128]
                    rh = qT[po:po + 64, hp, c0:512]
                    mask = sbm if d >= 0 else None
                    acc = [(pf[ob][po:po + 64, c0:512], VC[:, kb, h * 64:(h + 1) * 64], c0, 512 - c0, kb == 0, kb == nkb - 1,
                            ['VC'], f'pf{ob}')]
                    sb_block(Zb, [(lt, rh, c0, 512 - c0)], 512, c0, hsl, sbb[:, h:h + 1], mask, (c0, 128), kb == 0,
                             e1[:, h2, :], spb[:, h2, :], wtb[:, h2, :], rsb[:, h2, :], rsf[:, h2, :], ['KT', 'qT'], acc)
            for h2 in range(2):
                sb_finish(hp * 2 + h2, 512, 4 + h2, rsb[:, h2, :], str(h2), fac, brTu, ones64, h2)

    def sb_sample(l, qT, kTs, vsn, brTu, spb, wtb, rsb, rsf, e1, fac, sbb, sbm_s, ones64, KTm, VCm, gst, piota):
        NP = NPAGES
        ptb = ia[:, 0:NSS * NP]
        idx = ia[:, 256:256 + NSS * NP]
        P.dma('sp', ptb, ptab.partition_broadcast(128), w=['ptb'])
        ts('dve', idx, ptb, 128.0, piota[:, 0:1], ALU.mult, ALU.add, r=['ptb', 'piota'], w=['idx'])
        kbf = v3(KTm[:, 0:4096], NSS, 256)
        ktb = KTm[:, 4096:8192].rearrange("p (c s k) -> p c s k", c=2, s=NSS, k=128)
        vbf = v3(VCm[:, 0:4096], NSS, 256)
        maskb = sbm_s.unsqueeze(1).to_broadcast([128, 2, 128])

        for ob in (4, 5):
            memset('dve', pf[ob][:, 0:256], 0.0, w=[f'pf{ob}'])

        def hv(t, par):
            return t[:, par:4:2, 0:128]
        for kb in range(NP + 1):
            last = kb == NP
            first = kb == 0
            if not last:
                for si in range(NSS):
                    col = si * NP + kb
                    P.op('pool', (lambda e, si=si, col=col: e.indirect_dma_start(
                        out=gst[:, si, :], out_offset=None, in_=cache_k[l][:, :],
                        in_offset=bass.IndirectOffsetOnAxis(ap=idx[:, col:col + 1], axis=0))),
                        r=['idx'], w=['gst'], dma=True)
                cp('dve', kbf, gst, r=['gst'], w=['kbf'])
                for si in range(NSS):
                    col = si * NP + kb
                    P.op('pool', (lambda e, si=si, col=col: e.indirect_dma_start(
                        out=gst[:, si, :], out_offset=None, in_=cache_v[l][:, :],
                        in_offset=bass.IndirectOffsetOnAxis(ap=idx[:, col:col + 1], axis=0))),
                        r=['idx'], w=['gst'], dma=True)
                cp('dve', vbf, gst, r=['gst'], w=['vbf'])
                for g in range(4):
                    c, sh = g // 2, g % 2
                    pt = v3(pb[g % 2][:, :], 8, 128)
                    for s8 in range(8):
                        si = sh * 8 + s8
                        tr(pt[:, s8, :], kbf[:, si, c * 128:(c + 1) * 128], ident_b, r=['kbf', 'ident_b'], w=[f'pb{g % 2}'])
                    cp('act', ktb[:, c, sh * 8:sh * 8 + 8, :], pt, r=[f'pb{g % 2}'], w=['ktb'])

            def zb(h):
                return (kb % 2) * 2 + (h % 2)

            def zcol(h):
                return (h // 2) * 128

            def qk(h, startf, stop_last):
                hp, po = h // 2, (h % 2) * 64
                Zb, zc0, zk = zb(h), zcol(h), f'pf{zb(h)}'
                if last:
                    mm(pf[Zb][:, zc0:zc0 + 128], kTs[po:po + 64, hp, :], qT[po:po + 64, hp, 0:128], startf, stop_last,
                       r=['kTs', 'qT'], w=[zk])
                else:
                    for si in range(NSS):
                        mm(pf[Zb][:, zc0 + si * SL:zc0 + (si + 1) * SL], ktb[po:po + 64, hp, si, :],
                           qT[po:po + 64, hp, si * SL:(si + 1) * SL], startf, (stop_last and si == NSS - 1) or startf,
                           r=['ktb', 'qT'], w=[zk])
            for h in range(4):
                qk(h, True, True)
            for h in range(4):
                Zb, zc0 = zb(h), zcol(h)
                act(e1[:, h, 0:128], pf[Zb][:, zc0:zc0 + 128], AF.Exp, r=[f'pf{Zb}', 'sbb'], w=['e1s'], bias=sbb[:, h:h + 1])
            act(spb[:, :, 0:128], e1[:, :, 0:128], AF.Ln, r=['e1s'], w=['sps'], bias=1.0)
            if last:
                for par in range(2):
                    tt('pool', hv(spb, par), hv(spb, par), maskb, ALU.mult, r=['sps', 'sbm_s'], w=['sps'])
            for h in range(4):
                Zb, zc0 = zb(h), zcol(h)
                mm(pf[Zb][:, zc0:zc0 + 128], lx_b, spb[:, h, 0:128], True, False, r=['lx_b', 'sps'], w=[f'pf{Zb}'])
                if not first:
                    mm(pf[Zb][:, zc0:zc0 + 128], ones_b, rsb[:, h, 0:128], False, False, r=['ones_b', 'rsbs'], w=[f'pf{Zb}'])
                qk(h, False, True)
            for h in range(4):
                Zb, zc0 = zb(h), zcol(h)
                act(wtb[:, h, 0:128], pf[Zb][:, zc0:zc0 + 128], AF.Exp, r=[f'pf{Zb}', 'sbb'], w=['wts'], bias=sbb[:, h:h + 1])
            if last:
                for par in range(2):
                    tt('pool', hv(wtb, par), hv(wtb, par), maskb, ALU.mult, r=['wts', 'sbm_s'], w=['wts'])
            for h in range(4):
                po = (h % 2) * 64
                ob = 4 + (h % 2)
                oc0 = (h // 2) * 128
                if last:
                    P.op('pe', (lambda e, ob=ob, po=po, oc0=oc0, h=h: e.matmul(
                        pf[ob][po:po + 64, oc0:oc0 + 128], lhsT=vsn[:, h * 64:(h + 1) * 64], rhs=wtb[:, h, 0:128],
                        start=False, stop=(h >= 2), skip_group_check=True)), r=['vsn', 'wts'], w=[f'pf{ob}'])
                else:
                    for si in range(NSS):
                        P.op('pe', (lambda e, ob=ob, po=po, oc0=oc0, h=h, si=si: e.matmul(
                            pf[ob][po:po + 64, oc0 + si * SL:oc0 + (si + 1) * SL], lhsT=vbf[:, si, h * 64:(h + 1) * 64],
                            rhs=wtb[:, h, si * SL:(si + 1) * SL],
                            start=False, stop=False, skip_group_check=True)),
                            r=['vbf', 'wts'], w=[f'pf{ob}'])
            if first:
                cp('dve', rsf[:, :, 0:128], spb[:, :, 0:128], r=['sps'], w=['rsfs'])
            else:
                tt('dve', rsf[:, :, 0:128], rsf[:, :, 0:128], spb[:, :, 0:128], ALU.add, r=['sps', 'rsfs'], w=['rsfs'])
            cp('pool', rsb[:, :, 0:128], rsf[:, :, 0:128], r=['rsfs'], w=['rsbs'])
        for h in range(4):
            po = (h % 2) * 64
            tb = h % 2
            ob = 4 + (h % 2)
            oc0 = (h // 2) * 128
            mm(pf[tb][po:po + 64, 0:128], ones64, rsb[:, h, 0:128], True, True, r=['ones_b', 'rsbs'], w=[f'pf{tb}'])
            act(fac[po:po + 64, 0:128], pf[tb][po:po + 64, 0:128], AF.Exp, r=[f'pf{tb}'], w=['fac'], scale=-1.0)
            tt('dve', brTu[po:po + 64, h // 2, 0:128], pf[ob][po:po + 64, oc0:oc0 + 128], fac[po:po + 64, 0:128], ALU.mult,
               r=[f'pf{ob}', 'fac'], w=['brTu'])


    ffp_s = dscr("ffp_s", [NTA, 128, D])

    def xres_src(l, t):
        if l == 0:
            return xp[t * 128:(t + 1) * 128, :] if t < NT else xs[:, :]
        return xres_s[t]

    def stageB(l):
        AF_.mark(); AB_.mark()
        Wg = v3(balloc(8 * 4096), 8, 4096)
        Wb = v3(balloc(8 * 1024), 8, 1024)
        load_w(Wg, w_gate[l], 0, 4096, 'Wg')
        load_w(Wb, w_branch[l], 0, 1024, 'Wb')
        bg = falloc(32)
        P.dma('sp', bg, b_gate[l].rearrange("(j p) -> p j", p=128), w=['bg'])
        xTu = [v3(balloc(8 * 512), 8, 512) for _ in range(2)]
        brTu = [v3(balloc(8 * 512), 8, 512)] * 2
        mixTu = [v3(balloc(8 * 512), 8, 512)] * 2
        sg = [falloc(512) for _ in range(2)]
        acc = [falloc(512) for _ in range(2)]
        tmp = [falloc(512) for _ in range(2)]
        n = 0
        for u in range(NUA):
            kind, ui, W, ntl = units[u]
            ub = u % 2
            P.dma('sp', xTu[ub][:, :, 0:W], xT_s[u, :, :, 0:W], r=[f'xT_s{u}'], w=[f'xTu{ub}'])
            P.dma('sp', brTu[ub][:, :, 0:W], brT_s[u, :, :, 0:W], r=[f'brT_s{u}a', f'brT_s{u}b', f'brT_s{u}c'], w=['brTuB'])
            for c in range(8):
                ab = c % 2
                for i in range(4):
                    sb_ = n % 2
                    n += 1
                    bgk = nb()
                    for kc in range(8):
                        mm(pf[bgk][:, 0:W], Wg[:, kc, i * 1024 + c * 128:i * 1024 + (c + 1) * 128], xTu[ub][:, kc, 0:W],
                           kc == 0, kc == 7, r=['Wg', f'xTu{ub}'], w=[f'pf{bgk}'])
                    act(sg[sb_][:, 0:W], pf[bgk][:, 0:W], AF.Sigmoid, r=[f'pf{bgk}', 'bg'], w=[f'sg{sb_}'],
                        bias=bg[:, i * 8 + c:i * 8 + c + 1])
                    bpk = nb()
                    for k2 in range(2):
                        mm(pf[bpk][:, 0:W], Wb[:, 2 * i + k2, c * 128:(c + 1) * 128], brTu[ub][:, 2 * i + k2, 0:W],
                           k2 == 0, k2 == 1, r=['Wb', 'brTuB'], w=[f'pf{bpk}'])
                    if i == 0:
                        tt('dve', acc[ab][:, 0:W], pf[bpk][:, 0:W], sg[sb_][:, 0:W], ALU.mult, r=[f'pf{bpk}', f'sg{sb_}'], w=[f'acc{ab}'])
                    else:
                        tt('dve', tmp[sb_][:, 0:W], pf[bpk][:, 0:W], sg[sb_][:, 0:W], ALU.mult, r=[f'pf{bpk}', f'sg{sb_}'], w=[f'tmp{sb_}'])
                        dst = acc[ab][:, 0:W] if i < 3 else mixTu[ub][:, c, 0:W]
                        dk = f'acc{ab}' if i < 3 else 'mixTuB'
                        tt('pool', dst, acc[ab][:, 0:W], tmp[sb_][:, 0:W], ALU.add, r=[f'acc{ab}', f'tmp{sb_}'], w=[dk])
            P.dma('sp', mixT_s[u, :, :, 0:W], mixTu[ub][:, :, 0:W], r=['mixTuB'], w=[f'mixT_s{u}'])
        AF_.release(); AB_.release()
        P.barrier()

    def load_ln(l, i, gt, bt):
        P.dma('sp', gt, lng[i][l].partition_broadcast(128), w=['lnp'])
        P.dma('sp', bt, lnb[i][l].partition_broadcast(128), w=['lnp'])

    def proj_tm_ln(lhs3, lhsk, Wt, Wk, ncol_kc, ti, xr, xrk, tbuf, tk):
        for hh in range(2):
            b = nb()
            for kc in range(ncol_kc):
                mm(pf[b][:, :], lhs3[:, kc, ti * 128:(ti + 1) * 128], Wt[:, kc, hh * 512:(hh + 1) * 512], kc == 0, kc == ncol_kc - 1,
                   r=[lhsk, Wk], w=[f'pf{b}'])
            stt(tbuf[:, hh * 512:(hh + 1) * 512], xr[:, hh * 512:(hh + 1) * 512], ALPHA, pf[b][:, :], ALU.mult, ALU.add,
                r=[xrk, f'pf{b}'], w=[tk])

    def stageC(l):
        AF_.mark(); AB_.mark()
        Wo = v3(balloc(8 * 1024), 8, 1024)
        Wq = v3(balloc(8 * 1024), 8, 1024)
        Wmo = v3(balloc(8 * 1024), 8, 1024)
        load_w(Wo, w_o[l], 0, 1024, 'Wo')
        load_w(Wq, w_mq[l], 0, 1024, 'Wq')
        load_w(Wmo, w_mo[l], 0, 1024, 'Wmo')
        g1 = falloc(1024); b1 = falloc(1024); g2 = falloc(1024); b2 = falloc(1024)
        load_ln(l, 0, g1, b1)
        load_ln(l, 1, g2, b2)
        mkT = v3(balloc(8 * 256), 8, 256)
        mvb = v3(balloc(2 * 1024), 2, 1024)
        stg = [falloc(512) for _ in range(2)]
        AB_.mark()
        memT = v3(balloc(8 * 256), 8, 256)
        Wmk = v3(balloc(8 * 1024), 8, 1024)
        Wmv = v3(balloc(8 * 1024), 8, 1024)
        load_w(Wmk, w_mk[l], 0, 1024, 'Wmk')
        load_w(Wmv, w_mv[l], 0, 1024, 'Wmv')
        mtl = [balloc(1024) for _ in range(2)]
        for mt in range(2):
            P.dma('pool', mtl[mt], memp[mt * 128:(mt + 1) * 128, :], w=[f'mtl{mt}'])
            transpose_tile(mtl[mt], f'mtl{mt}', memT[:, :, mt * 128:(mt + 1) * 128], 'memT', mt)
        n = 0
        for (Wt, Wk, outd, isv) in ((Wmk, 'Wmk', p_mk, False), (Wmv, 'Wmv', p_mv, True)):
            for mt in range(2):
                for hh in range(2):
                    b = nb()
                    for kc in range(8):
                        mm(pf[b][:, :], memT[:, kc, mt * 128:(mt + 1) * 128], Wt[:, kc, hh * 512:(hh + 1) * 512], kc == 0, kc == 7,
                           r=['memT', Wk], w=[f'pf{b}'])
                    sl = n % 2
                    n += 1
                    cp('act', stg[sl], pf[b][:, :], r=[f'pf{b}'], w=[f'stg{sl}'])
                    if isv:
                        cp('dve', mvb[:, mt, hh * 512:(hh + 1) * 512], pf[b][:, :], r=[f'pf{b}'], w=['mvb'])
                    P.dma('sp', outd[l, mt * 128:(mt + 1) * 128, hh * 512:(hh + 1) * 512], stg[sl], r=[f'stg{sl}'], w=['o_pm'])
        for c in range(8):
            b = nb()
            for kc in range(8):
                mm(pf[b][:, 0:256], Wmk[:, kc, c * 128:(c + 1) * 128], memT[:, kc, 0:256], kc == 0, kc == 7, r=['memT', 'Wmk'], w=[f'pf{b}'])
            cp('dve', mkT[:, c, :], pf[b][:, 0:256], r=[f'pf{b}'], w=['mkT'])
        P.barrier()
        AB_.release()
        mixTu = [v3(balloc(8 * 512), 8, 512)] * 2
        x1Tu = v3(balloc(8 * 512), 8, 512)
        qmT = v3(balloc(8 * 512), 8, 512)
        x2Tu = qmT
        attT = v3(balloc(8 * 512), 8, 512)
        prb = v3(balloc(2 * 512), 2, 512)
        xb = [balloc(1024) for _ in range(2)]
        smk = [v3(balloc(2 * 1024), 2, 1024)] * 2
        smv = [v3(balloc(2 * 1024), 2, 1024)] * 2
        smkT = v3(balloc(8 * 256), 8, 256)
        prs = balloc(1024)
        xr = [falloc(1024) for _ in range(2)]
        x1 = v3(falloc(4 * 1024), 4, 1024)
        tb_ = [falloc(1024) for _ in range(2)]
        rden = falloc(512)
        lnscr = falloc(16)
        for u in range(NUA):
            kind, ui, W, ntl = units[u]
            ub = u % 2
            tiles = unit_tiles(u)
            P.dma('sp', mixTu[ub][:, :, 0:W], mixT_s[u, :, :, 0:W], r=[f'mixT_s{u}'], w=['mixTuC'])
            for ti, t in enumerate(tiles):
                sl = t % 2
                P.dma('sp', xr[sl], xres_src(l, t), r=[f'xres_s{t}'], w=[f'xr{sl}'])
                proj_tm_ln(mixTu[ub], 'mixTuC', Wo, 'Wo', 8, ti, xr[sl], f'xr{sl}', tb_[sl], f'tb{sl}')
                layernorm(tb_[sl], f'tb{sl}', g1, b1, x1[:, ti, :], 'x1', xb[sl], f'xb{sl}', lnscr, 'lnscrC')
                transpose_tile(xb[sl], f'xb{sl}', x1Tu[:, :, ti * 128:(ti + 1) * 128], 'x1Tu', sl)
            for c in range(8):
                b = nb()
                for kc in range(8):
                    mm(pf[b][:, 0:W], Wq[:, kc, c * 128:(c + 1) * 128], x1Tu[:, kc, 0:W], kc == 0, kc == 7, r=['Wq', 'x1Tu'], w=[f'pf{b}'])
                act(qmT[:, c, 0:W], pf[b][:, 0:W], AF.Copy, r=[f'pf{b}'], w=['qmT'], scale=1.0 / 16.0)
            if dbg and os.environ.get('DBGC'):
                srcd = {'x1T': x1Tu, 'qmT': qmT}[os.environ['DBGC']]
                P.dma('sp', mixT_s[u, :, :, 0:W], srcd[:, :, 0:W], r=['x1Tu', 'qmT'], w=[f'mixT_s{u}'])
            if kind == 'p':
                for h in range(4):
                    for km in range(2):
                        b = nb()
                        for ec in range(2):
                            mm(pf[b][:, 0:W], mkT[:, 2 * h + ec, km * 128:(km + 1) * 128], qmT[:, 2 * h + ec, 0:W], ec == 0, ec == 1,
                               r=['mkT', 'qmT'], w=[f'pf{b}'])
                        act(prb[:, km, 0:W], pf[b][:, 0:W], AF.Exp, r=[f'pf{b}'], w=['prb'])
                    b = nb()
                    for km in range(2):
                        mm(pf[b][:, 0:W], ones_b, prb[:, km, 0:W], km == 0, km == 1, r=['ones_b', 'prb'], w=[f'pf{b}'])
                    P.op('dve', lambda e, b=b, W=W: e.reciprocal(out=rden[:, 0:W], in_=pf[b][:, 0:W]), r=[f'pf{b}'], w=['rden'])
                    for ec in range(2):
                        b = nb()
                        for km in range(2):
                            mm(pf[b][:, 0:W], mvb[:, km, (2 * h + ec) * 128:(2 * h + ec + 1) * 128], prb[:, km, 0:W], km == 0, km == 1,
                               r=['mvb', 'prb'], w=[f'pf{b}'])
                        tt('dve', attT[:, 2 * h + ec, 0:W], pf[b][:, 0:W], rden[:, 0:W], ALU.mult, r=[f'pf{b}', 'rden'], w=['attT'])
            else:
                xattn_sample(l, qmT, attT, smk, smv, smkT, prs, rden)
            for ti, t in enumerate(tiles):
                sl = t % 2
                proj_tm_ln(attT, 'attT', Wmo, 'Wmo', 8, ti, x1[:, ti, :], 'x1', tb_[sl], f'tb{sl}')
                if dbg:
                    P.dma('sp', ffp_s[t], tb_[sl], r=[f'tb{sl}'], w=[f'ffp_s{t}'])
                layernorm(tb_[sl], f'tb{sl}', g2, b2, xr[sl], f'xr{sl}', xb[sl], f'xb{sl}', lnscr, 'lnscrC')
                P.dma('sp', xres_s[t], xr[sl], r=[f'xr{sl}'], w=[f'xres_s{t}'])
                transpose_tile(xb[sl], f'xb{sl}', x2Tu[:, :, ti * 128:(ti + 1) * 128], 'qmT', sl)
            P.dma('sp', xT_s[u, :, :, 0:W], x2Tu[:, :, 0:W], r=['qmT'], w=[f'xT_s{u}'])
        AF_.release(); AB_.release()
        P.barrier()

    def xattn_sample(l, qmT, attT, smk, smv, smkT, prs, rden):
        prv = prs.rearrange("p (k s h q) -> p k s h q", k=2, s=NSS, h=4, q=SL)
        for si in range(NSS):
            sb_ = si % 2
            P.dma('pool', smk[sb_], cmk[l, si].rearrange("(m p) d -> p m d", p=128), w=['smkC'])
            P.dma('pool', smv[sb_], cmv[l, si].rearrange("(m p) d -> p m d", p=128), w=['smvC'])
            for half in range(2):
                pt = v3(pb[half][:, :], 8, 128)
                for j in range(8):
                    c = half * 4 + j // 2
                    km = j % 2
                    tr(pt[:, j, :], smk[sb_][:, km, c * 128:(c + 1) * 128], ident_b, r=['smkC', 'ident_b'], w=[f'pb{half}'])
                cp('act', smkT[:, half * 4:half * 4 + 4, :].rearrange("p c (m k) -> p (c m) k", m=2, k=128), pt, r=[f'pb{half}'], w=['smkT'])
            for km in range(2):
                for h in range(4):
                    for ec in range(2):
                        mm(pf[km][:, si * 32 + h * SL:si * 32 + (h + 1) * SL], smkT[:, 2 * h + ec, km * 128:(km + 1) * 128],
                           qmT[:, 2 * h + ec, si * SL:(si + 1) * SL], ec == 0, ec == 1, r=['smkT', 'qmT'], w=[f'pf{km}'])
            for km in range(2):
                act(prv[:, km, si], pf[km][:, si * 32:(si + 1) * 32].rearrange("p (h q) -> p h q", h=4, q=SL), AF.Exp,
                    r=[f'pf{km}'], w=['prs'])
            for h in range(4):
                for ec in range(2):
                    c = 2 * h + ec
                    ob = 2 + c // 4
                    oc = (c % 4) * 128 + si * SL
                    for km in range(2):
                        mm(pf[ob][:, oc:oc + SL], smv[sb_][:, km, c * 128:(c + 1) * 128], prv[:, km, si, h, :], km == 0, km == 1,
                           r=['smvC', 'prs'], w=[f'pf{ob}'])
        db = 4
        for km in range(2):
            mm(pf[db][:, :], ones_b, prs[:, km * 512:(km + 1) * 512], km == 0, km == 1, r=['ones_b', 'prs'], w=[f'pf{db}'])
        P.op('dve', lambda e: e.reciprocal(out=rden[:, 0:512], in_=pf[db][:, :]), r=[f'pf{db}'], w=['rden'])
        rv = rden[:, 0:512].rearrange("p (s h q) -> p s h q", s=NSS, h=4, q=SL)
        for c in range(8):
            h = c // 2
            ob = 2 + c // 4
            oc = (c % 4) * 128
            tt('dve', attT[:, c, 0:128].rearrange("p (s q) -> p s q", s=NSS, q=SL),
               pf[ob][:, oc:oc + 128].rearrange("p (s q) -> p s q", s=NSS, q=SL), rv[:, :, h, :], ALU.mult,
               r=[f'pf{ob}', 'rden'], w=['attT'])

    def stageD(l, f):
        AF_.mark(); AB_.mark()
        Wu = v3(balloc(8 * 2048), 8, 2048)
        Wd = v3(balloc(16 * 1024), 16, 1024)
        load_w(Wu, w_up[l], f * 2048, (f + 1) * 2048, 'Wu')
        load_w(Wd, w_down[l][f * 2048:(f + 1) * 2048, :], 0, 1024, 'Wd')
        g3 = falloc(1024); b3 = falloc(1024)
        if f == 1:
            load_ln(l, 2, g3, b3)
        xTu = [v3(balloc(8 * 512), 8, 512) for _ in range(2)]
        hid = v3(balloc(16 * 512), 16, 512)
        xoT = v3(balloc(8 * 512), 8, 512)
        xb = [balloc(1024) for _ in range(2)]
        rl = [falloc(512) for _ in range(2)]
        fft = [falloc(1024) for _ in range(2)]
        xr = [falloc(1024) for _ in range(2)]
        fp_ = [falloc(1024) for _ in range(2)]
        lnscr = falloc(16)
        lastl = (l == cfg.nlayers - 1)
        for u in range(NUA):
            kind, ui, W, ntl = units[u]
            ub = u % 2
            tiles = unit_tiles(u)
            P.dma('sp', xTu[ub][:, :, 0:W], xT_s[u, :, :, 0:W], r=[f'xT_s{u}'], w=[f'xTu{ub}'])
            for c in range(16):
                b = nb()
                for kc in range(8):
                    mm(pf[b][:, 0:W], Wu[:, kc, c * 128:(c + 1) * 128], xTu[ub][:, kc, 0:W], kc == 0, kc == 7, r=['Wu', f'xTu{ub}'], w=[f'pf{b}'])
                sl = c % 2
                act(rl[sl][:, 0:W], pf[b][:, 0:W], AF.Relu, r=[f'pf{b}'], w=[f'rl{sl}'])
                tt('dve' if c % 4 else 'pool', hid[:, c, 0:W], rl[sl][:, 0:W], rl[sl][:, 0:W], ALU.mult, r=[f'rl{sl}'], w=['hid'])
            for ti, t in enumerate(tiles):
                sl = t % 2
                if f == 1:
                    P.dma('sp', xr[sl], xres_s[t], r=[f'xres_s{t}'], w=[f'xr{sl}'])
                    P.dma('sp', fp_[sl], ffp_s[t], r=[f'ffp_s{t}'], w=[f'fp{sl}'])
                for hh in range(2):
                    b = nb()
                    for kc in range(16):
                        mm(pf[b][:, :], hid[:, kc, ti * 128:(ti + 1) * 128], Wd[:, kc, hh * 512:(hh + 1) * 512], kc == 0, kc == 15,
                           r=['hid', 'Wd'], w=[f'pf{b}'])
                    cs = slice(hh * 512, (hh + 1) * 512)
                    if f == 0:
                        cp('act', fft[sl][:, cs], pf[b][:, :], r=[f'pf{b}'], w=[f'fft{sl}'])
                    else:
                        tt('dve', fft[sl][:, cs], pf[b][:, :], fp_[sl][:, cs], ALU.add, r=[f'pf{b}', f'fp{sl}'], w=[f'fft{sl}'])
                        stt(fft[sl][:, cs], xr[sl][:, cs], ALPHA, fft[sl][:, cs], ALU.mult, ALU.add, r=[f'xr{sl}', f'fft{sl}'], w=[f'fft{sl}'])
                if f == 0:
                    P.dma('sp', ffp_s[t], fft[sl], r=[f'fft{sl}'], w=[f'ffp_s{t}'])
                else:
                    layernorm(fft[sl], f'fft{sl}', g3, b3, xr[sl], f'xr{sl}', None if lastl else xb[sl], f'xb{sl}', lnscr, 'lnscrD')
                    if lastl:
                        dst = y_p[t * 128:(t + 1) * 128, :] if kind == 'p' else y_s[:, :]
                        P.dma('sp', dst, xr[sl], r=[f'xr{sl}'], w=[f'o_y{t}'])
                    else:
                        P.dma('sp', xres_s[t], xr[sl], r=[f'xr{sl}'], w=[f'xres_s{t}'])
                        transpose_tile(xb[sl], f'xb{sl}', xoT[:, :, ti * 128:(ti + 1) * 128], 'xoT', sl)
            if f == 1 and not lastl:
                P.dma('sp', xT_s[u, :, :, 0:W], xoT[:, :, 0:W], r=['xoT'], w=[f'xT_s{u}'])
        AF_.release(); AB_.release()
        P.barrier()


    def stageA2(l):
        AF_.mark(); AB_.mark()
        W1 = v3(balloc(8 * RWC), 8, RWC)
        W2 = v3(balloc(8 * RWC), 8, RWC)
        load_w(W1, w_in[l], 1536, 1536 + RWC, 'W1')
        mu_t = falloc(RWC)
        bcast_row(mu_t, rw_mu[l], 'mu_t')
        for kc in range(8):
            tt('dve', W2[:, kc, :], W1[:, kc, :], mu_t, ALU.mult, r=['W1', 'mu_t'], w=['W2'])
            tt('pool', W1[:, kc, :], W1[:, kc, :], W2[:, kc, :], ALU.subtract, r=['W1', 'W2'], w=['W1'])
        LW = balloc(768)
        memset('pool', LW, 0.0, w=['LW'])
        P.dma('pool', LW[0:32, 0:256], rw_w2[l], r=['LW'], w=['LW'])
        P.dma('pool', LW[32:64, 256:512], rw_a2[l], r=['LW'], w=['LW'])
        P.dma('pool', LW[64:128, 512:768], rw_g2[l], r=['LW'], w=['LW'])
        prm = {}
        for nm, src in (('w0', rw_w0), ('a0', rw_a0), ('kks', rw_kk), ('ka', rw_ka), ('rk', rw_rk), ('gng', rw_gn_g), ('gnb', rw_gn_b)):
            prm[nm] = falloc(256)
            bcast_row(prm[nm], src[l], 'rwp')
        omka = falloc(256)
        ts('dve', omka, prm['ka'], -1.0, 1.0, ALU.mult, ALU.add, r=['rwp'], w=['rwp2'])
        m1 = {}
        m3 = {}
        tri = {}
        for kd in ('p', 's'):
            m1[kd] = balloc(1024); m3[kd] = balloc(512); tri[kd] = falloc(128)
            P.dma('pool', m1[kd], cin['rwm1_' + kd][:, :], w=['rwmask'])
            P.dma('pool', m3[kd], cin['rwm3_' + kd][:, :], w=['rwmask'])
            P.dma('sp', tri[kd], cin['tri_' + kd][:, :], w=['rwmask'])
        bdm = falloc(128)
        P.dma('sp', bdm, cin['bdm'][:, :], w=['rwmask'])
        selp = falloc(128)
        P.dma('sp', selp, cin['ident'][:, :], w=['rwmask'])
        oh = falloc(NSS * 128)
        P.dma('sp', oh[0:NSS, :], cin['oh'][:, :], w=['rwmask'])
        ssm = falloc(RWC)
        P.dma('sp', ssm[0:NSS, :], sshift[l], w=['ssm'])
        tt('dve', ssm[0:NSS, :], ssm[0:NSS, :], mu_t[0:NSS, :], ALU.mult, r=['ssm', 'mu_t'], w=['ssm'])
        def make_rw(sfx, bk, pbi, g_mm=mm, g_tr=tr, g_act=act, g_tt=tt, g_ts=ts, g_stt=stt, g_cp=cp, g_memset=memset):
            LK = {'loraT', 'rks', 'vsb', 'vbf', 'xw', 'lw', 'aa', 'gg', 'kk', 't1', 'sm', 'kkn', 't2', 'kef', 'Dinc', 'Dinv', 'Dexc',
                  'TM', 'TT', 'M1', 'M2', 'Q0', 'Q1', 'QT0', 'QT1', 'PT0', 'PT1', 'DCt', 'RHSb', 'Ub', 'ysb', 'sm2', 'sm3', 'ycb', 'STf',
                  'STb', 'xTu0', 'xTu1', 'xTq', 'Xn', 'brC', 'psh'}
            PM = {'pf0': f'pf{bk[0]}', 'pf1': f'pf{bk[1]}', 'pf2': f'pf{bk[2]}', 'pf3': f'pf{bk[0]}', 'pf4': f'pf{bk[1]}', 'pf5': f'pf{bk[2]}',
                  'pb0': f'pb{pbi}', 'pb1': f'pb{pbi}'}

            def kx(keys):
                return [PM.get(k, k + sfx if k in LK else k) for k in keys]

            def mm(out, lhsT, rhs, start, stop, r, w):
                return g_mm(out, lhsT, rhs, start, stop, kx(r), kx(w))

            def tr(out, in_, idt, r, w):
                return g_tr(out, in_, idt, kx(r), kx(w))

            def act(out, in_, func, r, w, **kw):
                return g_act(out, in_, func, kx(r), kx(w), **kw)

            def tt(eng, out, in0, in1, op, r, w):
                return g_tt(eng, out, in0, in1, op, kx(r), kx(w))

            def ts(eng, out, in0, s1, s2, op0, op1, r, w):
                return g_ts(eng, out, in0, s1, s2, op0, op1, kx(r), kx(w))

            def stt(out, in0, sc, in1, op0, op1, r, w):
                return g_stt(out, in0, sc, in1, op0, op1, kx(r), kx(w))

            def cp(eng, out, in_, r, w):
                return g_cp(eng, out, in_, kx(r), kx(w))

            def memset(eng, ap, val, w):
                return g_memset(eng, ap, val, kx(w))

            class PL_:
                def op(self, eng, fn, r=(), w=(), dma=False):
                    return P.op(eng, fn, r=kx(r), w=kx(w), dma=dma)

                def dma(self, q, out, in_, r=(), w=(), **kw):
                    return P.dma(q, out, in_, r=kx(r), w=kx(w), **kw)
            PL = PL_()
            pf = [pf_g[bk[0]], pf_g[bk[1]], pf_g[bk[2]], pf_g[bk[0]], pf_g[bk[1]], pf_g[bk[2]]]
            pb = [pb_g[pbi], pb_g[pbi]]
            xTu = [v3(balloc(8 * 513), 8, 513) for _ in range(2)]
            xTq = v3(balloc(8 * 136), 8, 136)
            TM = v3(balloc(4 * 256), 4, 256)
            TT = v3(balloc(8 * 128), 8, 128)
            M1 = v3(balloc(4 * 256), 4, 256)
            M2 = v3(balloc(4 * 256), 4, 256)
            Qb = [v3(balloc(4 * 128), 4, 128) for _ in range(2)]
            QTb = [v3(balloc(4 * 128), 4, 128) for _ in range(2)]
            PTb = [v3(balloc(4 * 128), 4, 128) for _ in range(2)]
            RHSb = balloc(256)
            Ub = balloc(256)
            vbf = balloc(256)
            STb = v3(balloc(256), 2, 128)
            loraT = balloc(128)
            ycb = balloc(256)
            brC = v3(balloc(2 * 512), 2, 512)
            rks = falloc(512); vsb = falloc(256); xw = falloc(256); lw = falloc(256); aa = falloc(256); gg = falloc(256)
            kk = falloc(256); kkn = falloc(256); kef = falloc(256); t1 = falloc(256); t2 = falloc(256)
            Dinc = falloc(256); Dexc = falloc(256); Dinv = falloc(256); ysb = falloc(256)
            sm = falloc(64)
            STf = v3(falloc(256), 2, 128)
            Xn = v3(falloc(256), 2, 128)
            DCt = falloc(4)
            psh = falloc(RWC)
            memset('dve', RHSb, 0.0, w=['RHSb'])
            memset('dve', Ub, 0.0, w=['Ub'])
            memset('pool', xTq, 0.0, w=['xTq'])
            memset('pool', Xn, 0.0, w=['Xn'])

            def rw_tile(xt, xk, c0, kd, chunks, extra_s=None):
                cur = lambda kc: xt[:, kc, c0:c0 + 128]
                prv = lambda kc: xt[:, kc, c0 - 1:c0 + 127]
                bl = 0
                n_mm = 16 + (1 if extra_s is not None else 0)
                i = 0
                for kc in range(8):
                    for (Wt, Wk, src) in ((W1, 'W1', cur(kc)), (W2, 'W2', prv(kc))):
                        mm(pf[bl][:, 0:128], Wt[:, kc, 768:896], src, i == 0, i == n_mm - 1, r=[Wk, xk], w=[f'pf{bl}'])
                        i += 1
                if extra_s is not None:
                    mm(pf[bl][:, 0:128], ssm[0:NSS, 768:896], oh[0:NSS, extra_s * 128:(extra_s + 1) * 128], False, True, r=['ssm', 'rwmask'], w=[f'pf{bl}'])
                act(loraT[0:32, :], pf[bl][0:32, 0:128], AF.Tanh, r=[f'pf{bl}'], w=['loraT'])
                act(loraT[32:64, :], pf[bl][32:64, 0:128], AF.Copy, r=[f'pf{bl}'], w=['loraT'])
                act(loraT[64:128, :], pf[bl][64:128, 0:128], AF.Sigmoid, r=[f'pf{bl}'], w=['loraT'])
                for (bk, ca, cb) in ((1, 0, 512), (2, 512, 768)):
                    i = 0
                    for kc in range(8):
                        for (Wt, Wk, src) in ((W1, 'W1', cur(kc)), (W2, 'W2', prv(kc))):
                            mm(pf[bk][:, 0:cb - ca], src, Wt[:, kc, ca:cb], i == 0, i == n_mm - 1, r=[Wk, xk], w=[f'pf{bk}'])
                            i += 1
                    if extra_s is not None:
                        mm(pf[bk][:, 0:cb - ca], oh[0:NSS, extra_s * 128:(extra_s + 1) * 128], ssm[0:NSS, ca:cb], False, True,
                           r=['ssm', 'rwmask'], w=[f'pf{bk}'])
                cp('act', rks, pf[1][:, :], r=['pf1'], w=['rks'])
                cp('act', vsb, pf[2][:, 0:256], r=['pf2'], w=['vsb'])
                cp('pool', vbf, vsb, r=['vsb'], w=['vbf'])
                mm(pf[3][:, :], loraT, LW[:, 0:512], True, True, r=['loraT', 'LW'], w=['pf3'])
                mm(pf[4][:, 0:256], loraT, LW[:, 512:768], True, True, r=['loraT', 'LW'], w=['pf4'])
                tt('dve', xw, pf[3][:, 0:256], prm['w0'], ALU.add, r=['pf3', 'rwp'], w=['xw'])
                act(xw, xw, AF.Sigmoid, r=['xw'], w=['xw'])
                ts('dve', lw, xw, -0.6065306597126334, None, ALU.mult, None, r=['xw'], w=['lw'])
                tt('dve', aa, pf[3][:, 256:512], prm['a0'], ALU.add, r=['pf3', 'rwp'], w=['aa'])
                act(aa, aa, AF.Sigmoid, r=['aa'], w=['aa'])
                cp('act', gg, pf[4][:, 0:256], r=['pf4'], w=['gg'])
                rr = rks[:, 0:256]
                kx = rks[:, 256:512]
                tt('pool', kk, kx, prm['kks'], ALU.mult, r=['rks', 'rwp'], w=['kk'])
                tt('dve', t1, kk, kk, ALU.mult, r=['kk'], w=['t1'])
                PL.op('dve', lambda e: e.tensor_reduce(out=sm[:, 0:4], in_=v3(t1, 4, 64), axis=AX.X, op=ALU.add), r=['t1'], w=['sm'])
                act(sm[:, 0:4], sm[:, 0:4], AF.Sqrt, r=['sm'], w=['sm'])
                ts('dve', sm[:, 0:4], sm[:, 0:4], 1e-12, None, ALU.max, None, r=['sm'], w=['sm'])
                PL.op('dve', lambda e: e.reciprocal(out=sm[:, 4:8], in_=sm[:, 0:4]), r=['sm'], w=['sm'])
                tt('dve', v3(kkn, 4, 64), v3(kk, 4, 64), sm[:, 4:8].unsqueeze(2).to_broadcast([128, 4, 64]), ALU.mult, r=['kk', 'sm'], w=['kkn'])
                tt('pool', t2, aa, prm['ka'], ALU.mult, r=['aa', 'rwp'], w=['t2'])
                tt('pool', t2, t2, omka, ALU.add, r=['t2', 'rwp2'], w=['t2'])
                tt('pool', kef, kx, t2, ALU.mult, r=['rks', 't2'], w=['kef'])
                tt('dve', t1, rr, kef, ALU.mult, r=['rks', 'kef'], w=['t1'])
                tt('dve', t1, t1, prm['rk'], ALU.mult, r=['t1', 'rwp'], w=['t1'])
                PL.op('dve', lambda e: e.tensor_reduce(out=sm[:, 8:12], in_=v3(t1, 4, 64), axis=AX.X, op=ALU.add), r=['t1'], w=['sm'])
                mm(pf[0][:, 0:256], tri[kd], lw, True, True, r=['rwmask', 'lw'], w=['pf0'])
                act(Dinc, pf[0][:, 0:256], AF.Exp, r=['pf0'], w=['Dinc'])
                act(Dinv, pf[0][:, 0:256], AF.Exp, r=['pf0'], w=['Dinv'], scale=-1.0)
                tt('dve', Dexc, pf[0][:, 0:256], lw, ALU.subtract, r=['pf0', 'lw'], w=['Dexc'])
                act(Dexc, Dexc, AF.Exp, r=['Dexc'], w=['Dexc'])
                stt(TM[:, 0, :], kkn, -1.0, Dexc, ALU.mult, ALU.mult, r=['kkn', 'Dexc'], w=['TM'])
                tt('pool', TM[:, 1, :], rr, Dinc, ALU.mult, r=['rks', 'Dinc'], w=['TM'])
                tt('dve', t1, kkn, aa, ALU.mult, r=['kkn', 'aa', 'sm'], w=['t1'])
                tt('dve', TM[:, 2, :], t1, Dinv, ALU.mult, r=['t1', 'Dinv'], w=['TM'])
                tt('pool', TM[:, 3, :], kef, Dinv, ALU.mult, r=['kef', 'Dinv'], w=['TM'])
                ptt = v3(pb[0][:, :], 8, 128)
                for hp in range(2):
                    for arr in range(4):
                        tr(ptt[:, hp * 4 + arr, :], TM[:, arr, hp * 128:(hp + 1) * 128], ident_b, r=['TM', 'ident_b'], w=['pb0'])
                cp('act', TT, ptt, r=['pb0'], w=['TT'])
                mk1 = v3(m1[kd], 4, 256)
                mk3 = v3(m3[kd], 4, 128)
                for par in range(2):
                    po = par * 64
                    for hp in range(2):
                        co = hp * 256
                        ar = TT[po:po + 64, hp * 4 + 0:hp * 4 + 2, :]
                        mm(pf[0][:, co:co + 256].rearrange("p (a t) -> p a t", a=2, t=128), TT[po:po + 64, hp * 4 + 2, :], ar, True, True,
                           r=['TT'], w=['pf0'])
                        mm(pf[1][:, co:co + 256].rearrange("p (a t) -> p a t", a=2, t=128), TT[po:po + 64, hp * 4 + 3, :], ar, True, True,
                           r=['TT'], w=['pf1'])
                        mm(pf[2][:, hp * 128:(hp + 1) * 128], TT[po:po + 64, hp * 4 + 0, :], TT[po:po + 64, hp * 4 + 2, :], True, True,
                           r=['TT'], w=['pf2'])
                    tt('dve', M1[:, par:4:2, :], v3(pf[0][:, :], 2, 256), mk1[:, 0:2, :], ALU.mult, r=['pf0', 'rwmask'], w=['M1'])
                    tt('dve', M2[:, par:4:2, :], v3(pf[1][:, :], 2, 256), mk1[:, 0:2, :], ALU.mult, r=['pf1', 'rwmask'], w=['M2'])
                    tt('dve', Qb[0][:, par:4:2, :], v3(pf[2][:, 0:256], 2, 128), mk3[:, 0:2, :], ALU.mult, r=['pf2', 'rwmask'], w=['Q0'])
                cp('pool', QTb[0], M1[:, :, 0:128], r=['M1'], w=['QT0'])
                tt('pool', PTb[0], M1[:, :, 0:128], ident_b.unsqueeze(1).to_broadcast([128, 4, 128]), ALU.add, r=['M1', 'ident_b'], w=['PT0'])
                nstep = 5 if kd == 'p' else 2
                for st_ in range(nstep):
                    a, b = st_ % 2, (st_ + 1) % 2
                    lastst = st_ == nstep - 1
                    for h in range(4):
                        mm(pf[0][:, h * 128:(h + 1) * 128], QTb[a][:, h, :], Qb[a][:, h, :], True, True, r=[f'QT{a}', f'Q{a}'], w=['pf0'])
                    cp('act', Qb[b], v3(pf[0][:, :], 4, 128), r=['pf0'], w=[f'Q{b}'])
                    if not lastst:
                        for h in range(4):
                            mm(pf[1][:, h * 128:(h + 1) * 128], Qb[a][:, h, :], QTb[a][:, h, :], True, True, r=[f'QT{a}', f'Q{a}'], w=['pf1'])
                        cp('dve', QTb[b], v3(pf[1][:, :], 4, 128), r=['pf1'], w=[f'QT{b}'])
                    for h in range(4):
                        mm(pf[2][:, h * 128:(h + 1) * 128], Qb[b][:, h, :], PTb[a][:, h, :], True, True, r=[f'Q{b}', f'PT{a}'], w=['pf2'])
                    tt('dve', PTb[b], v3(pf[2][:, :], 4, 128), PTb[a], ALU.add, r=['pf2', f'PT{a}'], w=[f'PT{b}'])
                PT = PTb[nstep % 2]
                ptk = f'PT{nstep % 2}'
                for ci, (r0, R) in enumerate(chunks):
                    for hp in range(2):
                        mm(pf[3][:, hp * 2 + ci:hp * 2 + ci + 1], Dinc[:, hp * 128:(hp + 1) * 128], selp[:, r0 + R - 1:r0 + R], True, True,
                           r=['Dinc', 'rwmask'], w=['pf3'])
                cp('act', DCt, pf[3][:, 0:4], r=['pf3'], w=['DCt'])
                for ci, (r0, R) in enumerate(chunks):
                    rs = slice(r0, r0 + 64)
                    cs_ = slice(r0, r0 + 64)
                    for hp in range(2):
                        mm(pf[4][rs, hp * 128:(hp + 1) * 128], TT[:, hp * 4 + 0, cs_], STb[:, hp, :], True, False, r=['TT', 'STb'], w=['pf4'])
                        for h2 in range(2):
                            h = hp * 2 + h2
                            mm(pf[4][rs, h * 64:(h + 1) * 64], M2[:, h, cs_], vbf[:, h * 64:(h + 1) * 64], False, h2 == 1, r=['M2', 'vbf'], w=['pf4'])
                    cp('act', RHSb[rs, :], pf[4][rs, 0:256], r=['pf4'], w=['RHSb'])
                    for h in range(4):
                        mm(pf[5][rs, h * 64:(h + 1) * 64], PT[:, h, cs_], RHSb[:, h * 64:(h + 1) * 64], True, True, r=[ptk, 'RHSb'], w=['pf5'])
                    cp('dve', Ub[rs, :], pf[5][rs, 0:256], r=['pf5'], w=['Ub'])
                    for hp in range(2):
                        mm(pf[4][rs, hp * 128:(hp + 1) * 128], TT[:, hp * 4 + 1, cs_], STb[:, hp, :], True, False, r=['TT', 'STb'], w=['pf4'])
                        for h2 in range(2):
                            h = hp * 2 + h2
                            mm(pf[4][rs, h * 64:(h + 1) * 64], M1[:, h, 128 + r0:128 + r0 + 64], Ub[:, h * 64:(h + 1) * 64], False, False, r=['M1', 'Ub'], w=['pf4'])
                            mm(pf[4][rs, h * 64:(h + 1) * 64], M2[:, h, 128 + r0:128 + r0 + 64], vbf[:, h * 64:(h + 1) * 64], False, h2 == 1, r=['M2', 'vbf'], w=['pf4'])
                    cp('act', ysb[rs, :], pf[4][rs, 0:256], r=['pf4'], w=['ysb'])
                    rr_ = slice(r0, r0 + R)
                    for hp in range(2):
                        mm(pf[5][:, hp * 128:(hp + 1) * 128], TM[rr_, 2, hp * 128:(hp + 1) * 128], Ub[rr_, hp * 128:(hp + 1) * 128], True, False, r=['TM', 'Ub'], w=['pf5'])
                        mm(pf[5][:, hp * 128:(hp + 1) * 128], TM[rr_, 3, hp * 128:(hp + 1) * 128], vbf[rr_, hp * 128:(hp + 1) * 128], False, True, r=['TM', 'vbf'], w=['pf5'])
                    for hp in range(2):
                        dc = DCt[:, hp * 2 + ci:hp * 2 + ci + 1]
                        tt('dve', t1[:, 0:128], pf[5][:, hp * 128:(hp + 1) * 128], bdm, ALU.mult, r=['pf5', 'rwmask'], w=['t1'])
                        ts('dve', STf[:, hp, :], STf[:, hp, :], dc, None, ALU.mult, None, r=['STf', 'DCt'], w=['STf'])
                        stt(STf[:, hp, :], t1[:, 0:128], dc, STf[:, hp, :], ALU.mult, ALU.add, r=['t1', 'DCt', 'STf'], w=['STf'])
                    cp('pool', STb, STf, r=['STf'], w=['STb'])
                y3 = v3(ysb, 4, 64)
                for h in range(4):
                    PL.op('dve', lambda e, h=h: e.bn_stats(out=sm[:, 16 + 6 * h:22 + 6 * h], in_=ysb[:, h * 64:(h + 1) * 64]), r=['ysb'], w=['sm2'])
                    PL.op('dve', lambda e, h=h: e.bn_aggr(out=sm[:, 40 + 2 * h:42 + 2 * h], in_=sm[:, 16 + 6 * h:22 + 6 * h]), r=['sm2'], w=['sm2'])
                mvv = v3(sm[:, 40:48], 4, 2)
                act(sm[:, 48:52], mvv[:, :, 1], AF.Sqrt, r=['sm2'], w=['sm3'], bias=GN_EPS, scale=1.0)
                PL.op('dve', lambda e: e.reciprocal(out=sm[:, 48:52], in_=sm[:, 48:52]), r=['sm3'], w=['sm3'])
                tt('dve', y3, y3, mvv[:, :, 0].unsqueeze(2).to_broadcast([128, 4, 64]), ALU.subtract, r=['ysb', 'sm2'], w=['ysb'])
                tt('dve', y3, y3, sm[:, 48:52].unsqueeze(2).to_broadcast([128, 4, 64]), ALU.mult, r=['ysb', 'sm3'], w=['ysb'])
                tt('pool', ysb, ysb, prm['gng'], ALU.mult, r=['ysb', 'rwp'], w=['ysb'])
                tt('pool', ysb, ysb, prm['gnb'], ALU.add, r=['ysb', 'rwp'], w=['ysb'])
                tt('dve', v3(t2, 4, 64), v3(vsb, 4, 64), sm[:, 8:12].unsqueeze(2).to_broadcast([128, 4, 64]), ALU.mult, r=['vsb', 'sm', 't2'], w=['t2'])
                tt('dve', ysb, ysb, t2, ALU.add, r=['ysb', 't2'], w=['ysb'])
                tt('dve', ycb, ysb, gg, ALU.mult, r=['ysb', 'gg'], w=['ycb'])

            def yc_to_brT(dst, dstk, ncols):
                pty = v3(pb[1][:, 0:256], 2, 128)
                for c in range(2):
                    tr(pty[:, c, :], ycb[:, c * 128:(c + 1) * 128], ident_b, r=['ycb', 'ident_b'], w=['pb1'])
                cp('act', dst, pty[:, :, 0:ncols], r=['pb1'], w=[dstk])

            def shift_out(xt, xk, col, dst):
                for (bk, ca, cb) in ((1, 0, 512), (2, 512, RWC)):
                    i = 0
                    for kc in range(8):
                        for (Wt, Wk) in ((W1, 'W1'), (W2, 'W2')):
                            mm(pf[bk][0:1, 0:cb - ca], xt[:, kc, col:col + 1], Wt[:, kc, ca:cb], i == 0, i == 15, r=[Wk, xk], w=[f'pf{bk}'])
                            i += 1
                    cp('act', psh[0:1, ca:cb], pf[bk][0:1, 0:cb - ca], r=[f'pf{bk}'], w=['psh'])
                PL.dma('sp', dst, psh[0:1, :], r=['psh'], w=['o_shift'])

            def state_out(dst4):
                for hp in range(2):
                    tr(pf[3][:, hp * 128:(hp + 1) * 128], STf[:, hp, :], ident_f, r=['STf', 'ident_f'], w=['pf3'])
                cp('act', v3(t1, 2, 128), v3(pf[3][:, 0:256], 2, 128), r=['pf3'], w=['t1'])
                t1v = v3(t1, 2, 128)
                for hp in range(2):
                    for h2 in range(2):
                        PL.dma('sp', dst4[hp * 2 + h2], t1v[h2 * 64:(h2 + 1) * 64, hp, h2 * 64:(h2 + 1) * 64], r=['t1'], w=['o_wkv'])


            def prompt_stream():
                memset('dve', STf, 0.0, w=['STf'])
                memset('pool', STb, 0.0, w=['STb'])
                for u in range(NU):
                    ub = u % 2
                    xt = xTu[ub]
                    xk = f'xTu{ub}'
                    PL.dma('sp', xt[:, :, 1:513], xT_s[u, :, :, :], r=[f'xT_s{u}'], w=[xk])
                    if u == 0:
                        memset('pool', xt[:, :, 0:1], 0.0, w=[xk])
                    else:
                        cp('pool', xt[:, :, 0:1], xTu[1 - ub][:, :, 512:513], r=[f'xTu{1 - ub}'], w=[xk])
                    for ti in range(4):
                        rw_tile(xt, xk, 1 + ti * 128, 'p', [(0, 64), (64, 64)])
                        yc_to_brT(brC[:, :, ti * 128:(ti + 1) * 128], 'brC', 128)
                    PL.dma('sp', brT_s[u, :, 4:6, :], brC, r=['brC'], w=[f'brT_s{u}c'])
                    if u == NU - 1:
                        shift_out(xt, xk, 512, p_shift[l:l + 1, :])
                state_out(p_wkv[l])

            def sample_stream():
                xts = xTu[0]
                PL.dma('sp', xts[:, :, 0:128], xT_s[NU, :, :, 0:128], r=[f'xT_s{NU}'], w=['xTu0'])
                for si in range(NSS):
                    cp('pool', xTq[:, :, 1:1 + SL], xts[:, :, si * SL:(si + 1) * SL], r=['xTu0'], w=['xTq'])
                    for hp in range(2):
                        for h2 in range(2):
                            PL.dma('sp', Xn[h2 * 64:(h2 + 1) * 64, hp, h2 * 64:(h2 + 1) * 64], swkv[l, si, hp * 2 + h2], r=['Xn'], w=['Xn'])
                    for hp in range(2):
                        tr(pf[3][:, hp * 128:(hp + 1) * 128], Xn[:, hp, :], ident_f, r=['Xn', 'ident_f'], w=['pf3'])
                    cp('act', STf, v3(pf[3][:, 0:256], 2, 128), r=['pf3'], w=['STf'])
                    cp('pool', STb, STf, r=['STf'], w=['STb'])
                    rw_tile(xTq, 'xTq', 1, 's', [(0, SL)], extra_s=si)
                    yc_to_brT(brC[:, :, si * SL:(si + 1) * SL], 'brC', SL)
                    shift_out(xTq, 'xTq', SL, s_shift[l, si:si + 1, :])
                    state_out(s_wkv[l, si])
                PL.dma('sp', brT_s[NU, :, 4:6, 0:128], brC[:, :, 0:128], r=['brC'], w=[f'brT_s{NU}c'])

            return prompt_stream, sample_stream

        pf_g, pb_g = pf, pb
        pstream, _ = make_rw('P', (0, 1, 2), 0)
        _, sstream = make_rw('S', (3, 4, 5), 1)
        P.rec_start()
        pstream()
        lp_ = P.rec_stop()
        P.rec_start()
        sstream()
        ls_ = P.rec_stop()
        P.merge([lp_, ls_])
        AF_.release(); AB_.release()
        P.barrier()

    if '0' in cfg.stages:
        stage0()
    for l in range(cfg.nlayers):
        if 'A' in cfg.stages:
            stageA1(l)
        if 'R' in cfg.stages:
            stageA2(l)
        if 'B' in cfg.stages:
            stageB(l)
        if 'C' in cfg.stages:
            stageC(l)
        if 'D' in cfg.stages:
            stageD(l, 0)
            stageD(l, 1)

    with nc.allow_non_contiguous_dma(reason="small parameter / state transfers"):
        P.emit()
    return P


def make_in_maps(cfg, inputs, ncores=8):
    consts = host_constants(cfg)
    f = lambda a: np.ascontiguousarray(np.asarray(a))
    maps = []
    nb = inputs['x_prompt'].shape[0]
    for c in range(ncores):
        b = c % nb
        ss = slice(NSS * c, NSS * (c + 1))
        m = {
            'xp': f(inputs['x_prompt'][b]),
            'xs': f(inputs['x_sample'][ss]).reshape(128, D),
            'memp': f(inputs['mem_prompt'][b]),
            'cache_k0': f(inputs['cache_k'][0]).reshape(-1, 256),
            'cache_k1': f(inputs['cache_k'][1]).reshape(-1, 256),
            'cache_v0': f(inputs['cache_v'][0]).reshape(-1, 256),
            'cache_v1': f(inputs['cache_v'][1]).reshape(-1, 256),
            'ptab': f(inputs['page_table'][ss]).reshape(-1).astype(np.int32),
            'cmk': f(inputs['cache_mem_k'][:, ss]).reshape(DEPTH, NSS, NMEM, D),
            'cmv': f(inputs['cache_mem_v'][:, ss]).reshape(DEPTH, NSS, NMEM, D),
            'sconv': f(inputs['state_conv'][:, ss]),
            'swkv': f(inputs['state_wkv'][:, ss]),
            'sshift': f(inputs['state_shift'][:, ss]),
            'w_branch': f(inputs['w_branch']).reshape(DEPTH, 4 * MIXW, D),
            'rw_rk': f(inputs['rw_rk']).reshape(DEPTH, MIXW),
        }
        for k in ['w_in', 'sb_bias', 'w_gate', 'b_gate', 'w_o', 'conv_w', 'rw_mu', 'rw_w0', 'rw_w2', 'rw_a0',
                  'rw_a2', 'rw_g2', 'rw_kk', 'rw_ka', 'rw_gn_g', 'rw_gn_b', 'sgu_ln_g', 'sgu_ln_b', 'sgu_ws',
                  'sgu_b', 'w_mq', 'w_mk', 'w_mv', 'w_mo', 'w_up', 'w_down', 'ln1_g', 'ln1_b', 'ln2_g', 'ln2_b',
                  'ln3_g', 'ln3_b']:
            m[k] = f(inputs[k])
        for k, v in consts.items():
            m['c_' + k] = v
        maps.append(m)
    return maps


def run(cfg, inputs, ncores=8):
    nc = build_program(cfg)
    maps = make_in_maps(cfg, inputs, ncores)
    res = run_bass_kernel_spmd(nc, maps, core_ids=list(range(ncores)))
    return res.results


def assemble(cfg, R, nb=4, ncores=8):
    SEQ = cfg.SEQ
    g = lambda name, cores: np.stack([R[c][name] for c in cores])
    pc = list(range(nb))
    ac = list(range(ncores))
    y_p = g('y_p', pc)
    y_s = g('y_s', ac).reshape(ncores * NSS, SL, D)
    def pl(name, shp):
        a = g(name, pc)
        return np.ascontiguousarray(np.moveaxis(a, 0, 1)).reshape(shp)
    def sl_(name, shp):
        a = g(name, ac)
        return np.ascontiguousarray(np.moveaxis(a, 0, 1)).reshape(shp)
    NS = ncores * NSS
    return (y_p, y_s,
            pl('p_k', (DEPTH, nb, SEQ, 4, 64)), pl('p_v', (DEPTH, nb, SEQ, 4, 64)),
            pl('p_mk', (DEPTH, nb, NMEM, 4, 256)), pl('p_mv', (DEPTH, nb, NMEM, 4, 256)),
            pl('p_conv', (DEPTH, nb, 2, MIXW)), pl('p_wkv', (DEPTH, nb, 4, 64, 64)), pl('p_shift', (DEPTH, nb, RWC)),
            sl_('s_k', (DEPTH, NS, SL, 4, 64)), sl_('s_v', (DEPTH, NS, SL, 4, 64)),
            sl_('s_conv', (DEPTH, NS, 2, MIXW)), sl_('s_wkv', (DEPTH, NS, 4, 64, 64)),
            sl_('s_shift', (DEPTH, NS, RWC)), sl_('s_chunk', (DEPTH, NS, SL, MIXW)))


def kernel(**inputs):
    SEQ = inputs['x_prompt'].shape[1]
    NPAGES = inputs['page_table'].shape[1]
    NPHYS = inputs['cache_k'].shape[1]
    cfg = Cfg(SEQ=SEQ, NPAGES=NPAGES, NPHYS=NPHYS)
    R = run(cfg, inputs)
    outs = assemble(cfg, R, nb=inputs['x_prompt'].shape[0])
    return tuple(np.ascontiguousarray(o.astype(np.float32)) for o in outs)
```

```python
import contextlib
import os
import numpy as np
SKIP = os.environ.get('KSKIP', '')
import concourse.bass as bass
import concourse.mybir as mybir
from concourse.bass_utils import run_bass_kernel_spmd

F32 = mybir.dt.float32
BF16 = mybir.dt.bfloat16
I32 = mybir.dt.int32
AF = mybir.ActivationFunctionType
ALU = mybir.AluOpType
AX = mybir.AxisListType

D = 1024
DEPTH = 2
MIXW = 256
RWC = 896
INC = 2944
DFF = 4096
NMEM = 256
ALPHA = (2 * DEPTH) ** 0.25
LN_EPS = 1e-5
GN_EPS = 64e-5
NSS = 16
SL = 8


class Prog:
    def __init__(self, nc, es):
        self.nc = nc
        self.es = es
        self.eh = {'pe': nc.tensor, 'act': nc.scalar, 'dve': nc.vector, 'pool': nc.gpsimd, 'sp': nc.sync}
        self.ops = []
        self.last_w = {}
        self.readers = {}
        self.EPOCH = 12000
        self._rec = None
        self.barriers = []
        self.NSLOT = 12

    def rec_start(self):
        self._rec = []

    def rec_stop(self):
        r_ = self._rec
        self._rec = None
        return r_

    def merge(self, lists):
        n = [len(x) for x in lists]
        pos = [0] * len(lists)
        tot = max(n)
        for step in range(tot):
            for i, lst in enumerate(lists):
                tgt = (step + 1) * n[i] // tot
                while pos[i] < tgt:
                    e_, f_, r_, w_, d_ = lst[pos[i]]
                    self.op(e_, f_, r=r_, w=w_, dma=d_)
                    pos[i] += 1

    def op(self, eng, fn, r=(), w=(), dma=False):
        if self._rec is not None:
            self._rec.append((eng, fn, tuple(r), tuple(w), dma))
            return -1
        pr = [k for k in r if isinstance(k, str) and (k.startswith('pf') or k.startswith('pb'))]
        if pr:
            r = [k for k in r if k not in pr]
            w = list(w) + pr
        deps = set()
        for k in r:
            if k in self.last_w:
                deps.add(self.last_w[k])
        for k in w:
            if k in self.last_w:
                deps.add(self.last_w[k])
            deps.update(self.readers.get(k, ()))
        idx = len(self.ops)
        self.ops.append(dict(eng=eng, fn=fn, deps=deps, dma=dma, sig=False, bar=len(self.barriers)))
        for k in r:
            self.readers.setdefault(k, []).append(idx)
        for k in w:
            self.last_w[k] = idx
            self.readers[k] = []
        return idx

    def dma(self, q, out, in_, r=(), w=(), **kw):
        return self.op(q, lambda e: e.dma_start(out=out, in_=in_, **kw), r=r, w=w, dma=True)

    def barrier(self):
        lastc = {}
        ndma = {e: 0 for e in self.eh}
        for i, o in enumerate(self.ops):
            if o['dma']:
                ndma[o['eng']] += 1
            else:
                lastc[o['eng']] = i
        self.barriers.append((lastc, ndma))
        self.bar_at = getattr(self, 'bar_at', []) + [len(self.ops)]
        self.last_w = {}
        self.readers = {}

    def emit(self):
        nc = self.nc
        kstop = int(os.environ.get('KSTOP', '0'))
        print('PROG n_ops', len(self.ops), 'kstop', kstop, flush=True)
        if kstop:
            self.ops = self.ops[:kstop]
            nbar = sum(1 for a in getattr(self, 'bar_at', []) if a <= kstop)
            self.barriers = self.barriers[:nbar]
            o_ = self.ops[-1]
            print('LAST OP', o_['eng'], o_['dma'], o_['fn'].__code__.co_firstlineno if o_['fn'] else None, flush=True)
        self.barrier()
        ops = self.ops
        n = len(ops)
        for (lastc, ndma) in self.barriers:
            for e, i in lastc.items():
                ops[i]['sig'] = True
        for i, o in enumerate(ops):
            best = {}
            dd = []
            for d in o['deps']:
                od = ops[d]
                if od['dma']:
                    dd.append(d)
                else:
                    e = od['eng']
                    if e == 'pe' and o['eng'] == 'pe' and not o['dma']:
                        continue
                    if e not in best or best[e] < d:
                        best[e] = d
            o['pd'] = dd + list(best.values())
            for d in o['pd']:
                ops[d]['sig'] = True
        cnt = {e: 0 for e in self.eh}
        dcnt = {e: 0 for e in self.eh}
        nsig = {e: 0 for e in self.eh}
        for o in ops:
            if (not o['dma']) and o['sig']:
                nsig[o['eng']] += 1
        sems = {}
        for e in self.eh:
            ne = nsig[e] // self.EPOCH + 1
            sems[e] = [self.es.enter_context(nc.semaphore(f"s_{e}_{k}")) for k in range(ne)]
        dsems = {e: [self.es.enter_context(nc.semaphore(f"d_{e}_{k}")) for k in range(self.NSLOT)]
                 for e in ['sp', 'pool', 'act']}
        for o in ops:
            e = o['eng']
            if o['dma']:
                j = dcnt[e]
                dcnt[e] += 1
                o['tok'] = (dsems[e][j % self.NSLOT], 16 * (j // self.NSLOT + 1))
                o['prev'] = (dsems[e][j % self.NSLOT], 16 * (j // self.NSLOT)) if j >= self.NSLOT else None
            elif o['sig']:
                c = cnt[e]
                cnt[e] += 1
                o['tok'] = (sems[e][c // self.EPOCH], c % self.EPOCH + 1)
        waited = {e: {} for e in self.eh}
        nwait = [0]

        def wait(e, tok):
            s, v = tok
            k = id(s)
            if waited[e].get(k, 0) < v:
                self.eh[e].wait_ge(s, v)
                waited[e][k] = v
                nwait[0] += 1

        def bar_wait(e, b):
            lastc, ndma = self.barriers[b]
            for e2, i in lastc.items():
                if e2 != e:
                    wait(e, ops[i]['tok'])
            for q in ['sp', 'pool', 'act']:
                m = ndma[q]
                for sl in range(self.NSLOT):
                    if m <= sl:
                        continue
                    j = ((m - 1 - sl) // self.NSLOT) * self.NSLOT + sl
                    wait(e, (dsems[q][sl], 16 * (j // self.NSLOT + 1)))

        curbar = {e: 0 for e in self.eh}
        for o in ops:
            e = o['eng']
            while curbar[e] < o['bar']:
                bar_wait(e, curbar[e])
                curbar[e] += 1
            toks = {}
            for d in o['pd']:
                s_, v_ = ops[d]['tok']
                if id(s_) not in toks or toks[id(s_)][1] < v_:
                    toks[id(s_)] = (s_, v_)
            for tk in toks.values():
                wait(e, tk)
            if o['dma'] and o['prev'] is not None:
                wait(e, o['prev'])
            ins = o['fn'](self.eh[e])
            if o['dma']:
                ins.then_inc(o['tok'][0], 16)
            elif o['sig']:
                ins.then_inc(o['tok'][0], 1)
        bar_wait('sp', len(self.barriers) - 1)
        self.n_ops = n
        self.n_wait = nwait[0]


class Cfg:
    def __init__(self, SEQ=4096, NPAGES=16, NPHYS=2560, debug=False, nlayers=DEPTH, stages="0ARBCD"):
        self.SEQ = SEQ
        self.NPAGES = NPAGES
        self.NPHYS = NPHYS
        self.NU = SEQ // 512
        self.NT = SEQ // 128
        self.debug = debug
        self.nlayers = nlayers
        self.stages = stages


def host_constants(cfg):
    c = {}
    c['ident'] = np.eye(128, dtype=np.float32)
    j = np.arange(128)
    c['lx'] = (j[:, None] < j[None, :]).astype(np.float32)
    c['ones'] = np.ones((128, 128), np.float32)
    c['sbmask'] = (j[:, None] < j[None, :]).astype(np.float32)
    sj, tj = j // SL, j % SL
    c['sbmask_s'] = ((sj[:, None] == sj[None, :]) & (tj[:, None] < tj[None, :])).astype(np.float32)
    def rwm(ch):
        cj = j // ch
        same = cj[:, None] == cj[None, :]
        su = same & (j[:, None] < j[None, :])
        iu = same & (j[:, None] <= j[None, :])
        sl = same & (j[:, None] > j[None, :])
        m1 = np.concatenate([su, iu], 1).astype(np.float32)
        return (np.tile(m1[:, None, :], (1, 4, 1)).reshape(128, 1024),
                np.tile(sl.astype(np.float32)[:, None, :], (1, 4, 1)).reshape(128, 512),
                iu.astype(np.float32))
    c['rwm1_p'], c['rwm3_p'], c['tri_p'] = rwm(64)
    c['rwm1_s'], c['rwm3_s'], c['tri_s'] = rwm(SL)
    sel = np.zeros((128, 2), np.float32)
    sel[63, 0] = 1
    sel[127, 1] = 1
    c['sel_p'] = sel
    sels = np.zeros((128, NSS), np.float32)
    for s in range(NSS):
        sels[s * SL + SL - 1, s] = 1
    c['sel_s'] = sels
    sf = np.zeros((NSS, 128), np.float32)
    for s in range(NSS):
        sf[s, s * SL] = 1
    c['self_s'] = sf
    c['seqmask'] = (sj[:, None] == np.arange(NSS)[None, :]).astype(np.float32)
    c['piota'] = np.arange(128, dtype=np.float32).reshape(128, 1)
    c['sgumask_s'] = ((sj[:, None] == sj[None, :]) & (tj[:, None] <= tj[None, :])).astype(np.float32)
    c['tril'] = (j[:, None] <= j[None, :]).astype(np.float32)
    c['bdm'] = ((j[:, None] // 64) == (j[None, :] // 64)).astype(np.float32)
    ohm = np.zeros((NSS, NSS * 128), np.float32)
    for s_ in range(NSS):
        ohm[s_, s_ * 128] = 1
    c['oh'] = ohm
    c['selrep'] = (j[:, None] == (j[None, :] % SL)).astype(np.float32)
    return c


CONST_SHAPES = None


def build_program(cfg):
    nc = bass.Bass("TRN2", target_bir_lowering=False)
    es = contextlib.ExitStack()
    with es:
        _build(nc, es, cfg)
    return nc


def _build(nc, es, cfg):
    P = Prog(nc, es)
    SEQ, NU, NT, NPAGES, NPHYS = cfg.SEQ, cfg.NU, cfg.NT, cfg.NPAGES, cfg.NPHYS
    NUA = NU + 1
    NTA = NT + 1
    dbg = cfg.debug

    def din(name, shape, dt=F32):
        return nc.dram_tensor(name, list(shape), dt, kind="ExternalInput").ap()

    def dout(name, shape, dt=F32):
        return nc.dram_tensor(name, list(shape), dt, kind="ExternalOutput").ap()

    def dscr(name, shape, dt=F32):
        if dbg:
            return nc.dram_tensor(name, list(shape), dt, kind="ExternalOutput").ap()
        return nc.dram_tensor(name, list(shape), dt, kind="Internal").ap()

    xp = din("xp", [SEQ, D])
    xs = din("xs", [128, D])
    memp = din("memp", [NMEM, D])
    cache_k = [din(f"cache_k{i}", [NPHYS * 128, 256]) for i in range(DEPTH)]
    cache_v = [din(f"cache_v{i}", [NPHYS * 128, 256]) for i in range(DEPTH)]
    ptab = din("ptab", [NSS * NPAGES], I32)
    cmk = din("cmk", [DEPTH, NSS, NMEM, D])
    cmv = din("cmv", [DEPTH, NSS, NMEM, D])
    sconv = din("sconv", [DEPTH, NSS, 2, MIXW])
    swkv = din("swkv", [DEPTH, NSS, 4, 64, 64])
    sshift = din("sshift", [DEPTH, NSS, RWC])
    w_in = din("w_in", [DEPTH, D, INC])
    sb_bias = din("sb_bias", [DEPTH, 4])
    w_gate = din("w_gate", [DEPTH, D, 4 * D])
    b_gate = din("b_gate", [DEPTH, 4 * D])
    w_branch = din("w_branch", [DEPTH, 4 * MIXW, D])
    w_o = din("w_o", [DEPTH, D, D])
    conv_w = din("conv_w", [DEPTH, 3, MIXW])
    rw_mu = din("rw_mu", [DEPTH, RWC])
    rw_w0 = din("rw_w0", [DEPTH, MIXW])
    rw_w2 = din("rw_w2", [DEPTH, 32, MIXW])
    rw_a0 = din("rw_a0", [DEPTH, MIXW])
    rw_a2 = din("rw_a2", [DEPTH, 32, MIXW])
    rw_g2 = din("rw_g2", [DEPTH, 64, MIXW])
    rw_kk = din("rw_kk", [DEPTH, MIXW])
    rw_ka = din("rw_ka", [DEPTH, MIXW])
    rw_rk = din("rw_rk", [DEPTH, MIXW])
    rw_gn_g = din("rw_gn_g", [DEPTH, MIXW])
    rw_gn_b = din("rw_gn_b", [DEPTH, MIXW])
    sgu_ln_g = din("sgu_ln_g", [DEPTH, MIXW])
    sgu_ln_b = din("sgu_ln_b", [DEPTH, MIXW])
    sgu_ws = din("sgu_ws", [DEPTH, 4, 128, 128])
    sgu_b = din("sgu_b", [DEPTH, 4, 128])
    w_mq = din("w_mq", [DEPTH, D, D])
    w_mk = din("w_mk", [DEPTH, D, D])
    w_mv = din("w_mv", [DEPTH, D, D])
    w_mo = din("w_mo", [DEPTH, D, D])
    w_up = din("w_up", [DEPTH, D, DFF])
    w_down = din("w_down", [DEPTH, DFF, D])
    lng = [din(f"ln{i}_g", [DEPTH, D]) for i in (1, 2, 3)]
    lnb = [din(f"ln{i}_b", [DEPTH, D]) for i in (1, 2, 3)]
    consts = host_constants(cfg)
    cin = {k: din("c_" + k, v.shape) for k, v in consts.items()}

    y_p = dout("y_p", [SEQ, D])
    y_s = dout("y_s", [128, D])
    p_k = dout("p_k", [DEPTH, SEQ, 256])
    p_v = dout("p_v", [DEPTH, SEQ, 256])
    p_mk = dout("p_mk", [DEPTH, NMEM, D])
    p_mv = dout("p_mv", [DEPTH, NMEM, D])
    p_conv = dout("p_conv", [DEPTH, 2, MIXW])
    p_wkv = dout("p_wkv", [DEPTH, 4, 64, 64])
    p_shift = dout("p_shift", [DEPTH, RWC])
    s_k = dout("s_k", [DEPTH, 128, 256])
    s_v = dout("s_v", [DEPTH, 128, 256])
    s_conv = dout("s_conv", [DEPTH, NSS, 2, MIXW])
    s_wkv = dout("s_wkv", [DEPTH, NSS, 4, 64, 64])
    s_shift = dout("s_shift", [DEPTH, NSS, RWC])
    s_chunk = dout("s_chunk", [DEPTH, 128, 256])

    xT_s = dscr("xT_s", [NUA, 128, 8, 512], BF16)
    xres_s = dscr("xres_s", [NTA, 128, D])
    brT_s = dscr("brT_s", [NUA, 128, 8, 512], BF16)
    mixT_s = dscr("mixT_s", [NUA, 128, 8, 512], BF16)

    FA = 20480
    BA = 61440
    fa = es.enter_context(nc.sbuf_tensor("fa", [128, FA], F32))
    ba = es.enter_context(nc.sbuf_tensor("ba", [128, BA], BF16))
    ia = es.enter_context(nc.sbuf_tensor("ia", [128, 512], I32))
    pf = [es.enter_context(nc.psum_tensor(f"pf{i}", [128, 512], F32)) for i in range(6)]
    pb = [es.enter_context(nc.psum_tensor(f"pb{i}", [128, 1024], BF16)) for i in range(2)]

    class Arena:
        def __init__(self, t, size, nm):
            self.t, self.size, self.nm, self.off, self.marks = t, size, nm, 0, []

        def alloc(self, n):
            n2 = (n + 15) // 16 * 16
            assert self.off + n2 <= self.size, (self.nm, self.off, n2, self.size)
            a = self.t[:, self.off:self.off + n]
            self.off += n2
            return a

        def mark(self):
            self.marks.append(self.off)

        def release(self):
            self.off = self.marks.pop()

    AF_, AB_ = Arena(fa, FA, 'fa'), Arena(ba, BA, 'ba')

    def falloc(n):
        return AF_.alloc(n)

    def balloc(n):
        return AB_.alloc(n)

    units = [('p', u, 512, 4) for u in range(NU)] + [('s', NU, 128, 1)]

    def unit_tiles(u):
        kind, ui, W, nt = units[u]
        return list(range(4 * ui, 4 * ui + nt)) if kind == 'p' else [NT]

    ident_b = balloc(128)
    lx_b = balloc(128)
    ones_b = balloc(128)
    ident_f = falloc(128)
    P.dma('pool', ident_b, cin['ident'][:, :], w=['ident_b'])
    P.dma('pool', lx_b, cin['lx'][:, :], w=['lx_b'])
    P.dma('pool', ones_b, cin['ones'][:, :], w=['ones_b'])
    P.dma('sp', ident_f, cin['ident'][:, :], w=['ident_f'])
    AF_.mark()
    AB_.mark()

    def mm(out, lhsT, rhs, start, stop, r, w):
        return P.op('pe', lambda e: e.matmul(out, lhsT=lhsT, rhs=rhs, start=start, stop=stop), r=r, w=w)

    def tr(out, in_, idt, r, w):
        return P.op('pe', lambda e: e.transpose(out=out, in_=in_, identity=idt), r=r, w=w)

    def act(out, in_, func, r, w, **kw):
        return P.op('act', lambda e: e.activation(out=out, in_=in_, func=func, **kw), r=r, w=w)

    def tt(eng, out, in0, in1, op, r, w):
        return P.op(eng, lambda e: e.tensor_tensor(out=out, in0=in0, in1=in1, op=op), r=r, w=w)

    def ts(eng, out, in0, s1, s2, op0, op1, r, w):
        if op1 is None:
            return P.op(eng, lambda e: e.tensor_scalar(out=out, in0=in0, scalar1=s1, scalar2=None, op0=op0), r=r, w=w)
        return P.op(eng, lambda e: e.tensor_scalar(out=out, in0=in0, scalar1=s1, scalar2=s2, op0=op0, op1=op1), r=r, w=w)

    def stt(out, in0, sc, in1, op0, op1, r, w):
        return P.op('dve', lambda e: e.scalar_tensor_tensor(out=out, in0=in0, scalar=sc, in1=in1, op0=op0, op1=op1), r=r, w=w)

    def cp(eng, out, in_, r, w):
        if eng == 'act':
            return P.op('act', lambda e: e.activation(out=out, in_=in_, func=AF.Copy), r=r, w=w)
        return P.op(eng, lambda e: e.tensor_copy(out=out, in_=in_), r=r, w=w)

    def rsqrt_(out, in_, eps, r, w):
        act(out, in_, AF.Sqrt, r=r, w=w, bias=eps, scale=1.0)
        P.op('dve', lambda e: e.reciprocal(out=out, in_=out), r=w, w=w)

    def memset(eng, ap, val, w):
        return P.op(eng, lambda e: e.memset(ap, val), w=w)

    bankc = [0]

    def nb():
        b = bankc[0] % 4
        bankc[0] += 1
        return b

    def v3(ap, a, b):
        return ap.rearrange("p (a b) -> p a b", a=a, b=b)

    def layernorm(t, tk, g_t, b_t, out, outk, outb, outbk, scr, scrk):
        st = scr[:, 0:12]
        mv = scr[:, 12:14]
        rs = scr[:, 14:15]
        P.op('dve', lambda e: e.bn_stats(out=st[:, 0:6], in_=t[:, 0:512]), r=[tk], w=[scrk])
        P.op('dve', lambda e: e.bn_stats(out=st[:, 6:12], in_=t[:, 512:1024]), r=[tk], w=[scrk])
        P.op('dve', lambda e: e.bn_aggr(out=mv, in_=st), r=[scrk], w=[scrk])
        rsqrt_(rs, mv[:, 1:2], LN_EPS, [scrk], [scrk])
        ts('dve', t, t, mv[:, 0:1], rs, ALU.subtract, ALU.mult, r=[tk, scrk], w=[tk])
        tt('pool', t, t, g_t, ALU.mult, r=[tk, 'lnp'], w=[tk])
        tt('dve', out, t, b_t, ALU.add, r=[tk, 'lnp'], w=[outk])
        if outb is not None:
            cp('act', outb, out, r=[outk], w=[outbk])

    def transpose_tile(src_b, srck, dst, dstk, pbi):
        pt = v3(pb[pbi][:, :], 8, 128)
        for kc in range(8):
            tr(pt[:, kc, :], src_b[:, kc * 128:(kc + 1) * 128], ident_b, r=[srck, 'ident_b'], w=[f'pb{pbi}'])
        cp('act', dst, pt, r=[f'pb{pbi}'], w=[dstk])

    def stage0():
        AF_.mark(); AB_.mark()
        xin = [balloc(1024) for _ in range(2)]
        xu = [v3(balloc(8 * 512), 8, 512) for _ in range(2)]
        for u in range(NUA):
            kind, ui, W, nt = units[u]
            ub = u % 2
            for ti, t in enumerate(unit_tiles(u)):
                sl = (t) % 2
                src = xp[t * 128:(t + 1) * 128, :] if kind == 'p' else xs[:, :]
                P.dma('pool', xin[sl], src, w=[f'xin{sl}'])
                transpose_tile(xin[sl], f'xin{sl}', xu[ub][:, :, ti * 128:(ti + 1) * 128], f'xu{ub}', sl)
            P.dma('sp', xT_s[u, :, :, 0:W], xu[ub][:, :, 0:W], r=[f'xu{ub}'], w=[f'xT_s{u}'])
        AF_.release(); AB_.release()
        P.barrier()


    def load_w(dst3, src2d, c0, c1, key, d0=0):
        kc_n = src2d.shape[0] // 128
        for kc in range(kc_n):
            P.dma('pool', dst3[:, kc, d0:d0 + (c1 - c0)], src2d[kc * 128:(kc + 1) * 128, c0:c1], w=[key])

    def bcast_row(dst, src1d, key, q='sp'):
        P.dma(q, dst, src1d.partition_broadcast(128), w=[key])

    def stageA1(l):
        AF_.mark(); AB_.mark()
        Wc = v3(balloc(8 * 2048), 8, 2048)
        load_w(Wc, w_in[l], 0, 1536, 'Wc', 0)
        load_w(Wc, w_in[l], 2432, 2944, 'Wc', 1536)
        KTm = balloc(max(2 * SEQ, 8192))
        VCm = balloc(max(NT * 256, 4096))
        KT = v3(KTm[:, 0:2 * SEQ], 2, SEQ)
        VC = v3(VCm[:, 0:NT * 256], NT, 256)
        vsn = balloc(256)
        gst = v3(falloc(4096), NSS, 256)
        piota = falloc(1)
        P.dma('sp', piota, cin['piota'][:, :], w=['piota'])
        xTu = [v3(balloc(8 * 512), 8, 512) for _ in range(2)]
        qT = v3(balloc(2 * 512), 2, 512)
        kTs = v3(balloc(2 * 128), 2, 128)
        brTu = v3(balloc(8 * 512), 8, 512)
        spb = v3(balloc(4 * 512), 4, 512)
        wtb = v3(balloc(4 * 512), 4, 512)
        rsb = v3(balloc(4 * 512), 4, 512)
        sbm = balloc(128)
        sbm_s = balloc(128)
        tril_b = balloc(128)
        sgm_s = balloc(128)
        WsT = v3(balloc(4 * 128), 4, 128)
        WsT_s = v3(balloc(4 * 128), 4, 128)
        wsraw = v3(balloc(4 * 128), 4, 128)
        wsx = v3(balloc(4 * 128), 4, 128)
        selrep = balloc(128)
        vlnb = balloc(256)
        ones64 = ones_b[:, 0:64]
        e1 = v3(falloc(4 * 512), 4, 512)
        rsf = v3(falloc(4 * 512), 4, 512)
        fac = falloc(512)
        kvst = [falloc(512) for _ in range(2)]
        sbb = falloc(4)
        cw = v3(falloc(6), 2, 3)
        lg_t = falloc(256)
        lb_t = falloc(256)
        sgb_p = v3(falloc(256), 2, 128)
        sgb_s = v3(falloc(256), 2, 128)
        hbs = v3(falloc(2 * 512), 2, 512)
        zc = falloc(2 * 640)
        cva = v3(falloc(2 * 512), 2, 512)
        uT = v3(falloc(2 * 512), 2, 512)
        gtmp = falloc(512)
        svt = falloc(256)
        svn = falloc(256)
        lnscr = falloc(16)
        zhist = falloc(4)
        wsxf = v3(falloc(4 * 128), 4, 128)

        P.dma('pool', sbm, cin['sbmask'][:, :], w=['sbm'])
        P.dma('pool', sbm_s, cin['sbmask_s'][:, :], w=['sbm_s'])
        P.dma('pool', tril_b, cin['tril'][:, :], w=['tril_b'])
        P.dma('pool', sgm_s, cin['sgumask_s'][:, :], w=['sgm_s'])
        P.dma('pool', selrep, cin['selrep'][:, :], w=['selrep'])
        P.dma('sp', sbb, sb_bias[l].partition_broadcast(128), w=['sbb'])
        for c in range(2):
            P.dma('sp', cw[:, c, :], conv_w[l][:, c * 128:(c + 1) * 128].rearrange("j p -> p j"), w=['cw'])
        bcast_row(lg_t, sgu_ln_g[l], 'sgp')
        bcast_row(lb_t, sgu_ln_b[l], 'sgp')
        for g in range(4):
            po = (g % 2) * 64
            P.dma('sp', sgb_p[po:po + 64, g // 2, :], sgu_b[l, g].partition_broadcast(64), w=['sgb'])
            src = bass.AP(tensor=sgu_b.tensor, offset=(l * 4 + g) * 128, ap=[[0, 64], [0, NSS], [1, SL]])
            P.dma('sp', sgb_s[po:po + 64, g // 2, :].rearrange("p (a b) -> p a b", a=NSS, b=SL), src, w=['sgb'])
            P.dma('pool', wsraw[:, g, :], sgu_ws[l, g], w=['wsraw'])
        b0 = nb()
        pw = v3(pb[0][:, 0:512], 4, 128)
        for g in range(4):
            tr(pw[:, g, :], wsraw[:, g, :], ident_b, r=['wsraw', 'ident_b'], w=['pb0'])
        for g in range(4):
            tt('dve', WsT[:, g, :], pw[:, g, :], tril_b, ALU.mult, r=['pb0', 'tril_b'], w=['WsT'])
        for g in range(4):
            cp('dve', wsx[0:SL, g, :].rearrange("p (a b) -> p a b", a=NSS, b=SL),
               WsT[0:SL, g, 0:SL].unsqueeze(1).to_broadcast([SL, NSS, SL]), r=['WsT'], w=['wsx'])
        pw2 = v3(pf[b0][:, :], 4, 128)
        for g in range(4):
            mm(pw2[:, g, :], selrep[0:SL, :], wsx[0:SL, g, :], True, True, r=['selrep', 'wsx'], w=[f'pf{b0}'])
        for g in range(4):
            tt('dve', WsT_s[:, g, :], pw2[:, g, :], sgm_s, ALU.mult, r=[f'pf{b0}', 'sgm_s'], w=['WsT'])
        memset('pool', zhist, 0.0, w=['zhist'])

        def proj_fm(c0, W, xt, xk, bank):
            for kc in range(8):
                mm(pf[bank][:, 0:W], Wc[:, kc, c0:c0 + 128], xt[:, kc, 0:W], kc == 0, kc == 7,
                   r=['Wc', xk], w=[f'pf{bank}'])

        for u in range(NUA):
            kind, ui, W, ntl = units[u]
            ub = u % 2
            xt = xTu[ub]
            xk = f'xTu{ub}'
            P.dma('sp', xt[:, :, 0:W], xT_s[u, :, :, 0:W], r=[f'xT_s{u}'], w=[xk])
            tiles = unit_tiles(u)
            for c in range(2):
                b = nb()
                proj_fm(c * 128, W, xt, xk, b)
                act(qT[:, c, 0:W], pf[b][:, 0:W], AF.Copy, r=[f'pf{b}'], w=['qT'], scale=0.125)
                b = nb()
                proj_fm(256 + c * 128, W, xt, xk, b)
                if kind == 'p':
                    cp('dve', KT[:, c, ui * 512:ui * 512 + W], pf[b][:, 0:W], r=[f'pf{b}'], w=['KT'])
                else:
                    cp('dve', kTs[:, c, :], pf[b][:, 0:W], r=[f'pf{b}'], w=['kTs'])
            for ti, t in enumerate(tiles):
                b = nb()
                for kc in range(8):
                    mm(pf[b][:, :], xt[:, kc, ti * 128:(ti + 1) * 128], Wc[:, kc, 256:768], kc == 0, kc == 7,
                       r=['Wc', xk], w=[f'pf{b}'])
                sl = t % 2
                cp('act', kvst[sl], pf[b][:, :], r=[f'pf{b}'], w=[f'kvst{sl}'])
                if kind == 'p':
                    cp('dve', VC[:, t, :], pf[b][:, 256:512], r=[f'pf{b}'], w=['VC'])
                    P.dma('sp', p_k[l, t * 128:(t + 1) * 128, :], kvst[sl][:, 0:256], r=[f'kvst{sl}'], w=['o_pk'])
                    P.dma('sp', p_v[l, t * 128:(t + 1) * 128, :], kvst[sl][:, 256:512], r=[f'kvst{sl}'], w=['o_pv'])
                else:
                    cp('dve', vsn, pf[b][:, 256:512], r=[f'pf{b}'], w=['vsn'])
                    P.dma('sp', s_k[l, :, :], kvst[sl][:, 0:256], r=[f'kvst{sl}'], w=['o_sk'])
                    P.dma('sp', s_v[l, :, :], kvst[sl][:, 256:512], r=[f'kvst{sl}'], w=['o_sv'])
            if 'sb' in SKIP:
                pass
            elif kind == 'p':
                sb_prompt(l, ui, qT, KT, VC, brTu, spb, wtb, rsb, rsf, e1, fac, sbb, sbm, ones64)
            else:
                sb_sample(l, qT, kTs, vsn, brTu, spb, wtb, rsb, rsf, e1, fac, sbb, sbm_s, ones64, KTm, VCm, gst, piota)
            nseq, Lq = (1, 512) if kind == 'p' else (NSS, SL)
            zv = zc[:, 0:2 * nseq * (Lq + 2)].rearrange("p (c s t) -> p c s t", c=2, s=nseq, t=Lq + 2)
            for c in range(2):
                b1 = nb()
                proj_fm(768 + 256 + 256 + c * 128, W, xt, xk, b1)
                cp('act', hbs[:, c, 0:W], pf[b1][:, 0:W], r=[f'pf{b1}'], w=['hbs'])
                b2 = nb()
                proj_fm(768 + 256 + c * 128, W, xt, xk, b2)
                tt('dve', zv[:, c, :, 2:Lq + 2],
                   pf[b2][:, 0:W].rearrange("p (s t) -> p s t", s=nseq, t=Lq),
                   hbs[:, c, 0:W].rearrange("p (s t) -> p s t", s=nseq, t=Lq), ALU.mult,
                   r=[f'pf{b2}', 'hbs'], w=['zc'])
            if kind == 'p':
                cp('pool', zv[:, :, 0, 0:2], v3(zhist[:, 0:4], 2, 2), r=['zhist'], w=['zc'])
            else:
                for c in range(2):
                    for jj in range(2):
                        P.dma('sp', zv[:, c, :, jj], sconv[l][:, jj, c * 128:(c + 1) * 128].rearrange("s p -> p s"), w=['zc'])
            for c in range(2):
                cv = cva[:, c, 0:W].rearrange("p (s t) -> p s t", s=nseq, t=Lq)
                ts('dve', cv, zv[:, c, :, 0:Lq], cw[:, c, 0:1], None, ALU.mult, None, r=['zc', 'cw'], w=['cva'])
                stt(cv, zv[:, c, :, 1:Lq + 1], cw[:, c, 1:2], cv, ALU.mult, ALU.add, r=['zc', 'cw', 'cva'], w=['cva'])
                stt(cv, zv[:, c, :, 2:Lq + 2], cw[:, c, 2:3], cv, ALU.mult, ALU.add, r=['zc', 'cw', 'cva'], w=['cva'])
                b3 = nb()
                proj_fm(768 + c * 128, W, xt, xk, b3)
                tt('dve', brTu[:, 2 + c, 0:W], pf[b3][:, 0:W], cva[:, c, 0:W], ALU.mult, r=[f'pf{b3}', 'cva'], w=['brTu'])
            if kind == 'p':
                cp('pool', v3(zhist[:, 0:4], 2, 2), zv[:, :, 0, Lq:Lq + 2], r=['zc'], w=['zhist'])
                if ui == NU - 1:
                    for c in range(2):
                        P.dma('sp', p_conv[l][:, c * 128:(c + 1) * 128].rearrange("j p -> p j"), zhist[:, 2 * c:2 * c + 2], r=['zhist'], w=['o_pconv'])
            else:
                for c in range(2):
                    for jj in range(2):
                        P.dma('sp', s_conv[l][:, jj, c * 128:(c + 1) * 128].rearrange("s p -> p s"), zv[:, c, :, Lq + jj], r=['zc'], w=['o_sconv'])
            for c in range(2):
                b = nb()
                proj_fm(1536 + c * 128, W, xt, xk, b)
                gelu(uT[:, c, 0:W], pf[b][:, 0:W], f'pf{b}', 'uT', gtmp[:, 0:W], W)
            for ti, t in enumerate(tiles):
                b = nb()
                for kc in range(8):
                    mm(pf[b][:, 0:256], xt[:, kc, ti * 128:(ti + 1) * 128], Wc[:, kc, 1792:2048], kc == 0, kc == 7,
                       r=['Wc', xk], w=[f'pf{b}'])
                gelu(svt, pf[b][:, 0:256], f'pf{b}', 'svt', gtmp[:, 0:256], 256)
                P.op('dve', lambda e: e.bn_stats(out=lnscr[:, 0:6], in_=svt), r=['svt'], w=['lnscr'])
                P.op('dve', lambda e: e.bn_aggr(out=lnscr[:, 6:8], in_=lnscr[:, 0:6]), r=['lnscr'], w=['lnscr'])
                rsqrt_(lnscr[:, 8:9], lnscr[:, 7:8], LN_EPS, ['lnscr'], ['lnscr'])
                ts('dve', svt, svt, lnscr[:, 6:7], lnscr[:, 8:9], ALU.subtract, ALU.mult, r=['svt', 'lnscr'], w=['svt'])
                tt('pool', svt, svt, lg_t, ALU.mult, r=['svt', 'sgp'], w=['svt'])
                tt('dve', svn, svt, lb_t, ALU.add, r=['svt', 'sgp'], w=['svn'])
                cp('act', vlnb, svn, r=['svn'], w=['vlnb'])
                if kind == 's':
                    P.dma('sp', s_chunk[l, :, :], svn, r=['svn'], w=['o_schunk'])
                b = nb()
                wst = WsT if kind == 'p' else WsT_s
                sgb = sgb_p if kind == 'p' else sgb_s
                for g in range(4):
                    po = (g % 2) * 64
                    mm(pf[b][po:po + 64, (g // 2) * 128:(g // 2 + 1) * 128], vlnb[:, g * 64:(g + 1) * 64], wst[:, g, :],
                       True, True, r=['vlnb', 'WsT'], w=[f'pf{b}'])
                mx = v3(pf[b][:, 0:256], 2, 128)
                tt('dve', cva[:, :, 0:128], mx, sgb, ALU.add, r=[f'pf{b}', 'sgb'], w=['cva'])
                tt('pool', brTu[:, 6:8, ti * 128:(ti + 1) * 128], cva[:, :, 0:128], uT[:, :, ti * 128:(ti + 1) * 128], ALU.mult,
                   r=['cva', 'uT'], w=['brTu'])
            P.dma('sp', brT_s[u, :, 0:4, 0:W], brTu[:, 0:4, 0:W], r=['brTu'], w=[f'brT_s{u}a'])
            P.dma('sp', brT_s[u, :, 6:8, 0:W], brTu[:, 6:8, 0:W], r=['brTu'], w=[f'brT_s{u}b'])
            if 'R' not in cfg.stages:
                memset('pool', brTu[:, 4:6, 0:W], 0.0, w=['brTu'])
                P.dma('sp', brT_s[u, :, 4:6, 0:W], brTu[:, 4:6, 0:W], r=['brTu'], w=[f'brT_s{u}c'])
        AF_.release(); AB_.release()
        P.barrier()

    def gelu(out, in_ps, ink, outk, tmp, W):
        tk = 'gtmp'
        act(tmp, in_ps, AF.Square, r=[ink], w=[tk])
        ts('dve', tmp, tmp, 0.044715, 1.0, ALU.mult, ALU.add, r=[tk], w=[tk])
        tt('dve', tmp, tmp, in_ps, ALU.mult, r=[tk, ink], w=[tk])
        act(tmp, tmp, AF.Sigmoid, r=[tk], w=[tk], scale=1.5957691216057308)
        tt('dve', out, tmp, in_ps, ALU.mult, r=[tk, ink], w=[outk])

    def sb_block(Zb, qk_list, nq, c0, hsl, bias_ap, mask, maskcols, first, e1h, sph, wth, rsbh, rsfh, keys, acc_list):
        zk = f'pf{Zb}'
        for (lt, rh, a, n_) in qk_list:
            mm(pf[Zb][:, a:a + n_], lt, rh, True, True, r=keys, w=[zk])
        act(e1h[:, c0:nq], pf[Zb][:, c0:nq], AF.Exp, r=[zk, 'sbb'], w=['e1' + hsl], bias=bias_ap)
        act(sph[:, c0:nq], e1h[:, c0:nq], AF.Ln, r=['e1' + hsl], w=['sp' + hsl], bias=1.0)
        if mask is not None:
            a, n_ = maskcols
            tt('pool', sph[:, a:a + n_], sph[:, a:a + n_], mask, ALU.mult, r=['sp' + hsl, 'sbm', 'sbm_s'], w=['sp' + hsl])
        mm(pf[Zb][:, c0:nq], lx_b, sph[:, c0:nq], True, False, r=['lx_b', 'sp' + hsl], w=[zk])
        if not first:
            mm(pf[Zb][:, c0:nq], ones_b, rsbh[:, c0:nq], False, False, r=['ones_b', 'rsb' + hsl], w=[zk])
        for i, (lt, rh, a, n_) in enumerate(qk_list):
            mm(pf[Zb][:, a:a + n_], lt, rh, False, i == len(qk_list) - 1, r=keys, w=[zk])
        act(wth[:, c0:nq], pf[Zb][:, c0:nq], AF.Exp, r=[zk, 'sbb'], w=['wt' + hsl], bias=bias_ap)
        if mask is not None:
            a, n_ = maskcols
            tt('pool', wth[:, a:a + n_], wth[:, a:a + n_], mask, ALU.mult, r=['wt' + hsl, 'sbm', 'sbm_s'], w=['wt' + hsl])
        for (o_ap, lv, a, n_, st_, sp_, ks, ok) in acc_list:
            mm(o_ap, lv, wth[:, a:a + n_], st_, sp_, r=['wt' + hsl] + ks, w=[ok])
        if first:
            cp('dve', rsfh[:, c0:nq], sph[:, c0:nq], r=['sp' + hsl], w=['rsf' + hsl])
        else:
            tt('dve', rsfh[:, c0:nq], rsfh[:, c0:nq], sph[:, c0:nq], ALU.add, r=['sp' + hsl, 'rsf' + hsl], w=['rsf' + hsl])
        cp('pool', rsbh[:, c0:nq], rsfh[:, c0:nq], r=['rsf' + hsl], w=['rsb' + hsl])

    def sb_finish(h, nq, ob, rsbh, hsl, fac, brTu, ones64, tb):
        po = (h % 2) * 64
        mm(pf[tb][po:po + 64, 0:nq], ones64, rsbh[:, 0:nq], True, True, r=['ones_b', 'rsb' + hsl], w=[f'pf{tb}'])
        act(fac[po:po + 64, 0:nq], pf[tb][po:po + 64, 0:nq], AF.Exp, r=[f'pf{tb}'], w=['fac'], scale=-1.0)
        tt('dve', brTu[po:po + 64, h // 2, 0:nq], pf[ob][po:po + 64, 0:nq], fac[po:po + 64, 0:nq], ALU.mult,
           r=[f'pf{ob}', 'fac'], w=['brTu'])

    def sb_prompt(l, ui, qT, KT, VC, brTu, spb, wtb, rsb, rsf, e1, fac, sbb, sbm, ones64):
        nkb = 4 * ui + 4
        for hp in range(2):
            def geo(kb):
                d = kb - 4 * ui
                return d, (128 * d if d > 0 else 0)

            def phaseA(kb):
                d, c0 = geo(kb)
                for h2 in range(2):
                    po = h2 * 64
                    Zb = h2 * 2 + (kb % 2)
                    mm(pf[Zb][:, c0:512], KT[po:po + 64, hp, kb * 128:(kb + 1) * 128], qT[po:po + 64, hp, c0:512], True, True,
                       r=['KT', 'qT'], w=[f'pf{Zb}'])
                for h2 in range(2):
                    sl = h2 * 2 + (kb % 2)
                    h = hp * 2 + h2
                    act(e1[:, sl, c0:512], pf[sl][:, c0:512], AF.Exp, r=[f'pf{sl}', 'sbb'], w=[f'e1_{sl}'], bias=sbb[:, h:h + 1])
                for h2 in range(2):
                    sl = h2 * 2 + (kb % 2)
                    act(spb[:, sl, c0:512], e1[:, sl, c0:512], AF.Ln, r=[f'e1_{sl}'], w=[f'sp_{sl}'], bias=1.0)
                if d >= 0:
                    for h2 in range(2):
                        sl = h2 * 2 + (kb % 2)
                        tt('pool', spb[:, sl, c0:c0 + 128], spb[:, sl, c0:c0 + 128], sbm, ALU.mult, r=[f'sp_{sl}', 'sbm'], w=[f'sp_{sl}'])

            def phaseB(kb):
                d, c0 = geo(kb)
                first = kb == 0
                for h2 in range(2):
                    po = h2 * 64
                    sl = h2 * 2 + (kb % 2)
                    zk = f'pf{sl}'
                    mm(pf[sl][:, c0:512], lx_b, spb[:, sl, c0:512], True, False, r=['lx_b', f'sp_{sl}'], w=[zk])
                    if not first:
                        mm(pf[sl][:, c0:512], ones_b, rsb[:, h2, c0:512], False, False, r=['ones_b', f'rsb{h2}'], w=[zk])
                    mm(pf[sl][:, c0:512], KT[po:po + 64, hp, kb * 128:(kb + 1) * 128], qT[po:po + 64, hp, c0:512], False, True,
                       r=['KT', 'qT'], w=[zk])
                for h2 in range(2):
                    sl = h2 * 2 + (kb % 2)
                    h = hp * 2 + h2
                    act(wtb[:, h2, c0:512], pf[sl][:, c0:512], AF.Exp, r=[f'pf{sl}', 'sbb'], w=[f'wt{h2}'], bias=sbb[:, h:h + 1])
                if d >= 0:
                    for h2 in range(2):
                        tt('pool', wtb[:, h2, c0:c0 + 128], wtb[:, h2, c0:c0 + 128], sbm, ALU.mult, r=[f'wt{h2}', 'sbm'], w=[f'wt{h2}'])
                for h2 in range(2):
                    po = h2 * 64
                    h = hp * 2 + h2
                    ob = 4 + h2
                    mm(pf[ob][po:po + 64, c0:512], VC[:, kb, h * 64:(h + 1) * 64], wtb[:, h2, c0:512], kb == 0, kb == nkb - 1,
                       r=['VC', f'wt{h2}'], w=[f'pf{ob}'])
                for h2 in range(2):
                    sl = h2 * 2 + (kb % 2)
                    if first:
                        cp('dve', rsf[:, h2, c0:512], spb[:, sl, c0:512], r=[f'sp_{sl}'], w=[f'rsf{h2}'])
                    else:
                        tt('dve', rsf[:, h2, c0:512], rsf[:, h2, c0:512], spb[:, sl, c0:512], ALU.add, r=[f'sp_{sl}', f'rsf{h2}'], w=[f'rsf{h2}'])
                for h2 in range(2):
                    cp('pool', rsb[:, h2, 0:512], rsf[:, h2, 0:512], r=[f'rsf{h2}'], w=[f'rsb{h2}'])

            phaseA(0)
            for kb in range(nkb):
                if kb + 1 < nkb:
                    phaseA(kb + 1)
                phaseB(kb)
            for h2 in range(2):
                sb_finish(hp * 2 + h2, 512, 4 + h2, rsb[:, h2, :], str(h2), fac, brTu, ones64, h2)

    def sb_sample(l, qT, kTs, vsn, brTu, spb, wtb, rsb, rsf, e1, fac, sbb, sbm_s, ones64, KTm, VCm, gst, piota):
        NP = NPAGES
        ptb = ia[:, 0:NSS * NP]
        idx = ia[:, 256:256 + NSS * NP]
        P.dma('sp', ptb, ptab.partition_broadcast(128), w=['ptb'])
        ts('dve', idx, ptb, 128.0, piota[:, 0:1], ALU.mult, ALU.add, r=['ptb', 'piota'], w=['idx'])
        kbf = v3(KTm[:, 0:4096], NSS, 256)
        ktb = KTm[:, 4096:8192].rearrange("p (c s k) -> p c s k", c=2, s=NSS, k=128)
        vbf = v3(VCm[:, 0:4096], NSS, 256)
        maskb = sbm_s.unsqueeze(1).to_broadcast([128, 2, 128])

        for ob in (4, 5):
            memset('dve', pf[ob][:, 0:256], 0.0, w=[f'pf{ob}'])

        def hv(t, par):
            return t[:, par:4:2, 0:128]
        for kb in range(NP + 1):
            last = kb == NP
            first = kb == 0
            if not last:
                for si in range(NSS):
                    col = si * NP + kb
                    P.op('pool', (lambda e, si=si, col=col: e.indirect_dma_start(
                        out=gst[:, si, :], out_offset=None, in_=cache_k[l][:, :],
                        in_offset=bass.IndirectOffsetOnAxis(ap=idx[:, col:col + 1], axis=0))),
                        r=['idx'], w=['gst'], dma=True)
                cp('dve', kbf, gst, r=['gst'], w=['kbf'])
                for si in range(NSS):
                    col = si * NP + kb
                    P.op('pool', (lambda e, si=si, col=col: e.indirect_dma_start(
                        out=gst[:, si, :], out_offset=None, in_=cache_v[l][:, :],
                        in_offset=bass.IndirectOffsetOnAxis(ap=idx[:, col:col + 1], axis=0))),
                        r=['idx'], w=['gst'], dma=True)
                cp('dve', vbf, gst, r=['gst'], w=['vbf'])
                for g in range(4):
                    c, sh = g // 2, g % 2
                    pt = v3(pb[g % 2][:, :], 8, 128)
                    for s8 in range(8):
                        si = sh * 8 + s8
                        tr(pt[:, s8, :], kbf[:, si, c * 128:(c + 1) * 128], ident_b, r=['kbf', 'ident_b'], w=[f'pb{g % 2}'])
                    cp('act', ktb[:, c, sh * 8:sh * 8 + 8, :], pt, r=[f'pb{g % 2}'], w=['ktb'])

            def zb(h):
                return (kb % 2) * 2 + (h % 2)

            def zcol(h):
                return (h // 2) * 128

            def qk(h, startf, stop_last):
                hp, po = h // 2, (h % 2) * 64
                Zb, zc0, zk = zb(h), zcol(h), f'pf{zb(h)}'
                if last:
                    mm(pf[Zb][:, zc0:zc0 + 128], kTs[po:po + 64, hp, :], qT[po:po + 64, hp, 0:128], startf, stop_last,
                       r=['kTs', 'qT'], w=[zk])
                else:
                    for si in range(NSS):
                        mm(pf[Zb][:, zc0 + si * SL:zc0 + (si + 1) * SL], ktb[po:po + 64, hp, si, :],
                           qT[po:po + 64, hp, si * SL:(si + 1) * SL], startf, (stop_last and si == NSS - 1) or startf,
                           r=['ktb', 'qT'], w=[zk])
            for h in range(4):
                qk(h, True, True)
            for h in range(4):
                Zb, zc0 = zb(h), zcol(h)
                act(e1[:, h, 0:128], pf[Zb][:, zc0:zc0 + 128], AF.Exp, r=[f'pf{Zb}', 'sbb'], w=['e1s'], bias=sbb[:, h:h + 1])
            act(spb[:, :, 0:128], e1[:, :, 0:128], AF.Ln, r=['e1s'], w=['sps'], bias=1.0)
            if last:
                for par in range(2):
                    tt('pool', hv(spb, par), hv(spb, par), maskb, ALU.mult, r=['sps', 'sbm_s'], w=['sps'])
            for h in range(4):
                Zb, zc0 = zb(h), zcol(h)
                mm(pf[Zb][:, zc0:zc0 + 128], lx_b, spb[:, h, 0:128], True, False, r=['lx_b', 'sps'], w=[f'pf{Zb}'])
                if not first:
                    mm(pf[Zb][:, zc0:zc0 + 128], ones_b, rsb[:, h, 0:128], False, False, r=['ones_b', 'rsbs'], w=[f'pf{Zb}'])
                qk(h, False, True)
            for h in range(4):
                Zb, zc0 = zb(h), zcol(h)
                act(wtb[:, h, 0:128], pf[Zb][:, zc0:zc0 + 128], AF.Exp, r=[f'pf{Zb}', 'sbb'], w=['wts'], bias=sbb[:, h:h + 1])
            if last:
                for par in range(2):
                    tt('pool', hv(wtb, par), hv(wtb, par), maskb, ALU.mult, r=['wts', 'sbm_s'], w=['wts'])
            for h in range(4):
                po = (h % 2) * 64
                ob = 4 + (h % 2)
                oc0 = (h // 2) * 128
                if last:
                    P.op('pe', (lambda e, ob=ob, po=po, oc0=oc0, h=h: e.matmul(
                        pf[ob][po:po + 64, oc0:oc0 + 128], lhsT=vsn[:, h * 64:(h + 1) * 64], rhs=wtb[:, h, 0:128],
                        start=False, stop=(h >= 2), skip_group_check=True)), r=['vsn', 'wts'], w=[f'pf{ob}'])
                else:
                    for si in range(NSS):
                        P.op('pe', (lambda e, ob=ob, po=po, oc0=oc0, h=h, si=si: e.matmul(
                            pf[ob][po:po + 64, oc0 + si * SL:oc0 + (si + 1) * SL], lhsT=vbf[:, si, h * 64:(h + 1) * 64],
                            rhs=wtb[:, h, si * SL:(si + 1) * SL],
                            start=False, stop=False, skip_group_check=True)),
                            r=['vbf', 'wts'], w=[f'pf{ob}'])
            if first:
                cp('dve', rsf[:, :, 0:128], spb[:, :, 0:128], r=['sps'], w=['rsfs'])
            else:
                tt('dve', rsf[:, :, 0:128], rsf[:, :, 0:128], spb[:, :, 0:128], ALU.add, r=['sps', 'rsfs'], w=['rsfs'])
            cp('pool', rsb[:, :, 0:128], rsf[:, :, 0:128], r=['rsfs'], w=['rsbs'])
        for h in range(4):
            po = (h % 2) * 64
            tb = h % 2
            ob = 4 + (h % 2)
            oc0 = (h // 2) * 128
            mm(pf[tb][po:po + 64, 0:128], ones64, rsb[:, h, 0:128], True, True, r=['ones_b', 'rsbs'], w=[f'pf{tb}'])
            act(fac[po:po + 64, 0:128], pf[tb][po:po + 64, 0:128], AF.Exp, r=[f'pf{tb}'], w=['fac'], scale=-1.0)
            tt('dve', brTu[po:po + 64, h // 2, 0:128], pf[ob][po:po + 64, oc0:oc0 + 128], fac[po:po + 64, 0:128], ALU.mult,
               r=[f'pf{ob}', 'fac'], w=['brTu'])


    ffp_s = dscr("ffp_s", [NTA, 128, D])

    def xres_src(l, t):
        if l == 0:
            return xp[t * 128:(t + 1) * 128, :] if t < NT else xs[:, :]
        return xres_s[t]

    def stageB(l):
        AF_.mark(); AB_.mark()
        Wg = v3(balloc(8 * 4096), 8, 4096)
        Wb = v3(balloc(8 * 1024), 8, 1024)
        load_w(Wg, w_gate[l], 0, 4096, 'Wg')
        load_w(Wb, w_branch[l], 0, 1024, 'Wb')
        bg = falloc(32)
        P.dma('sp', bg, b_gate[l].rearrange("(j p) -> p j", p=128), w=['bg'])
        xTu = [v3(balloc(8 * 512), 8, 512) for _ in range(2)]
        brTu = [v3(balloc(8 * 512), 8, 512)] * 2
        mixTu = [v3(balloc(8 * 512), 8, 512)] * 2
        sg = [falloc(512) for _ in range(2)]
        acc = [falloc(512) for _ in range(2)]
        tmp = [falloc(512) for _ in range(2)]
        n = 0
        for u in range(NUA):
            kind, ui, W, ntl = units[u]
            ub = u % 2
            P.dma('sp', xTu[ub][:, :, 0:W], xT_s[u, :, :, 0:W], r=[f'xT_s{u}'], w=[f'xTu{ub}'])
            P.dma('sp', brTu[ub][:, :, 0:W], brT_s[u, :, :, 0:W], r=[f'brT_s{u}a', f'brT_s{u}b', f'brT_s{u}c'], w=['brTuB'])
            for c in range(8):
                ab = c % 2
                for i in range(4):
                    sb_ = n % 2
                    n += 1
                    bgk = nb()
                    for kc in range(8):
                        mm(pf[bgk][:, 0:W], Wg[:, kc, i * 1024 + c * 128:i * 1024 + (c + 1) * 128], xTu[ub][:, kc, 0:W],
                           kc == 0, kc == 7, r=['Wg', f'xTu{ub}'], w=[f'pf{bgk}'])
                    act(sg[sb_][:, 0:W], pf[bgk][:, 0:W], AF.Sigmoid, r=[f'pf{bgk}', 'bg'], w=[f'sg{sb_}'],
                        bias=bg[:, i * 8 + c:i * 8 + c + 1])
                    bpk = nb()
                    for k2 in range(2):
                        mm(pf[bpk][:, 0:W], Wb[:, 2 * i + k2, c * 128:(c + 1) * 128], brTu[ub][:, 2 * i + k2, 0:W],
                           k2 == 0, k2 == 1, r=['Wb', 'brTuB'], w=[f'pf{bpk}'])
                    if i == 0:
                        tt('dve', acc[ab][:, 0:W], pf[bpk][:, 0:W], sg[sb_][:, 0:W], ALU.mult, r=[f'pf{bpk}', f'sg{sb_}'], w=[f'acc{ab}'])
                    else:
                        tt('dve', tmp[sb_][:, 0:W], pf[bpk][:, 0:W], sg[sb_][:, 0:W], ALU.mult, r=[f'pf{bpk}', f'sg{sb_}'], w=[f'tmp{sb_}'])
                        dst = acc[ab][:, 0:W] if i < 3 else mixTu[ub][:, c, 0:W]
                        dk = f'acc{ab}' if i < 3 else 'mixTuB'
                        tt('pool', dst, acc[ab][:, 0:W], tmp[sb_][:, 0:W], ALU.add, r=[f'acc{ab}', f'tmp{sb_}'], w=[dk])
            P.dma('sp', mixT_s[u, :, :, 0:W], mixTu[ub][:, :, 0:W], r=['mixTuB'], w=[f'mixT_s{u}'])
        AF_.release(); AB_.release()
        P.barrier()

    def load_ln(l, i, gt, bt):
        P.dma('sp', gt, lng[i][l].partition_broadcast(128), w=['lnp'])
        P.dma('sp', bt, lnb[i][l].partition_broadcast(128), w=['lnp'])

    def proj_tm_ln(lhs3, lhsk, Wt, Wk, ncol_kc, ti, xr, xrk, tbuf, tk):
        for hh in range(2):
            b = nb()
            for kc in range(ncol_kc):
                mm(pf[b][:, :], lhs3[:, kc, ti * 128:(ti + 1) * 128], Wt[:, kc, hh * 512:(hh + 1) * 512], kc == 0, kc == ncol_kc - 1,
                   r=[lhsk, Wk], w=[f'pf{b}'])
            stt(tbuf[:, hh * 512:(hh + 1) * 512], xr[:, hh * 512:(hh + 1) * 512], ALPHA, pf[b][:, :], ALU.mult, ALU.add,
                r=[xrk, f'pf{b}'], w=[tk])

    def stageC(l):
        AF_.mark(); AB_.mark()
        Wo = v3(balloc(8 * 1024), 8, 1024)
        Wq = v3(balloc(8 * 1024), 8, 1024)
        Wmo = v3(balloc(8 * 1024), 8, 1024)
        load_w(Wo, w_o[l], 0, 1024, 'Wo')
        load_w(Wq, w_mq[l], 0, 1024, 'Wq')
        load_w(Wmo, w_mo[l], 0, 1024, 'Wmo')
        g1 = falloc(1024); b1 = falloc(1024); g2 = falloc(1024); b2 = falloc(1024)
        load_ln(l, 0, g1, b1)
        load_ln(l, 1, g2, b2)
        mkT = v3(balloc(8 * 256), 8, 256)
        mvb = v3(balloc(2 * 1024), 2, 1024)
        stg = [falloc(512) for _ in range(2)]
        AB_.mark()
        memT = v3(balloc(8 * 256), 8, 256)
        Wmk = v3(balloc(8 * 1024), 8, 1024)
        Wmv = v3(balloc(8 * 1024), 8, 1024)
        load_w(Wmk, w_mk[l], 0, 1024, 'Wmk')
        load_w(Wmv, w_mv[l], 0, 1024, 'Wmv')
        mtl = [balloc(1024) for _ in range(2)]
        for mt in range(2):
            P.dma('pool', mtl[mt], memp[mt * 128:(mt + 1) * 128, :], w=[f'mtl{mt}'])
            transpose_tile(mtl[mt], f'mtl{mt}', memT[:, :, mt * 128:(mt + 1) * 128], 'memT', mt)
        n = 0
        for (Wt, Wk, outd, isv) in ((Wmk, 'Wmk', p_mk, False), (Wmv, 'Wmv', p_mv, True)):
            for mt in range(2):
                for hh in range(2):
                    b = nb()
                    for kc in range(8):
                        mm(pf[b][:, :], memT[:, kc, mt * 128:(mt + 1) * 128], Wt[:, kc, hh * 512:(hh + 1) * 512], kc == 0, kc == 7,
                           r=['memT', Wk], w=[f'pf{b}'])
                    sl = n % 2
                    n += 1
                    cp('act', stg[sl], pf[b][:, :], r=[f'pf{b}'], w=[f'stg{sl}'])
                    if isv:
                        cp('dve', mvb[:, mt, hh * 512:(hh + 1) * 512], pf[b][:, :], r=[f'pf{b}'], w=['mvb'])
                    P.dma('sp', outd[l, mt * 128:(mt + 1) * 128, hh * 512:(hh + 1) * 512], stg[sl], r=[f'stg{sl}'], w=['o_pm'])
        for c in range(8):
            b = nb()
            for kc in range(8):
                mm(pf[b][:, 0:256], Wmk[:, kc, c * 128:(c + 1) * 128], memT[:, kc, 0:256], kc == 0, kc == 7, r=['memT', 'Wmk'], w=[f'pf{b}'])
            cp('dve', mkT[:, c, :], pf[b][:, 0:256], r=[f'pf{b}'], w=['mkT'])
        P.barrier()
        AB_.release()
        mixTu = [v3(balloc(8 * 512), 8, 512)] * 2
        x1Tu = v3(balloc(8 * 512), 8, 512)
        qmT = v3(balloc(8 * 512), 8, 512)
        x2Tu = qmT
        attT = v3(balloc(8 * 512), 8, 512)
        prb = v3(balloc(2 * 512), 2, 512)
        xb = [balloc(1024) for _ in range(2)]
        smk = [v3(balloc(2 * 1024), 2, 1024)] * 2
        smv = [v3(balloc(2 * 1024), 2, 1024)] * 2
        smkT = v3(balloc(8 * 256), 8, 256)
        prs = balloc(1024)
        xr = [falloc(1024) for _ in range(2)]
        x1 = v3(falloc(4 * 1024), 4, 1024)
        tb_ = [falloc(1024) for _ in range(2)]
        rden = falloc(512)
        lnscr = falloc(16)
        for u in range(NUA):
            kind, ui, W, ntl = units[u]
            ub = u % 2
            tiles = unit_tiles(u)
            P.dma('sp', mixTu[ub][:, :, 0:W], mixT_s[u, :, :, 0:W], r=[f'mixT_s{u}'], w=['mixTuC'])
            for ti, t in enumerate(tiles):
                sl = t % 2
                P.dma('sp', xr[sl], xres_src(l, t), r=[f'xres_s{t}'], w=[f'xr{sl}'])
                proj_tm_ln(mixTu[ub], 'mixTuC', Wo, 'Wo', 8, ti, xr[sl], f'xr{sl}', tb_[sl], f'tb{sl}')
                layernorm(tb_[sl], f'tb{sl}', g1, b1, x1[:, ti, :], 'x1', xb[sl], f'xb{sl}', lnscr, 'lnscrC')
                transpose_tile(xb[sl], f'xb{sl}', x1Tu[:, :, ti * 128:(ti + 1) * 128], 'x1Tu', sl)
            for c in range(8):
                b = nb()
                for kc in range(8):
                    mm(pf[b][:, 0:W], Wq[:, kc, c * 128:(c + 1) * 128], x1Tu[:, kc, 0:W], kc == 0, kc == 7, r=['Wq', 'x1Tu'], w=[f'pf{b}'])
                act(qmT[:, c, 0:W], pf[b][:, 0:W], AF.Copy, r=[f'pf{b}'], w=['qmT'], scale=1.0 / 16.0)
            if dbg and os.environ.get('DBGC'):
                srcd = {'x1T': x1Tu, 'qmT': qmT}[os.environ['DBGC']]
                P.dma('sp', mixT_s[u, :, :, 0:W], srcd[:, :, 0:W], r=['x1Tu', 'qmT'], w=[f'mixT_s{u}'])
            if kind == 'p':
                for h in range(4):
                    for km in range(2):
                        b = nb()
                        for ec in range(2):
                            mm(pf[b][:, 0:W], mkT[:, 2 * h + ec, km * 128:(km + 1) * 128], qmT[:, 2 * h + ec, 0:W], ec == 0, ec == 1,
                               r=['mkT', 'qmT'], w=[f'pf{b}'])
                        act(prb[:, km, 0:W], pf[b][:, 0:W], AF.Exp, r=[f'pf{b}'], w=['prb'])
                    b = nb()
                    for km in range(2):
                        mm(pf[b][:, 0:W], ones_b, prb[:, km, 0:W], km == 0, km == 1, r=['ones_b', 'prb'], w=[f'pf{b}'])
                    P.op('dve', lambda e, b=b, W=W: e.reciprocal(out=rden[:, 0:W], in_=pf[b][:, 0:W]), r=[f'pf{b}'], w=['rden'])
                    for ec in range(2):
                        b = nb()
                        for km in range(2):
                            mm(pf[b][:, 0:W], mvb[:, km, (2 * h + ec) * 128:(2 * h + ec + 1) * 128], prb[:, km, 0:W], km == 0, km == 1,
                               r=['mvb', 'prb'], w=[f'pf{b}'])
                        tt('dve', attT[:, 2 * h + ec, 0:W], pf[b][:, 0:W], rden[:, 0:W], ALU.mult, r=[f'pf{b}', 'rden'], w=['attT'])
            else:
                xattn_sample(l, qmT, attT, smk, smv, smkT, prs, rden)
            for ti, t in enumerate(tiles):
                sl = t % 2
                proj_tm_ln(attT, 'attT', Wmo, 'Wmo', 8, ti, x1[:, ti, :], 'x1', tb_[sl], f'tb{sl}')
                if dbg:
                    P.dma('sp', ffp_s[t], tb_[sl], r=[f'tb{sl}'], w=[f'ffp_s{t}'])
                layernorm(tb_[sl], f'tb{sl}', g2, b2, xr[sl], f'xr{sl}', xb[sl], f'xb{sl}', lnscr, 'lnscrC')
                P.dma('sp', xres_s[t], xr[sl], r=[f'xr{sl}'], w=[f'xres_s{t}'])
                transpose_tile(xb[sl], f'xb{sl}', x2Tu[:, :, ti * 128:(ti + 1) * 128], 'qmT', sl)
            P.dma('sp', xT_s[u, :, :, 0:W], x2Tu[:, :, 0:W], r=['qmT'], w=[f'xT_s{u}'])
        AF_.release(); AB_.release()
        P.barrier()

    def xattn_sample(l, qmT, attT, smk, smv, smkT, prs, rden):
        prv = prs.rearrange("p (k s h q) -> p k s h q", k=2, s=NSS, h=4, q=SL)
        for si in range(NSS):
            sb_ = si % 2
            P.dma('pool', smk[sb_], cmk[l, si].rearrange("(m p) d -> p m d", p=128), w=['smkC'])
            P.dma('pool', smv[sb_], cmv[l, si].rearrange("(m p) d -> p m d", p=128), w=['smvC'])
            for half in range(2):
                pt = v3(pb[half][:, :], 8, 128)
                for j in range(8):
                    c = half * 4 + j // 2
                    km = j % 2
                    tr(pt[:, j, :], smk[sb_][:, km, c * 128:(c + 1) * 128], ident_b, r=['smkC', 'ident_b'], w=[f'pb{half}'])
                cp('act', smkT[:, half * 4:half * 4 + 4, :].rearrange("p c (m k) -> p (c m) k", m=2, k=128), pt, r=[f'pb{half}'], w=['smkT'])
            for km in range(2):
                for h in range(4):
                    for ec in range(2):
                        mm(pf[km][:, si * 32 + h * SL:si * 32 + (h + 1) * SL], smkT[:, 2 * h + ec, km * 128:(km + 1) * 128],
                           qmT[:, 2 * h + ec, si * SL:(si + 1) * SL], ec == 0, ec == 1, r=['smkT', 'qmT'], w=[f'pf{km}'])
            for km in range(2):
                act(prv[:, km, si], pf[km][:, si * 32:(si + 1) * 32].rearrange("p (h q) -> p h q", h=4, q=SL), AF.Exp,
                    r=[f'pf{km}'], w=['prs'])
            for h in range(4):
                for ec in range(2):
                    c = 2 * h + ec
                    ob = 2 + c // 4
                    oc = (c % 4) * 128 + si * SL
                    for km in range(2):
                        mm(pf[ob][:, oc:oc + SL], smv[sb_][:, km, c * 128:(c + 1) * 128], prv[:, km, si, h, :], km == 0, km == 1,
                           r=['smvC', 'prs'], w=[f'pf{ob}'])
        db = 4
        for km in range(2):
            mm(pf[db][:, :], ones_b, prs[:, km * 512:(km + 1) * 512], km == 0, km == 1, r=['ones_b', 'prs'], w=[f'pf{db}'])
        P.op('dve', lambda e: e.reciprocal(out=rden[:, 0:512], in_=pf[db][:, :]), r=[f'pf{db}'], w=['rden'])
        rv = rden[:, 0:512].rearrange("p (s h q) -> p s h q", s=NSS, h=4, q=SL)
        for c in range(8):
            h = c // 2
            ob = 2 + c // 4
            oc = (c % 4) * 128
            tt('dve', attT[:, c, 0:128].rearrange("p (s q) -> p s q", s=NSS, q=SL),
               pf[ob][:, oc:oc + 128].rearrange("p (s q) -> p s q", s=NSS, q=SL), rv[:, :, h, :], ALU.mult,
               r=[f'pf{ob}', 'rden'], w=['attT'])

    def stageD(l, f):
        AF_.mark(); AB_.mark()
        Wu = v3(balloc(8 * 2048), 8, 2048)
        Wd = v3(balloc(16 * 1024), 16, 1024)
        load_w(Wu, w_up[l], f * 2048, (f + 1) * 2048, 'Wu')
        load_w(Wd, w_down[l][f * 2048:(f + 1) * 2048, :], 0, 1024, 'Wd')
        g3 = falloc(1024); b3 = falloc(1024)
        if f == 1:
            load_ln(l, 2, g3, b3)
        xTu = [v3(balloc(8 * 512), 8, 512) for _ in range(2)]
        hid = v3(balloc(16 * 512), 16, 512)
        xoT = v3(balloc(8 * 512), 8, 512)
        xb = [balloc(1024) for _ in range(2)]
        rl = [falloc(512) for _ in range(2)]
        fft = [falloc(1024) for _ in range(2)]
        xr = [falloc(1024) for _ in range(2)]
        fp_ = [falloc(1024) for _ in range(2)]
        lnscr = falloc(16)
        lastl = (l == cfg.nlayers - 1)
        for u in range(NUA):
            kind, ui, W, ntl = units[u]
            ub = u % 2
            tiles = unit_tiles(u)
            P.dma('sp', xTu[ub][:, :, 0:W], xT_s[u, :, :, 0:W], r=[f'xT_s{u}'], w=[f'xTu{ub}'])
            for c in range(16):
                b = nb()
                for kc in range(8):
                    mm(pf[b][:, 0:W], Wu[:, kc, c * 128:(c + 1) * 128], xTu[ub][:, kc, 0:W], kc == 0, kc == 7, r=['Wu', f'xTu{ub}'], w=[f'pf{b}'])
                sl = c % 2
                act(rl[sl][:, 0:W], pf[b][:, 0:W], AF.Relu, r=[f'pf{b}'], w=[f'rl{sl}'])
                tt('dve' if c % 4 else 'pool', hid[:, c, 0:W], rl[sl][:, 0:W], rl[sl][:, 0:W], ALU.mult, r=[f'rl{sl}'], w=['hid'])
            for ti, t in enumerate(tiles):
                sl = t % 2
                if f == 1:
                    P.dma('sp', xr[sl], xres_s[t], r=[f'xres_s{t}'], w=[f'xr{sl}'])
                    P.dma('sp', fp_[sl], ffp_s[t], r=[f'ffp_s{t}'], w=[f'fp{sl}'])
                for hh in range(2):
                    b = nb()
                    for kc in range(16):
                        mm(pf[b][:, :], hid[:, kc, ti * 128:(ti + 1) * 128], Wd[:, kc, hh * 512:(hh + 1) * 512], kc == 0, kc == 15,
                           r=['hid', 'Wd'], w=[f'pf{b}'])
                    cs = slice(hh * 512, (hh + 1) * 512)
                    if f == 0:
                        cp('act', fft[sl][:, cs], pf[b][:, :], r=[f'pf{b}'], w=[f'fft{sl}'])
                    else:
                        tt('dve', fft[sl][:, cs], pf[b][:, :], fp_[sl][:, cs], ALU.add, r=[f'pf{b}', f'fp{sl}'], w=[f'fft{sl}'])
                        stt(fft[sl][:, cs], xr[sl][:, cs], ALPHA, fft[sl][:, cs], ALU.mult, ALU.add, r=[f'xr{sl}', f'fft{sl}'], w=[f'fft{sl}'])
                if f == 0:
                    P.dma('sp', ffp_s[t], fft[sl], r=[f'fft{sl}'], w=[f'ffp_s{t}'])
                else:
                    layernorm(fft[sl], f'fft{sl}', g3, b3, xr[sl], f'xr{sl}', None if lastl else xb[sl], f'xb{sl}', lnscr, 'lnscrD')
                    if lastl:
                        dst = y_p[t * 128:(t + 1) * 128, :] if kind == 'p' else y_s[:, :]
                        P.dma('sp', dst, xr[sl], r=[f'xr{sl}'], w=[f'o_y{t}'])
                    else:
                        P.dma('sp', xres_s[t], xr[sl], r=[f'xr{sl}'], w=[f'xres_s{t}'])
                        transpose_tile(xb[sl], f'xb{sl}', xoT[:, :, ti * 128:(ti + 1) * 128], 'xoT', sl)
            if f == 1 and not lastl:
                P.dma('sp', xT_s[u, :, :, 0:W], xoT[:, :, 0:W], r=['xoT'], w=[f'xT_s{u}'])
        AF_.release(); AB_.release()
        P.barrier()


    def stageA2(l):
        AF_.mark(); AB_.mark()
        W1 = v3(balloc(8 * RWC), 8, RWC)
        W2 = v3(balloc(8 * RWC), 8, RWC)
        load_w(W1, w_in[l], 1536, 1536 + RWC, 'W1')
        mu_t = falloc(RWC)
        bcast_row(mu_t, rw_mu[l], 'mu_t')
        for kc in range(8):
            tt('dve', W2[:, kc, :], W1[:, kc, :], mu_t, ALU.mult, r=['W1', 'mu_t'], w=['W2'])
            tt('pool', W1[:, kc, :], W1[:, kc, :], W2[:, kc, :], ALU.subtract, r=['W1', 'W2'], w=['W1'])
        LW = balloc(768)
        memset('pool', LW, 0.0, w=['LW'])
        P.dma('pool', LW[0:32, 0:256], rw_w2[l], r=['LW'], w=['LW'])
        P.dma('pool', LW[32:64, 256:512], rw_a2[l], r=['LW'], w=['LW'])
        P.dma('pool', LW[64:128, 512:768], rw_g2[l], r=['LW'], w=['LW'])
        prm = {}
        for nm, src in (('w0', rw_w0), ('a0', rw_a0), ('kks', rw_kk), ('ka', rw_ka), ('rk', rw_rk), ('gng', rw_gn_g), ('gnb', rw_gn_b)):
            prm[nm] = falloc(256)
            bcast_row(prm[nm], src[l], 'rwp')
        omka = falloc(256)
        ts('dve', omka, prm['ka'], -1.0, 1.0, ALU.mult, ALU.add, r=['rwp'], w=['rwp2'])
        m1 = {}
        m3 = {}
        tri = {}
        for kd in ('p', 's'):
            m1[kd] = balloc(1024); m3[kd] = balloc(512); tri[kd] = falloc(128)
            P.dma('pool', m1[kd], cin['rwm1_' + kd][:, :], w=['rwmask'])
            P.dma('pool', m3[kd], cin['rwm3_' + kd][:, :], w=['rwmask'])
            P.dma('sp', tri[kd], cin['tri_' + kd][:, :], w=['rwmask'])
        bdm = falloc(128)
        P.dma('sp', bdm, cin['bdm'][:, :], w=['rwmask'])
        selp = falloc(128)
        P.dma('sp', selp, cin['ident'][:, :], w=['rwmask'])
        oh = falloc(NSS * 128)
        P.dma('sp', oh[0:NSS, :], cin['oh'][:, :], w=['rwmask'])
        ssm = falloc(RWC)
        P.dma('sp', ssm[0:NSS, :], sshift[l], w=['ssm'])
        tt('dve', ssm[0:NSS, :], ssm[0:NSS, :], mu_t[0:NSS, :], ALU.mult, r=['ssm', 'mu_t'], w=['ssm'])
        def make_rw(sfx, bk, pbi, g_mm=mm, g_tr=tr, g_act=act, g_tt=tt, g_ts=ts, g_stt=stt, g_cp=cp, g_memset=memset):
            LK = {'loraT', 'rks', 'vsb', 'vbf', 'xw', 'lw', 'aa', 'gg', 'kk', 't1', 'sm', 'kkn', 't2', 'kef', 'Dinc', 'Dinv', 'Dexc',
                  'TM', 'TT', 'M1', 'M2', 'Q0', 'Q1', 'QT0', 'QT1', 'PT0', 'PT1', 'DCt', 'RHSb', 'Ub', 'ysb', 'sm2', 'sm3', 'ycb', 'STf',
                  'STb', 'xTu0', 'xTu1', 'xTq', 'Xn', 'brC', 'psh'}
            PM = {'pf0': f'pf{bk[0]}', 'pf1': f'pf{bk[1]}', 'pf2': f'pf{bk[2]}', 'pf3': f'pf{bk[0]}', 'pf4': f'pf{bk[1]}', 'pf5': f'pf{bk[2]}',
                  'pb0': f'pb{pbi}', 'pb1': f'pb{pbi}'}

            def kx(keys):
                return [PM.get(k, k + sfx if k in LK else k) for k in keys]

            def mm(out, lhsT, rhs, start, stop, r, w):
                return g_mm(out, lhsT, rhs, start, stop, kx(r), kx(w))

            def tr(out, in_, idt, r, w):
                return g_tr(out, in_, idt, kx(r), kx(w))

            def act(out, in_, func, r, w, **kw):
                return g_act(out, in_, func, kx(r), kx(w), **kw)

            def tt(eng, out, in0, in1, op, r, w):
                return g_tt(eng, out, in0, in1, op, kx(r), kx(w))

            def ts(eng, out, in0, s1, s2, op0, op1, r, w):
                return g_ts(eng, out, in0, s1, s2, op0, op1, kx(r), kx(w))

            def stt(out, in0, sc, in1, op0, op1, r, w):
                return g_stt(out, in0, sc, in1, op0, op1, kx(r), kx(w))

            def cp(eng, out, in_, r, w):
                return g_cp(eng, out, in_, kx(r), kx(w))

            def memset(eng, ap, val, w):
                return g_memset(eng, ap, val, kx(w))

            class PL_:
                def op(self, eng, fn, r=(), w=(), dma=False):
                    return P.op(eng, fn, r=kx(r), w=kx(w), dma=dma)

                def dma(self, q, out, in_, r=(), w=(), **kw):
                    return P.dma(q, out, in_, r=kx(r), w=kx(w), **kw)
            PL = PL_()
            pf = [pf_g[bk[0]], pf_g[bk[1]], pf_g[bk[2]], pf_g[bk[0]], pf_g[bk[1]], pf_g[bk[2]]]
            pb = [pb_g[pbi], pb_g[pbi]]
            xTu = [v3(balloc(8 * 513), 8, 513) for _ in range(2)]
            xTq = v3(balloc(8 * 136), 8, 136)
            TM = v3(balloc(4 * 256), 4, 256)
            TT = v3(balloc(8 * 128), 8, 128)
            M1 = v3(balloc(4 * 256), 4, 256)
            M2 = v3(balloc(4 * 256), 4, 256)
            Qb = [v3(balloc(4 * 128), 4, 128) for _ in range(2)]
            QTb = [v3(balloc(4 * 128), 4, 128) for _ in range(2)]
            PTb = [v3(balloc(4 * 128), 4, 128) for _ in range(2)]
            RHSb = balloc(256)
            Ub = balloc(256)
            vbf = balloc(256)
            STb = v3(balloc(256), 2, 128)
            loraT = balloc(128)
            ycb = balloc(256)
            brC = v3(balloc(2 * 512), 2, 512)
            rks = falloc(512); vsb = falloc(256); xw = falloc(256); lw = falloc(256); aa = falloc(256); gg = falloc(256)
            kk = falloc(256); kkn = falloc(256); kef = falloc(256); t1 = falloc(256); t2 = falloc(256)
            Dinc = falloc(256); Dexc = falloc(256); Dinv = falloc(256); ysb = falloc(256)
            sm = falloc(64)
            STf = v3(falloc(256), 2, 128)
            Xn = v3(falloc(256), 2, 128)
            DCt = falloc(4)
            psh = falloc(RWC)
            memset('dve', RHSb, 0.0, w=['RHSb'])
            memset('dve', Ub, 0.0, w=['Ub'])
            memset('pool', xTq, 0.0, w=['xTq'])
            memset('pool', Xn, 0.0, w=['Xn'])

            def rw_tile(xt, xk, c0, kd, chunks, extra_s=None):
                cur = lambda kc: xt[:, kc, c0:c0 + 128]
                prv = lambda kc: xt[:, kc, c0 - 1:c0 + 127]
                bl = 0
                n_mm = 16 + (1 if extra_s is not None else 0)
                i = 0
                for kc in range(8):
                    for (Wt, Wk, src) in ((W1, 'W1', cur(kc)), (W2, 'W2', prv(kc))):
                        mm(pf[bl][:, 0:128], Wt[:, kc, 768:896], src, i == 0, i == n_mm - 1, r=[Wk, xk], w=[f'pf{bl}'])
                        i += 1
                if extra_s is not None:
                    mm(pf[bl][:, 0:128], ssm[0:NSS, 768:896], oh[0:NSS, extra_s * 128:(extra_s + 1) * 128], False, True, r=['ssm', 'rwmask'], w=[f'pf{bl}'])
                act(loraT[0:32, :], pf[bl][0:32, 0:128], AF.Tanh, r=[f'pf{bl}'], w=['loraT'])
                act(loraT[32:64, :], pf[bl][32:64, 0:128], AF.Copy, r=[f'pf{bl}'], w=['loraT'])
                act(loraT[64:128, :], pf[bl][64:128, 0:128], AF.Sigmoid, r=[f'pf{bl}'], w=['loraT'])
                for (bk, ca, cb) in ((1, 0, 512), (2, 512, 768)):
                    i = 0
                    for kc in range(8):
                        for (Wt, Wk, src) in ((W1, 'W1', cur(kc)), (W2, 'W2', prv(kc))):
                            mm(pf[bk][:, 0:cb - ca], src, Wt[:, kc, ca:cb], i == 0, i == n_mm - 1, r=[Wk, xk], w=[f'pf{bk}'])
                            i += 1
                    if extra_s is not None:
                        mm(pf[bk][:, 0:cb - ca], oh[0:NSS, extra_s * 128:(extra_s + 1) * 128], ssm[0:NSS, ca:cb], False, True,
                           r=['ssm', 'rwmask'], w=[f'pf{bk}'])
                cp('act', rks, pf[1][:, :], r=['pf1'], w=['rks'])
                cp('act', vsb, pf[2][:, 0:256], r=['pf2'], w=['vsb'])
                cp('pool', vbf, vsb, r=['vsb'], w=['vbf'])
                mm(pf[3][:, :], loraT, LW[:, 0:512], True, True, r=['loraT', 'LW'], w=['pf3'])
                mm(pf[4][:, 0:256], loraT, LW[:, 512:768], True, True, r=['loraT', 'LW'], w=['pf4'])
                tt('dve', xw, pf[3][:, 0:256], prm['w0'], ALU.add, r=['pf3', 'rwp'], w=['xw'])
                act(xw, xw, AF.Sigmoid, r=['xw'], w=['xw'])
                ts('dve', lw, xw, -0.6065306597126334, None, ALU.mult, None, r=['xw'], w=['lw'])
                tt('dve', aa, pf[3][:, 256:512], prm['a0'], ALU.add, r=['pf3', 'rwp'], w=['aa'])
                act(aa, aa, AF.Sigmoid, r=['aa'], w=['aa'])
                cp('act', gg, pf[4][:, 0:256], r=['pf4'], w=['gg'])
                rr = rks[:, 0:256]
                kx = rks[:, 256:512]
                tt('pool', kk, kx, prm['kks'], ALU.mult, r=['rks', 'rwp'], w=['kk'])
                tt('dve', t1, kk, kk, ALU.mult, r=['kk'], w=['t1'])
                PL.op('dve', lambda e: e.tensor_reduce(out=sm[:, 0:4], in_=v3(t1, 4, 64), axis=AX.X, op=ALU.add), r=['t1'], w=['sm'])
                act(sm[:, 0:4], sm[:, 0:4], AF.Sqrt, r=['sm'], w=['sm'])
                ts('dve', sm[:, 0:4], sm[:, 0:4], 1e-12, None, ALU.max, None, r=['sm'], w=['sm'])
                PL.op('dve', lambda e: e.reciprocal(out=sm[:, 4:8], in_=sm[:, 0:4]), r=['sm'], w=['sm'])
                tt('dve', v3(kkn, 4, 64), v3(kk, 4, 64), sm[:, 4:8].unsqueeze(2).to_broadcast([128, 4, 64]), ALU.mult, r=['kk', 'sm'], w=['kkn'])
                tt('pool', t2, aa, prm['ka'], ALU.mult, r=['aa', 'rwp'], w=['t2'])
                tt('pool', t2, t2, omka, ALU.add, r=['t2', 'rwp2'], w=['t2'])
                tt('pool', kef, kx, t2, ALU.mult, r=['rks', 't2'], w=['kef'])
                tt('dve', t1, rr, kef, ALU.mult, r=['rks', 'kef'], w=['t1'])
                tt('dve', t1, t1, prm['rk'], ALU.mult, r=['t1', 'rwp'], w=['t1'])
                PL.op('dve', lambda e: e.tensor_reduce(out=sm[:, 8:12], in_=v3(t1, 4, 64), axis=AX.X, op=ALU.add), r=['t1'], w=['sm'])
                mm(pf[0][:, 0:256], tri[kd], lw, True, True, r=['rwmask', 'lw'], w=['pf0'])
                act(Dinc, pf[0][:, 0:256], AF.Exp, r=['pf0'], w=['Dinc'])
                act(Dinv, pf[0][:, 0:256], AF.Exp, r=['pf0'], w=['Dinv'], scale=-1.0)
                tt('dve', Dexc, pf[0][:, 0:256], lw, ALU.subtract, r=['pf0', 'lw'], w=['Dexc'])
                act(Dexc, Dexc, AF.Exp, r=['Dexc'], w=['Dexc'])
                stt(TM[:, 0, :], kkn, -1.0, Dexc, ALU.mult, ALU.mult, r=['kkn', 'Dexc'], w=['TM'])
                tt('pool', TM[:, 1, :], rr, Dinc, ALU.mult, r=['rks', 'Dinc'], w=['TM'])
                tt('dve', t1, kkn, aa, ALU.mult, r=['kkn', 'aa', 'sm'], w=['t1'])
                tt('dve', TM[:, 2, :], t1, Dinv, ALU.mult, r=['t1', 'Dinv'], w=['TM'])
                tt('pool', TM[:, 3, :], kef, Dinv, ALU.mult, r=['kef', 'Dinv'], w=['TM'])
                ptt = v3(pb[0][:, :], 8, 128)
                for hp in range(2):
                    for arr in range(4):
                        tr(ptt[:, hp * 4 + arr, :], TM[:, arr, hp * 128:(hp + 1) * 128], ident_b, r=['TM', 'ident_b'], w=['pb0'])
                cp('act', TT, ptt, r=['pb0'], w=['TT'])
                mk1 = v3(m1[kd], 4, 256)
                mk3 = v3(m3[kd], 4, 128)
                for par in range(2):
                    po = par * 64
                    for hp in range(2):
                        co = hp * 256
                        ar = TT[po:po + 64, hp * 4 + 0:hp * 4 + 2, :]
                        mm(pf[0][:, co:co + 256].rearrange("p (a t) -> p a t", a=2, t=128), TT[po:po + 64, hp * 4 + 2, :], ar, True, True,
                           r=['TT'], w=['pf0'])
                        mm(pf[1][:, co:co + 256].rearrange("p (a t) -> p a t", a=2, t=128), TT[po:po + 64, hp * 4 + 3, :], ar, True, True,
                           r=['TT'], w=['pf1'])
                        mm(pf[2][:, hp * 128:(hp + 1) * 128], TT[po:po + 64, hp * 4 + 0, :], TT[po:po + 64, hp * 4 + 2, :], True, True,
                           r=['TT'], w=['pf2'])
                    tt('dve', M1[:, par:4:2, :], v3(pf[0][:, :], 2, 256), mk1[:, 0:2, :], ALU.mult, r=['pf0', 'rwmask'], w=['M1'])
                    tt('dve', M2[:, par:4:2, :], v3(pf[1][:, :], 2, 256), mk1[:, 0:2, :], ALU.mult, r=['pf1', 'rwmask'], w=['M2'])
                    tt('dve', Qb[0][:, par:4:2, :], v3(pf[2][:, 0:256], 2, 128), mk3[:, 0:2, :], ALU.mult, r=['pf2', 'rwmask'], w=['Q0'])
                cp('pool', QTb[0], M1[:, :, 0:128], r=['M1'], w=['QT0'])
                tt('pool', PTb[0], M1[:, :, 0:128], ident_b.unsqueeze(1).to_broadcast([128, 4, 128]), ALU.add, r=['M1', 'ident_b'], w=['PT0'])
                nstep = 5 if kd == 'p' else 2
                for st_ in range(nstep):
                    a, b = st_ % 2, (st_ + 1) % 2
                    lastst = st_ == nstep - 1
                    for h in range(4):
                        mm(pf[0][:, h * 128:(h + 1) * 128], QTb[a][:, h, :], Qb[a][:, h, :], True, True, r=[f'QT{a}', f'Q{a}'], w=['pf0'])
                    cp('act', Qb[b], v3(pf[0][:, :], 4, 128), r=['pf0'], w=[f'Q{b}'])
                    if not lastst:
                        for h in range(4):
                            mm(pf[1][:, h * 128:(h + 1) * 128], Qb[a][:, h, :], QTb[a][:, h, :], True, True, r=[f'QT{a}', f'Q{a}'], w=['pf1'])
                        cp('dve', QTb[b], v3(pf[1][:, :], 4, 128), r=['pf1'], w=[f'QT{b}'])
                    for h in range(4):
                        mm(pf[2][:, h * 128:(h + 1) * 128], Qb[b][:, h, :], PTb[a][:, h, :], True, True, r=[f'Q{b}', f'PT{a}'], w=['pf2'])
                    tt('dve', PTb[b], v3(pf[2][:, :], 4, 128), PTb[a], ALU.add, r=['pf2', f'PT{a}'], w=[f'PT{b}'])
                PT = PTb[nstep % 2]
                ptk = f'PT{nstep % 2}'
                for ci, (r0, R) in enumerate(chunks):
                    for hp in range(2):
                        mm(pf[3][:, hp * 2 + ci:hp * 2 + ci + 1], Dinc[:, hp * 128:(hp + 1) * 128], selp[:, r0 + R - 1:r0 + R], True, True,
                           r=['Dinc', 'rwmask'], w=['pf3'])
                cp('act', DCt, pf[3][:, 0:4], r=['pf3'], w=['DCt'])
                for ci, (r0, R) in enumerate(chunks):
                    rs = slice(r0, r0 + 64)
                    cs_ = slice(r0, r0 + 64)
                    for hp in range(2):
                        mm(pf[4][rs, hp * 128:(hp + 1) * 128], TT[:, hp * 4 + 0, cs_], STb[:, hp, :], True, False, r=['TT', 'STb'], w=['pf4'])
                        for h2 in range(2):
                            h = hp * 2 + h2
                            mm(pf[4][rs, h * 64:(h + 1) * 64], M2[:, h, cs_], vbf[:, h * 64:(h + 1) * 64], False, h2 == 1, r=['M2', 'vbf'], w=['pf4'])
                    cp('act', RHSb[rs, :], pf[4][rs, 0:256], r=['pf4'], w=['RHSb'])
                    for h in range(4):
                        mm(pf[5][rs, h * 64:(h + 1) * 64], PT[:, h, cs_], RHSb[:, h * 64:(h + 1) * 64], True, True, r=[ptk, 'RHSb'], w=['pf5'])
                    cp('dve', Ub[rs, :], pf[5][rs, 0:256], r=['pf5'], w=['Ub'])
                    for hp in range(2):
                        mm(pf[4][rs, hp * 128:(hp + 1) * 128], TT[:, hp * 4 + 1, cs_], STb[:, hp, :], True, False, r=['TT', 'STb'], w=['pf4'])
                        for h2 in range(2):
                            h = hp * 2 + h2
                            mm(pf[4][rs, h * 64:(h + 1) * 64], M1[:, h, 128 + r0:128 + r0 + 64], Ub[:, h * 64:(h + 1) * 64], False, False, r=['M1', 'Ub'], w=['pf4'])
                            mm(pf[4][rs, h * 64:(h + 1) * 64], M2[:, h, 128 + r0:128 + r0 + 64], vbf[:, h * 64:(h + 1) * 64], False, h2 == 1, r=['M2', 'vbf'], w=['pf4'])
                    cp('act', ysb[rs, :], pf[4][rs, 0:256], r=['pf4'], w=['ysb'])
                    rr_ = slice(r0, r0 + R)
                    for hp in range(2):
                        mm(pf[5][:, hp * 128:(hp + 1) * 128], TM[rr_, 2, hp * 128:(hp + 1) * 128], Ub[rr_, hp * 128:(hp + 1) * 128], True, False, r=['TM', 'Ub'], w=['pf5'])
                        mm(pf[5][:, hp * 128:(hp + 1) * 128], TM[rr_, 3, hp * 128:(hp + 1) * 128], vbf[rr_, hp * 128:(hp + 1) * 128], False, True, r=['TM', 'vbf'], w=['pf5'])
                    for hp in range(2):
                        dc = DCt[:, hp * 2 + ci:hp * 2 + ci + 1]
                        tt('dve', t1[:, 0:128], pf[5][:, hp * 128:(hp + 1) * 128], bdm, ALU.mult, r=['pf5', 'rwmask'], w=['t1'])
                        ts('dve', STf[:, hp, :], STf[:, hp, :], dc, None, ALU.mult, None, r=['STf', 'DCt'], w=['STf'])
                        stt(STf[:, hp, :], t1[:, 0:128], dc, STf[:, hp, :], ALU.mult, ALU.add, r=['t1', 'DCt', 'STf'], w=['STf'])
                    cp('pool', STb, STf, r=['STf'], w=['STb'])
                y3 = v3(ysb, 4, 64)
                for h in range(4):
                    PL.op('dve', lambda e, h=h: e.bn_stats(out=sm[:, 16 + 6 * h:22 + 6 * h], in_=ysb[:, h * 64:(h + 1) * 64]), r=['ysb'], w=['sm2'])
                    PL.op('dve', lambda e, h=h: e.bn_aggr(out=sm[:, 40 + 2 * h:42 + 2 * h], in_=sm[:, 16 + 6 * h:22 + 6 * h]), r=['sm2'], w=['sm2'])
                mvv = v3(sm[:, 40:48], 4, 2)
                act(sm[:, 48:52], mvv[:, :, 1], AF.Sqrt, r=['sm2'], w=['sm3'], bias=GN_EPS, scale=1.0)
                PL.op('dve', lambda e: e.reciprocal(out=sm[:, 48:52], in_=sm[:, 48:52]), r=['sm3'], w=['sm3'])
                tt('dve', y3, y3, mvv[:, :, 0].unsqueeze(2).to_broadcast([128, 4, 64]), ALU.subtract, r=['ysb', 'sm2'], w=['ysb'])
                tt('dve', y3, y3, sm[:, 48:52].unsqueeze(2).to_broadcast([128, 4, 64]), ALU.mult, r=['ysb', 'sm3'], w=['ysb'])
                tt('pool', ysb, ysb, prm['gng'], ALU.mult, r=['ysb', 'rwp'], w=['ysb'])
                tt('pool', ysb, ysb, prm['gnb'], ALU.add, r=['ysb', 'rwp'], w=['ysb'])
                tt('dve', v3(t2, 4, 64), v3(vsb, 4, 64), sm[:, 8:12].unsqueeze(2).to_broadcast([128, 4, 64]), ALU.mult, r=['vsb', 'sm', 't2'], w=['t2'])
                tt('dve', ysb, ysb, t2, ALU.add, r=['ysb', 't2'], w=['ysb'])
                tt('dve', ycb, ysb, gg, ALU.mult, r=['ysb', 'gg'], w=['ycb'])

            def yc_to_brT(dst, dstk, ncols):
                pty = v3(pb[1][:, 0:256], 2, 128)
                for c in range(2):
                    tr(pty[:, c, :], ycb[:, c * 128:(c + 1) * 128], ident_b, r=['ycb', 'ident_b'], w=['pb1'])
                cp('act', dst, pty[:, :, 0:ncols], r=['pb1'], w=[dstk])

            def shift_out(xt, xk, col, dst):
                for (bk, ca, cb) in ((1, 0, 512), (2, 512, RWC)):
                    i = 0
                    for kc in range(8):
                        for (Wt, Wk) in ((W1, 'W1'), (W2, 'W2')):
                            mm(pf[bk][0:1, 0:cb - ca], xt[:, kc, col:col + 1], Wt[:, kc, ca:cb], i == 0, i == 15, r=[Wk, xk], w=[f'pf{bk}'])
                            i += 1
                    cp('act', psh[0:1, ca:cb], pf[bk][0:1, 0:cb - ca], r=[f'pf{bk}'], w=['psh'])
                PL.dma('sp', dst, psh[0:1, :], r=['psh'], w=['o_shift'])

            def state_out(dst4):
                for hp in range(2):
                    tr(pf[3][:, hp * 128:(hp + 1) * 128], STf[:, hp, :], ident_f, r=['STf', 'ident_f'], w=['pf3'])
                cp('act', v3(t1, 2, 128), v3(pf[3][:, 0:256], 2, 128), r=['pf3'], w=['t1'])
                t1v = v3(t1, 2, 128)
                for hp in range(2):
                    for h2 in range(2):
                        PL.dma('sp', dst4[hp * 2 + h2], t1v[h2 * 64:(h2 + 1) * 64, hp, h2 * 64:(h2 + 1) * 64], r=['t1'], w=['o_wkv'])


            def prompt_stream():
                memset('dve', STf, 0.0, w=['STf'])
                memset('pool', STb, 0.0, w=['STb'])
                for u in range(NU):
                    ub = u % 2
                    xt = xTu[ub]
                    xk = f'xTu{ub}'
                    PL.dma('sp', xt[:, :, 1:513], xT_s[u, :, :, :], r=[f'xT_s{u}'], w=[xk])
                    if u == 0:
                        memset('pool', xt[:, :, 0:1], 0.0, w=[xk])
                    else:
                        cp('pool', xt[:, :, 0:1], xTu[1 - ub][:, :, 512:513], r=[f'xTu{1 - ub}'], w=[xk])
                    for ti in range(4):
                        rw_tile(xt, xk, 1 + ti * 128, 'p', [(0, 64), (64, 64)])
                        yc_to_brT(brC[:, :, ti * 128:(ti + 1) * 128], 'brC', 128)
                    PL.dma('sp', brT_s[u, :, 4:6, :], brC, r=['brC'], w=[f'brT_s{u}c'])
                    if u == NU - 1:
                        shift_out(xt, xk, 512, p_shift[l:l + 1, :])
                state_out(p_wkv[l])

            def sample_stream():
                xts = xTu[0]
                PL.dma('sp', xts[:, :, 0:128], xT_s[NU, :, :, 0:128], r=[f'xT_s{NU}'], w=['xTu0'])
                for si in range(NSS):
                    cp('pool', xTq[:, :, 1:1 + SL], xts[:, :, si * SL:(si + 1) * SL], r=['xTu0'], w=['xTq'])
                    for hp in range(2):
                        for h2 in range(2):
                            PL.dma('sp', Xn[h2 * 64:(h2 + 1) * 64, hp, h2 * 64:(h2 + 1) * 64], swkv[l, si, hp * 2 + h2], r=['Xn'], w=['Xn'])
                    for hp in range(2):
                        tr(pf[3][:, hp * 128:(hp + 1) * 128], Xn[:, hp, :], ident_f, r=['Xn', 'ident_f'], w=['pf3'])
                    cp('act', STf, v3(pf[3][:, 0:256], 2, 128), r=['pf3'], w=['STf'])
                    cp('pool', STb, STf, r=['STf'], w=['STb'])
                    rw_tile(xTq, 'xTq', 1, 's', [(0, SL)], extra_s=si)
                    yc_to_brT(brC[:, :, si * SL:(si + 1) * SL], 'brC', SL)
                    shift_out(xTq, 'xTq', SL, s_shift[l, si:si + 1, :])
                    state_out(s_wkv[l, si])
                PL.dma('sp', brT_s[NU, :, 4:6, 0:128], brC[:, :, 0:128], r=['brC'], w=[f'brT_s{NU}c'])

            return prompt_stream, sample_stream

        pf_g, pb_g = pf, pb
        pstream, _ = make_rw('P', (0, 1, 2), 0)
        _, sstream = make_rw('S', (3, 4, 5), 1)
        P.rec_start()
        pstream()
        lp_ = P.rec_stop()
        P.rec_start()
        sstream()
        ls_ = P.rec_stop()
        P.merge([lp_, ls_])
        AF_.release(); AB_.release()
        P.barrier()

    if '0' in cfg.stages:
        stage0()
    for l in range(cfg.nlayers):
        if 'A' in cfg.stages:
            stageA1(l)
        if 'R' in cfg.stages:
            stageA2(l)
        if 'B' in cfg.stages:
            stageB(l)
        if 'C' in cfg.stages:
            stageC(l)
        if 'D' in cfg.stages:
            stageD(l, 0)
            stageD(l, 1)

    with nc.allow_non_contiguous_dma(reason="small parameter / state transfers"):
        P.emit()
    return P


def make_in_maps(cfg, inputs, ncores=8):
    consts = host_constants(cfg)
    f = lambda a: np.ascontiguousarray(np.asarray(a))
    maps = []
    nb = inputs['x_prompt'].shape[0]
    for c in range(ncores):
        b = c % nb
        ss = slice(NSS * c, NSS * (c + 1))
        m = {
            'xp': f(inputs['x_prompt'][b]),
            'xs': f(inputs['x_sample'][ss]).reshape(128, D),
            'memp': f(inputs['mem_prompt'][b]),
            'cache_k0': f(inputs['cache_k'][0]).reshape(-1, 256),
            'cache_k1': f(inputs['cache_k'][1]).reshape(-1, 256),
            'cache_v0': f(inputs['cache_v'][0]).reshape(-1, 256),
            'cache_v1': f(inputs['cache_v'][1]).reshape(-1, 256),
            'ptab': f(inputs['page_table'][ss]).reshape(-1).astype(np.int32),
            'cmk': f(inputs['cache_mem_k'][:, ss]).reshape(DEPTH, NSS, NMEM, D),
            'cmv': f(inputs['cache_mem_v'][:, ss]).reshape(DEPTH, NSS, NMEM, D),
            'sconv': f(inputs['state_conv'][:, ss]),
            'swkv': f(inputs['state_wkv'][:, ss]),
            'sshift': f(inputs['state_shift'][:, ss]),
            'w_branch': f(inputs['w_branch']).reshape(DEPTH, 4 * MIXW, D),
            'rw_rk': f(inputs['rw_rk']).reshape(DEPTH, MIXW),
        }
        for k in ['w_in', 'sb_bias', 'w_gate', 'b_gate', 'w_o', 'conv_w', 'rw_mu', 'rw_w0', 'rw_w2', 'rw_a0',
                  'rw_a2', 'rw_g2', 'rw_kk', 'rw_ka', 'rw_gn_g', 'rw_gn_b', 'sgu_ln_g', 'sgu_ln_b', 'sgu_ws',
                  'sgu_b', 'w_mq', 'w_mk', 'w_mv', 'w_mo', 'w_up', 'w_down', 'ln1_g', 'ln1_b', 'ln2_g', 'ln2_b',
                  'ln3_g', 'ln3_b']:
            m[k] = f(inputs[k])
        for k, v in consts.items():
            m['c_' + k] = v
        maps.append(m)
    return maps


def run(cfg, inputs, ncores=8):
    nc = build_program(cfg)
    maps = make_in_maps(cfg, inputs, ncores)
    res = run_bass_kernel_spmd(nc, maps, core_ids=list(range(ncores)))
    return res.results


def assemble(cfg, R, nb=4, ncores=8):
    SEQ = cfg.SEQ
    g = lambda name, cores: np.stack([R[c][name] for c in cores])
    pc = list(range(nb))
    ac = list(range(ncores))
    y_p = g('y_p', pc)
    y_s = g('y_s', ac).reshape(ncores * NSS, SL, D)
    def pl(name, shp):
        a = g(name, pc)
        return np.ascontiguousarray(np.moveaxis(a, 0, 1)).reshape(shp)
    def sl_(name, shp):
        a = g(name, ac)
        return np.ascontiguousarray(np.moveaxis(a, 0, 1)).reshape(shp)
    NS = ncores * NSS
    return (y_p, y_s,
            pl('p_k', (DEPTH, nb, SEQ, 4, 64)), pl('p_v', (DEPTH, nb, SEQ, 4, 64)),
            pl('p_mk', (DEPTH, nb, NMEM, 4, 256)), pl('p_mv', (DEPTH, nb, NMEM, 4, 256)),
            pl('p_conv', (DEPTH, nb, 2, MIXW)), pl('p_wkv', (DEPTH, nb, 4, 64, 64)), pl('p_shift', (DEPTH, nb, RWC)),
            sl_('s_k', (DEPTH, NS, SL, 4, 64)), sl_('s_v', (DEPTH, NS, SL, 4, 64)),
            sl_('s_conv', (DEPTH, NS, 2, MIXW)), sl_('s_wkv', (DEPTH, NS, 4, 64, 64)),
            sl_('s_shift', (DEPTH, NS, RWC)), sl_('s_chunk', (DEPTH, NS, SL, MIXW)))


def kernel(**inputs):
    SEQ = inputs['x_prompt'].shape[1]
    NPAGES = inputs['page_table'].shape[1]
    NPHYS = inputs['cache_k'].shape[1]
    cfg = Cfg(SEQ=SEQ, NPAGES=NPAGES, NPHYS=NPHYS)
    R = run(cfg, inputs)
    outs = assemble(cfg, R, nb=inputs['x_prompt'].shape[0])
    return tuple(np.ascontiguousarray(o.astype(np.float32)) for o in outs)
```

```python
import contextlib
import os
import numpy as np
SKIP = os.environ.get('KSKIP', '')
import concourse.bass as bass
import concourse.mybir as mybir
from concourse.bass_utils import run_bass_kernel_spmd

F32 = mybir.dt.float32
BF16 = mybir.dt.bfloat16
I32 = mybir.dt.int32
AF = mybir.ActivationFunctionType
ALU = mybir.AluOpType
AX = mybir.AxisListType

D = 1024
DEPTH = 2
MIXW = 256
RWC = 896
INC = 2944
DFF = 4096
NMEM = 256
ALPHA = (2 * DEPTH) ** 0.25
LN_EPS = 1e-5
GN_EPS = 64e-5
NSS = 16
SL = 8


class Prog:
    def __init__(self, nc, es):
        self.nc = nc
        self.es = es
        self.eh = {'pe': nc.tensor, 'act': nc.scalar, 'dve': nc.vector, 'pool': nc.gpsimd, 'sp': nc.sync}
        self.ops = []
        self.last_w = {}
        self.readers = {}
        self.EPOCH = 12000
        self._rec = None
        self.barriers = []
        self.NSLOT = 12

    def rec_start(self):
        self._rec = []

    def rec_stop(self):
        r_ = self._rec
        self._rec = None
        return r_

    def merge(self, lists):
        n = [len(x) for x in lists]
        pos = [0] * len(lists)
        tot = max(n)
        for step in range(tot):
            for i, lst in enumerate(lists):
                tgt = (step + 1) * n[i] // tot
                while pos[i] < tgt:
                    e_, f_, r_, w_, d_ = lst[pos[i]]
                    self.op(e_, f_, r=r_, w=w_, dma=d_)
                    pos[i] += 1

    def op(self, eng, fn, r=(), w=(), dma=False):
        if self._rec is not None:
            self._rec.append((eng, fn, tuple(r), tuple(w), dma))
            return -1
        pr = [k for k in r if isinstance(k, str) and (k.startswith('pf') or k.startswith('pb'))]
        if pr:
            r = [k for k in r if k not in pr]
            w = list(w) + pr
        deps = set()
        for k in r:
            if k in self.last_w:
                deps.add(self.last_w[k])
        for k in w:
            if k in self.last_w:
                deps.add(self.last_w[k])
            deps.update(self.readers.get(k, ()))
        idx = len(self.ops)
        self.ops.append(dict(eng=eng, fn=fn, deps=deps, dma=dma, sig=False, bar=len(self.barriers)))
        for k in r:
            self.readers.setdefault(k, []).append(idx)
        for k in w:
            self.last_w[k] = idx
            self.readers[k] = []
        return idx

    def dma(self, q, out, in_, r=(), w=(), **kw):
        return self.op(q, lambda e: e.dma_start(out=out, in_=in_, **kw), r=r, w=w, dma=True)

    def barrier(self):
        lastc = {}
        ndma = {e: 0 for e in self.eh}
        for i, o in enumerate(self.ops):
            if o['dma']:
                ndma[o['eng']] += 1
            else:
                lastc[o['eng']] = i
        self.barriers.append((lastc, ndma))
        self.bar_at = getattr(self, 'bar_at', []) + [len(self.ops)]
        self.last_w = {}
        self.readers = {}

    def emit(self):
        nc = self.nc
        kstop = int(os.environ.get('KSTOP', '0'))
        print('PROG n_ops', len(self.ops), 'kstop', kstop, flush=True)
        if kstop:
            self.ops = self.ops[:kstop]
            nbar = sum(1 for a in getattr(self, 'bar_at', []) if a <= kstop)
            self.barriers = self.barriers[:nbar]
            o_ = self.ops[-1]
            print('LAST OP', o_['eng'], o_['dma'], o_['fn'].__code__.co_firstlineno if o_['fn'] else None, flush=True)
        self.barrier()
        ops = self.ops
        n = len(ops)
        for (lastc, ndma) in self.barriers:
            for e, i in lastc.items():
                ops[i]['sig'] = True
        for i, o in enumerate(ops):
            best = {}
            dd = []
            for d in o['deps']:
                od = ops[d]
                if od['dma']:
                    dd.append(d)
                else:
                    e = od['eng']
                    if e == 'pe' and o['eng'] == 'pe' and not o['dma']:
                        continue
                    if e not in best or best[e] < d:
                        best[e] = d
            o['pd'] = dd + list(best.values())
            for d in o['pd']:
                ops[d]['sig'] = True
        cnt = {e: 0 for e in self.eh}
        dcnt = {e: 0 for e in self.eh}
        nsig = {e: 0 for e in self.eh}
        for o in ops:
            if (not o['dma']) and o['sig']:
                nsig[o['eng']] += 1
        sems = {}
        for e in self.eh:
            ne = nsig[e] // self.EPOCH + 1
            sems[e] = [self.es.enter_context(nc.semaphore(f"s_{e}_{k}")) for k in range(ne)]
        dsems = {e: [self.es.enter_context(nc.semaphore(f"d_{e}_{k}")) for k in range(self.NSLOT)]
                 for e in ['sp', 'pool', 'act']}
        for o in ops:
            e = o['eng']
            if o['dma']:
                j = dcnt[e]
                dcnt[e] += 1
                o['tok'] = (dsems[e][j % self.NSLOT], 16 * (j // self.NSLOT + 1))
                o['prev'] = (dsems[e][j % self.NSLOT], 16 * (j // self.NSLOT)) if j >= self.NSLOT else None
            elif o['sig']:
                c = cnt[e]
                cnt[e] += 1
                o['tok'] = (sems[e][c // self.EPOCH], c % self.EPOCH + 1)
        waited = {e: {} for e in self.eh}
        nwait = [0]

        def wait(e, tok):
            s, v = tok
            k = id(s)
            if waited[e].get(k, 0) < v:
                self.eh[e].wait_ge(s, v)
                waited[e][k] = v
                nwait[0] += 1

        def bar_wait(e, b):
            lastc, ndma = self.barriers[b]
            for e2, i in lastc.items():
                if e2 != e:
                    wait(e, ops[i]['tok'])
            for q in ['sp', 'pool', 'act']:
                m = ndma[q]
                for sl in range(self.NSLOT):
                    if m <= sl:
                        continue
                    j = ((m - 1 - sl) // self.NSLOT) * self.NSLOT + sl
                    wait(e, (dsems[q][sl], 16 * (j // self.NSLOT + 1)))

        curbar = {e: 0 for e in self.eh}
        for o in ops:
            e = o['eng']
            while curbar[e] < o['bar']:
                bar_wait(e, curbar[e])
                curbar[e] += 1
            toks = {}
            for d in o['pd']:
                s_, v_ = ops[d]['tok']
                if id(s_) not in toks or toks[id(s_)][1] < v_:
                    toks[id(s_)] = (s_, v_)
            for tk in toks.values():
                wait(e, tk)
            if o['dma'] and o['prev'] is not None:
                wait(e, o['prev'])
            ins = o['fn'](self.eh[e])
            if o['dma']:
                ins.then_inc(o['tok'][0], 16)
            elif o['sig']:
                ins.then_inc(o['tok'][0], 1)
        bar_wait('sp', len(self.barriers) - 1)
        self.n_ops = n
        self.n_wait = nwait[0]


class Cfg:
    def __init__(self, SEQ=4096, NPAGES=16, NPHYS=2560, debug=False, nlayers=DEPTH, stages="0ARBCD"):
        self.SEQ = SEQ
        self.NPAGES = NPAGES
        self.NPHYS = NPHYS
        self.NU = SEQ // 512
        self.NT = SEQ // 128
        self.debug = debug
        self.nlayers = nlayers
        self.stages = stages


def host_constants(cfg):
    c = {}
    c['ident'] = np.eye(128, dtype=np.float32)
    j = np.arange(128)
    c['lx'] = (j[:, None] < j[None, :]).astype(np.float32)
    c['ones'] = np.ones((128, 128), np.float32)
    c['sbmask'] = (j[:, None] < j[None, :]).astype(np.float32)
    sj, tj = j // SL, j % SL
    c['sbmask_s'] = ((sj[:, None] == sj[None, :]) & (tj[:, None] < tj[None, :])).astype(np.float32)
    def rwm(ch):
        cj = j // ch
        same = cj[:, None] == cj[None, :]
        su = same & (j[:, None] < j[None, :])
        iu = same & (j[:, None] <= j[None, :])
        sl = same & (j[:, None] > j[None, :])
        m1 = np.concatenate([su, iu], 1).astype(np.float32)
        return (np.tile(m1[:, None, :], (1, 4, 1)).reshape(128, 1024),
                np.tile(sl.astype(np.float32)[:, None, :], (1, 4, 1)).reshape(128, 512),
                iu.astype(np.float32))
    c['rwm1_p'], c['rwm3_p'], c['tri_p'] = rwm(64)
    c['rwm1_s'], c['rwm3_s'], c['tri_s'] = rwm(SL)
    sel = np.zeros((128, 2), np.float32)
    sel[63, 0] = 1
    sel[127, 1] = 1
    c['sel_p'] = sel
    sels = np.zeros((128, NSS), np.float32)
    for s in range(NSS):
        sels[s * SL + SL - 1, s] = 1
    c['sel_s'] = sels
    sf = np.zeros((NSS, 128), np.float32)
    for s in range(NSS):
        sf[s, s * SL] = 1
    c['self_s'] = sf
    c['seqmask'] = (sj[:, None] == np.arange(NSS)[None, :]).astype(np.float32)
    c['piota'] = np.arange(128, dtype=np.float32).reshape(128, 1)
    c['sgumask_s'] = ((sj[:, None] == sj[None, :]) & (tj[:, None] <= tj[None, :])).astype(np.float32)
    c['tril'] = (j[:, None] <= j[None, :]).astype(np.float32)
    c['bdm'] = ((j[:, None] // 64) == (j[None, :] // 64)).astype(np.float32)
    ohm = np.zeros((NSS, NSS * 128), np.float32)
    for s_ in range(NSS):
        ohm[s_, s_ * 128] = 1
    c['oh'] = ohm
    c['selrep'] = (j[:, None] == (j[None, :] % SL)).astype(np.float32)
    return c


CONST_SHAPES = None


def build_program(cfg):
    nc = bass.Bass("TRN2", target_bir_lowering=False)
    es = contextlib.ExitStack()
    with es:
        _build(nc, es, cfg)
    return nc


def _build(nc, es, cfg):
    P = Prog(nc, es)
    SEQ, NU, NT, NPAGES, NPHYS = cfg.SEQ, cfg.NU, cfg.NT, cfg.NPAGES, cfg.NPHYS
    NUA = NU + 1
    NTA = NT + 1
    dbg = cfg.debug

    def din(name, shape, dt=F32):
        return nc.dram_tensor(name, list(shape), dt, kind="ExternalInput").ap()

    def dout(name, shape, dt=F32):
        return nc.dram_tensor(name, list(shape), dt, kind="ExternalOutput").ap()

    def dscr(name, shape, dt=F32):
        if dbg:
            return nc.dram_tensor(name, list(shape), dt, kind="ExternalOutput").ap()
        return nc.dram_tensor(name, list(shape), dt, kind="Internal").ap()

    xp = din("xp", [SEQ, D])
    xs = din("xs", [128, D])
    memp = din("memp", [NMEM, D])
    cache_k = [din(f"cache_k{i}", [NPHYS * 128, 256]) for i in range(DEPTH)]
    cache_v = [din(f"cache_v{i}", [NPHYS * 128, 256]) for i in range(DEPTH)]
    ptab = din("ptab", [NSS * NPAGES], I32)
    cmk = din("cmk", [DEPTH, NSS, NMEM, D])
    cmv = din("cmv", [DEPTH, NSS, NMEM, D])
    sconv = din("sconv", [DEPTH, NSS, 2, MIXW])
    swkv = din("swkv", [DEPTH, NSS, 4, 64, 64])
    sshift = din("sshift", [DEPTH, NSS, RWC])
    w_in = din("w_in", [DEPTH, D, INC])
    sb_bias = din("sb_bias", [DEPTH, 4])
    w_gate = din("w_gate", [DEPTH, D, 4 * D])
    b_gate = din("b_gate", [DEPTH, 4 * D])
    w_branch = din("w_branch", [DEPTH, 4 * MIXW, D])
    w_o = din("w_o", [DEPTH, D, D])
    conv_w = din("conv_w", [DEPTH, 3, MIXW])
    rw_mu = din("rw_mu", [DEPTH, RWC])
    rw_w0 = din("rw_w0", [DEPTH, MIXW])
    rw_w2 = din("rw_w2", [DEPTH, 32, MIXW])
    rw_a0 = din("rw_a0", [DEPTH, MIXW])
    rw_a2 = din("rw_a2", [DEPTH, 32, MIXW])
    rw_g2 = din("rw_g2", [DEPTH, 64, MIXW])
    rw_kk = din("rw_kk", [DEPTH, MIXW])
    rw_ka = din("rw_ka", [DEPTH, MIXW])
    rw_rk = din("rw_rk", [DEPTH, MIXW])
    rw_gn_g = din("rw_gn_g", [DEPTH, MIXW])
    rw_gn_b = din("rw_gn_b", [DEPTH, MIXW])
    sgu_ln_g = din("sgu_ln_g", [DEPTH, MIXW])
    sgu_ln_b = din("sgu_ln_b", [DEPTH, MIXW])
    sgu_ws = din("sgu_ws", [DEPTH, 4, 128, 128])
    sgu_b = din("sgu_b", [DEPTH, 4, 128])
    w_mq = din("w_mq", [DEPTH, D, D])
    w_mk = din("w_mk", [DEPTH, D, D])
    w_mv = din("w_mv", [DEPTH, D, D])
    w_mo = din("w_mo", [DEPTH, D, D])
    w_up = din("w_up", [DEPTH, D, DFF])
    w_down = din("w_down", [DEPTH, DFF, D])
    lng = [din(f"ln{i}_g", [DEPTH, D]) for i in (1, 2, 3)]
    lnb = [din(f"ln{i}_b", [DEPTH, D]) for i in (1, 2, 3)]
    consts = host_constants(cfg)
    cin = {k: din("c_" + k, v.shape) for k, v in consts.items()}

    y_p = dout("y_p", [SEQ, D])
    y_s = dout("y_s", [128, D])
    p_k = dout("p_k", [DEPTH, SEQ, 256])
    p_v = dout("p_v", [DEPTH, SEQ, 256])
    p_mk = dout("p_mk", [DEPTH, NMEM, D])
    p_mv = dout("p_mv", [DEPTH, NMEM, D])
    p_conv = dout("p_conv", [DEPTH, 2, MIXW])
    p_wkv = dout("p_wkv", [DEPTH, 4, 64, 64])
    p_shift = dout("p_shift", [DEPTH, RWC])
    s_k = dout("s_k", [DEPTH, 128, 256])
    s_v = dout("s_v", [DEPTH, 128, 256])
    s_conv = dout("s_conv", [DEPTH, NSS, 2, MIXW])
    s_wkv = dout("s_wkv", [DEPTH, NSS, 4, 64, 64])
    s_shift = dout("s_shift", [DEPTH, NSS, RWC])
    s_chunk = dout("s_chunk", [DEPTH, 128, 256])

    xT_s = dscr("xT_s", [NUA, 128, 8, 512], BF16)
    xres_s = dscr("xres_s", [NTA, 128, D])
    brT_s = dscr("brT_s", [NUA, 128, 8, 512], BF16)
    mixT_s = dscr("mixT_s", [NUA, 128, 8, 512], BF16)

    FA = 20480
    BA = 61440
    fa = es.enter_context(nc.sbuf_tensor("fa", [128, FA], F32))
    ba = es.enter_context(nc.sbuf_tensor("ba", [128, BA], BF16))
    ia = es.enter_context(nc.sbuf_tensor("ia", [128, 512], I32))
    pf = [es.enter_context(nc.psum_tensor(f"pf{i}", [128, 512], F32)) for i in range(6)]
    pb = [es.enter_context(nc.psum_tensor(f"pb{i}", [128, 1024], BF16)) for i in range(2)]

    class Arena:
        def __init__(self, t, size, nm):
            self.t, self.size, self.nm, self.off, self.marks = t, size, nm, 0, []

        def alloc(self, n):
            n2 = (n + 15) // 16 * 16
            assert self.off + n2 <= self.size, (self.nm, self.off, n2, self.size)
            a = self.t[:, self.off:self.off + n]
            self.off += n2
            return a

        def mark(self):
            self.marks.append(self.off)

        def release(self):
            self.off = self.marks.pop()

    AF_, AB_ = Arena(fa, FA, 'fa'), Arena(ba, BA, 'ba')

    def falloc(n):
        return AF_.alloc(n)

    def balloc(n):
        return AB_.alloc(n)

    units = [('p', u, 512, 4) for u in range(NU)] + [('s', NU, 128, 1)]

    def unit_tiles(u):
        kind, ui, W, nt = units[u]
        return list(range(4 * ui, 4 * ui + nt)) if kind == 'p' else [NT]

    ident_b = balloc(128)
    lx_b = balloc(128)
    ones_b = balloc(128)
    ident_f = falloc(128)
    P.dma('pool', ident_b, cin['ident'][:, :], w=['ident_b'])
    P.dma('pool', lx_b, cin['lx'][:, :], w=['lx_b'])
    P.dma('pool', ones_b, cin['ones'][:, :], w=['ones_b'])
    P.dma('sp', ident_f, cin['ident'][:, :], w=['ident_f'])
    AF_.mark()
    AB_.mark()

    def mm(out, lhsT, rhs, start, stop, r, w):
        return P.op('pe', lambda e: e.matmul(out, lhsT=lhsT, rhs=rhs, start=start, stop=stop), r=r, w=w)

    def tr(out, in_, idt, r, w):
        return P.op('pe', lambda e: e.transpose(out=out, in_=in_, identity=idt), r=r, w=w)

    def act(out, in_, func, r, w, **kw):
        return P.op('act', lambda e: e.activation(out=out, in_=in_, func=func, **kw), r=r, w=w)

    def tt(eng, out, in0, in1, op, r, w):
        return P.op(eng, lambda e: e.tensor_tensor(out=out, in0=in0, in1=in1, op=op), r=r, w=w)

    def ts(eng, out, in0, s1, s2, op0, op1, r, w):
        if op1 is None:
            return P.op(eng, lambda e: e.tensor_scalar(out=out, in0=in0, scalar1=s1, scalar2=None, op0=op0), r=r, w=w)
        return P.op(eng, lambda e: e.tensor_scalar(out=out, in0=in0, scalar1=s1, scalar2=s2, op0=op0, op1=op1), r=r, w=w)

    def stt(out, in0, sc, in1, op0, op1, r, w):
        return P.op('dve', lambda e: e.scalar_tensor_tensor(out=out, in0=in0, scalar=sc, in1=in1, op0=op0, op1=op1), r=r, w=w)

    def cp(eng, out, in_, r, w):
        if eng == 'act':
            return P.op('act', lambda e: e.activation(out=out, in_=in_, func=AF.Copy), r=r, w=w)
        return P.op(eng, lambda e: e.tensor_copy(out=out, in_=in_), r=r, w=w)

    def rsqrt_(out, in_, eps, r, w):
        act(out, in_, AF.Sqrt, r=r, w=w, bias=eps, scale=1.0)
        P.op('dve', lambda e: e.reciprocal(out=out, in_=out), r=w, w=w)

    def memset(eng, ap, val, w):
        return P.op(eng, lambda e: e.memset(ap, val), w=w)

    bankc = [0]

    def nb():
        b = bankc[0] % 4
        bankc[0] += 1
        return b

    def v3(ap, a, b):
        return ap.rearrange("p (a b) -> p a b", a=a, b=b)

    def layernorm(t, tk, g_t, b_t, out, outk, outb, outbk, scr, scrk):
        st = scr[:, 0:12]
        mv = scr[:, 12:14]
        rs = scr[:, 14:15]
        P.op('dve', lambda e: e.bn_stats(out=st[:, 0:6], in_=t[:, 0:512]), r=[tk], w=[scrk])
        P.op('dve', lambda e: e.bn_stats(out=st[:, 6:12], in_=t[:, 512:1024]), r=[tk], w=[scrk])
        P.op('dve', lambda e: e.bn_aggr(out=mv, in_=st), r=[scrk], w=[scrk])
        rsqrt_(rs, mv[:, 1:2], LN_EPS, [scrk], [scrk])
        ts('dve', t, t, mv[:, 0:1], rs, ALU.subtract, ALU.mult, r=[tk, scrk], w=[tk])
        tt('pool', t, t, g_t, ALU.mult, r=[tk, 'lnp'], w=[tk])
        tt('dve', out, t, b_t, ALU.add, r=[tk, 'lnp'], w=[outk])
        if outb is not None:
            cp('act', outb, out, r=[outk], w=[outbk])

    def transpose_tile(src_b, srck, dst, dstk, pbi):
        pt = v3(pb[pbi][:, :], 8, 128)
        for kc in range(8):
            tr(pt[:, kc, :], src_b[:, kc * 128:(kc + 1) * 128], ident_b, r=[srck, 'ident_b'], w=[f'pb{pbi}'])
        cp('act', dst, pt, r=[f'pb{pbi}'], w=[dstk])

    def stage0():
        AF_.mark(); AB_.mark()
        xin = [balloc(1024) for _ in range(2)]
        xu = [v3(balloc(8 * 512), 8, 512) for _ in range(2)]
        for u in range(NUA):
            kind, ui, W, nt = units[u]
            ub = u % 2
            for ti, t in enumerate(unit_tiles(u)):
                sl = (t) % 2
                src = xp[t * 128:(t + 1) * 128, :] if kind == 'p' else xs[:, :]
                P.dma('pool', xin[sl], src, w=[f'xin{sl}'])
                transpose_tile(xin[sl], f'xin{sl}', xu[ub][:, :, ti * 128:(ti + 1) * 128], f'xu{ub}', sl)
            P.dma('sp', xT_s[u, :, :, 0:W], xu[ub][:, :, 0:W], r=[f'xu{ub}'], w=[f'xT_s{u}'])
        AF_.release(); AB_.release()
        P.barrier()


    def load_w(dst3, src2d, c0, c1, key, d0=0):
        kc_n = src2d.shape[0] // 128
        for kc in range(kc_n):
            P.dma('pool', dst3[:, kc, d0:d0 + (c1 - c0)], src2d[kc * 128:(kc + 1) * 128, c0:c1], w=[key])

    def bcast_row(dst, src1d, key, q='sp'):
        P.dma(q, dst, src1d.partition_broadcast(128), w=[key])

    def stageA1(l):
        AF_.mark(); AB_.mark()
        Wc = v3(balloc(8 * 2048), 8, 2048)
        load_w(Wc, w_in[l], 0, 1536, 'Wc', 0)
        load_w(Wc, w_in[l], 2432, 2944, 'Wc', 1536)
        KTm = balloc(max(2 * SEQ, 8192))
        VCm = balloc(max(NT * 256, 4096))
        KT = v3(KTm[:, 0:2 * SEQ], 2, SEQ)
        VC = v3(VCm[:, 0:NT * 256], NT, 256)
        vsn = balloc(256)
        gst = v3(falloc(4096), NSS, 256)
        piota = falloc(1)
        P.dma('sp', piota, cin['piota'][:, :], w=['piota'])
        xTu = [v3(balloc(8 * 512), 8, 512) for _ in range(2)]
        qT = v3(balloc(2 * 512), 2, 512)
        kTs = v3(balloc(2 * 128), 2, 128)
        brTu = v3(balloc(8 * 512), 8, 512)
        spb = v3(balloc(4 * 512), 4, 512)
        wtb = v3(balloc(4 * 512), 4, 512)
        rsb = v3(balloc(4 * 512), 4, 512)
        sbm = balloc(128)
        sbm_s = balloc(128)
        tril_b = balloc(128)
        sgm_s = balloc(128)
        WsT = v3(balloc(4 * 128), 4, 128)
        WsT_s = v3(balloc(4 * 128), 4, 128)
        wsraw = v3(balloc(4 * 128), 4, 128)
        wsx = v3(balloc(4 * 128), 4, 128)
        selrep = balloc(128)
        vlnb = balloc(256)
        ones64 = ones_b[:, 0:64]
        e1 = v3(falloc(4 * 512), 4, 512)
        rsf = v3(falloc(4 * 512), 4, 512)
        fac = falloc(512)
        kvst = [falloc(512) for _ in range(2)]
        sbb = falloc(4)
        cw = v3(falloc(6), 2, 3)
        lg_t = falloc(256)
        lb_t = falloc(256)
        sgb_p = v3(falloc(256), 2, 128)
        sgb_s = v3(falloc(256), 2, 128)
        hbs = v3(falloc(2 * 512), 2, 512)
        zc = falloc(2 * 640)
        cva = v3(falloc(2 * 512), 2, 512)
        uT = v3(falloc(2 * 512), 2, 512)
        gtmp = falloc(512)
        svt = falloc(256)
        svn = falloc(256)
        lnscr = falloc(16)
        zhist = falloc(4)
        wsxf = v3(falloc(4 * 128), 4, 128)

        P.dma('pool', sbm, cin['sbmask'][:, :], w=['sbm'])
        P.dma('pool', sbm_s, cin['sbmask_s'][:, :], w=['sbm_s'])
        P.dma('pool', tril_b, cin['tril'][:, :], w=['tril_b'])
        P.dma('pool', sgm_s, cin['sgumask_s'][:, :], w=['sgm_s'])
        P.dma('pool', selrep, cin['selrep'][:, :], w=['selrep'])
        P.dma('sp', sbb, sb_bias[l].partition_broadcast(128), w=['sbb'])
        for c in range(2):
            P.dma('sp', cw[:, c, :], conv_w[l][:, c * 128:(c + 1) * 128].rearrange("j p -> p j"), w=['cw'])
        bcast_row(lg_t, sgu_ln_g[l], 'sgp')
        bcast_row(lb_t, sgu_ln_b[l], 'sgp')
        for g in range(4):
            po = (g % 2) * 64
            P.dma('sp', sgb_p[po:po + 64, g // 2, :], sgu_b[l, g].partition_broadcast(64), w=['sgb'])
            src = bass.AP(tensor=sgu_b.tensor, offset=(l * 4 + g) * 128, ap=[[0, 64], [0, NSS], [1, SL]])
            P.dma('sp', sgb_s[po:po + 64, g // 2, :].rearrange("p (a b) -> p a b", a=NSS, b=SL), src, w=['sgb'])
            P.dma('pool', wsraw[:, g, :], sgu_ws[l, g], w=['wsraw'])
        b0 = nb()
        pw = v3(pb[0][:, 0:512], 4, 128)
        for g in range(4):
            tr(pw[:, g, :], wsraw[:, g, :], ident_b, r=['wsraw', 'ident_b'], w=['pb0'])
        for g in range(4):
            tt('dve', WsT[:, g, :], pw[:, g, :], tril_b, ALU.mult, r=['pb0', 'tril_b'], w=['WsT'])
        for g in range(4):
            cp('dve', wsx[0:SL, g, :].rearrange("p (a b) -> p a b", a=NSS, b=SL),
               WsT[0:SL, g, 0:SL].unsqueeze(1).to_broadcast([SL, NSS, SL]), r=['WsT'], w=['wsx'])
        pw2 = v3(pf[b0][:, :], 4, 128)
        for g in range(4):
            mm(pw2[:, g, :], selrep[0:SL, :], wsx[0:SL, g, :], True, True, r=['selrep', 'wsx'], w=[f'pf{b0}'])
        for g in range(4):
            tt('dve', WsT_s[:, g, :], pw2[:, g, :], sgm_s, ALU.mult, r=[f'pf{b0}', 'sgm_s'], w=['WsT'])
        memset('pool', zhist, 0.0, w=['zhist'])

        def proj_fm(c0, W, xt, xk, bank):
            for kc in range(8):
                mm(pf[bank][:, 0:W], Wc[:, kc, c0:c0 + 128], xt[:, kc, 0:W], kc == 0, kc == 7,
                   r=['Wc', xk], w=[f'pf{bank}'])

        for u in range(NUA):
            kind, ui, W, ntl = units[u]
            ub = u % 2
            xt = xTu[ub]
            xk = f'xTu{ub}'
            P.dma('sp', xt[:, :, 0:W], xT_s[u, :, :, 0:W], r=[f'xT_s{u}'], w=[xk])
            tiles = unit_tiles(u)
            for c in range(2):
                b = nb()
                proj_fm(c * 128, W, xt, xk, b)
                act(qT[:, c, 0:W], pf[b][:, 0:W], AF.Copy, r=[f'pf{b}'], w=['qT'], scale=0.125)
                b = nb()
                proj_fm(256 + c * 128, W, xt, xk, b)
                if kind == 'p':
                    cp('dve', KT[:, c, ui * 512:ui * 512 + W], pf[b][:, 0:W], r=[f'pf{b}'], w=['KT'])
                else:
                    cp('dve', kTs[:, c, :], pf[b][:, 0:W], r=[f'pf{b}'], w=['kTs'])
            for ti, t in enumerate(tiles):
                b = nb()
                for kc in range(8):
                    mm(pf[b][:, :], xt[:, kc, ti * 128:(ti + 1) * 128], Wc[:, kc, 256:768], kc == 0, kc == 7,
                       r=['Wc', xk], w=[f'pf{b}'])
                sl = t % 2
                cp('act', kvst[sl], pf[b][:, :], r=[f'pf{b}'], w=[f'kvst{sl}'])
                if kind == 'p':
                    cp('dve', VC[:, t, :], pf[b][:, 256:512], r=[f'pf{b}'], w=['VC'])
                    P.dma('sp', p_k[l, t * 128:(t + 1) * 128, :], kvst[sl][:, 0:256], r=[f'kvst{sl}'], w=['o_pk'])
                    P.dma('sp', p_v[l, t * 128:(t + 1) * 128, :], kvst[sl][:, 256:512], r=[f'kvst{sl}'], w=['o_pv'])
                else:
                    cp('dve', vsn, pf[b][:, 256:512], r=[f'pf{b}'], w=['vsn'])
                    P.dma('sp', s_k[l, :, :], kvst[sl][:, 0:256], r=[f'kvst{sl}'], w=['o_sk'])
                    P.dma('sp', s_v[l, :, :], kvst[sl][:, 256:512], r=[f'kvst{sl}'], w=['o_sv'])
            if 'sb' in SKIP:
                pass
            elif kind == 'p':
                sb_prompt(l, ui, qT, KT, VC, brTu, spb, wtb, rsb, rsf, e1, fac, sbb, sbm, ones64)
            else:
                sb_sample(l, qT, kTs, vsn, brTu, spb, wtb, rsb, rsf, e1, fac, sbb, sbm_s, ones64, KTm, VCm, gst, piota)
            nseq, Lq = (1, 512) if kind == 'p' else (NSS, SL)
            zv = zc[:, 0:2 * nseq * (Lq + 2)].rearrange("p (c s t) -> p c s t", c=2, s=nseq, t=Lq + 2)
            for c in range(2):
                b1 = nb()
                proj_fm(768 + 256 + 256 + c * 128, W, xt, xk, b1)
                cp('act', hbs[:, c, 0:W], pf[b1][:, 0:W], r=[f'pf{b1}'], w=['hbs'])
                b2 = nb()
                proj_fm(768 + 256 + c * 128, W, xt, xk, b2)
                tt('dve', zv[:, c, :, 2:Lq + 2],
                   pf[b2][:, 0:W].rearrange("p (s t) -> p s t", s=nseq, t=Lq),
                   hbs[:, c, 0:W].rearrange("p (s t) -> p s t", s=nseq, t=Lq), ALU.mult,
                   r=[f'pf{b2}', 'hbs'], w=['zc'])
            if kind == 'p':
                cp('pool', zv[:, :, 0, 0:2], v3(zhist[:, 0:4], 2, 2), r=['zhist'], w=['zc'])
            else:
                for c in range(2):
                    for jj in range(2):
                        P.dma('sp', zv[:, c, :, jj], sconv[l][:, jj, c * 128:(c + 1) * 128].rearrange("s p -> p s"), w=['zc'])
            for c in range(2):
                cv = cva[:, c, 0:W].rearrange("p (s t) -> p s t", s=nseq, t=Lq)
                ts('dve', cv, zv[:, c, :, 0:Lq], cw[:, c, 0:1], None, ALU.mult, None, r=['zc', 'cw'], w=['cva'])
                stt(cv, zv[:, c, :, 1:Lq + 1], cw[:, c, 1:2], cv, ALU.mult, ALU.add, r=['zc', 'cw', 'cva'], w=['cva'])
                stt(cv, zv[:, c, :, 2:Lq + 2], cw[:, c, 2:3], cv, ALU.mult, ALU.add, r=['zc', 'cw', 'cva'], w=['cva'])
                b3 = nb()
                proj_fm(768 + c * 128, W, xt, xk, b3)
                tt('dve', brTu[:, 2 + c, 0:W], pf[b3][:, 0:W], cva[:, c, 0:W], ALU.mult, r=[f'pf{b3}', 'cva'], w=['brTu'])
            if kind == 'p':
                cp('pool', v3(zhist[:, 0:4], 2, 2), zv[:, :, 0, Lq:Lq + 2], r=['zc'], w=['zhist'])
                if ui == NU - 1:
                    for c in range(2):
                        P.dma('sp', p_conv[l][:, c * 128:(c + 1) * 128].rearrange("j p -> p j"), zhist[:, 2 * c:2 * c + 2], r=['zhist'], w=['o_pconv'])
            else:
                for c in range(2):
                    for jj in range(2):
                        P.dma('sp', s_conv[l][:, jj, c * 128:(c + 1) * 128].rearrange("s p -> p s"), zv[:, c, :, Lq + jj], r=['zc'], w=['o_sconv'])
            for c in range(2):
                b = nb()
                proj_fm(1536 + c * 128, W, xt, xk, b)
                gelu(uT[:, c, 0:W], pf[b][:, 0:W], f'pf{b}', 'uT', gtmp[:, 0:W], W)
            for ti, t in enumerate(tiles):
                b = nb()
                for kc in range(8):
                    mm(pf[b][:, 0:256], xt[:, kc, ti * 128:(ti + 1) * 128], Wc[:, kc, 1792:2048], kc == 0, kc == 7,
                       r=['Wc', xk], w=[f'pf{b}'])
                gelu(svt, pf[b][:, 0:256], f'pf{b}', 'svt', gtmp[:, 0:256], 256)
                P.op('dve', lambda e: e.bn_stats(out=lnscr[:, 0:6], in_=svt), r=['svt'], w=['lnscr'])
                P.op('dve', lambda e: e.bn_aggr(out=lnscr[:, 6:8], in_=lnscr[:, 0:6]), r=['lnscr'], w=['lnscr'])
                rsqrt_(lnscr[:, 8:9], lnscr[:, 7:8], LN_EPS, ['lnscr'], ['lnscr'])
                ts('dve', svt, svt, lnscr[:, 6:7], lnscr[:, 8:9], ALU.subtract, ALU.mult, r=['svt', 'lnscr'], w=['svt'])
                tt('pool', svt, svt, lg_t, ALU.mult, r=['svt', 'sgp'], w=['svt'])
                tt('dve', svn, svt, lb_t, ALU.add, r=['svt', 'sgp'], w=['svn'])
                cp('act', vlnb, svn, r=['svn'], w=['vlnb'])
                if kind == 's':
                    P.dma('sp', s_chunk[l, :, :], svn, r=['svn'], w=['o_schunk'])
                b = nb()
                wst = WsT if kind == 'p' else WsT_s
                sgb = sgb_p if kind == 'p' else sgb_s
                for g in range(4):
                    po = (g % 2) * 64
                    mm(pf[b][po:po + 64, (g // 2) * 128:(g // 2 + 1) * 128], vlnb[:, g * 64:(g + 1) * 64], wst[:, g, :],
                       True, True, r=['vlnb', 'WsT'], w=[f'pf{b}'])
                mx = v3(pf[b][:, 0:256], 2, 128)
                tt('dve', cva[:, :, 0:128], mx, sgb, ALU.add, r=[f'pf{b}', 'sgb'], w=['cva'])
                tt('pool', brTu[:, 6:8, ti * 128:(ti + 1) * 128], cva[:, :, 0:128], uT[:, :, ti * 128:(ti + 1) * 128], ALU.mult,
                   r=['cva', 'uT'], w=['brTu'])
            P.dma('sp', brT_s[u, :, 0:4, 0:W], brTu[:, 0:4, 0:W], r=['brTu'], w=[f'brT_s{u}a'])
            P.dma('sp', brT_s[u, :, 6:8, 0:W], brTu[:, 6:8, 0:W], r=['brTu'], w=[f'brT_s{u}b'])
            if 'R' not in cfg.stages:
                memset('pool', brTu[:, 4:6, 0:W], 0.0, w=['brTu'])
                P.dma('sp', brT_s[u, :, 4:6, 0:W], brTu[:, 4:6, 0:W], r=['brTu'], w=[f'brT_s{u}c'])
        AF_.release(); AB_.release()
        P.barrier()

    def gelu(out, in_ps, ink, outk, tmp, W):
        tk = 'gtmp'
        act(tmp, in_ps, AF.Square, r=[ink], w=[tk])
        ts('dve', tmp, tmp, 0.044715, 1.0, ALU.mult, ALU.add, r=[tk], w=[tk])
        tt('dve', tmp, tmp, in_ps, ALU.mult, r=[tk, ink], w=[tk])
        act(tmp, tmp, AF.Sigmoid, r=[tk], w=[tk], scale=1.5957691216057308)
        tt('dve', out, tmp, in_ps, ALU.mult, r=[tk, ink], w=[outk])

    def sb_block(Zb, qk_list, nq, c0, hsl, bias_ap, mask, maskcols, first, e1h, sph, wth, rsbh, rsfh, keys, acc_list):
        zk = f'pf{Zb}'
        for (lt, rh, a, n_) in qk_list:
            mm(pf[Zb][:, a:a + n_], lt, rh, True, True, r=keys, w=[zk])
        act(e1h[:, c0:nq], pf[Zb][:, c0:nq], AF.Exp, r=[zk, 'sbb'], w=['e1' + hsl], bias=bias_ap)
        act(sph[:, c0:nq], e1h[:, c0:nq], AF.Ln, r=['e1' + hsl], w=['sp' + hsl], bias=1.0)
        if mask is not None:
            a, n_ = maskcols
            tt('pool', sph[:, a:a + n_], sph[:, a:a + n_], mask, ALU.mult, r=['sp' + hsl, 'sbm', 'sbm_s'], w=['sp' + hsl])
        mm(pf[Zb][:, c0:nq], lx_b, sph[:, c0:nq], True, False, r=['lx_b', 'sp' + hsl], w=[zk])
        if not first:
            mm(pf[Zb][:, c0:nq], ones_b, rsbh[:, c0:nq], False, False, r=['ones_b', 'rsb' + hsl], w=[zk])
        for i, (lt, rh, a, n_) in enumerate(qk_list):
            mm(pf[Zb][:, a:a + n_], lt, rh, False, i == len(qk_list) - 1, r=keys, w=[zk])
        act(wth[:, c0:nq], pf[Zb][:, c0:nq], AF.Exp, r=[zk, 'sbb'], w=['wt' + hsl], bias=bias_ap)
        if mask is not None:
            a, n_ = maskcols
            tt('pool', wth[:, a:a + n_], wth[:, a:a + n_], mask, ALU.mult, r=['wt' + hsl, 'sbm', 'sbm_s'], w=['wt' + hsl])
        for (o_ap, lv, a, n_, st_, sp_, ks, ok) in acc_list:
            mm(o_ap, lv, wth[:, a:a + n_], st_, sp_, r=['wt' + hsl] + ks, w=[ok])
        if first:
            cp('dve', rsfh[:, c0:nq], sph[:, c0:nq], r=['sp' + hsl], w=['rsf' + hsl])
        else:
            tt('dve', rsfh[:, c0:nq], rsfh[:, c0:nq], sph[:, c0:nq], ALU.add, r=['sp' + hsl, 'rsf' + hsl], w=['rsf' + hsl])
        cp('pool', rsbh[:, c0:nq], rsfh[:, c0:nq], r=['rsf' + hsl], w=['rsb' + hsl])

    def sb_finish(h, nq, ob, rsbh, hsl, fac, brTu, ones64, tb):
        po = (h % 2) * 64
        mm(pf[tb][po:po + 64, 0:nq], ones64, rsbh[:, 0:nq], True, True, r=['ones_b', 'rsb' + hsl], w=[f'pf{tb}'])
        act(fac[po:po + 64, 0:nq], pf[tb][po:po + 64, 0:nq], AF.Exp, r=[f'pf{tb}'], w=['fac'], scale=-1.0)
        tt('dve', brTu[po:po + 64, h // 2, 0:nq], pf[ob][po:po + 64, 0:nq], fac[po:po + 64, 0:nq], ALU.mult,
           r=[f'pf{ob}', 'fac'], w=['brTu'])

    def sb_prompt(l, ui, qT, KT, VC, brTu, spb, wtb, rsb, rsf, e1, fac, sbb, sbm, ones64):
        nkb = 4 * ui + 4
        for hp in range(2):
            def geo(kb):
                d = kb - 4 * ui
                return d, (128 * d if d > 0 else 0)

            def phaseA(kb):
                d, c0 = geo(kb)
                for h2 in range(2):
                    po = h2 * 64
                    Zb = h2 * 2 + (kb % 2)
                    mm(pf[Zb][:, c0:512], KT[po:po + 64, hp, kb * 128:(kb + 1) * 128], qT[po:po + 64, hp, c0:512], True, True,
                       r=['KT', 'qT'], w=[f'pf{Zb}'])
                for h2 in range(2):
                    sl = h2 * 2 + (kb % 2)
                    h = hp * 2 + h2
                    act(e1[:, sl, c0:512], pf[sl][:, c0:512], AF.Exp, r=[f'pf{sl}', 'sbb'], w=[f'e1_{sl}'], bias=sbb[:, h:h + 1])
                for h2 in range(2):
                    sl = h2 * 2 + (kb % 2)
                    act(spb[:, sl, c0:512], e1[:, sl, c0:512], AF.Ln, r=[f'e1_{sl}'], w=[f'sp_{sl}'], bias=1.0)
                if d >= 0:
                    for h2 in range(2):
                        sl = h2 * 2 + (kb % 2)
                        tt('pool', spb[:, sl, c0:c0 + 128], spb[:, sl, c0:c0 + 128], sbm, ALU.mult, r=[f'sp_{sl}', 'sbm'], w=[f'sp_{sl}'])

            def phaseB(kb):
                d, c0 = geo(kb)
                first = kb == 0
                for h2 in range(2):
                    po = h2 * 64
                    sl = h2 * 2 + (kb % 2)
                    zk = f'pf{sl}'
                    mm(pf[sl][:, c0:512], lx_b, spb[:, sl, c0:512], True, False, r=['lx_b', f'sp_{sl}'], w=[zk])
                    if not first:
                        mm(pf[sl][:, c0:512], ones_b, rsb[:, h2, c0:512], False, False, r=['ones_b', f'rsb{h2}'], w=[zk])
                    mm(pf[sl][:, c0:512], KT[po:po + 64, hp, kb * 128:(kb + 1) * 128], qT[po:po + 64, hp, c0:512], False, True,
                       r=['KT', 'qT'], w=[zk])
                for h2 in range(2):
                    sl = h2 * 2 + (kb % 2)
                    h = hp * 2 + h2
                    act(wtb[:, h2, c0:512], pf[sl][:, c0:512], AF.Exp, r=[f'pf{sl}', 'sbb'], w=[f'wt{h2}'], bias=sbb[:, h:h + 1])
                if d >= 0:
                    for h2 in range(2):
                        tt('pool', wtb[:, h2, c0:c0 + 128], wtb[:, h2, c0:c0 + 128], sbm, ALU.mult, r=[f'wt{h2}', 'sbm'], w=[f'wt{h2}'])
                for h2 in range(2):
                    po = h2 * 64
                    h = hp * 2 + h2
                    ob = 4 + h2
                    mm(pf[ob][po:po + 64, c0:512], VC[:, kb, h * 64:(h + 1) * 64], wtb[:, h2, c0:512], kb == 0, kb == nkb - 1,
                       r=['VC', f'wt{h2}'], w=[f'pf{ob}'])
                for h2 in range(2):
                    sl = h2 * 2 + (kb % 2)
                    if first:
                        cp('dve', rsf[:, h2, c0:512], spb[:, sl, c0:512], r=[f'sp_{sl}'], w=[f'rsf{h2}'])
                    else:
                        tt('dve', rsf[:, h2, c0:512], rsf[:, h2, c0:512], spb[:, sl, c0:512], ALU.add, r=[f'sp_{sl}', f'rsf{h2}'], w=[f'rsf{h2}'])
                for h2 in range(2):
                    cp('dve', rsb[:, h2, 0:512], rsf[:, h2, 0:512], r=[f'rsf{h2}'], w=[f'rsb{h2}'])

            phaseA(0)
            for kb in range(nkb):
                if kb + 1 < nkb:
                    phaseA(kb + 1)
                phaseB(kb)
            for h2 in range(2):
                sb_finish(hp * 2 + h2, 512, 4 + h2, rsb[:, h2, :], str(h2), fac, brTu, ones64, h2)

    def sb_sample(l, qT, kTs, vsn, brTu, spb, wtb, rsb, rsf, e1, fac, sbb, sbm_s, ones64, KTm, VCm, gst, piota):
        NP = NPAGES
        ptb = ia[:, 0:NSS * NP]
        idx = ia[:, 256:256 + NSS * NP]
        P.dma('sp', ptb, ptab.partition_broadcast(128), w=['ptb'])
        ts('dve', idx, ptb, 128.0, piota[:, 0:1], ALU.mult, ALU.add, r=['ptb', 'piota'], w=['idx'])
        kbf = v3(KTm[:, 0:4096], NSS, 256)
        ktb = KTm[:, 4096:8192].rearrange("p (c s k) -> p c s k", c=2, s=NSS, k=128)
        vbf = v3(VCm[:, 0:4096], NSS, 256)
        maskb = sbm_s.unsqueeze(1).to_broadcast([128, 2, 128])

        for ob in (4, 5):
            memset('dve', pf[ob][:, 0:256], 0.0, w=[f'pf{ob}'])

        def hv(t, par):
            return t[:, par:4:2, 0:128]
        for kb in range(NP + 1):
            last = kb == NP
            first = kb == 0
            if not last:
                for si in range(NSS):
                    col = si * NP + kb
                    P.op('pool', (lambda e, si=si, col=col: e.indirect_dma_start(
                        out=gst[:, si, :], out_offset=None, in_=cache_k[l][:, :],
                        in_offset=bass.IndirectOffsetOnAxis(ap=idx[:, col:col + 1], axis=0))),
                        r=['idx'], w=['gst'], dma=True)
                cp('dve', kbf, gst, r=['gst'], w=['kbf'])
                for si in range(NSS):
                    col = si * NP + kb
                    P.op('pool', (lambda e, si=si, col=col: e.indirect_dma_start(
                        out=gst[:, si, :], out_offset=None, in_=cache_v[l][:, :],
                        in_offset=bass.IndirectOffsetOnAxis(ap=idx[:, col:col + 1], axis=0))),
                        r=['idx'], w=['gst'], dma=True)
                cp('dve', vbf, gst, r=['gst'], w=['vbf'])
                for g in range(4):
                    c, sh = g // 2, g % 2
                    pt = v3(pb[g % 2][:, :], 8, 128)
                    for s8 in range(8):
                        si = sh * 8 + s8
                        tr(pt[:, s8, :], kbf[:, si, c * 128:(c + 1) * 128], ident_b, r=['kbf', 'ident_b'], w=[f'pb{g % 2}'])
                    cp('act', ktb[:, c, sh * 8:sh * 8 + 8, :], pt, r=[f'pb{g % 2}'], w=['ktb'])

            def zb(h):
                return (kb % 2) * 2 + (h % 2)

            def zcol(h):
                return (h // 2) * 128

            def qk(h, startf, stop_last):
                hp, po = h // 2, (h % 2) * 64
                Zb, zc0, zk = zb(h), zcol(h), f'pf{zb(h)}'
                if last:
                    mm(pf[Zb][:, zc0:zc0 + 128], kTs[po:po + 64, hp, :], qT[po:po + 64, hp, 0:128], startf, stop_last,
                       r=['kTs', 'qT'], w=[zk])
                else:
                    for si in range(NSS):
                        mm(pf[Zb][:, zc0 + si * SL:zc0 + (si + 1) * SL], ktb[po:po + 64, hp, si, :],
                           qT[po:po + 64, hp, si * SL:(si + 1) * SL], startf, (stop_last and si == NSS - 1) or startf,
                           r=['ktb', 'qT'], w=[zk])
            for h in range(4):
                qk(h, True, True)
            for h in range(4):
                Zb, zc0 = zb(h), zcol(h)
                act(e1[:, h, 0:128], pf[Zb][:, zc0:zc0 + 128], AF.Exp, r=[f'pf{Zb}', 'sbb'], w=['e1s'], bias=sbb[:, h:h + 1])
            act(spb[:, :, 0:128], e1[:, :, 0:128], AF.Ln, r=['e1s'], w=['sps'], bias=1.0)
            if last:
                for par in range(2):
                    tt('pool', hv(spb, par), hv(spb, par), maskb, ALU.mult, r=['sps', 'sbm_s'], w=['sps'])
            for h in range(4):
                Zb, zc0 = zb(h), zcol(h)
                mm(pf[Zb][:, zc0:zc0 + 128], lx_b, spb[:, h, 0:128], True, False, r=['lx_b', 'sps'], w=[f'pf{Zb}'])
                if not first:
                    mm(pf[Zb][:, zc0:zc0 + 128], ones_b, rsb[:, h, 0:128], False, False, r=['ones_b', 'rsbs'], w=[f'pf{Zb}'])
                qk(h, False, True)
            for h in range(4):
                Zb, zc0 = zb(h), zcol(h)
                act(wtb[:, h, 0:128], pf[Zb][:, zc0:zc0 + 128], AF.Exp, r=[f'pf{Zb}', 'sbb'], w=['wts'], bias=sbb[:, h:h + 1])
            if last:
                for par in range(2):
                    tt('pool', hv(wtb, par), hv(wtb, par), maskb, ALU.mult, r=['wts', 'sbm_s'], w=['wts'])
            for h in range(4):
                po = (h % 2) * 64
                ob = 4 + (h % 2)
                oc0 = (h // 2) * 128
                if last:
                    P.op('pe', (lambda e, ob=ob, po=po, oc0=oc0, h=h: e.matmul(
                        pf[ob][po:po + 64, oc0:oc0 + 128], lhsT=vsn[:, h * 64:(h + 1) * 64], rhs=wtb[:, h, 0:128],
                        start=False, stop=(h >= 2), skip_group_check=True)), r=['vsn', 'wts'], w=[f'pf{ob}'])
                else:
                    for si in range(NSS):
                        P.op('pe', (lambda e, ob=ob, po=po, oc0=oc0, h=h, si=si: e.matmul(
                            pf[ob][po:po + 64, oc0 + si * SL:oc0 + (si + 1) * SL], lhsT=vbf[:, si, h * 64:(h + 1) * 64],
                            rhs=wtb[:, h, si * SL:(si + 1) * SL],
                            start=False, stop=False, skip_group_check=True)),
                            r=['vbf', 'wts'], w=[f'pf{ob}'])
            if first:
                cp('dve', rsf[:, :, 0:128], spb[:, :, 0:128], r=['sps'], w=['rsfs'])
            else:
                tt('dve', rsf[:, :, 0:128], rsf[:, :, 0:128], spb[:, :, 0:128], ALU.add, r=['sps', 'rsfs'], w=['rsfs'])
            cp('dve', rsb[:, :, 0:128], rsf[:, :, 0:128], r=['rsfs'], w=['rsbs'])
        for h in range(4):
            po = (h % 2) * 64
            tb = h % 2
            ob = 4 + (h % 2)
            oc0 = (h // 2) * 128
            mm(pf[tb][po:po + 64, 0:128], ones64, rsb[:, h, 0:128], True, True, r=['ones_b', 'rsbs'], w=[f'pf{tb}'])
            act(fac[po:po + 64, 0:128], pf[tb][po:po + 64, 0:128], AF.Exp, r=[f'pf{tb}'], w=['fac'], scale=-1.0)
            tt('dve', brTu[po:po + 64, h // 2, 0:128], pf[ob][po:po + 64, oc0:oc0 + 128], fac[po:po + 64, 0:128], ALU.mult,
               r=[f'pf{ob}', 'fac'], w=['brTu'])


    ffp_s = dscr("ffp_s", [NTA, 128, D])

    def xres_src(l, t):
        if l == 0:
            return xp[t * 128:(t + 1) * 128, :] if t < NT else xs[:, :]
        return xres_s[t]

    def stageB(l):
        AF_.mark(); AB_.mark()
        Wg = v3(balloc(8 * 4096), 8, 4096)
        Wb = v3(balloc(8 * 1024), 8, 1024)
        load_w(Wg, w_gate[l], 0, 4096, 'Wg')
        load_w(Wb, w_branch[l], 0, 1024, 'Wb')
        bg = falloc(32)
        P.dma('sp', bg, b_gate[l].rearrange("(j p) -> p j", p=128), w=['bg'])
        xTu = [v3(balloc(8 * 512), 8, 512) for _ in range(2)]
        brTu = [v3(balloc(8 * 512), 8, 512)] * 2
        mixTu = [v3(balloc(8 * 512), 8, 512)] * 2
        sg = [falloc(512) for _ in range(2)]
        acc = [falloc(512) for _ in range(2)]
        tmp = [falloc(512) for _ in range(2)]
        n = 0
        for u in range(NUA):
            kind, ui, W, ntl = units[u]
            ub = u % 2
            P.dma('sp', xTu[ub][:, :, 0:W], xT_s[u, :, :, 0:W], r=[f'xT_s{u}'], w=[f'xTu{ub}'])
            P.dma('sp', brTu[ub][:, :, 0:W], brT_s[u, :, :, 0:W], r=[f'brT_s{u}a', f'brT_s{u}b', f'brT_s{u}c'], w=['brTuB'])
            for c in range(8):
                ab = c % 2
                for i in range(4):
                    sb_ = n % 2
                    n += 1
                    bgk = nb()
                    for kc in range(8):
                        mm(pf[bgk][:, 0:W], Wg[:, kc, i * 1024 + c * 128:i * 1024 + (c + 1) * 128], xTu[ub][:, kc, 0:W],
                           kc == 0, kc == 7, r=['Wg', f'xTu{ub}'], w=[f'pf{bgk}'])
                    act(sg[sb_][:, 0:W], pf[bgk][:, 0:W], AF.Sigmoid, r=[f'pf{bgk}', 'bg'], w=[f'sg{sb_}'],
                        bias=bg[:, i * 8 + c:i * 8 + c + 1])
                    bpk = nb()
                    for k2 in range(2):
                        mm(pf[bpk][:, 0:W], Wb[:, 2 * i + k2, c * 128:(c + 1) * 128], brTu[ub][:, 2 * i + k2, 0:W],
                           k2 == 0, k2 == 1, r=['Wb', 'brTuB'], w=[f'pf{bpk}'])
                    if i == 0:
                        tt('dve', acc[ab][:, 0:W], pf[bpk][:, 0:W], sg[sb_][:, 0:W], ALU.mult, r=[f'pf{bpk}', f'sg{sb_}'], w=[f'acc{ab}'])
                    else:
                        tt('dve', tmp[sb_][:, 0:W], pf[bpk][:, 0:W], sg[sb_][:, 0:W], ALU.mult, r=[f'pf{bpk}', f'sg{sb_}'], w=[f'tmp{sb_}'])
                        dst = acc[ab][:, 0:W] if i < 3 else mixTu[ub][:, c, 0:W]
                        dk = f'acc{ab}' if i < 3 else 'mixTuB'
                        tt('pool', dst, acc[ab][:, 0:W], tmp[sb_][:, 0:W], ALU.add, r=[f'acc{ab}', f'tmp{sb_}'], w=[dk])
            P.dma('sp', mixT_s[u, :, :, 0:W], mixTu[ub][:, :, 0:W], r=['mixTuB'], w=[f'mixT_s{u}'])
        AF_.release(); AB_.release()
        P.barrier()

    def load_ln(l, i, gt, bt):
        P.dma('sp', gt, lng[i][l].partition_broadcast(128), w=['lnp'])
        P.dma('sp', bt, lnb[i][l].partition_broadcast(128), w=['lnp'])

    def proj_tm_ln(lhs3, lhsk, Wt, Wk, ncol_kc, ti, xr, xrk, tbuf, tk):
        for hh in range(2):
            b = nb()
            for kc in range(ncol_kc):
                mm(pf[b][:, :], lhs3[:, kc, ti * 128:(ti + 1) * 128], Wt[:, kc, hh * 512:(hh + 1) * 512], kc == 0, kc == ncol_kc - 1,
                   r=[lhsk, Wk], w=[f'pf{b}'])
            stt(tbuf[:, hh * 512:(hh + 1) * 512], xr[:, hh * 512:(hh + 1) * 512], ALPHA, pf[b][:, :], ALU.mult, ALU.add,
                r=[xrk, f'pf{b}'], w=[tk])

    def stageC(l):
        AF_.mark(); AB_.mark()
        Wo = v3(balloc(8 * 1024), 8, 1024)
        Wq = v3(balloc(8 * 1024), 8, 1024)
        Wmo = v3(balloc(8 * 1024), 8, 1024)
        load_w(Wo, w_o[l], 0, 1024, 'Wo')
        load_w(Wq, w_mq[l], 0, 1024, 'Wq')
        load_w(Wmo, w_mo[l], 0, 1024, 'Wmo')
        g1 = falloc(1024); b1 = falloc(1024); g2 = falloc(1024); b2 = falloc(1024)
        load_ln(l, 0, g1, b1)
        load_ln(l, 1, g2, b2)
        mkT = v3(balloc(8 * 256), 8, 256)
        mvb = v3(balloc(2 * 1024), 2, 1024)
        stg = [falloc(512) for _ in range(2)]
        AB_.mark()
        memT = v3(balloc(8 * 256), 8, 256)
        Wmk = v3(balloc(8 * 1024), 8, 1024)
        Wmv = v3(balloc(8 * 1024), 8, 1024)
        load_w(Wmk, w_mk[l], 0, 1024, 'Wmk')
        load_w(Wmv, w_mv[l], 0, 1024, 'Wmv')
        mtl = [balloc(1024) for _ in range(2)]
        for mt in range(2):
            P.dma('pool', mtl[mt], memp[mt * 128:(mt + 1) * 128, :], w=[f'mtl{mt}'])
            transpose_tile(mtl[mt], f'mtl{mt}', memT[:, :, mt * 128:(mt + 1) * 128], 'memT', mt)
        n = 0
        for (Wt, Wk, outd, isv) in ((Wmk, 'Wmk', p_mk, False), (Wmv, 'Wmv', p_mv, True)):
            for mt in range(2):
                for hh in range(2):
                    b = nb()
                    for kc in range(8):
                        mm(pf[b][:, :], memT[:, kc, mt * 128:(mt + 1) * 128], Wt[:, kc, hh * 512:(hh + 1) * 512], kc == 0, kc == 7,
                           r=['memT', Wk], w=[f'pf{b}'])
                    sl = n % 2
                    n += 1
                    cp('act', stg[sl], pf[b][:, :], r=[f'pf{b}'], w=[f'stg{sl}'])
                    if isv:
                        cp('dve', mvb[:, mt, hh * 512:(hh + 1) * 512], pf[b][:, :], r=[f'pf{b}'], w=['mvb'])
                    P.dma('sp', outd[l, mt * 128:(mt + 1) * 128, hh * 512:(hh + 1) * 512], stg[sl], r=[f'stg{sl}'], w=['o_pm'])
        for c in range(8):
            b = nb()
            for kc in range(8):
                mm(pf[b][:, 0:256], Wmk[:, kc, c * 128:(c + 1) * 128], memT[:, kc, 0:256], kc == 0, kc == 7, r=['memT', 'Wmk'], w=[f'pf{b}'])
            cp('dve', mkT[:, c, :], pf[b][:, 0:256], r=[f'pf{b}'], w=['mkT'])
        P.barrier()
        AB_.release()
        mixTu = [v3(balloc(8 * 512), 8, 512)] * 2
        x1Tu = v3(balloc(8 * 512), 8, 512)
        qmT = v3(balloc(8 * 512), 8, 512)
        x2Tu = qmT
        attT = v3(balloc(8 * 512), 8, 512)
        prb = v3(balloc(2 * 512), 2, 512)
        xb = [balloc(1024) for _ in range(2)]
        smk = [v3(balloc(2 * 1024), 2, 1024)] * 2
        smv = [v3(balloc(2 * 1024), 2, 1024)] * 2
        smkT = v3(balloc(8 * 256), 8, 256)
        prs = balloc(1024)
        xr = [falloc(1024) for _ in range(2)]
        x1 = v3(falloc(4 * 1024), 4, 1024)
        tb_ = [falloc(1024) for _ in range(2)]
        rden = falloc(512)
        lnscr = falloc(16)
        for u in range(NUA):
            kind, ui, W, ntl = units[u]
            ub = u % 2
            tiles = unit_tiles(u)
            P.dma('sp', mixTu[ub][:, :, 0:W], mixT_s[u, :, :, 0:W], r=[f'mixT_s{u}'], w=['mixTuC'])
            for ti, t in enumerate(tiles):
                sl = t % 2
                P.dma('sp', xr[sl], xres_src(l, t), r=[f'xres_s{t}'], w=[f'xr{sl}'])
                proj_tm_ln(mixTu[ub], 'mixTuC', Wo, 'Wo', 8, ti, xr[sl], f'xr{sl}', tb_[sl], f'tb{sl}')
                layernorm(tb_[sl], f'tb{sl}', g1, b1, x1[:, ti, :], 'x1', xb[sl], f'xb{sl}', lnscr, 'lnscrC')
                transpose_tile(xb[sl], f'xb{sl}', x1Tu[:, :, ti * 128:(ti + 1) * 128], 'x1Tu', sl)
            for c in range(8):
                b = nb()
                for kc in range(8):
                    mm(pf[b][:, 0:W], Wq[:, kc, c * 128:(c + 1) * 128], x1Tu[:, kc, 0:W], kc == 0, kc == 7, r=['Wq', 'x1Tu'], w=[f'pf{b}'])
                act(qmT[:, c, 0:W], pf[b][:, 0:W], AF.Copy, r=[f'pf{b}'], w=['qmT'], scale=1.0 / 16.0)
            if dbg and os.environ.get('DBGC'):
                srcd = {'x1T': x1Tu, 'qmT': qmT}[os.environ['DBGC']]
                P.dma('sp', mixT_s[u, :, :, 0:W], srcd[:, :, 0:W], r=['x1Tu', 'qmT'], w=[f'mixT_s{u}'])
            if kind == 'p':
                for h in range(4):
                    for km in range(2):
                        b = nb()
                        for ec in range(2):
                            mm(pf[b][:, 0:W], mkT[:, 2 * h + ec, km * 128:(km + 1) * 128], qmT[:, 2 * h + ec, 0:W], ec == 0, ec == 1,
                               r=['mkT', 'qmT'], w=[f'pf{b}'])
                        act(prb[:, km, 0:W], pf[b][:, 0:W], AF.Exp, r=[f'pf{b}'], w=['prb'])
                    b = nb()
                    for km in range(2):
                        mm(pf[b][:, 0:W], ones_b, prb[:, km, 0:W], km == 0, km == 1, r=['ones_b', 'prb'], w=[f'pf{b}'])
                    P.op('dve', lambda e, b=b, W=W: e.reciprocal(out=rden[:, 0:W], in_=pf[b][:, 0:W]), r=[f'pf{b}'], w=['rden'])
                    for ec in range(2):
                        b = nb()
                        for km in range(2):
                            mm(pf[b][:, 0:W], mvb[:, km, (2 * h + ec) * 128:(2 * h + ec + 1) * 128], prb[:, km, 0:W], km == 0, km == 1,
                               r=['mvb', 'prb'], w=[f'pf{b}'])
                        tt('dve', attT[:, 2 * h + ec, 0:W], pf[b][:, 0:W], rden[:, 0:W], ALU.mult, r=[f'pf{b}', 'rden'], w=['attT'])
            else:
                xattn_sample(l, qmT, attT, smk, smv, smkT, prs, rden)
            for ti, t in enumerate(tiles):
                sl = t % 2
                proj_tm_ln(attT, 'attT', Wmo, 'Wmo', 8, ti, x1[:, ti, :], 'x1', tb_[sl], f'tb{sl}')
                if dbg:
                    P.dma('sp', ffp_s[t], tb_[sl], r=[f'tb{sl}'], w=[f'ffp_s{t}'])
                layernorm(tb_[sl], f'tb{sl}', g2, b2, xr[sl], f'xr{sl}', xb[sl], f'xb{sl}', lnscr, 'lnscrC')
                P.dma('sp', xres_s[t], xr[sl], r=[f'xr{sl}'], w=[f'xres_s{t}'])
                transpose_tile(xb[sl], f'xb{sl}', x2Tu[:, :, ti * 128:(ti + 1) * 128], 'qmT', sl)
            P.dma('sp', xT_s[u, :, :, 0:W], x2Tu[:, :, 0:W], r=['qmT'], w=[f'xT_s{u}'])
        AF_.release(); AB_.release()
        P.barrier()

    def xattn_sample(l, qmT, attT, smk, smv, smkT, prs, rden):
        prv = prs.rearrange("p (k s h q) -> p k s h q", k=2, s=NSS, h=4, q=SL)
        for si in range(NSS):
            sb_ = si % 2
            P.dma('pool', smk[sb_], cmk[l, si].rearrange("(m p) d -> p m d", p=128), w=['smkC'])
            P.dma('pool', smv[sb_], cmv[l, si].rearrange("(m p) d -> p m d", p=128), w=['smvC'])
            for half in range(2):
                pt = v3(pb[half][:, :], 8, 128)
                for j in range(8):
                    c = half * 4 + j // 2
                    km = j % 2
                    tr(pt[:, j, :], smk[sb_][:, km, c * 128:(c + 1) * 128], ident_b, r=['smkC', 'ident_b'], w=[f'pb{half}'])
                cp('act', smkT[:, half * 4:half * 4 + 4, :].rearrange("p c (m k) -> p (c m) k", m=2, k=128), pt, r=[f'pb{half}'], w=['smkT'])
            for km in range(2):
                for h in range(4):
                    for ec in range(2):
                        mm(pf[km][:, si * 32 + h * SL:si * 32 + (h + 1) * SL], smkT[:, 2 * h + ec, km * 128:(km + 1) * 128],
                           qmT[:, 2 * h + ec, si * SL:(si + 1) * SL], ec == 0, ec == 1, r=['smkT', 'qmT'], w=[f'pf{km}'])
            for km in range(2):
                act(prv[:, km, si], pf[km][:, si * 32:(si + 1) * 32].rearrange("p (h q) -> p h q", h=4, q=SL), AF.Exp,
                    r=[f'pf{km}'], w=['prs'])
            for h in range(4):
                for ec in range(2):
                    c = 2 * h + ec
                    ob = 2 + c // 4
                    oc = (c % 4) * 128 + si * SL
                    for km in range(2):
                        mm(pf[ob][:, oc:oc + SL], smv[sb_][:, km, c * 128:(c + 1) * 128], prv[:, km, si, h, :], km == 0, km == 1,
                           r=['smvC', 'prs'], w=[f'pf{ob}'])
        db = 4
        for km in range(2):
            mm(pf[db][:, :], ones_b, prs[:, km * 512:(km + 1) * 512], km == 0, km == 1, r=['ones_b', 'prs'], w=[f'pf{db}'])
        P.op('dve', lambda e: e.reciprocal(out=rden[:, 0:512], in_=pf[db][:, :]), r=[f'pf{db}'], w=['rden'])
        rv = rden[:, 0:512].rearrange("p (s h q) -> p s h q", s=NSS, h=4, q=SL)
        for c in range(8):
            h = c // 2
            ob = 2 + c // 4
            oc = (c % 4) * 128
            tt('dve', attT[:, c, 0:128].rearrange("p (s q) -> p s q", s=NSS, q=SL),
               pf[ob][:, oc:oc + 128].rearrange("p (s q) -> p s q", s=NSS, q=SL), rv[:, :, h, :], ALU.mult,
               r=[f'pf{ob}', 'rden'], w=['attT'])

    def stageD(l, f):
        AF_.mark(); AB_.mark()
        Wu = v3(balloc(8 * 2048), 8, 2048)
        Wd = v3(balloc(16 * 1024), 16, 1024)
        load_w(Wu, w_up[l], f * 2048, (f + 1) * 2048, 'Wu')
        load_w(Wd, w_down[l][f * 2048:(f + 1) * 2048, :], 0, 1024, 'Wd')
        g3 = falloc(1024); b3 = falloc(1024)
        if f == 1:
            load_ln(l, 2, g3, b3)
        xTu = [v3(balloc(8 * 512), 8, 512) for _ in range(2)]
        hid = v3(balloc(16 * 512), 16, 512)
        xoT = v3(balloc(8 * 512), 8, 512)
        xb = [balloc(1024) for _ in range(2)]
        rl = [falloc(512) for _ in range(2)]
        fft = [falloc(1024) for _ in range(2)]
        xr = [falloc(1024) for _ in range(2)]
        fp_ = [falloc(1024) for _ in range(2)]
        lnscr = falloc(16)
        lastl = (l == cfg.nlayers - 1)
        for u in range(NUA):
            kind, ui, W, ntl = units[u]
            ub = u % 2
            tiles = unit_tiles(u)
            P.dma('sp', xTu[ub][:, :, 0:W], xT_s[u, :, :, 0:W], r=[f'xT_s{u}'], w=[f'xTu{ub}'])
            for c in range(16):
                b = nb()
                for kc in range(8):
                    mm(pf[b][:, 0:W], Wu[:, kc, c * 128:(c + 1) * 128], xTu[ub][:, kc, 0:W], kc == 0, kc == 7, r=['Wu', f'xTu{ub}'], w=[f'pf{b}'])
                sl = c % 2
                act(rl[sl][:, 0:W], pf[b][:, 0:W], AF.Relu, r=[f'pf{b}'], w=[f'rl{sl}'])
                tt('dve' if c % 4 else 'pool', hid[:, c, 0:W], rl[sl][:, 0:W], rl[sl][:, 0:W], ALU.mult, r=[f'rl{sl}'], w=['hid'])
            for ti, t in enumerate(tiles):
                sl = t % 2
                if f == 1:
                    P.dma('sp', xr[sl], xres_s[t], r=[f'xres_s{t}'], w=[f'xr{sl}'])
                    P.dma('sp', fp_[sl], ffp_s[t], r=[f'ffp_s{t}'], w=[f'fp{sl}'])
                for hh in range(2):
                    b = nb()
                    for kc in range(16):
                        mm(pf[b][:, :], hid[:, kc, ti * 128:(ti + 1) * 128], Wd[:, kc, hh * 512:(hh + 1) * 512], kc == 0, kc == 15,
                           r=['hid', 'Wd'], w=[f'pf{b}'])
                    cs = slice(hh * 512, (hh + 1) * 512)
                    if f == 0:
                        cp('act', fft[sl][:, cs], pf[b][:, :], r=[f'pf{b}'], w=[f'fft{sl}'])
                    else:
                        tt('dve', fft[sl][:, cs], pf[b][:, :], fp_[sl][:, cs], ALU.add, r=[f'pf{b}', f'fp{sl}'], w=[f'fft{sl}'])
                        stt(fft[sl][:, cs], xr[sl][:, cs], ALPHA, fft[sl][:, cs], ALU.mult, ALU.add, r=[f'xr{sl}', f'fft{sl}'], w=[f'fft{sl}'])
                if f == 0:
                    P.dma('sp', ffp_s[t], fft[sl], r=[f'fft{sl}'], w=[f'ffp_s{t}'])
                else:
                    layernorm(fft[sl], f'fft{sl}', g3, b3, xr[sl], f'xr{sl}', None if lastl else xb[sl], f'xb{sl}', lnscr, 'lnscrD')
                    if lastl:
                        dst = y_p[t * 128:(t + 1) * 128, :] if kind == 'p' else y_s[:, :]
                        P.dma('sp', dst, xr[sl], r=[f'xr{sl}'], w=[f'o_y{t}'])
                    else:
                        P.dma('sp', xres_s[t], xr[sl], r=[f'xr{sl}'], w=[f'xres_s{t}'])
                        transpose_tile(xb[sl], f'xb{sl}', xoT[:, :, ti * 128:(ti + 1) * 128], 'xoT', sl)
            if f == 1 and not lastl:
                P.dma('sp', xT_s[u, :, :, 0:W], xoT[:, :, 0:W], r=['xoT'], w=[f'xT_s{u}'])
        AF_.release(); AB_.release()
        P.barrier()


    def stageA2(l):
        AF_.mark(); AB_.mark()
        W1 = v3(balloc(8 * RWC), 8, RWC)
        W2 = v3(balloc(8 * RWC), 8, RWC)
        load_w(W1, w_in[l], 1536, 1536 + RWC, 'W1')
        mu_t = falloc(RWC)
        bcast_row(mu_t, rw_mu[l], 'mu_t')
        for kc in range(8):
            tt('dve', W2[:, kc, :], W1[:, kc, :], mu_t, ALU.mult, r=['W1', 'mu_t'], w=['W2'])
            tt('pool', W1[:, kc, :], W1[:, kc, :], W2[:, kc, :], ALU.subtract, r=['W1', 'W2'], w=['W1'])
        LW = balloc(768)
        memset('pool', LW, 0.0, w=['LW'])
        P.dma('pool', LW[0:32, 0:256], rw_w2[l], r=['LW'], w=['LW'])
        P.dma('pool', LW[32:64, 256:512], rw_a2[l], r=['LW'], w=['LW'])
        P.dma('pool', LW[64:128, 512:768], rw_g2[l], r=['LW'], w=['LW'])
        prm = {}
        for nm, src in (('w0', rw_w0), ('a0', rw_a0), ('kks', rw_kk), ('ka', rw_ka), ('rk', rw_rk), ('gng', rw_gn_g), ('gnb', rw_gn_b)):
            prm[nm] = falloc(256)
            bcast_row(prm[nm], src[l], 'rwp')
        omka = falloc(256)
        ts('dve', omka, prm['ka'], -1.0, 1.0, ALU.mult, ALU.add, r=['rwp'], w=['rwp2'])
        m1 = {}
        m3 = {}
        tri = {}
        for kd in ('p', 's'):
            m1[kd] = balloc(1024); m3[kd] = balloc(512); tri[kd] = falloc(128)
            P.dma('pool', m1[kd], cin['rwm1_' + kd][:, :], w=['rwmask'])
            P.dma('pool', m3[kd], cin['rwm3_' + kd][:, :], w=['rwmask'])
            P.dma('sp', tri[kd], cin['tri_' + kd][:, :], w=['rwmask'])
        bdm = falloc(128)
        P.dma('sp', bdm, cin['bdm'][:, :], w=['rwmask'])
        selp = falloc(128)
        P.dma('sp', selp, cin['ident'][:, :], w=['rwmask'])
        oh = falloc(NSS * 128)
        P.dma('sp', oh[0:NSS, :], cin['oh'][:, :], w=['rwmask'])
        ssm = falloc(RWC)
        P.dma('sp', ssm[0:NSS, :], sshift[l], w=['ssm'])
        tt('dve', ssm[0:NSS, :], ssm[0:NSS, :], mu_t[0:NSS, :], ALU.mult, r=['ssm', 'mu_t'], w=['ssm'])
        def make_rw(sfx, bk, pbi, g_mm=mm, g_tr=tr, g_act=act, g_tt=tt, g_ts=ts, g_stt=stt, g_cp=cp, g_memset=memset):
            LK = {'loraT', 'rks', 'vsb', 'vbf', 'xw', 'lw', 'aa', 'gg', 'kk', 't1', 'sm', 'kkn', 't2', 'kef', 'Dinc', 'Dinv', 'Dexc',
                  'TM', 'TT', 'M1', 'M2', 'Q0', 'Q1', 'QT0', 'QT1', 'PT0', 'PT1', 'DCt', 'RHSb', 'Ub', 'ysb', 'sm2', 'sm3', 'ycb', 'STf',
                  'STb', 'xTu0', 'xTu1', 'xTq', 'Xn', 'brC', 'psh'}
            PM = {'pf0': f'pf{bk[0]}', 'pf1': f'pf{bk[1]}', 'pf2': f'pf{bk[2]}', 'pf3': f'pf{bk[0]}', 'pf4': f'pf{bk[1]}', 'pf5': f'pf{bk[2]}',
                  'pb0': f'pb{pbi}', 'pb1': f'pb{pbi}'}

            def kx(keys):
                return [PM.get(k, k + sfx if k in LK else k) for k in keys]

            def mm(out, lhsT, rhs, start, stop, r, w):
                return g_mm(out, lhsT, rhs, start, stop, kx(r), kx(w))

            def tr(out, in_, idt, r, w):
                return g_tr(out, in_, idt, kx(r), kx(w))

            def act(out, in_, func, r, w, **kw):
                return g_act(out, in_, func, kx(r), kx(w), **kw)

            def tt(eng, out, in0, in1, op, r, w):
                return g_tt(eng, out, in0, in1, op, kx(r), kx(w))

            def ts(eng, out, in0, s1, s2, op0, op1, r, w):
                return g_ts(eng, out, in0, s1, s2, op0, op1, kx(r), kx(w))

            def stt(out, in0, sc, in1, op0, op1, r, w):
                return g_stt(out, in0, sc, in1, op0, op1, kx(r), kx(w))

            def cp(eng, out, in_, r, w):
                return g_cp(eng, out, in_, kx(r), kx(w))

            def memset(eng, ap, val, w):
                return g_memset(eng, ap, val, kx(w))

            class PL_:
                def op(self, eng, fn, r=(), w=(), dma=False):
                    return P.op(eng, fn, r=kx(r), w=kx(w), dma=dma)

                def dma(self, q, out, in_, r=(), w=(), **kw):
                    return P.dma(q, out, in_, r=kx(r), w=kx(w), **kw)
            PL = PL_()
            pf = [pf_g[bk[0]], pf_g[bk[1]], pf_g[bk[2]], pf_g[bk[0]], pf_g[bk[1]], pf_g[bk[2]]]
            pb = [pb_g[pbi], pb_g[pbi]]
            xTu = [v3(balloc(8 * 513), 8, 513) for _ in range(2)]
            xTq = v3(balloc(8 * 136), 8, 136)
            TM = v3(balloc(4 * 256), 4, 256)
            TT = v3(balloc(8 * 128), 8, 128)
            M1 = v3(balloc(4 * 256), 4, 256)
            M2 = v3(balloc(4 * 256), 4, 256)
            Qb = [v3(balloc(4 * 128), 4, 128) for _ in range(2)]
            QTb = [v3(balloc(4 * 128), 4, 128) for _ in range(2)]
            PTb = [v3(balloc(4 * 128), 4, 128) for _ in range(2)]
            RHSb = balloc(256)
            Ub = balloc(256)
            vbf = balloc(256)
            STb = v3(balloc(256), 2, 128)
            loraT = balloc(128)
            ycb = balloc(256)
            brC = v3(balloc(2 * 512), 2, 512)
            rks = falloc(512); vsb = falloc(256); xw = falloc(256); lw = falloc(256); aa = falloc(256); gg = falloc(256)
            kk = falloc(256); kkn = falloc(256); kef = falloc(256); t1 = falloc(256); t2 = falloc(256)
            Dinc = falloc(256); Dexc = falloc(256); Dinv = falloc(256); ysb = falloc(256)
            sm = falloc(64)
            STf = v3(falloc(256), 2, 128)
            Xn = v3(falloc(256), 2, 128)
            DCt = falloc(4)
            psh = falloc(RWC)
            memset('dve', RHSb, 0.0, w=['RHSb'])
            memset('dve', Ub, 0.0, w=['Ub'])
            memset('pool', xTq, 0.0, w=['xTq'])
            memset('pool', Xn, 0.0, w=['Xn'])

            def rw_tile(xt, xk, c0, kd, chunks, extra_s=None):
                cur = lambda kc: xt[:, kc, c0:c0 + 128]
                prv = lambda kc: xt[:, kc, c0 - 1:c0 + 127]
                bl = 0
                n_mm = 16 + (1 if extra_s is not None else 0)
                i = 0
                for kc in range(8):
                    for (Wt, Wk, src) in ((W1, 'W1', cur(kc)), (W2, 'W2', prv(kc))):
                        mm(pf[bl][:, 0:128], Wt[:, kc, 768:896], src, i == 0, i == n_mm - 1, r=[Wk, xk], w=[f'pf{bl}'])
                        i += 1
                if extra_s is not None:
                    mm(pf[bl][:, 0:128], ssm[0:NSS, 768:896], oh[0:NSS, extra_s * 128:(extra_s + 1) * 128], False, True, r=['ssm', 'rwmask'], w=[f'pf{bl}'])
                act(loraT[0:32, :], pf[bl][0:32, 0:128], AF.Tanh, r=[f'pf{bl}'], w=['loraT'])
                act(loraT[32:64, :], pf[bl][32:64, 0:128], AF.Copy, r=[f'pf{bl}'], w=['loraT'])
                act(loraT[64:128, :], pf[bl][64:128, 0:128], AF.Sigmoid, r=[f'pf{bl}'], w=['loraT'])
                for (bk, ca, cb) in ((1, 0, 512), (2, 512, 768)):
                    i = 0
                    for kc in range(8):
                        for (Wt, Wk, src) in ((W1, 'W1', cur(kc)), (W2, 'W2', prv(kc))):
                            mm(pf[bk][:, 0:cb - ca], src, Wt[:, kc, ca:cb], i == 0, i == n_mm - 1, r=[Wk, xk], w=[f'pf{bk}'])
                            i += 1
                    if extra_s is not None:
                        mm(pf[bk][:, 0:cb - ca], oh[0:NSS, extra_s * 128:(extra_s + 1) * 128], ssm[0:NSS, ca:cb], False, True,
                           r=['ssm', 'rwmask'], w=[f'pf{bk}'])
                cp('act', rks, pf[1][:, :], r=['pf1'], w=['rks'])
                cp('act', vsb, pf[2][:, 0:256], r=['pf2'], w=['vsb'])
                cp('pool', vbf, vsb, r=['vsb'], w=['vbf'])
                mm(pf[3][:, :], loraT, LW[:, 0:512], True, True, r=['loraT', 'LW'], w=['pf3'])
                mm(pf[4][:, 0:256], loraT, LW[:, 512:768], True, True, r=['loraT', 'LW'], w=['pf4'])
                tt('dve', xw, pf[3][:, 0:256], prm['w0'], ALU.add, r=['pf3', 'rwp'], w=['xw'])
                act(xw, xw, AF.Sigmoid, r=['xw'], w=['xw'])
                ts('dve', lw, xw, -0.6065306597126334, None, ALU.mult, None, r=['xw'], w=['lw'])
                tt('dve', aa, pf[3][:, 256:512], prm['a0'], ALU.add, r=['pf3', 'rwp'], w=['aa'])
                act(aa, aa, AF.Sigmoid, r=['aa'], w=['aa'])
                cp('act', gg, pf[4][:, 0:256], r=['pf4'], w=['gg'])
                rr = rks[:, 0:256]
                kx = rks[:, 256:512]
                tt('pool', kk, kx, prm['kks'], ALU.mult, r=['rks', 'rwp'], w=['kk'])
                tt('dve', t1, kk, kk, ALU.mult, r=['kk'], w=['t1'])
                PL.op('dve', lambda e: e.tensor_reduce(out=sm[:, 0:4], in_=v3(t1, 4, 64), axis=AX.X, op=ALU.add), r=['t1'], w=['sm'])
                act(sm[:, 0:4], sm[:, 0:4], AF.Sqrt, r=['sm'], w=['sm'])
                ts('dve', sm[:, 0:4], sm[:, 0:4], 1e-12, None, ALU.max, None, r=['sm'], w=['sm'])
                PL.op('dve', lambda e: e.reciprocal(out=sm[:, 4:8], in_=sm[:, 0:4]), r=['sm'], w=['sm'])
                tt('dve', v3(kkn, 4, 64), v3(kk, 4, 64), sm[:, 4:8].unsqueeze(2).to_broadcast([128, 4, 64]), ALU.mult, r=['kk', 'sm'], w=['kkn'])
                tt('pool', t2, aa, prm['ka'], ALU.mult, r=['aa', 'rwp'], w=['t2'])
                tt('pool', t2, t2, omka, ALU.add, r=['t2', 'rwp2'], w=['t2'])
                tt('pool', kef, kx, t2, ALU.mult, r=['rks', 't2'], w=['kef'])
                tt('dve', t1, rr, kef, ALU.mult, r=['rks', 'kef'], w=['t1'])
                tt('dve', t1, t1, prm['rk'], ALU.mult, r=['t1', 'rwp'], w=['t1'])
                PL.op('dve', lambda e: e.tensor_reduce(out=sm[:, 8:12], in_=v3(t1, 4, 64), axis=AX.X, op=ALU.add), r=['t1'], w=['sm'])
                mm(pf[0][:, 0:256], tri[kd], lw, True, True, r=['rwmask', 'lw'], w=['pf0'])
                act(Dinc, pf[0][:, 0:256], AF.Exp, r=['pf0'], w=['Dinc'])
                act(Dinv, pf[0][:, 0:256], AF.Exp, r=['pf0'], w=['Dinv'], scale=-1.0)
                tt('dve', Dexc, pf[0][:, 0:256], lw, ALU.subtract, r=['pf0', 'lw'], w=['Dexc'])
                act(Dexc, Dexc, AF.Exp, r=['Dexc'], w=['Dexc'])
                stt(TM[:, 0, :], kkn, -1.0, Dexc, ALU.mult, ALU.mult, r=['kkn', 'Dexc'], w=['TM'])
                tt('pool', TM[:, 1, :], rr, Dinc, ALU.mult, r=['rks', 'Dinc'], w=['TM'])
                tt('dve', t1, kkn, aa, ALU.mult, r=['kkn', 'aa', 'sm'], w=['t1'])
                tt('dve', TM[:, 2, :], t1, Dinv, ALU.mult, r=['t1', 'Dinv'], w=['TM'])
                tt('pool', TM[:, 3, :], kef, Dinv, ALU.mult, r=['kef', 'Dinv'], w=['TM'])
                ptt = v3(pb[0][:, :], 8, 128)
                for hp in range(2):
                    for arr in range(4):
                        tr(ptt[:, hp * 4 + arr, :], TM[:, arr, hp * 128:(hp + 1) * 128], ident_b, r=['TM', 'ident_b'], w=['pb0'])
                cp('act', TT, ptt, r=['pb0'], w=['TT'])
                mk1 = v3(m1[kd], 4, 256)
                mk3 = v3(m3[kd], 4, 128)
                for par in range(2):
                    po = par * 64
                    for hp in range(2):
                        co = hp * 256
                        ar = TT[po:po + 64, hp * 4 + 0:hp * 4 + 2, :]
                        mm(pf[0][:, co:co + 256].rearrange("p (a t) -> p a t", a=2, t=128), TT[po:po + 64, hp * 4 + 2, :], ar, True, True,
                           r=['TT'], w=['pf0'])
                        mm(pf[1][:, co:co + 256].rearrange("p (a t) -> p a t", a=2, t=128), TT[po:po + 64, hp * 4 + 3, :], ar, True, True,
                           r=['TT'], w=['pf1'])
                        mm(pf[2][:, hp * 128:(hp + 1) * 128], TT[po:po + 64, hp * 4 + 0, :], TT[po:po + 64, hp * 4 + 2, :], True, True,
                           r=['TT'], w=['pf2'])
                    tt('dve', M1[:, par:4:2, :], v3(pf[0][:, :], 2, 256), mk1[:, 0:2, :], ALU.mult, r=['pf0', 'rwmask'], w=['M1'])
                    tt('dve', M2[:, par:4:2, :], v3(pf[1][:, :], 2, 256), mk1[:, 0:2, :], ALU.mult, r=['pf1', 'rwmask'], w=['M2'])
                    tt('dve', Qb[0][:, par:4:2, :], v3(pf[2][:, 0:256], 2, 128), mk3[:, 0:2, :], ALU.mult, r=['pf2', 'rwmask'], w=['Q0'])
                cp('pool', QTb[0], M1[:, :, 0:128], r=['M1'], w=['QT0'])
                tt('pool', PTb[0], M1[:, :, 0:128], ident_b.unsqueeze(1).to_broadcast([128, 4, 128]), ALU.add, r=['M1', 'ident_b'], w=['PT0'])
                nstep = 5 if kd == 'p' else 2
                for st_ in range(nstep):
                    a, b = st_ % 2, (st_ + 1) % 2
                    lastst = st_ == nstep - 1
                    for h in range(4):
                        mm(pf[0][:, h * 128:(h + 1) * 128], QTb[a][:, h, :], Qb[a][:, h, :], True, True, r=[f'QT{a}', f'Q{a}'], w=['pf0'])
                    cp('act', Qb[b], v3(pf[0][:, :], 4, 128), r=['pf0'], w=[f'Q{b}'])
                    if not lastst:
                        for h in range(4):
                            mm(pf[1][:, h * 128:(h + 1) * 128], Qb[a][:, h, :], QTb[a][:, h, :], True, True, r=[f'QT{a}', f'Q{a}'], w=['pf1'])
                        cp('dve', QTb[b], v3(pf[1][:, :], 4, 128), r=['pf1'], w=[f'QT{b}'])
                    for h in range(4):
                        mm(pf[2][:, h * 128:(h + 1) * 128], Qb[b][:, h, :], PTb[a][:, h, :], True, True, r=[f'Q{b}', f'PT{a}'], w=['pf2'])
                    tt('dve', PTb[b], v3(pf[2][:, :], 4, 128), PTb[a], ALU.add, r=['pf2', f'PT{a}'], w=[f'PT{b}'])
                PT = PTb[nstep % 2]
                ptk = f'PT{nstep % 2}'
                for ci, (r0, R) in enumerate(chunks):
                    for hp in range(2):
                        mm(pf[3][:, hp * 2 + ci:hp * 2 + ci + 1], Dinc[:, hp * 128:(hp + 1) * 128], selp[:, r0 + R - 1:r0 + R], True, True,
                           r=['Dinc', 'rwmask'], w=['pf3'])
                cp('act', DCt, pf[3][:, 0:4], r=['pf3'], w=['DCt'])
                for ci, (r0, R) in enumerate(chunks):
                    rs = slice(r0, r0 + 64)
                    cs_ = slice(r0, r0 + 64)
                    for hp in range(2):
                        mm(pf[4][rs, hp * 128:(hp + 1) * 128], TT[:, hp * 4 + 0, cs_], STb[:, hp, :], True, False, r=['TT', 'STb'], w=['pf4'])
                        for h2 in range(2):
                            h = hp * 2 + h2
                            mm(pf[4][rs, h * 64:(h + 1) * 64], M2[:, h, cs_], vbf[:, h * 64:(h + 1) * 64], False, h2 == 1, r=['M2', 'vbf'], w=['pf4'])
                    cp('act', RHSb[rs, :], pf[4][rs, 0:256], r=['pf4'], w=['RHSb'])
                    for h in range(4):
                        mm(pf[5][rs, h * 64:(h + 1) * 64], PT[:, h, cs_], RHSb[:, h * 64:(h + 1) * 64], True, True, r=[ptk, 'RHSb'], w=['pf5'])
                    cp('dve', Ub[rs, :], pf[5][rs, 0:256], r=['pf5'], w=['Ub'])
                    for hp in range(2):
                        mm(pf[4][rs, hp * 128:(hp + 1) * 128], TT[:, hp * 4 + 1, cs_], STb[:, hp, :], True, False, r=['TT', 'STb'], w=['pf4'])
                        for h2 in range(2):
                            h = hp * 2 + h2
                            mm(pf[4][rs, h * 64:(h + 1) * 64], M1[:, h, 128 + r0:128 + r0 + 64], Ub[:, h * 64:(h + 1) * 64], False, False, r=['M1', 'Ub'], w=['pf4'])
                            mm(pf[4][rs, h * 64:(h + 1) * 64], M2[:, h, 128 + r0:128 + r0 + 64], vbf[:, h * 64:(h + 1) * 64], False, h2 == 1, r=['M2', 'vbf'], w=['pf4'])
                    cp('act', ysb[rs, :], pf[4][rs, 0:256], r=['pf4'], w=['ysb'])
                    rr_ = slice(r0, r0 + R)
                    for hp in range(2):
                        mm(pf[5][:, hp * 128:(hp + 1) * 128], TM[rr_, 2, hp * 128:(hp + 1) * 128], Ub[rr_, hp * 128:(hp + 1) * 128], True, False, r=['TM', 'Ub'], w=['pf5'])
                        mm(pf[5][:, hp * 128:(hp + 1) * 128], TM[rr_, 3, hp * 128:(hp + 1) * 128], vbf[rr_, hp * 128:(hp + 1) * 128], False, True, r=['TM', 'vbf'], w=['pf5'])
                    for hp in range(2):
                        dc = DCt[:, hp * 2 + ci:hp * 2 + ci + 1]
                        tt('dve', t1[:, 0:128], pf[5][:, hp * 128:(hp + 1) * 128], bdm, ALU.mult, r=['pf5', 'rwmask'], w=['t1'])
                        ts('dve', STf[:, hp, :], STf[:, hp, :], dc, None, ALU.mult, None, r=['STf', 'DCt'], w=['STf'])
                        stt(STf[:, hp, :], t1[:, 0:128], dc, STf[:, hp, :], ALU.mult, ALU.add, r=['t1', 'DCt', 'STf'], w=['STf'])
                    cp('pool', STb, STf, r=['STf'], w=['STb'])
                y3 = v3(ysb, 4, 64)
                for h in range(4):
                    PL.op('dve', lambda e, h=h: e.bn_stats(out=sm[:, 16 + 6 * h:22 + 6 * h], in_=ysb[:, h * 64:(h + 1) * 64]), r=['ysb'], w=['sm2'])
                    PL.op('dve', lambda e, h=h: e.bn_aggr(out=sm[:, 40 + 2 * h:42 + 2 * h], in_=sm[:, 16 + 6 * h:22 + 6 * h]), r=['sm2'], w=['sm2'])
                mvv = v3(sm[:, 40:48], 4, 2)
                act(sm[:, 48:52], mvv[:, :, 1], AF.Sqrt, r=['sm2'], w=['sm3'], bias=GN_EPS, scale=1.0)
                PL.op('dve', lambda e: e.reciprocal(out=sm[:, 48:52], in_=sm[:, 48:52]), r=['sm3'], w=['sm3'])
                tt('dve', y3, y3, mvv[:, :, 0].unsqueeze(2).to_broadcast([128, 4, 64]), ALU.subtract, r=['ysb', 'sm2'], w=['ysb'])
                tt('dve', y3, y3, sm[:, 48:52].unsqueeze(2).to_broadcast([128, 4, 64]), ALU.mult, r=['ysb', 'sm3'], w=['ysb'])
                tt('pool', ysb, ysb, prm['gng'], ALU.mult, r=['ysb', 'rwp'], w=['ysb'])
                tt('pool', ysb, ysb, prm['gnb'], ALU.add, r=['ysb', 'rwp'], w=['ysb'])
                tt('dve', v3(t2, 4, 64), v3(vsb, 4, 64), sm[:, 8:12].unsqueeze(2).to_broadcast([128, 4, 64]), ALU.mult, r=['vsb', 'sm', 't2'], w=['t2'])
                tt('dve', ysb, ysb, t2, ALU.add, r=['ysb', 't2'], w=['ysb'])
                tt('dve', ycb, ysb, gg, ALU.mult, r=['ysb', 'gg'], w=['ycb'])

            def yc_to_brT(dst, dstk, ncols):
                pty = v3(pb[1][:, 0:256], 2, 128)
                for c in range(2):
                    tr(pty[:, c, :], ycb[:, c * 128:(c + 1) * 128], ident_b, r=['ycb', 'ident_b'], w=['pb1'])
                cp('act', dst, pty[:, :, 0:ncols], r=['pb1'], w=[dstk])

            def shift_out(xt, xk, col, dst):
                for (bk, ca, cb) in ((1, 0, 512), (2, 512, RWC)):
                    i = 0
                    for kc in range(8):
                        for (Wt, Wk) in ((W1, 'W1'), (W2, 'W2')):
                            mm(pf[bk][0:1, 0:cb - ca], xt[:, kc, col:col + 1], Wt[:, kc, ca:cb], i == 0, i == 15, r=[Wk, xk], w=[f'pf{bk}'])
                            i += 1
                    cp('act', psh[0:1, ca:cb], pf[bk][0:1, 0:cb - ca], r=[f'pf{bk}'], w=['psh'])
                PL.dma('sp', dst, psh[0:1, :], r=['psh'], w=['o_shift'])

            def state_out(dst4):
                for hp in range(2):
                    tr(pf[3][:, hp * 128:(hp + 1) * 128], STf[:, hp, :], ident_f, r=['STf', 'ident_f'], w=['pf3'])
                cp('act', v3(t1, 2, 128), v3(pf[3][:, 0:256], 2, 128), r=['pf3'], w=['t1'])
                t1v = v3(t1, 2, 128)
                for hp in range(2):
                    for h2 in range(2):
                        PL.dma('sp', dst4[hp * 2 + h2], t1v[h2 * 64:(h2 + 1) * 64, hp, h2 * 64:(h2 + 1) * 64], r=['t1'], w=['o_wkv'])


            def prompt_stream():
                memset('dve', STf, 0.0, w=['STf'])
                memset('pool', STb, 0.0, w=['STb'])
                for u in range(NU):
                    ub = u % 2
                    xt = xTu[ub]
                    xk = f'xTu{ub}'
                    PL.dma('sp', xt[:, :, 1:513], xT_s[u, :, :, :], r=[f'xT_s{u}'], w=[xk])
                    if u == 0:
                        memset('pool', xt[:, :, 0:1], 0.0, w=[xk])
                    else:
                        cp('pool', xt[:, :, 0:1], xTu[1 - ub][:, :, 512:513], r=[f'xTu{1 - ub}'], w=[xk])
                    for ti in range(4):
                        rw_tile(xt, xk, 1 + ti * 128, 'p', [(0, 64), (64, 64)])
                        yc_to_brT(brC[:, :, ti * 128:(ti + 1) * 128], 'brC', 128)
                    PL.dma('sp', brT_s[u, :, 4:6, :], brC, r=['brC'], w=[f'brT_s{u}c'])
                    if u == NU - 1:
                        shift_out(xt, xk, 512, p_shift[l:l + 1, :])
                state_out(p_wkv[l])

            def sample_stream():
                xts = xTu[0]
                PL.dma('sp', xts[:, :, 0:128], xT_s[NU, :, :, 0:128], r=[f'xT_s{NU}'], w=['xTu0'])
                for si in range(NSS):
                    cp('pool', xTq[:, :, 1:1 + SL], xts[:, :, si * SL:(si + 1) * SL], r=['xTu0'], w=['xTq'])
                    for hp in range(2):
                        for h2 in range(2):
                            PL.dma('sp', Xn[h2 * 64:(h2 + 1) * 64, hp, h2 * 64:(h2 + 1) * 64], swkv[l, si, hp * 2 + h2], r=['Xn'], w=['Xn'])
                    for hp in range(2):
                        tr(pf[3][:, hp * 128:(hp + 1) * 128], Xn[:, hp, :], ident_f, r=['Xn', 'ident_f'], w=['pf3'])
                    cp('act', STf, v3(pf[3][:, 0:256], 2, 128), r=['pf3'], w=['STf'])
                    cp('pool', STb, STf, r=['STf'], w=['STb'])
                    rw_tile(xTq, 'xTq', 1, 's', [(0, SL)], extra_s=si)
                    yc_to_brT(brC[:, :, si * SL:(si + 1) * SL], 'brC', SL)
                    shift_out(xTq, 'xTq', SL, s_shift[l, si:si + 1, :])
                    state_out(s_wkv[l, si])
                PL.dma('sp', brT_s[NU, :, 4:6, 0:128], brC[:, :, 0:128], r=['brC'], w=[f'brT_s{NU}c'])

            return prompt_stream, sample_stream

        pf_g, pb_g = pf, pb
        pstream, _ = make_rw('P', (0, 1, 2), 0)
        _, sstream = make_rw('S', (3, 4, 5), 1)
        P.rec_start()
        pstream()
        lp_ = P.rec_stop()
        P.rec_start()
        sstream()
        ls_ = P.rec_stop()
        P.merge([lp_, ls_])
        AF_.release(); AB_.release()
        P.barrier()

    if '0' in cfg.stages:
        stage0()
    for l in range(cfg.nlayers):
        if 'A' in cfg.stages:
            stageA1(l)
        if 'R' in cfg.stages:
            stageA2(l)
        if 'B' in cfg.stages:
            stageB(l)
        if 'C' in cfg.stages:
            stageC(l)
        if 'D' in cfg.stages:
            stageD(l, 0)
            stageD(l, 1)

    with nc.allow_non_contiguous_dma(reason="small parameter / state transfers"):
        P.emit()
    return P


def make_in_maps(cfg, inputs, ncores=8):
    consts = host_constants(cfg)
    f = lambda a: np.ascontiguousarray(np.asarray(a))
    maps = []
    nb = inputs['x_prompt'].shape[0]
    for c in range(ncores):
        b = c % nb
        ss = slice(NSS * c, NSS * (c + 1))
        m = {
            'xp': f(inputs['x_prompt'][b]),
            'xs': f(inputs['x_sample'][ss]).reshape(128, D),
            'memp': f(inputs['mem_prompt'][b]),
            'cache_k0': f(inputs['cache_k'][0]).reshape(-1, 256),
            'cache_k1': f(inputs['cache_k'][1]).reshape(-1, 256),
            'cache_v0': f(inputs['cache_v'][0]).reshape(-1, 256),
            'cache_v1': f(inputs['cache_v'][1]).reshape(-1, 256),
            'ptab': f(inputs['page_table'][ss]).reshape(-1).astype(np.int32),
            'cmk': f(inputs['cache_mem_k'][:, ss]).reshape(DEPTH, NSS, NMEM, D),
            'cmv': f(inputs['cache_mem_v'][:, ss]).reshape(DEPTH, NSS, NMEM, D),
            'sconv': f(inputs['state_conv'][:, ss]),
            'swkv': f(inputs['state_wkv'][:, ss]),
            'sshift': f(inputs['state_shift'][:, ss]),
            'w_branch': f(inputs['w_branch']).reshape(DEPTH, 4 * MIXW, D),
            'rw_rk': f(inputs['rw_rk']).reshape(DEPTH, MIXW),
        }
        for k in ['w_in', 'sb_bias', 'w_gate', 'b_gate', 'w_o', 'conv_w', 'rw_mu', 'rw_w0', 'rw_w2', 'rw_a0',
                  'rw_a2', 'rw_g2', 'rw_kk', 'rw_ka', 'rw_gn_g', 'rw_gn_b', 'sgu_ln_g', 'sgu_ln_b', 'sgu_ws',
                  'sgu_b', 'w_mq', 'w_mk', 'w_mv', 'w_mo', 'w_up', 'w_down', 'ln1_g', 'ln1_b', 'ln2_g', 'ln2_b',
                  'ln3_g', 'ln3_b']:
            m[k] = f(inputs[k])
        for k, v in consts.items():
            m['c_' + k] = v
        maps.append(m)
    return maps


def run(cfg, inputs, ncores=8):
    nc = build_program(cfg)
    maps = make_in_maps(cfg, inputs, ncores)
    res = run_bass_kernel_spmd(nc, maps, core_ids=list(range(ncores)))
    return res.results


def assemble(cfg, R, nb=4, ncores=8):
    SEQ = cfg.SEQ
    g = lambda name, cores: np.stack([R[c][name] for c in cores])
    pc = list(range(nb))
    ac = list(range(ncores))
    y_p = g('y_p', pc)
    y_s = g('y_s', ac).reshape(ncores * NSS, SL, D)
    def pl(name, shp):
        a = g(name, pc)
        return np.ascontiguousarray(np.moveaxis(a, 0, 1)).reshape(shp)
    def sl_(name, shp):
        a = g(name, ac)
        return np.ascontiguousarray(np.moveaxis(a, 0, 1)).reshape(shp)
    NS = ncores * NSS
    return (y_p, y_s,
            pl('p_k', (DEPTH, nb, SEQ, 4, 64)), pl('p_v', (DEPTH, nb, SEQ, 4, 64)),
            pl('p_mk', (DEPTH, nb, NMEM, 4, 256)), pl('p_mv', (DEPTH, nb, NMEM, 4, 256)),
            pl('p_conv', (DEPTH, nb, 2, MIXW)), pl('p_wkv', (DEPTH, nb, 4, 64, 64)), pl('p_shift', (DEPTH, nb, RWC)),
            sl_('s_k', (DEPTH, NS, SL, 4, 64)), sl_('s_v', (DEPTH, NS, SL, 4, 64)),
            sl_('s_conv', (DEPTH, NS, 2, MIXW)), sl_('s_wkv', (DEPTH, NS, 4, 64, 64)),
            sl_('s_shift', (DEPTH, NS, RWC)), sl_('s_chunk', (DEPTH, NS, SL, MIXW)))


def kernel(**inputs):
    SEQ = inputs['x_prompt'].shape[1]
    NPAGES = inputs['page_table'].shape[1]
    NPHYS = inputs['cache_k'].shape[1]
    cfg = Cfg(SEQ=SEQ, NPAGES=NPAGES, NPHYS=NPHYS)
    R = run(cfg, inputs)
    outs = assemble(cfg, R, nb=inputs['x_prompt'].shape[0])
    return tuple(np.ascontiguousarray(o.astype(np.float32)) for o in outs)
```
